# Optimizing a Trainium2 kernel written in Bass

```python
import jax, jax.numpy as jnp
from jax import lax
import numpy as np

D_MODEL = 1024
BATCH = 4
SEQ = 4096
DEPTH = 2

CHUNK = 64
N_META = 16
EPS = 1e-6
D_FF = 4 * D_MODEL

RET_HEADS = 4
RET_QK_DIM = D_MODEL // 8
RET_V_DIM = D_MODEL // 4
RET_QK = RET_HEADS * RET_QK_DIM
RET_V = RET_HEADS * RET_V_DIM
ROPE_BASE = 10000.0

LRU_WIDTH = D_MODEL
LRU_BLOCKS = 16
LRU_BLOCK_DIM = LRU_WIDTH // LRU_BLOCKS
LRU_C = 8.0
CONV_WIDTH = 4

S5_WIDTH = D_MODEL // 2
S5_GROUP = 16
S5_GROUPS = S5_WIDTH // S5_GROUP
S5_STATE = 64

HG_HEADS = 4
HG_DK = 128
HG_DV = 128
HG_WIDTH = HG_HEADS * HG_DK
HG_V = HG_HEADS * HG_DV

N_EVEN = (DEPTH + 1) // 2
N_ODD = DEPTH // 2
IN_AB = 2 * RET_QK + 2 * RET_V + 2 * LRU_WIDTH
OUT_AB = RET_V + LRU_WIDTH
IN_CD = S5_WIDTH + 3 * HG_WIDTH + HG_V
OUT_CD = S5_WIDTH + HG_V

kernel_name = "hybrid_retention_rglru_s5_hgrn2_block"

F32 = jnp.float32


def rms_norm(x, g):
    xf = x.astype(F32)
    y = xf * lax.rsqrt(jnp.mean(xf * xf, axis=-1, keepdims=True) + EPS)
    return (y * g.astype(F32)).astype(x.dtype)


def head_norm(o, g, center):
    if center:
        o = o - jnp.mean(o, axis=-1, keepdims=True)
    o = o * lax.rsqrt(jnp.mean(o * o, axis=-1, keepdims=True) + EPS)
    return o.reshape(o.shape[0], o.shape[1], -1) * g.astype(F32)


def split_cols(z, sizes):
    offs = np.cumsum([0] + list(sizes))
    return [z[..., int(offs[n]):int(offs[n + 1])] for n in range(len(sizes))]


def to_chunks(t):
    pad = CHUNK - N_META
    t = jnp.pad(t, [(0, 0), (pad, 0)] + [(0, 0)] * (t.ndim - 2))
    return t.reshape(t.shape[0], -1, CHUNK, *t.shape[2:])


def from_chunks(t):
    t = t.reshape(t.shape[0], -1, *t.shape[3:])
    return t[:, CHUNK - N_META:]


def rotary(x, pos):
    half = x.shape[-1] // 2
    inv = ROPE_BASE ** (-jnp.arange(half, dtype=F32) / half)
    ang = pos.astype(F32)[:, None] * inv[None, :]
    cos = jnp.cos(ang)[None, :, None, :]
    sin = jnp.sin(ang)[None, :, None, :]
    x1, x2 = x[..., :half], x[..., half:]
    return jnp.concatenate([x1 * cos - x2 * sin, x1 * sin + x2 * cos], axis=-1)


def retention(q, k, v, gate, gn_g):
    Bn, L, H, dk = q.shape
    pos = jnp.arange(L)
    q = rotary(q.astype(F32), pos) * (dk ** -0.5)
    k = rotary(k.astype(F32), pos)
    v = v.astype(F32).reshape(Bn, L, H, RET_V_DIM)
    log_g = jnp.log1p(-(2.0 ** (-5.0 - jnp.arange(H, dtype=F32))))
    qc, kc, vc = to_chunks(q), to_chunks(k), to_chunks(v)
    idx = jnp.arange(CHUNK, dtype=F32)
    intra_decay = jnp.exp(jnp.abs(idx[:, None] - idx[None, :])[None] * log_g[:, None, None])
    scores = jnp.einsum('bnihd,bnjhd->bnhij', qc, kc) * intra_decay
    o_intra = jnp.einsum('bnhij,bnjhe->bnihe', scores, vc)
    k_dec = kc * jnp.exp((CHUNK - 1 - idx)[:, None] * log_g[None, :])[..., None]
    kv = jnp.einsum('bnjhd,bnjhe->nbhde', k_dec, vc)
    chunk_decay = jnp.exp(CHUNK * log_g)[None, :, None, None]

    def step(state, kv_c):
        return chunk_decay * state + kv_c, state

    _, prev = lax.scan(step, jnp.zeros_like(kv[0]), kv)
    q_dec = qc * jnp.exp((idx + 1.0)[:, None] * log_g[None, :])[..., None]
    o_inter = jnp.einsum('bnihd,nbhde->bnihe', q_dec, prev)
    o = from_chunks(o_intra + o_inter)
    return head_norm(o, gn_g, True) * jax.nn.silu(gate.astype(F32))


def rg_lru(xb, w_a, b_a, w_i, b_i, lam, conv_w, conv_b):
    Bn, L, W = xb.shape
    xc = lax.conv_general_dilated(
        xb, conv_w[:, None, :].astype(xb.dtype), window_strides=(1,),
        padding=[(CONV_WIDTH - 1, 0)], dimension_numbers=('NWC', 'WIO', 'NWC'),
        feature_group_count=W).astype(F32) + conv_b.astype(F32)
    xg = xc.reshape(Bn, L, LRU_BLOCKS, LRU_BLOCK_DIM)
    r = jax.nn.sigmoid(jnp.einsum('blhi,hij->blhj', xg, w_a.astype(F32)).reshape(Bn, L, W) + b_a)
    i = jax.nn.sigmoid(jnp.einsum('blhi,hij->blhj', xg, w_i.astype(F32)).reshape(Bn, L, W) + b_i)
    log_a = -LRU_C * r * jax.nn.softplus(-lam.astype(F32))
    a = jnp.exp(log_a)
    b = jnp.sqrt(-jnp.expm1(2.0 * log_a)) * (i * xc)

    def comb(lhs, rhs):
        a1, b1 = lhs
        a2, b2 = rhs
        return a1 * a2, a2 * b1 + b2

    _, h = lax.associative_scan(comb, (a, b), axis=1)
    return h


def s5(u, a_re_log, a_im, b_re, b_im, c_re, c_im, d, log_dt, glu_w, glu_b):
    Bn, L, _ = u.shape
    ug = u.astype(F32).reshape(Bn, L, S5_GROUPS, S5_GROUP)
    dt = jnp.exp(log_dt.astype(F32))[:, None]
    a_re = -jnp.exp(a_re_log.astype(F32))
    a_im = a_im.astype(F32)
    mag = jnp.exp(dt * a_re)
    ab_re = mag * jnp.cos(dt * a_im)
    ab_im = mag * jnp.sin(dt * a_im)
    n_re, n_im = ab_re - 1.0, ab_im
    den = a_re * a_re + a_im * a_im
    z_re = (n_re * a_re + n_im * a_im) / den
    z_im = (n_im * a_re - n_re * a_im) / den
    b_re = b_re.astype(F32)
    b_im = b_im.astype(F32)
    bb_re = z_re[..., None] * b_re - z_im[..., None] * b_im
    bb_im = z_re[..., None] * b_im + z_im[..., None] * b_re
    bu_re = jnp.einsum('blgc,gpc->blgp', ug, bb_re)
    bu_im = jnp.einsum('blgc,gpc->blgp', ug, bb_im)
    ar = jnp.broadcast_to(ab_re, bu_re.shape)
    ai = jnp.broadcast_to(ab_im, bu_re.shape)

    def comb(lhs, rhs):
        ar1, ai1, br1, bi1 = lhs
        ar2, ai2, br2, bi2 = rhs
        return (ar1 * ar2 - ai1 * ai2, ar1 * ai2 + ai1 * ar2,
                ar2 * br1 - ai2 * bi1 + br2, ar2 * bi1 + ai2 * br1 + bi2)

    _, _, h_re, h_im = lax.associative_scan(comb, (ar, ai, bu_re, bu_im), axis=1)
    y = (jnp.einsum('blgp,gcp->blgc', h_re, c_re.astype(F32))
         - jnp.einsum('blgp,gcp->blgc', h_im, c_im.astype(F32))
         + d.astype(F32) * ug)
    y = jax.nn.gelu(y.reshape(Bn, L, S5_WIDTH))
    return y * jax.nn.sigmoid(y @ glu_w.astype(F32) + glu_b.astype(F32))


def hgrn2(q, f_pre, i, g, lb, gn_g):
    Bn, L, H, dk = q.shape
    lb = lb.astype(F32).reshape(H, dk)
    f = lb + (1.0 - lb) * jax.nn.sigmoid(f_pre.astype(F32))
    log_f = jnp.log(f)
    k = 1.0 - f
    qc = to_chunks(q.astype(F32))
    kc = to_chunks(k)
    ic = to_chunks(i.astype(F32))
    cum = jnp.cumsum(to_chunks(log_f), axis=2)
    xs = tuple(jnp.moveaxis(t, 1, 0) for t in (qc, kc, ic, cum))
    mask = jnp.tril(jnp.ones((CHUNK, CHUNK), dtype=bool))[None, :, :, None, None]

    def step(S, inp):
        qb, kb, ib, cb = inp
        diff = cb[:, :, None] - cb[:, None, :]
        w = jnp.exp(jnp.where(mask, diff, -jnp.inf))
        att = jnp.einsum('bthd,btshd->bhts', qb, w * kb[:, None])
        o_intra = jnp.einsum('bhts,bshe->bthe', att, ib)
        o_inter = jnp.einsum('bthd,bhde->bthe', qb * jnp.exp(cb), S)
        total = cb[:, -1]
        k_dec = kb * jnp.exp(total[:, None] - cb)
        S_new = jnp.exp(total)[..., None] * S + jnp.einsum('bshd,bshe->bhde', k_dec, ib)
        return S_new, o_intra + o_inter

    S0 = jnp.zeros((Bn, H, dk, HG_DV), F32)
    _, o = lax.scan(step, S0, xs)
    o = from_chunks(jnp.moveaxis(o, 0, 1))
    return head_norm(o, gn_g, False) * jax.nn.silu(g.astype(F32))


def mixer_ab(h, w_in, w_out, ret_gn, rg_wa, rg_ba, rg_wi, rg_bi, rg_lam, rg_conv_w, rg_conv_b):
    Bn, L, _ = h.shape
    z = h @ w_in
    q, k, v, gate, bx, bg = split_cols(z, [RET_QK, RET_QK, RET_V, RET_V, LRU_WIDTH, LRU_WIDTH])
    ya = retention(q.reshape(Bn, L, RET_HEADS, RET_QK_DIM), k.reshape(Bn, L, RET_HEADS, RET_QK_DIM),
                   v, gate, ret_gn)
    yb = jax.nn.gelu(bg.astype(F32)) * rg_lru(bx, rg_wa, rg_ba, rg_wi, rg_bi, rg_lam, rg_conv_w, rg_conv_b)
    return jnp.concatenate([ya, yb], axis=-1) @ w_out.astype(F32)


def mixer_cd(h, w_in, w_out, a_re_log, a_im, b_re, b_im, c_re, c_im, d, log_dt, glu_w, glu_b, hg_gn, lb):
    Bn, L, _ = h.shape
    z = h @ w_in
    u, q, f, i, g = split_cols(z, [S5_WIDTH, HG_WIDTH, HG_WIDTH, HG_V, HG_V])
    yc = s5(u, a_re_log, a_im, b_re, b_im, c_re, c_im, d, log_dt, glu_w, glu_b)
    yd = hgrn2(q.reshape(Bn, L, HG_HEADS, HG_DK), f.reshape(Bn, L, HG_HEADS, HG_DK),
               i.reshape(Bn, L, HG_HEADS, HG_DV), g, lb, hg_gn)
    return jnp.concatenate([yc, yd], axis=-1) @ w_out.astype(F32)


def sq_relu_mlp(h, w1, w2):
    a = jax.nn.relu(h @ w1)
    return (a * a) @ w2


def setup_inputs(seed: int = 0) -> dict:
    key = jax.random.key(seed)
    ks = iter(jax.random.split(key, 40))
    nrm = lambda shape, s: jax.random.normal(next(ks), shape, F32) * s
    u_lru = jax.random.uniform(next(ks), (N_EVEN, LRU_WIDTH), F32, 0.9, 0.999)
    a_lru = u_lru ** (1.0 / LRU_C)
    return {
        "x": nrm((BATCH, SEQ, D_MODEL), 1.0),
        "meta": nrm((N_META, D_MODEL), 1.0),
        "w_in_ab": nrm((N_EVEN, D_MODEL, IN_AB), D_MODEL ** -0.5),
        "w_out_ab": nrm((N_EVEN, OUT_AB, D_MODEL), OUT_AB ** -0.5),
        "ret_gn": 1.0 + nrm((N_EVEN, RET_V), 0.02),
        "rg_wa": nrm((N_EVEN, LRU_BLOCKS, LRU_BLOCK_DIM, LRU_BLOCK_DIM), LRU_BLOCK_DIM ** -0.5),
        "rg_ba": nrm((N_EVEN, LRU_WIDTH), 0.02),
        "rg_wi": nrm((N_EVEN, LRU_BLOCKS, LRU_BLOCK_DIM, LRU_BLOCK_DIM), LRU_BLOCK_DIM ** -0.5),
        "rg_bi": nrm((N_EVEN, LRU_WIDTH), 0.02),
        "rg_lam": jnp.log(a_lru) - jnp.log1p(-a_lru),
        "rg_conv_w": nrm((N_EVEN, CONV_WIDTH, LRU_WIDTH), CONV_WIDTH ** -0.5),
        "rg_conv_b": nrm((N_EVEN, LRU_WIDTH), 0.02),
        "w_in_cd": nrm((N_ODD, D_MODEL, IN_CD), D_MODEL ** -0.5),
        "w_out_cd": nrm((N_ODD, OUT_CD, D_MODEL), OUT_CD ** -0.5),
        "s5_a_re_log": jnp.log(0.5) + nrm((N_ODD, S5_GROUPS, S5_STATE), 0.02),
        "s5_a_im": jnp.broadcast_to(jnp.pi * jnp.arange(S5_STATE, dtype=F32), (N_ODD, S5_GROUPS, S5_STATE))
                   + nrm((N_ODD, S5_GROUPS, S5_STATE), 0.02),
        "s5_b_re": nrm((N_ODD, S5_GROUPS, S5_STATE, S5_GROUP), (2 * S5_GROUP) ** -0.5),
        "s5_b_im": nrm((N_ODD, S5_GROUPS, S5_STATE, S5_GROUP), (2 * S5_GROUP) ** -0.5),
        "s5_c_re": nrm((N_ODD, S5_GROUPS, S5_GROUP, S5_STATE), (2 * S5_STATE) ** -0.5),
        "s5_c_im": nrm((N_ODD, S5_GROUPS, S5_GROUP, S5_STATE), (2 * S5_STATE) ** -0.5),
        "s5_d": nrm((N_ODD, S5_GROUPS, S5_GROUP), 1.0),
        "s5_log_dt": jax.random.uniform(next(ks), (N_ODD, S5_GROUPS), F32, np.log(1e-3), np.log(1e-1)),
        "s5_glu_w": nrm((N_ODD, S5_WIDTH, S5_WIDTH), S5_WIDTH ** -0.5),
        "s5_glu_b": nrm((N_ODD, S5_WIDTH), 0.02),
        "hg_gn": 1.0 + nrm((N_ODD, HG_V), 0.02),
        "hg_lb_logits": nrm((DEPTH, HG_WIDTH), 0.5),
        "norm_g": 1.0 + nrm((DEPTH, 4, D_MODEL), 0.02),
        "mlp_w1": nrm((DEPTH, D_MODEL, D_FF), D_MODEL ** -0.5),
        "mlp_w2": nrm((DEPTH, D_FF, D_MODEL), D_FF ** -0.5),
    }


def reference(x, meta, w_in_ab, w_out_ab, ret_gn, rg_wa, rg_ba, rg_wi, rg_bi, rg_lam, rg_conv_w, rg_conv_b,
              w_in_cd, w_out_cd, s5_a_re_log, s5_a_im, s5_b_re, s5_b_im, s5_c_re, s5_c_im, s5_d, s5_log_dt,
              s5_glu_w, s5_glu_b, hg_gn, hg_lb_logits, norm_g, mlp_w1, mlp_w2):
    Bn = x.shape[0]
    h = jnp.concatenate([jnp.broadcast_to(meta[None].astype(x.dtype), (Bn, N_META, D_MODEL)), x], axis=1)
    p = jax.nn.softmax(hg_lb_logits.astype(F32), axis=0)
    lb_all = jnp.cumsum(p, axis=0) - p
    for l in range(DEPTH):
        j = l // 2
        hn = rms_norm(h, norm_g[l, 0])
        if l % 2 == 0:
            m = mixer_ab(hn, w_in_ab[j], w_out_ab[j], ret_gn[j], rg_wa[j], rg_ba[j], rg_wi[j], rg_bi[j],
                         rg_lam[j], rg_conv_w[j], rg_conv_b[j])
        else:
            m = mixer_cd(hn, w_in_cd[j], w_out_cd[j], s5_a_re_log[j], s5_a_im[j], s5_b_re[j], s5_b_im[j],
                         s5_c_re[j], s5_c_im[j], s5_d[j], s5_log_dt[j], s5_glu_w[j], s5_glu_b[j],
                         hg_gn[j], lb_all[l])
        h = h + rms_norm(m, norm_g[l, 1])
        hn = rms_norm(h, norm_g[l, 2])
        h = h + rms_norm(sq_relu_mlp(hn, mlp_w1[l], mlp_w2[l]), norm_g[l, 3])
    return h[:, N_META:]
```

```python
import numpy as np
import concourse.bass as bass
import concourse.mybir as mybir
from concourse.bass_utils import run_bass_kernel_spmd

F32 = mybir.dt.float32
BF16 = mybir.dt.bfloat16
I32 = mybir.dt.int32
AF = mybir.ActivationFunctionType
ALU = mybir.AluOpType
AX = mybir.AxisListType

T = 256
NT = T // 128
SEQ = 4096
NMETA = 16
D = 1024
DFF = 4096
EPS = 1e-6
NG = 1 + SEQ // T
import os
DEBUG = bool(int(os.environ.get("KDEBUG", "0")))
KNG = int(os.environ.get("KNG", "1000"))
KSTOP = int(os.environ.get("KSTOP", "3"))
KPHASE = os.environ.get("KPHASE", "all")

ENGS = ("pe", "act", "dve", "pool", "sp")


class Ins:
    __slots__ = ("eng", "fn", "waits", "idx", "needs_inc", "chan", "clock", "ninc", "val", "tag")


class Sched:
    def __init__(self, nc):
        self.nc = nc
        self.lists = {e: [] for e in ENGS}
        self.lastw = {}
        self.readers = {}
        self.clock = {e: {} for e in ENGS}
        self.chan_last = {}
        self.chan_ins = {}

    def _need(self, eng, dep, waits):
        src = dep.chan if dep.chan is not None else dep.eng
        if dep.chan is None and dep.eng == "pe" and eng == "pe":
            return
        if self.clock[eng].get(src, -1) >= dep.idx:
            return
        waits[src] = max(waits.get(src, -1), dep.idx)

    def op(self, eng, fn, reads=(), writes=(), chan=None, extra_deps=(), ninc=1):
        ins = Ins()
        ins.eng, ins.fn, ins.chan, ins.needs_inc, ins.ninc = eng, fn, chan, False, ninc
        ins.tag = getattr(self, "tag", "")
        psr = [k for k in reads if k[:2] == "ps" and k[2:].isdigit()]
        if psr:
            reads = [k for k in reads if k not in psr]
            writes = list(writes) + [k for k in psr if k not in writes]
        deps = []
        for k in reads:
            deps.extend(self.lastw.get(k, ()))
        for k in writes:
            deps.extend(self.lastw.get(k, ()))
            deps.extend(self.readers.get(k, ()))
        deps.extend(extra_deps)
        if chan is not None and chan in self.chan_last:
            deps.append(self.chan_last[chan])
        waits = {}
        for d in deps:
            self._need(eng, d, waits)
        ins.waits = []
        ck = self.clock[eng]
        for src, idx in waits.items():
            prod = self.lists[src][idx] if src in ENGS else self.chan_ins[src][idx]
            prod.needs_inc = True
            ins.waits.append(prod)
            for s, v in prod.clock.items():
                if ck.get(s, -1) < v:
                    ck[s] = v
        if chan is not None:
            lst = self.chan_ins.setdefault(chan, [])
            ins.idx = len(lst)
            lst.append(ins)
            self.chan_last[chan] = ins
            ins.clock = dict(ck)
            ins.clock[chan] = ins.idx
            self.lists[eng].append(ins)
        else:
            ins.idx = len(self.lists[eng])
            ins.clock = dict(ck)
            ins.clock[eng] = ins.idx
            self.lists[eng].append(ins)
        for k in reads:
            self.readers.setdefault(k, []).append(ins)
        me = chan if chan is not None else eng
        for k in writes:
            lst = [w for w in self.lastw.get(k, ()) if (w.chan if w.chan is not None else w.eng) != me]
            lst.append(ins)
            self.lastw[k] = lst
            self.readers[k] = []
        return ins

    def emit(self):
        nc = self.nc
        sems = {}
        for e in ("pe", "act", "dve", "pool"):
            sems[e] = nc.alloc_semaphore("s_" + e)
        for c in self.chan_ins:
            sems[c] = nc.alloc_semaphore("c_" + str(c))
        for e in ENGS:
            r = 0
            for ins in self.lists[e]:
                if ins.chan is None and ins.needs_inc:
                    r += 1
                    ins.val = r
        for c, lst in self.chan_ins.items():
            v = 0
            for ins in lst:
                v += 16 * ins.ninc
                ins.val = v

        def run(e):
            def body(eng):
                for ins in self.lists[e]:
                    for p in ins.waits:
                        src = p.chan if p.chan is not None else p.eng
                        eng.wait_ge(sems[src], p.val)
                    r = ins.fn(eng)
                    rl = r if isinstance(r, (list, tuple)) else [r]
                    if ins.chan is not None:
                        assert len(rl) == ins.ninc
                        for x in rl:
                            x.then_inc(sems[ins.chan], 16)
                    elif ins.needs_inc:
                        rl[-1].then_inc(sems[e], 1)
            return body

        with nc.Block() as block:
            block.tensor(run("pe"))
            block.scalar(run("act"))
            block.vector(run("dve"))
            block.gpsimd(run("pool"))
            block.sync(run("sp"))


class DB:
    def __init__(self, nc, name, shape, dtype, nbuf=2):
        self.bufs = [nc.alloc_sbuf_tensor(f"sb_{name}_{i}", list(shape), dtype) for i in range(nbuf)]
        self.keys = [f"{name}_{i}" for i in range(nbuf)]
        self.i = 0

    def next(self):
        i = self.i
        self.i = (i + 1) % len(self.bufs)
        return self.bufs[i], self.keys[i]


class PsumPool:
    def __init__(self, nc):
        self.banks = [nc.alloc_psum_tensor(f"ps{i}", [128, 512], F32) for i in range(8)]
        self.i = 0
        self.j = 0

    def half(self):
        u = self.i
        self.i = (u + 1) % 16
        h, b = divmod(u, 8)
        return self.banks[b][:, h * 256:(h + 1) * 256], [f"ps{b}"]

    def full(self):
        b = self.j
        self.j = (b + 1) % 8
        return self.banks[b][:, :], [f"ps{b}"]


GAMMA = [1.0 - 2.0 ** (-5.0 - h) for h in range(4)]


def host_constants():
    c = {}
    c["ident"] = np.eye(128, dtype=np.float32)
    L = NMETA + SEQ
    pos = np.arange(L, dtype=np.float32)
    inv = (10000.0 ** (-np.arange(64, dtype=np.float32) / 64)).astype(np.float32)
    ang = (pos[:, None] * inv[None, :]).astype(np.float32)
    c["rope"] = np.concatenate([np.cos(ang), np.sin(ang)], axis=1).astype(np.float32)
    s = np.arange(128)[:, None]
    t = np.arange(128)[None, :]
    retD = np.zeros((128, 4, 128), np.float64)
    retG = np.zeros((128, 4), np.float64)
    retKD = np.zeros((128, 4), np.float64)
    retKDm = np.zeros((128, 4), np.float64)
    for h in range(4):
        g = GAMMA[h]
        same = (s // 64) == (t // 64)
        earlier = (s // 64) < (t // 64)
        w = np.where(same, g ** np.abs(t - s), np.where(earlier, g ** (t - s).clip(0), 0.0))
        retD[:, h, :] = w / (g ** (t + 1.0))
        retG[:, h] = (128 ** -0.5) * g ** (np.arange(128) + 1.0)
        retKD[:, h] = g ** (127.0 - np.arange(128))
        retKDm[:16, h] = g ** (15.0 - np.arange(16))
    c["retD"] = retD.astype(np.float32)
    c["retS"] = np.concatenate([retG, retKD, retKDm], axis=1).astype(np.float32)
    c["triu"] = (s <= t).astype(np.float32)
    rs = np.ones((128, T), np.float32)
    rs[:, ::64] = 0.0
    c["restart"] = rs
    c["iota"] = np.broadcast_to(np.arange(T, dtype=np.float32), (128, T)).copy()
    mb = np.zeros((128, 16, 8), np.float32)
    for m in range(16):
        for gl in range(2):
            mb[gl * 64:(gl + 1) * 64, m, (2 * m + gl) % 8] = 1.0
    c["maskB"] = mb
    mc = np.zeros((128, 4, 2, 2), np.float32)
    for mm in range(4):
        for gl in range(2):
            g8 = 2 * mm + gl
            mc[g8 * 16:(g8 + 1) * 16, mm, gl, 0] = 1.0
            mc[g8 * 16:(g8 + 1) * 16, mm, gl, 1] = -1.0
    c["maskC"] = mc
    return c


def host_layouts(inp):
    f = lambda a: np.ascontiguousarray(np.asarray(a, dtype=np.float32))
    o = {}
    o["meta"] = f(inp["meta"])
    o["w_in_ab"] = f(inp["w_in_ab"][0])
    o["w_out_ab"] = f(inp["w_out_ab"][0])
    o["w1"] = f(inp["mlp_w1"])
    o["w2"] = f(inp["mlp_w2"])
    o["w_in_cd"] = f(inp["w_in_cd"][0])
    o["w_out_cd"] = f(inp["w_out_cd"][0])
    o["glu_w"] = f(inp["s5_glu_w"][0])
    ng = np.asarray(inp["norm_g"], np.float32)
    pre = ng[:, [0, 2], :].reshape(2, 2, 8, 128)
    o["pgT"] = f(pre.transpose(3, 0, 1, 2).reshape(128, 32))
    o["postg"] = f(ng[:, [1, 3], :].reshape(4, 1024))
    gn = np.concatenate([np.asarray(inp["ret_gn"][0], np.float32).reshape(8, 128).T,
                         np.asarray(inp["hg_gn"][0], np.float32).reshape(4, 128).T], axis=1)
    o["gnT"] = f(gn)
    fm8 = lambda v: np.asarray(v, np.float32).reshape(8, 128).T
    cw = np.asarray(inp["rg_conv_w"][0], np.float32)
    lru = np.stack([fm8(inp["rg_lam"][0]), fm8(inp["rg_ba"][0]), fm8(inp["rg_bi"][0]), fm8(inp["rg_conv_b"][0]),
                    fm8(cw[0]), fm8(cw[1]), fm8(cw[2]), fm8(cw[3])], axis=2)
    o["lru_s"] = f(lru)
    o["rg_wa"] = f(inp["rg_wa"][0])
    o["rg_wi"] = f(inp["rg_wi"][0])
    hl = np.asarray(inp["hg_lb_logits"], np.float32).reshape(2, 4, 128)
    o["hgl"] = f(hl.transpose(2, 0, 1).reshape(128, 8))
    st = lambda a: np.asarray(a, np.float32).reshape(16, 2, 64).transpose(1, 2, 0).reshape(128, 16)
    ldt = np.repeat(np.asarray(inp["s5_log_dt"][0], np.float32)[:, None], 64, axis=1)
    o["s5s"] = f(np.stack([st(inp["s5_a_re_log"][0]), st(inp["s5_a_im"][0]), st(ldt)], axis=2))
    sb = lambda a: np.asarray(a, np.float32).reshape(16, 2, 64, 16).transpose(1, 2, 0, 3).reshape(128, 16, 16)
    o["s5b"] = f(np.stack([sb(inp["s5_b_re"][0]), sb(inp["s5_b_im"][0])], axis=2))
    sc = lambda a: np.asarray(a, np.float32).reshape(4, 8, 16, 64).transpose(1, 2, 0, 3).reshape(128, 4, 64)
    o["s5c"] = f(np.stack([sc(inp["s5_c_re"][0]), sc(inp["s5_c_im"][0])], axis=2))
    sd = np.asarray(inp["s5_d"][0], np.float32).reshape(4, 128).T
    gb = np.asarray(inp["s5_glu_b"][0], np.float32).reshape(4, 128).T
    o["s5dg"] = f(np.concatenate([sd, gb], axis=1))
    return o


def build_program():
    nc = bass.Bass("TRN2", target_bir_lowering=False)
    S = Sched(nc)
    consts = host_constants()

    def din(name, shape, dt=F32):
        return nc.dram_tensor(name, list(shape), dt, kind="ExternalInput").ap()

    x_d = din("x", [SEQ, D])
    meta_d = din("meta", [NMETA, D])
    wdefs = [("w_in_ab", 1024, 5120), ("w_out_ab", 2048, 1024), ("w1_0", 1024, 4096), ("w2_0", 4096, 1024),
             ("w_in_cd", 1024, 2560), ("w_out_cd", 1024, 1024), ("w1_1", 1024, 4096), ("w2_1", 4096, 1024),
             ("glu_w", 512, 512)]
    w1_d = din("w1", [2, 1024, 4096])
    w2_d = din("w2", [2, 4096, 1024])
    wsrc = {"w_in_ab": din("w_in_ab", [1024, 5120]), "w_out_ab": din("w_out_ab", [2048, 1024]),
            "w1_0": w1_d[0], "w1_1": w1_d[1], "w2_0": w2_d[0], "w2_1": w2_d[1],
            "w_in_cd": din("w_in_cd", [1024, 2560]), "w_out_cd": din("w_out_cd", [1024, 1024]),
            "glu_w": din("glu_w", [512, 512])}
    wscr = {n: (nc.dram_tensor("scr_" + n, [r, c], BF16, kind="Internal").ap() if n == "glu_w" else
                nc.dram_tensor("scr_" + n, [r // 1024, c // 512, 128, 8, 512], BF16, kind="Internal").ap())
            for n, r, c in wdefs}
    small = {}
    for n, shp in [("pgT", [128, 32]), ("postg", [4, 1024]), ("gnT", [128, 12]), ("lru_s", [128, 8, 8]),
                   ("rg_wa", [16, 64, 64]), ("rg_wi", [16, 64, 64]), ("hgl", [128, 8]), ("s5s", [128, 16, 3]),
                   ("s5b", [128, 16, 2, 16]), ("s5c", [128, 4, 2, 64]), ("s5dg", [128, 8])]:
        small[n] = din(n, shp)
    cd = {n: din("c_" + n, list(v.shape)) for n, v in consts.items()}
    out_d = nc.dram_tensor("out", [SEQ, D], F32, kind="ExternalOutput").ap()
    dbg_d = nc.dram_tensor("dbg", [4, NMETA + SEQ, D], F32, kind="ExternalOutput").ap() if DEBUG else None

    def sb(name, shape, dt=F32):
        return nc.alloc_sbuf_tensor("sb_" + name, list(shape), dt)

    def MM(out, lhsT, rhs, start, stop, r, w):
        S.op("pe", lambda e: e.matmul(out, lhsT=lhsT, rhs=rhs, start=start, stop=stop), reads=r, writes=w)

    def TR(out, in_, idn, r, w):
        S.op("pe", lambda e: e.transpose(out, in_, idn), reads=r, writes=w)

    def ACT(out, in_, func, r, w, **kw):
        S.op("act", lambda e: e.activation(out=out, in_=in_, func=func, **kw), reads=r, writes=w)

    def TT(out, a, b, op, r, w, eng="dve"):
        S.op(eng, lambda e: e.tensor_tensor(out=out, in0=a, in1=b, op=op), reads=r, writes=w)

    def TS(out, a, s1, s2, op0, op1, r, w, eng="dve"):
        if s2 is None:
            S.op(eng, lambda e: e.tensor_scalar(out=out, in0=a, scalar1=s1, scalar2=None, op0=op0), reads=r, writes=w)
        else:
            S.op(eng, lambda e: e.tensor_scalar(out=out, in0=a, scalar1=s1, scalar2=s2, op0=op0, op1=op1),
                 reads=r, writes=w)

    def STT(out, a, s, b, op0, op1, r, w, eng="dve"):
        S.op(eng, lambda e: e.scalar_tensor_tensor(out=out, in0=a, scalar=s, in1=b, op0=op0, op1=op1),
             reads=r, writes=w)

    def CP(out, a, r, w, eng="dve"):
        if eng == "act":
            S.op("act", lambda e: e.copy(out=out, in_=a), reads=r, writes=w)
        else:
            S.op(eng, lambda e: e.tensor_copy(out=out, in_=a), reads=r, writes=w)

    def DMA(q, out, in_, chan, r, w):
        return S.op(q, lambda e: e.dma_start(out=out, in_=in_), reads=r, writes=w, chan=chan)

    def MEMSET(ap, val, w, eng="dve"):
        S.op(eng, lambda e: e.memset(ap, val), writes=w)

    PS = PsumPool(nc)
    _rr = [0]

    def ew():
        _rr[0] += 1
        return "pool" if _rr[0] % 3 == 0 else "dve"

    ident_f = sb("ident_f", [128, 128])
    ident = sb("ident", [128, 128], BF16)
    H = [sb(f"H{i}", [128, NT, D]) for i in range(1)]
    hnT = sb("hnT", [128, 8, T], BF16)
    big = sb("big", [128, 32, T], BF16)
    NSLOT = 3
    Wr = [sb(f"W{i}", [128, 8 * 512], BF16) for i in range(NSLOT)]
    wslot = [0]
    pgT = sb("pgT", [128, 32])
    gnT = sb("gnT", [128, 12])
    postg = [sb(f"postg{i}", [128, D]) for i in range(2)]
    epsc = sb("epsc", [128, 2])
    ropeT = DB(nc, "rope", [128, NT, 128], F32, 2)
    retD = sb("retD", [128, 4, 128])
    retS = sb("retS", [128, 12])
    triu = sb("triu", [128, 128])
    restart = sb("restart", [128, T])
    RS = sb("RS", [128, 4, 256])
    RSb = sb("RSb", [128, 4, 256], BF16)
    lru_s = sb("lru_s", [128, 8, 8])
    lru_sp8 = sb("lru_sp8", [128, 8])
    Wab = sb("Wab", [128, 2, 8, 128], BF16)
    lru_carry = sb("lru_carry", [128, 8, 3])
    lru_h = sb("lru_h", [128, 8])
    hg_lb = sb("hg_lb", [128, 8])
    HS = sb("HS", [128, 4, 128])
    HSb = sb("HSb", [128, 4, 128], BF16)
    s5_mag = sb("s5_mag", [128, 16])
    s5_bnd = sb("s5_bnd", [128, 2, 2, 16])
    s5_st = sb("s5_st", [128, 2, 16])
    s5_ini = sb("s5_ini", [128, 2, 16])
    Bblk = sb("Bblk", [128, 2, 16, 128], BF16)
    Cblk = sb("Cblk", [128, 2, 16, 128], BF16)
    Rcs = sb("Rcs", [128, 2, 16, T], BF16)
    s5dg = sb("s5dg", [128, 8])
    gluW = sb("gluW", [128, 4, 512], BF16)

    f32a = DB(nc, "f32a", [128, T + 4], F32, 20)
    bf16a = DB(nc, "bf16a", [128, T], BF16, 4)
    hsb = DB(nc, "hsb", [128, T], F32, 5)
    s5h = DB(nc, "s5h", [128, T], BF16, 12)
    hn_tok = DB(nc, "hn_tok", [128, D], BF16, 1)
    junk_t = sb("junk", [128, D], BF16)
    colsA = DB(nc, "colsA", [128, 16], F32, 8)
    tmpTok = DB(nc, "tmpTok", [128, 512], F32, 4)
    qkr = sb("qkr", [128, NT, 2, 4, 128], BF16)
    v_tok = sb("v_tok", [128, NT, 1024], BF16)
    sg_tok = sb("sg_tok", [128, NT, 1024], BF16)
    qT = DB(nc, "qT", [128, 4, 128], BF16, 2)
    kT = DB(nc, "kT", [128, 4, 128], BF16, 2)
    sTb = DB(nc, "sTb", [128, 4, 128], BF16, 2)
    kdec = DB(nc, "kdec", [128, 4, 128], BF16, 2)
    o_sb = DB(nc, "o_sb", [128, 4, 256], F32, 2)
    ya = DB(nc, "ya", [128, D], BF16, 2)
    assert T == 256
    uf = o_sb.bufs[0]
    UFK = o_sb.keys[0]
    ub = sb("ub", [128, 4, T], BF16)
    ygf = o_sb.bufs[1]
    YGK = o_sb.keys[1]
    ygb = sb("ygb", [128, 4, T], BF16)
    hqT = sb("hqT", [128, 4, T], BF16)
    hkT = sb("hkT", [128, 4, T], BF16)
    hgE = sb("hgE", [128, 3, 4, T // 64])

    BIGK = lambda f: f"big.{f}"
    HNK = "hnT"

    def load(dst_ap, src_ap, key, q="act"):
        DMA(q, dst_ap, src_ap, "ld_small", [], [key])

    load(ident_f[:], cd["ident"], "ident_f")
    CP(ident[:], ident_f[:], ["ident_f"], ["ident"])
    load(pgT[:], small["pgT"], "pgT")
    load(gnT[:], small["gnT"], "gnT")
    load(retD[:], cd["retD"], "retD")
    load(retS[:], cd["retS"], "retS")
    load(triu[:], cd["triu"], "triu")
    load(restart[:], cd["restart"], "restart")
    load(lru_s[:], small["lru_s"], "lru_s")
    load(s5dg[:], small["s5dg"], "s5dg")
    MEMSET(epsc[:, 0:1], EPS, ["epsc"])
    MEMSET(epsc[:, 1:2], 1.0, ["epsc"])
    MEMSET(RS[:], 0.0, ["RS"])
    MEMSET(RSb[:], 0.0, ["RSb"])
    MEMSET(HS[:], 0.0, ["HS"])
    MEMSET(HSb[:], 0.0, ["HSb"])
    MEMSET(lru_carry[:], 0.0, ["lru_carry"])
    MEMSET(lru_h[:], 0.0, ["lru_h"])
    MEMSET(s5_st[:], 0.0, ["s5_st"])

    ACT(lru_sp8[:], lru_s[:, :, 0], AF.Exp, ["lru_s"], ["lru_sp8"], scale=-1.0)
    ACT(lru_sp8[:], lru_sp8[:], AF.Ln, ["lru_sp8", "epsc"], ["lru_sp8"], bias=epsc[:, 1:2])
    TS(lru_sp8[:], lru_sp8[:], -8.0, None, ALU.mult, None, ["lru_sp8"], ["lru_sp8"])
    for wi, nm in enumerate(("rg_wa", "rg_wi")):
        wabf = o_sb.bufs[wi][:].rearrange("p a (c d) -> p (a c) d", d=128)
        wk_ = o_sb.keys[wi]
        MEMSET(wabf, 0.0, [wk_])
        src = small[nm].rearrange("(ct bl) i j -> bl i ct j", bl=2)
        for bl in range(2):
            DMA("act", wabf[bl * 64:(bl + 1) * 64, :, bl * 64:(bl + 1) * 64], src[bl], "ld_small", [], [wk_])
        CP(Wab[:, wi, :, :], wabf, [wk_], ["Wab"])

    hgl = sb("hgl", [128, 8])
    load(hgl[:], small["hgl"], "hgl")
    TT(hg_lb[:, 0:4], hgl[:, 0:4], hgl[:, 4:8], ALU.subtract, ["hgl"], ["hg_lb"])
    ACT(hg_lb[:, 0:4], hg_lb[:, 0:4], AF.Sigmoid, ["hg_lb"], ["hg_lb"])
    TS(hg_lb[:, 4:8], hg_lb[:, 0:4], -1.0, 1.0, ALU.mult, ALU.add, ["hg_lb"], ["hg_lb"])

    s5s = sb("s5s", [128, 16, 3])
    load(s5s[:], small["s5s"], "s5s")
    pp = sb("s5pp", [128, 12, 16])
    PPK = ["s5pp"]
    iota = f32a.bufs[0][:, 0:T]
    IOK = f32a.keys[0]
    DMA("act", iota, cd["iota"], "ld_small", [], [IOK])
    DT_, ARE, AIM, TH, MAG, COS, SIN, NRE, DEN, ZRE, ZIM, TMP = range(12)
    ACT(pp[:, DT_, :], s5s[:, :, 2], AF.Exp, ["s5s"], PPK)
    ACT(pp[:, ARE, :], s5s[:, :, 0], AF.Exp, ["s5s"], PPK)
    TS(pp[:, ARE, :], pp[:, ARE, :], -1.0, None, ALU.mult, None, PPK, PPK)
    CP(pp[:, AIM, :], s5s[:, :, 1], ["s5s"], PPK)
    TT(pp[:, TH, :], pp[:, DT_, :], pp[:, AIM, :], ALU.mult, PPK, PPK)
    TT(pp[:, TMP, :], pp[:, DT_, :], pp[:, ARE, :], ALU.mult, PPK, PPK)
    ACT(pp[:, MAG, :], pp[:, TMP, :], AF.Exp, PPK, PPK)
    CP(s5_mag[:], pp[:, MAG, :], PPK, ["s5_mag"])

    TWO_PI = 2.0 * np.pi
    redi = f32a.bufs[1][:].bitcast(I32)[:, 0:T]
    redf = f32a.bufs[2][:, 0:T]
    RIK, RFK = f32a.keys[1], f32a.keys[2]

    def sincos(out_sin, out_cos, ang_ap, shape_n, keys_r, keys_w):
        ri = redi[:, 0:shape_n]
        rf = redf[:, 0:shape_n]
        for out_ap, shift in ((out_sin, 0.0), (out_cos, 0.5 * np.pi)):
            if out_ap is None:
                continue
            TS(ri, ang_ap, shift, 1.0 / TWO_PI, ALU.add, ALU.mult, keys_r, [RIK])
            STT(rf, ri, -TWO_PI, ang_ap, ALU.mult, ALU.add, [RIK] + keys_r, [RFK])
            if shift != 0.0:
                TS(rf, rf, shift, None, ALU.add, None, [RFK], [RFK])
            TS(rf, rf, 3.1415925, -3.1415925, ALU.min, ALU.max, [RFK], [RFK])
            ACT(out_ap, rf, AF.Sin, [RFK], keys_w)

    sincos(pp[:, SIN, :], pp[:, COS, :], pp[:, TH, :], 16, PPK, PPK)
    ang2 = sb("ang2", [128, 2, 16])
    TS(ang2[:, 0, :], pp[:, TH, :], float(T), None, ALU.mult, None, PPK, ["ang2"])
    TS(ang2[:, 1, :], pp[:, TH, :], float(NMETA), None, ALU.mult, None, PPK, ["ang2"])
    bndt = sb("bndt", [128, 2, 2, 16])
    for fr in range(2):
        sincos(bndt[:, fr, 1, :], bndt[:, fr, 0, :], ang2[:, fr, :], 16, ["ang2"], ["bndt"])
    CP(s5_bnd[:], bndt[:], ["bndt"], ["s5_bnd"])
    angT = f32a.bufs[3][:, 0:T]
    ATK = f32a.keys[3]
    for m in range(16):
        TS(angT, iota, pp[:, TH, m:m + 1], None, ALU.mult, None, PPK + [IOK], [ATK])
        sincos(Rcs[:, 1, m, :], Rcs[:, 0, m, :], angT, T, [ATK], ["Rcs"])
    TT(pp[:, COS, :], pp[:, COS, :], pp[:, MAG, :], ALU.mult, PPK, PPK)
    TT(pp[:, SIN, :], pp[:, SIN, :], pp[:, MAG, :], ALU.mult, PPK, PPK)
    TS(pp[:, NRE, :], pp[:, COS, :], -1.0, None, ALU.add, None, PPK, PPK)
    TT(pp[:, DEN, :], pp[:, ARE, :], pp[:, ARE, :], ALU.mult, PPK, PPK)
    TT(pp[:, TMP, :], pp[:, AIM, :], pp[:, AIM, :], ALU.mult, PPK, PPK)
    TT(pp[:, DEN, :], pp[:, DEN, :], pp[:, TMP, :], ALU.add, PPK, PPK)
    S.op("dve", lambda e: e.reciprocal(out=pp[:, DEN, :], in_=pp[:, DEN, :]), reads=PPK, writes=PPK)
    TT(pp[:, ZRE, :], pp[:, NRE, :], pp[:, ARE, :], ALU.mult, PPK, PPK)
    TT(pp[:, TMP, :], pp[:, SIN, :], pp[:, AIM, :], ALU.mult, PPK, PPK)
    TT(pp[:, ZRE, :], pp[:, ZRE, :], pp[:, TMP, :], ALU.add, PPK, PPK)
    TT(pp[:, ZRE, :], pp[:, ZRE, :], pp[:, DEN, :], ALU.mult, PPK, PPK)
    TT(pp[:, ZIM, :], pp[:, SIN, :], pp[:, ARE, :], ALU.mult, PPK, PPK)
    TT(pp[:, TMP, :], pp[:, NRE, :], pp[:, AIM, :], ALU.mult, PPK, PPK)
    TT(pp[:, ZIM, :], pp[:, ZIM, :], pp[:, TMP, :], ALU.subtract, PPK, PPK)
    TT(pp[:, ZIM, :], pp[:, ZIM, :], pp[:, DEN, :], ALU.mult, PPK, PPK)
    s5b = tmpTok.bufs[0][:].rearrange("p (m r c) -> p m r c", m=16, r=2)
    DMA("act", s5b, small["s5b"], "ld_small", [], [tmpTok.keys[0]])
    bbn = tmpTok.bufs[1][:].rearrange("p (r m c) -> p r m c", r=2, m=16)
    tb = tmpTok.bufs[2][:].rearrange("p (r m c) -> p r m c", r=2, m=16)
    zre_b = pp[:, ZRE, :].unsqueeze(2).to_broadcast([128, 16, 16])
    zim_b = pp[:, ZIM, :].unsqueeze(2).to_broadcast([128, 16, 16])
    TT(tb[:, 0], s5b[:, :, 0, :], zre_b, ALU.mult, PPK + [tmpTok.keys[0]], [tmpTok.keys[2]])
    TT(tb[:, 1], s5b[:, :, 1, :], zim_b, ALU.mult, PPK + [tmpTok.keys[0]], [tmpTok.keys[2]])
    TT(bbn[:, 0], tb[:, 0], tb[:, 1], ALU.subtract, [tmpTok.keys[2]], [tmpTok.keys[1]])
    TT(tb[:, 0], s5b[:, :, 1, :], zre_b, ALU.mult, PPK + [tmpTok.keys[0]], [tmpTok.keys[2]])
    TT(tb[:, 1], s5b[:, :, 0, :], zim_b, ALU.mult, PPK + [tmpTok.keys[0]], [tmpTok.keys[2]])
    TT(bbn[:, 1], tb[:, 0], tb[:, 1], ALU.add, [tmpTok.keys[2]], [tmpTok.keys[1]])
    maskB = sb("maskB", [128, 16, 8])
    load(maskB[:], cd["maskB"], "maskB")
    maskC = sb("maskC", [128, 4, 2, 2])
    load(maskC[:], cd["maskC"], "maskC")
    s5c = tmpTok.bufs[3][:].rearrange("p (a r q) -> p a r q", a=4, r=2)
    DMA("act", s5c, small["s5c"], "ld_small", [], [tmpTok.keys[3]])
    wide = DB(nc, "wide", [128, 128], BF16, 2)
    for ri in range(2):
        for m in range(16):
            wd, wk = wide.next()
            TT(wd[:].rearrange("p (g c) -> p g c", g=8), bbn[:, ri, m, :].unsqueeze(1).to_broadcast([128, 8, 16]),
               maskB[:, m, :].unsqueeze(2).to_broadcast([128, 8, 16]), ALU.mult, [tmpTok.keys[1], "maskB"], [wk])
            ps, pk = PS.half()
            psb = ps.bitcast(BF16)
            TR(psb[:, 0:128], wd[:], ident[:], [wk, "ident"], pk)
            CP(Bblk[:, ri, m, :], psb[:, 0:128], pk, ["Bblk"], eng="act")
    for ri in range(2):
        for m in range(16):
            wd, wk = wide.next()
            TT(wd[:].rearrange("p (g q) -> p g q", g=2), s5c[:, m // 4, ri, :].unsqueeze(1).to_broadcast([128, 2, 64]),
               maskC[:, m % 4, :, ri].unsqueeze(2).to_broadcast([128, 2, 64]), ALU.mult, [tmpTok.keys[3], "maskC"], [wk])
            ps, pk = PS.half()
            psb = ps.bitcast(BF16)
            TR(psb[:, 0:128], wd[:], ident[:], [wk, "ident"], pk)
            CP(Cblk[:, ri, m, :], psb[:, 0:128], pk, ["Cblk"], eng="act")

    cast_engs = ["dve", "act", "pool"]
    ci = 0
    for n, R, C in (wdefs if KPHASE != "pro" else []):
        for rt in range(R // 128):
            for c0 in range(0, C, 2048):
                w = min(2048, C - c0)
                s = wslot[0]
                wslot[0] = (s + 1) % NSLOT
                stg = Wr[s][:].bitcast(F32)
                DMA("sp", stg[:, 0:w], wsrc[n][rt * 128:(rt + 1) * 128, c0:c0 + w], f"w{s}", [], [f"W{s}"])
                bi = ci % 4
                cb = big[:, bi * 8:(bi + 1) * 8, :].rearrange("p a b -> p (a b)")
                ck = [BIGK(bi * 8 + t_) for t_ in range(8)]
                CP(cb[:, 0:w], stg[:, 0:w], [f"W{s}"], ck, eng=cast_engs[ci % 3])
                ci += 1
                if n == "glu_w":
                    dst = wscr[n][rt * 128:(rt + 1) * 128, c0:c0 + w]
                    srcv = cb[:, 0:w]
                else:
                    dst = wscr[n][rt // 8, c0 // 512:(c0 + w) // 512, :, rt % 8, :].rearrange("b p c -> p b c")
                    srcv = cb[:, 0:w].rearrange("p (b c) -> p b c", c=512)
                DMA("pool", dst, srcv, f"st_scr{bi}", ck, [f"scr_{n}.{rt}.{c0 // 2048}"])
    if KPHASE != "pro":
      DMA("sp", gluW[:], wscr["glu_w"].rearrange("(kt p) c -> p kt c", p=128), "ld_glu",
        [f"scr_glu_w.{rt}.0" for rt in range(4)], ["gluW"])

    def load_w(name, k0, c0, ncols=512, nk=8):
        s = wslot[0]
        wslot[0] = (s + 1) % NSLOT
        view = Wr[s][:].rearrange("p (k c) -> p k c", k=8)
        assert nk == 8 and ncols == 512 and k0 % 1024 == 0 and c0 % 512 == 0
        DMA("sp", view[:, 0:nk, 0:ncols], wscr[name][k0 // 1024, c0 // 512],
            f"w{s}", [f"scr_{name}.{k0 // 128 + i_}.{c0 // 2048}" for i_ in range(nk)], [f"W{s}"])
        return view, f"W{s}"

    def rstd_from_ss(ss_ap, rows, dim, ck):
        k = ss_ap.shape[1]
        ACT(ss_ap, ss_ap, AF.Sqrt, [ck, "epsc"], [ck], scale=1.0 / dim, bias=epsc[0:rows, 0:1])
        S.op("dve", lambda e: e.reciprocal(out=ss_ap, in_=ss_ap), reads=[ck], writes=[ck])

    def to_feature(src_tok, rows, ntile, dst, dst_f0, tok0, gain_ap, rkeys, wkeys_fn):
        for q0 in range(0, ntile, 4):
            nq = min(4, ntile - q0)
            ps, pk = PS.half()
            psb = ps.bitcast(BF16)
            for i in range(nq):
                TR(psb[:, i * 128:i * 128 + rows], src_tok[0:rows, (q0 + i) * 128:(q0 + i + 1) * 128],
                   ident[0:rows, 0:rows], rkeys + ["ident"], pk)
            src = psb.rearrange("p (k t) -> p k t", k=4)[:, 0:nq, 0:rows]
            dsl = dst[:, dst_f0 + q0:dst_f0 + q0 + nq, tok0:tok0 + rows]
            wk = [wkeys_fn(dst_f0 + q0 + i) for i in range(nq)]
            if gain_ap is None:
                CP(dsl, src, pk, wk, eng="act")
            else:
                g = gain_ap[:, q0:q0 + nq].unsqueeze(2).to_broadcast([128, nq, rows])
                TT(dsl, src, g, ALU.mult, pk + ["pgT", "gnT"], wk)

    def prenorm(Hc, HK, tiles, gcol):
        for (j, rows) in tiles:
            jb, jk = junk_t, None
            cl, ck = colsA.next()
            ACT(jb[0:rows, :], Hc[0:rows, j, :], AF.Square, [HK], [ck], accum_out=cl[0:rows, 0:1])
            rstd_from_ss(cl[0:rows, 0:1], rows, D, ck)
            hb, hk = hn_tok.next()
            TS(hb[0:rows, :], Hc[0:rows, j, :], cl[0:rows, 0:1], None, ALU.mult, None, [HK, ck], [hk])
            to_feature(hb, rows, 8, hnT, 0, j * 128, pgT[:, gcol:gcol + 8], [hk], lambda f: HNK)

    def postnorm_add(Hc, HK, j, rows, psA, pkA, psB, pkB, gtab, gk):
        cl, ck = colsA.next()
        jb, jk = junk_t, None
        ACT(jb[0:rows, 0:512], psA[0:rows, :], AF.Square, pkA, [ck], accum_out=cl[0:rows, 0:1])
        ACT(jb[0:rows, 512:1024], psB[0:rows, :], AF.Square, pkB, [ck], accum_out=cl[0:rows, 1:2])
        TT(cl[0:rows, 0:1], cl[0:rows, 0:1], cl[0:rows, 1:2], ALU.add, [ck], [ck])
        rstd_from_ss(cl[0:rows, 0:1], rows, D, ck)
        for (ps, pk, c0) in ((psA, pkA, 0), (psB, pkB, 512)):
            tb_, tk = tmpTok.next()
            STT(tb_[0:rows, :], ps[0:rows, :], cl[0:rows, 0:1], gtab[0:rows, c0:c0 + 512], ALU.mult, ALU.mult,
                pk + [ck, gk], [tk])
            TT(Hc[0:rows, j, c0:c0 + 512], Hc[0:rows, j, c0:c0 + 512], tb_[0:rows, :], ALU.add, [HK, tk], [HK],
               eng="pool")

    def proj_token_major(wname, F, srcbuf, src_key_fn, tiles, consume):
        banks = {}
        for (j, rows) in tiles:
            banks[j] = (PS.full(), PS.full())
        nunit = F // 8
        for cc in range(2):
            for u in range(nunit):
                wv, wk = load_w(wname, u * 1024, cc * 512)
                for (j, rows) in tiles:
                    ps, pk = banks[j][cc]
                    for fi in range(8):
                        f = u * 8 + fi
                        MM(ps[0:rows, :], srcbuf[:, f, j * 128:j * 128 + rows], wv[:, fi, :], f == 0, f == F - 1,
                           [src_key_fn(f), wk], pk)
        for (j, rows) in tiles:
            (psA, pkA), (psB, pkB) = banks[j]
            consume(j, rows, psA, pkA, psB, pkB)

    def load_postg(l):
        for i in range(2):
            DMA("pool", postg[i][:], small["postg"][l * 2 + i:l * 2 + i + 1, :].partition_broadcast(128), "ld_postg",
                [], [f"postg{i}"])

    def mlp(l, Hc, HK, tiles, n):
        prenorm(Hc, HK, tiles, (l * 2 + 1) * 8)
        wn1, wn2 = f"w1_{l}", f"w2_{l}"
        for blk in range(8):
            wv, wk = load_w(wn1, 0, blk * 512)
            for ft in range(4):
                f = blk * 4 + ft
                ps, pk = PS.half()
                for kt in range(8):
                    MM(ps[:, 0:n], wv[:, kt, ft * 128:(ft + 1) * 128], hnT[:, kt, 0:n], kt == 0, kt == 7, [HNK, wk], pk)
                tf, tk = f32a.next()
                ACT(tf[:, 0:n], ps[:, 0:n], AF.Relu, pk, [tk])
                TT(big[:, f, 0:n], tf[:, 0:n], tf[:, 0:n], ALU.mult, [tk], [BIGK(f)], eng=ew())
        proj_token_major(wn2, 32, big, BIGK, tiles,
                         lambda j, rows, a, ak, b, bk: postnorm_add(Hc, HK, j, rows, a, ak, b, bk, postg[1], "postg1"))

    def layer0_mixer(Hc, HK, tiles, n, meta_group, rope_ap, rope_key):
        prenorm(Hc, HK, tiles, 0)
        for qk in range(2):
            wv, wk = load_w("w_in_ab", 0, qk * 512)
            for (j, rows) in tiles:
                ps, pk = PS.full()
                for kt in range(8):
                    MM(ps[0:rows, :], hnT[:, kt, j * 128:j * 128 + rows], wv[:, kt, :], kt == 0, kt == 7, [HNK, wk], pk)
                x3 = ps.rearrange("p (h d) -> p h d", h=4)
                x1 = x3[0:rows, :, 0:64]
                x2 = x3[0:rows, :, 64:128]
                cosb = rope_ap[0:rows, j, 0:64].unsqueeze(1).to_broadcast([rows, 4, 64])
                sinb = rope_ap[0:rows, j, 64:128].unsqueeze(1).to_broadcast([rows, 4, 64])
                t1, k1 = tmpTok.next()
                t2, k2 = tmpTok.next()
                a1 = t1[0:rows, 0:256].rearrange("p (h d) -> p h d", h=4)
                a2 = t2[0:rows, 0:256].rearrange("p (h d) -> p h d", h=4)
                b1 = t1[0:rows, 256:512].rearrange("p (h d) -> p h d", h=4)
                b2 = t2[0:rows, 256:512].rearrange("p (h d) -> p h d", h=4)
                TT(a1, x1, cosb, ALU.mult, pk + [rope_key], [k1])
                TT(a2, x2, sinb, ALU.mult, pk + [rope_key], [k2])
                TT(b1, x1, sinb, ALU.mult, pk + [rope_key], [k1])
                TT(b2, x2, cosb, ALU.mult, pk + [rope_key], [k2])
                TT(qkr[0:rows, j, qk, :, 0:64], a1, a2, ALU.subtract, [k1, k2], [f"qkr{j}"], eng="pool")
                TT(qkr[0:rows, j, qk, :, 64:128], b1, b2, ALU.add, [k1, k2], [f"qkr{j}"], eng="pool")
        for vb in range(2):
            wv, wk = load_w("w_in_ab", 0, 1024 + vb * 512)
            for (j, rows) in tiles:
                ps, pk = PS.full()
                for kt in range(8):
                    MM(ps[0:rows, :], hnT[:, kt, j * 128:j * 128 + rows], wv[:, kt, :], kt == 0, kt == 7, [HNK, wk], pk)
                CP(v_tok[0:rows, j, vb * 512:(vb + 1) * 512], ps[0:rows, :], pk, [f"v{j}"], eng="act")
        for gb in range(2):
            wv, wk = load_w("w_in_ab", 0, 2048 + gb * 512)
            for (j, rows) in tiles:
                ps, pk = PS.full()
                for kt in range(8):
                    MM(ps[0:rows, :], hnT[:, kt, j * 128:j * 128 + rows], wv[:, kt, :], kt == 0, kt == 7, [HNK, wk], pk)
                ACT(sg_tok[0:rows, j, gb * 512:(gb + 1) * 512], ps[0:rows, :], AF.Silu, pk, [f"sg{j}"])
        for (j, rows) in tiles:
            qt, qtk = qT.next()
            kt_, ktk = kT.next()
            for (dst, dk_, qk) in ((qt, qtk, 0), (kt_, ktk, 1)):
                ps, pk = PS.half()
                psb = ps.bitcast(BF16)
                for h in range(4):
                    TR(psb[:, h * 128:h * 128 + rows], qkr[0:rows, j, qk, h, :], ident[0:rows, 0:rows],
                       [f"qkr{j}", "ident"], pk)
                CP(dst[:, :, 0:rows], psb.rearrange("p (h t) -> p h t", h=4)[:, :, 0:rows], pk, [dk_], eng="act")
            ps, pk = PS.full()
            p3 = ps.rearrange("p (h t) -> p h t", h=4)
            for h in range(4):
                MM(p3[0:rows, h, 0:rows], kt_[:, h, 0:rows], qt[:, h, 0:rows], True, True, [qtk, ktk], pk)
            st_, stk = sTb.next()
            TT(st_[0:rows, :, 0:rows], p3[0:rows, :, 0:rows], retD[0:rows, :, 0:rows], ALU.mult, pk + ["retD"], [stk])
            ob, obk = o_sb.next()
            for hp in range(2):
                ps, pk = PS.full()
                for hh in range(2):
                    h = hp * 2 + hh
                    osl = ps[0:rows, hh * 256:(hh + 1) * 256]
                    MM(osl, st_[0:rows, h, 0:rows], v_tok[0:rows, j, h * 256:(h + 1) * 256], True, False,
                       [stk, f"v{j}"], pk)
                    MM(osl, qt[:, h, 0:rows], RSb[:, h, :], False, True, [qtk, "RSb"], pk)
                TT(ob[0:rows, hp * 2:hp * 2 + 2, :], ps.rearrange("p (h e) -> p h e", h=2)[0:rows],
                   retS[0:rows, hp * 2:hp * 2 + 2].unsqueeze(2).to_broadcast([rows, 2, 256]), ALU.mult,
                   pk + ["retS"], [obk])
            kd, kdk = kdec.next()
            kdcol = 8 if meta_group else 4
            TT(kd[0:rows, :, :], qkr[0:rows, j, 1, :, :],
               retS[0:rows, kdcol:kdcol + 4].unsqueeze(2).to_broadcast([rows, 4, 128]), ALU.mult,
               [f"qkr{j}", "retS"], [kdk], eng="pool")
            for hp in range(2):
                ps, pk = PS.full()
                for hh in range(2):
                    h = hp * 2 + hh
                    MM(ps[:, hh * 256:(hh + 1) * 256], kd[0:rows, h, :], v_tok[0:rows, j, h * 256:(h + 1) * 256],
                       True, True, [kdk, f"v{j}"], pk)
                for hh in range(2):
                    h = hp * 2 + hh
                    STT(RS[:, h, :], RS[:, h, :], float(GAMMA[h] ** rows), ps[:, hh * 256:(hh + 1) * 256],
                        ALU.mult, ALU.add, ["RS"] + pk, ["RS"])
            CP(RSb[:], RS[:], ["RS"], ["RSb"], eng="act")
            cl, ck = colsA.next()
            S.op("dve", lambda e, ob=ob, cl=cl, rows=rows: e.reduce_sum(out=cl[0:rows, 0:4], in_=ob[0:rows], axis=AX.X),
                 reads=[obk], writes=[ck])
            TS(cl[0:rows, 0:4], cl[0:rows, 0:4], -1.0 / 256, None, ALU.mult, None, [ck], [ck])
            TT(ob[0:rows], ob[0:rows], cl[0:rows, 0:4].unsqueeze(2).to_broadcast([rows, 4, 256]), ALU.add,
               [obk, ck], [obk])
            jb, jk = junk_t, None
            for h in range(4):
                ACT(jb[0:rows, h * 256:(h + 1) * 256], ob[0:rows, h, :], AF.Square, [obk], [ck],
                    accum_out=cl[0:rows, 4 + h:5 + h])
            rstd_from_ss(cl[0:rows, 4:8], rows, 256, ck)
            yb_, ybk = ya.next()
            for h in range(4):
                STT(yb_[0:rows, h * 256:(h + 1) * 256], ob[0:rows, h, :], cl[0:rows, 4 + h:5 + h],
                    sg_tok[0:rows, j, h * 256:(h + 1) * 256], ALU.mult, ALU.mult, [obk, ck, f"sg{j}"], [ybk])
            to_feature(yb_, rows, 8, big, 0, j * 128, gnT[:, 0:8], [ybk], BIGK)
        for half in range(2):
            hs_list = []
            wv, wk = load_w("w_in_ab", 0, 3072 + half * 512)
            for ct in range(4):
                c = half * 4 + ct
                ps, pk = PS.half()
                for kt in range(8):
                    MM(ps[:, 0:n], wv[:, kt, ct * 128:(ct + 1) * 128], hnT[:, kt, 0:n], kt == 0, kt == 7, [HNK, wk], pk)
                xp, xk = f32a.next()
                CP(xp[:, 0:3], lru_carry[:, c, :], ["lru_carry"], [xk], eng="pool")
                CP(xp[:, 3:3 + n], ps[:, 0:n], pk, [xk], eng="act")
                acc, ak = f32a.next()
                P = lambda i: lru_s[:, c, i:i + 1]
                TS(acc[:, 0:n], xp[:, 0:n], P(4), P(3), ALU.mult, ALU.add, [xk, "lru_s"], [ak])
                STT(acc[:, 0:n], xp[:, 1:1 + n], P(5), acc[:, 0:n], ALU.mult, ALU.add, [xk, ak, "lru_s"], [ak])
                STT(acc[:, 0:n], xp[:, 2:2 + n], P(6), acc[:, 0:n], ALU.mult, ALU.add, [xk, ak, "lru_s"], [ak])
                STT(acc[:, 0:n], xp[:, 3:3 + n], P(7), acc[:, 0:n], ALU.mult, ALU.add, [xk, ak, "lru_s"], [ak])
                CP(lru_carry[:, c, :], xp[:, n:n + 3], [xk], ["lru_carry"], eng="pool")
                xb_, xbk = bf16a.next()
                CP(xb_[:, 0:n], acc[:, 0:n], [ak], [xbk], eng="act")
                gates = []
                for wi in range(2):
                    ps2, pk2 = PS.half()
                    MM(ps2[:, 0:n], Wab[:, wi, c, :], xb_[:, 0:n], True, True, ["Wab", xbk], pk2)
                    gt_, gk_ = f32a.next()
                    ACT(gt_[:, 0:n], ps2[:, 0:n], AF.Sigmoid, pk2 + ["lru_s"], [gk_], bias=lru_s[:, c, 1 + wi:2 + wi])
                    gates.append((gt_, gk_))
                (rg, rk), (ig, ik) = gates
                ACT(rg[:, 0:n], rg[:, 0:n], AF.Exp, [rk, "lru_sp8"], [rk], scale=lru_sp8[:, c:c + 1])
                TT(ig[:, 0:n], ig[:, 0:n], acc[:, 0:n], ALU.mult, [ik, ak], [ik])
                TT(acc[:, 0:n], rg[:, 0:n], rg[:, 0:n], ALU.mult, [rk], [ak], eng="pool")
                ACT(acc[:, 0:n], acc[:, 0:n], AF.Sqrt, [ak, "epsc"], [ak], scale=-1.0, bias=epsc[:, 1:2])
                TT(ig[:, 0:n], ig[:, 0:n], acc[:, 0:n], ALU.mult, [ik, ak], [ik])
                hb_, hbk = hsb.next()
                S.op("dve", lambda e, hb_=hb_, rg=rg, ig=ig, c=c: e.tensor_tensor_scan(
                    out=hb_[:, 0:n], data0=rg[:, 0:n], data1=ig[:, 0:n], initial=lru_h[:, c:c + 1],
                    op0=ALU.mult, op1=ALU.add), reads=[rk, ik, "lru_h"], writes=[hbk])
                CP(lru_h[:, c:c + 1], hb_[:, n - 1:n], [hbk], ["lru_h"], eng="pool")
                hs_list.append((hb_, hbk))
            wv, wk = load_w("w_in_ab", 0, 4096 + half * 512)
            for ct in range(4):
                c = half * 4 + ct
                ps, pk = PS.half()
                for kt in range(8):
                    MM(ps[:, 0:n], wv[:, kt, ct * 128:(ct + 1) * 128], hnT[:, kt, 0:n], kt == 0, kt == 7, [HNK, wk], pk)
                ge, gek = f32a.next()
                ACT(ge[:, 0:n], ps[:, 0:n], AF.Gelu_apprx_tanh, pk, [gek])
                hb_, hbk = hs_list[ct]
                TT(big[:, 8 + c, 0:n], ge[:, 0:n], hb_[:, 0:n], ALU.mult, [gek, hbk], [BIGK(8 + c)], eng=ew())
        proj_token_major("w_out_ab", 16, big, BIGK, tiles,
                         lambda j, rows, a, ak, b, bk: postnorm_add(Hc, HK, j, rows, a, ak, b, bk, postg[0], "postg0"))

    def layer1_mixer(Hc, HK, tiles, n, meta_group, prev_n):
        prenorm(Hc, HK, tiles, 16)
        wv, wk = load_w("w_in_cd", 0, 0)
        for ct in range(4):
            ps, pk = PS.half()
            for kt in range(8):
                MM(ps[:, 0:n], wv[:, kt, ct * 128:(ct + 1) * 128], hnT[:, kt, 0:n], kt == 0, kt == 7, [HNK, wk], pk)
            CP(uf[:, ct, 0:n], ps[:, 0:n], pk, [UFK], eng="act")
            CP(ub[:, ct, 0:n], ps[:, 0:n], pk, [f"ub{ct}"])
        if prev_n is not None:
            fr = 0 if prev_n == T else 1
            cb_ = s5_bnd[:, fr, 0, :]
            sb_ = s5_bnd[:, fr, 1, :]
            q1, qk1 = colsA.next()
            q2, qk2 = colsA.next()
            TT(q1[:, 0:16], s5_st[:, 0, :], cb_, ALU.mult, ["s5_st", "s5_bnd"], [qk1])
            TT(q2[:, 0:16], s5_st[:, 1, :], sb_, ALU.mult, ["s5_st", "s5_bnd"], [qk2])
            TT(s5_ini[:, 0, :], q1[:, 0:16], q2[:, 0:16], ALU.subtract, [qk1, qk2], ["s5_ini"])
            q3, qk3 = colsA.next()
            q4, qk4 = colsA.next()
            TT(q3[:, 0:16], s5_st[:, 0, :], sb_, ALU.mult, ["s5_st", "s5_bnd"], [qk3])
            TT(q4[:, 0:16], s5_st[:, 1, :], cb_, ALU.mult, ["s5_st", "s5_bnd"], [qk4])
            TT(s5_ini[:, 1, :], q3[:, 0:16], q4[:, 0:16], ALU.add, [qk3, qk4], ["s5_ini"])
        else:
            MEMSET(s5_ini[:], 0.0, ["s5_ini"])
        wq, wqk = load_w("w_in_cd", 0, 512)
        wf, wfk = load_w("w_in_cd", 0, 1024)
        bw = 16 if meta_group else 64
        htiles = [(0, NMETA)] if meta_group else [(c_, 64) for c_ in range(n // 64)]
        nblk = len(htiles)
        mid = bw // 2 - 1
        def hgrn_head(h):
            psq, pkq = PS.half()
            for kt in range(8):
                MM(psq[:, 0:n], wq[:, kt, h * 128:(h + 1) * 128], hnT[:, kt, 0:n], kt == 0, kt == 7, [HNK, wqk], pkq)
            psf, pkf = PS.half()
            for kt in range(8):
                MM(psf[:, 0:n], wf[:, kt, h * 128:(h + 1) * 128], hnT[:, kt, 0:n], kt == 0, kt == 7, [HNK, wfk], pkf)
            ff, fk = f32a.next()
            ACT(ff[:, 0:n], psf[:, 0:n], AF.Sigmoid, pkf, [fk])
            TS(ff[:, 0:n], ff[:, 0:n], hg_lb[:, 4 + h:5 + h], hg_lb[:, h:h + 1], ALU.mult, ALU.add, [fk, "hg_lb"], [fk])
            lf, lk = f32a.next()
            ACT(lf[:, 0:n], ff[:, 0:n], AF.Ln, [fk], [lk])
            ACT(ff[:, 0:n], ff[:, 0:n], AF.Identity, [fk, "epsc"], [fk], scale=-1.0, bias=epsc[:, 1:2])
            cum, cmk = f32a.next()
            S.op("dve", lambda e, cum=cum, lf=lf: e.tensor_tensor_scan(
                out=cum[:, 0:n], data0=restart[:, 0:n], data1=lf[:, 0:n], initial=0.0,
                op0=ALU.mult, op1=ALU.add), reads=[lk, "restart"], writes=[cmk])
            c3 = cum[:, 0:n].rearrange("p (j t) -> p j t", t=bw)
            ACT(hgE[:, 0, h, 0:nblk], c3[:, :, mid], AF.Exp, [cmk], ["hgE"])
            ACT(hgE[:, 1, h, 0:nblk], c3[:, :, bw - 1], AF.Exp, [cmk], ["hgE"])
            cm, cmk2 = f32a.next()
            cm3 = cm[:, 0:n].rearrange("p (j t) -> p j t", t=bw)
            TT(cm3, c3, c3[:, :, mid:mid + 1].to_broadcast([128, nblk, bw]), ALU.subtract, [cmk], [cmk2])
            ACT(lf[:, 0:n], cm[:, 0:n], AF.Exp, [cmk2], [lk])
            ACT(cm[:, 0:n], cm[:, 0:n], AF.Exp, [cmk2], [cmk2], scale=-1.0)
            l3 = lf[:, 0:n].rearrange("p (j t) -> p j t", t=bw)
            CP(hgE[:, 2, h, 0:nblk], l3[:, :, bw - 1], [lk], ["hgE"], eng="act")
            TT(hqT[:, h, 0:n], psq[:, 0:n], lf[:, 0:n], ALU.mult, pkq + [lk], [f"hq{h}"])
            TT(hkT[:, h, 0:n], ff[:, 0:n], cm[:, 0:n], ALU.mult, [fk, cmk2], [f"hk{h}"])

        for ct in range(4):
            ms = [ct * 4 + q for q in range(4)]
            W = {}
            for m in ms:
                psr, pkr = PS.half()
                MM(psr[:, 0:n], Bblk[:, 0, m, :], ub[:, ct, 0:n], True, True, ["Bblk", f"ub{ct}"], pkr)
                psi, pki = PS.half()
                MM(psi[:, 0:n], Bblk[:, 1, m, :], ub[:, ct, 0:n], True, True, ["Bblk", f"ub{ct}"], pki)
                W[m] = dict(psr=psr, pkr=pkr, psi=psi, pki=pki, t=[f32a.next() for _ in range(4)],
                            Rc=Rcs[:, 0, m, 0:n], Rs=Rcs[:, 1, m, 0:n])
            for m in ms:
                w_ = W[m]
                (t1, k1), (t2, k2), (t3, k3), (t4, k4) = w_["t"]
                TT(t1[:, 0:n], w_["psr"][:, 0:n], w_["Rc"], ALU.mult, w_["pkr"] + ["Rcs"], [k1])
                TT(t4[:, 0:n], w_["psr"][:, 0:n], w_["Rs"], ALU.mult, w_["pkr"] + ["Rcs"], [k4])
                TT(t2[:, 0:n], w_["psi"][:, 0:n], w_["Rs"], ALU.mult, w_["pki"] + ["Rcs"], [k2])
                TT(t3[:, 0:n], w_["psi"][:, 0:n], w_["Rc"], ALU.mult, w_["pki"] + ["Rcs"], [k3])
            for m in ms:
                (t1, k1), (t2, k2), (t3, k3), (t4, k4) = W[m]["t"]
                TT(t1[:, 0:n], t1[:, 0:n], t2[:, 0:n], ALU.add, [k1, k2], [k1], eng="pool")
                TT(t3[:, 0:n], t3[:, 0:n], t4[:, 0:n], ALU.subtract, [k3, k4], [k3], eng="pool")
            for m in ms:
                (t1, k1), (t2, k2), (t3, k3), (t4, k4) = W[m]["t"]
                magb = s5_mag[:, m:m + 1].to_broadcast([128, n])
                S.op("dve", lambda e, t2=t2, t1=t1, m=m, magb=magb: e.tensor_tensor_scan(
                    out=t2[:, 0:n], data0=magb, data1=t1[:, 0:n], initial=s5_ini[:, 0, m:m + 1],
                    op0=ALU.mult, op1=ALU.add), reads=[k1, "s5_mag", "s5_ini"], writes=[k2])
                S.op("dve", lambda e, t4=t4, t3=t3, m=m, magb=magb: e.tensor_tensor_scan(
                    out=t4[:, 0:n], data0=magb, data1=t3[:, 0:n], initial=s5_ini[:, 1, m:m + 1],
                    op0=ALU.mult, op1=ALU.add), reads=[k3, "s5_mag", "s5_ini"], writes=[k4])
            for m in ms:
                (t1, k1), (t2, k2), (t3, k3), (t4, k4) = W[m]["t"]
                CP(s5_st[:, 0, m:m + 1], t2[:, n - 1:n], [k2], ["s5_st"], eng="pool")
                CP(s5_st[:, 1, m:m + 1], t4[:, n - 1:n], [k4], ["s5_st"], eng="pool")
            for m in ms:
                w_ = W[m]
                (t1, k1), (t2, k2), (t3, k3), (t4, k4) = w_["t"]
                TT(t1[:, 0:n], t2[:, 0:n], w_["Rc"], ALU.mult, [k2, "Rcs"], [k1])
                TT(t3[:, 0:n], t4[:, 0:n], w_["Rs"], ALU.mult, [k4, "Rcs"], [k3])
            for m in ms:
                (t1, k1), (t2, k2), (t3, k3), (t4, k4) = W[m]["t"]
                hr, hrk = s5h.next()
                TT(hr[:, 0:n], t1[:, 0:n], t3[:, 0:n], ALU.subtract, [k1, k3], [hrk], eng="pool")
                W[m]["hr"] = (hr, hrk)
            for m in ms:
                w_ = W[m]
                (t1, k1), (t2, k2), (t3, k3), (t4, k4) = w_["t"]
                TT(t1[:, 0:n], t2[:, 0:n], w_["Rs"], ALU.mult, [k2, "Rcs"], [k1])
                TT(t3[:, 0:n], t4[:, 0:n], w_["Rc"], ALU.mult, [k4, "Rcs"], [k3])
            for m in ms:
                (t1, k1), (t2, k2), (t3, k3), (t4, k4) = W[m]["t"]
                hi, hik = s5h.next()
                TT(hi[:, 0:n], t1[:, 0:n], t3[:, 0:n], ALU.add, [k1, k3], [hik], eng="pool")
                W[m]["hi"] = (hi, hik)
            psy, pky = PS.half()
            for q, m in enumerate(ms):
                hr_, hrk_ = W[m]["hr"]
                hi_, hik_ = W[m]["hi"]
                MM(psy[:, 0:n], Cblk[:, 0, m, :], hr_[:, 0:n], q == 0, False, ["Cblk", hrk_], pky)
                MM(psy[:, 0:n], Cblk[:, 1, m, :], hi_[:, 0:n], False, q == 3, ["Cblk", hik_], pky)
            yt_, ytk = f32a.next()
            STT(yt_[:, 0:n], uf[:, ct, 0:n], s5dg[:, ct:ct + 1], psy[:, 0:n], ALU.mult, ALU.add,
                [UFK, "s5dg"] + pky, [ytk])
            ACT(ygf[:, ct, 0:n], yt_[:, 0:n], AF.Gelu_apprx_tanh, [ytk], [YGK])
            CP(ygb[:, ct, 0:n], ygf[:, ct, 0:n], [YGK], [f"ygb{ct}"], eng="pool")
            hgrn_head(ct)
        for co in range(4):
            ps, pk = PS.half()
            for ci_ in range(4):
                MM(ps[:, 0:n], gluW[:, ci_, co * 128:(co + 1) * 128], ygb[:, ci_, 0:n], ci_ == 0, ci_ == 3,
                   ["gluW", f"ygb{ci_}"], pk)
            sgm, sgk = f32a.next()
            ACT(sgm[:, 0:n], ps[:, 0:n], AF.Sigmoid, pk + ["s5dg"], [sgk], bias=s5dg[:, 4 + co:5 + co])
            TT(big[:, co, 0:n], ygf[:, co, 0:n], sgm[:, 0:n], ALU.mult, [YGK, sgk], [BIGK(co)])
        iv = v_tok[:].rearrange("p j (a c) -> p (j a) c", a=2)
        gv = sg_tok[:].rearrange("p j (a c) -> p (j a) c", a=2)
        IVK = lambda c_: f"v{c_ // 2}"
        GVK = lambda c_: f"sg{c_ // 2}"
        wi_, wik = load_w("w_in_cd", 0, 1536)
        for (c_, rows) in htiles:
            ps, pk = PS.full()
            for kt in range(8):
                MM(ps[0:rows, :], hnT[:, kt, c_ * 64:c_ * 64 + rows], wi_[:, kt, :], kt == 0, kt == 7, [HNK, wik], pk)
            CP(iv[0:rows, c_, :], ps[0:rows, :], pk, [IVK(c_)], eng="act")
        wg_, wgk = load_w("w_in_cd", 0, 2048)
        for (c_, rows) in htiles:
            ps, pk = PS.full()
            for kt in range(8):
                MM(ps[0:rows, :], hnT[:, kt, c_ * 64:c_ * 64 + rows], wg_[:, kt, :], kt == 0, kt == 7, [HNK, wgk], pk)
            ACT(gv[0:rows, c_, :], ps[0:rows, :], AF.Silu, pk, [GVK(c_)])
        HQK = [f"hq{h}" for h in range(4)]
        HKK = [f"hk{h}" for h in range(4)]
        for (j, rows) in htiles:
            t0 = j * 64
            ps, pk = PS.full()
            p3 = ps.rearrange("p (h t) -> p h t", h=4)
            for h in range(4):
                MM(p3[0:rows, h, 0:rows], hkT[:, h, t0:t0 + rows], hqT[:, h, t0:t0 + rows], True, True, HQK + HKK, pk)
            at, atk = sTb.next()
            STT(at[0:rows, :, 0:rows], p3[0:rows, :, 0:rows], 1e30,
                triu[0:rows, 0:rows].unsqueeze(1).to_broadcast([rows, 4, rows]), ALU.min, ALU.mult,
                pk + ["triu"], [atk])
            ps2, pk2 = PS.half()
            psb = ps2.bitcast(BF16)
            for h in range(4):
                TR(psb[0:rows, h * 128:(h + 1) * 128], hkT[:, h, t0:t0 + rows], ident[:, :], HKK + ["ident"], pk2)
            ktk_, ktkk = kdec.next()
            CP(ktk_[0:rows, :, :], psb.rearrange("p (h d) -> p h d", h=4)[0:rows], pk2, [ktkk], eng="act")
            TT(HSb[:], HS[:], hgE[:, 0, :, j:j + 1].to_broadcast([128, 4, 128]), ALU.mult, ["HS", "hgE"], ["HSb"])
            pso, pko = PS.full()
            o3 = pso.rearrange("p (h e) -> p h e", h=4)
            for h in range(4):
                MM(o3[0:rows, h, :], at[0:rows, h, 0:rows], iv[0:rows, j, h * 128:(h + 1) * 128], True, False,
                   [atk, IVK(j)], pko)
                MM(o3[0:rows, h, :], hqT[:, h, t0:t0 + rows], HSb[:, h, :], False, True, HQK + ["HSb"], pko)
            psk, pkk = PS.full()
            k3 = psk.rearrange("p (h e) -> p h e", h=4)
            for h in range(4):
                MM(k3[:, h, :], ktk_[0:rows, h, :], iv[0:rows, j, h * 128:(h + 1) * 128], True, True,
                   [ktkk, IVK(j)], pkk)
            ta, tak = tmpTok.next()
            TT(ta[:].rearrange("p (h e) -> p h e", h=4), k3, hgE[:, 2, :, j:j + 1].to_broadcast([128, 4, 128]),
               ALU.mult, pkk + ["hgE"], [tak])
            TT(HS[:], HS[:], hgE[:, 1, :, j:j + 1].to_broadcast([128, 4, 128]), ALU.mult, ["HS", "hgE"], ["HS"])
            TT(HS[:], HS[:], ta[:].rearrange("p (h e) -> p h e", h=4), ALU.add, ["HS", tak], ["HS"], eng="pool")
            jb, jk = junk_t, None
            cl, ck = colsA.next()
            for h in range(4):
                ACT(jb[0:rows, h * 128:(h + 1) * 128], o3[0:rows, h, :], AF.Square, pko, [ck],
                    accum_out=cl[0:rows, h:h + 1])
            rstd_from_ss(cl[0:rows, 0:4], rows, 128, ck)
            on, onk = tmpTok.next()
            TT(on[0:rows].rearrange("p (h e) -> p h e", h=4), o3[0:rows],
               cl[0:rows, 0:4].unsqueeze(2).to_broadcast([rows, 4, 128]), ALU.mult, pko + [ck], [onk])
            yb_, ybk = ya.next()
            TT(yb_[0:rows, 0:512], on[0:rows, :], gv[0:rows, j, :], ALU.mult, [onk, GVK(j)], [ybk], eng="pool")
            to_feature(yb_, rows, 4, big, 4, t0, gnT[:, 8:12], [ybk], BIGK)
        proj_token_major("w_out_cd", 8, big, BIGK, tiles,
                         lambda j, rows, a, ak, b, bk: postnorm_add(Hc, HK, j, rows, a, ak, b, bk, postg[0], "postg0"))

    last_store = None
    prev_n = None
    for g in range(min(NG, KNG) if KPHASE == "all" else 0):
        Hc = H[0]
        HK = "H0"
        meta_group = g == 0
        if meta_group:
            n, tiles, pos0 = NMETA, [(0, NMETA)], 0
            DMA("pool", Hc[0:NMETA, 0, :], meta_d, "ld_x", [], [HK])
        else:
            n, tiles = T, [(j, 128) for j in range(NT)]
            f0 = (g - 1) * T
            pos0 = NMETA + f0
            DMA("pool", Hc[:, :, :], x_d[f0:f0 + T, :].rearrange("(j p) d -> p j d", p=128), "ld_x", [], [HK])
        rb, rkey = ropeT.next()
        if meta_group:
            DMA("pool", rb[0:NMETA, 0, :], cd["rope"][0:NMETA, :], "ld_rope", [], [rkey])
        else:
            DMA("pool", rb[:, :, :], cd["rope"][pos0:pos0 + T, :].rearrange("(j p) d -> p j d", p=128), "ld_rope",
                [], [rkey])

        def dbg(slot):
            if DEBUG and (g < 3):
                if meta_group:
                    DMA("pool", dbg_d[slot, 0:NMETA, :], Hc[0:NMETA, 0, :], "st_dbg", [HK], [])
                else:
                    DMA("pool", dbg_d[slot, pos0:pos0 + T, :].rearrange("(j p) d -> p j d", p=128), Hc[:, :, :],
                        "st_dbg", [HK], [])

        load_postg(0)
        layer0_mixer(Hc, HK, tiles, n, meta_group, rb, rkey)
        dbg(0)
        if KSTOP >= 1:
            mlp(0, Hc, HK, tiles, n)
            dbg(1)
        if KSTOP >= 2:
            load_postg(1)
            layer1_mixer(Hc, HK, tiles, n, meta_group, prev_n)
            dbg(2)
        if KSTOP >= 3:
            mlp(1, Hc, HK, tiles, n)
            dbg(3)
        if not meta_group:
            last_store = DMA("pool", out_d[f0:f0 + T, :].rearrange("(j p) d -> p j d", p=128), Hc[:, :, :], "st_out",
                             [HK], [])
        prev_n = n
    fin = [d for d in [last_store, S.chan_last.get("st_dbg"), S.chan_last.get("ld_small"), S.chan_last.get("st_scr0")] if d is not None]
    S.op("sp", lambda e: e.nop(), extra_deps=fin)
    print("instr counts", {e: len(S.lists[e]) for e in ENGS}, flush=True)
    S.emit()
    S.stats = {e: max([i.val for i in S.lists[e] if i.chan is None and i.needs_inc] or [0]) for e in ENGS}
    S.stats.update({c: lst[-1].val for c, lst in S.chan_ins.items()})
    print("max sem values", S.stats, flush=True)
    return nc, consts


_CACHE = {}


def kernel(**inputs):
    if "prog" not in _CACHE:
        _CACHE["prog"] = build_program()
    nc, consts = _CACHE["prog"]
    lay = host_layouts(inputs)
    common = dict(lay)
    for n, v in consts.items():
        common["c_" + n] = v
    x = np.asarray(inputs["x"], dtype=np.float32)
    in_maps = []
    for c in range(8):
        m = dict(common)
        m["x"] = np.ascontiguousarray(x[c % 4])
        in_maps.append(m)
    res = run_bass_kernel_spmd(nc, in_maps, core_ids=list(range(8)))
    out = np.stack([np.asarray(res.results[b]["out"], dtype=np.float32) for b in range(4)], axis=0)
    if DEBUG:
        kernel.dbg = [np.asarray(res.results[b]["dbg"]) for b in range(4)]
    return out
```

```python
import numpy as np
import concourse.bass as bass
import concourse.mybir as mybir
from concourse.bass_utils import run_bass_kernel_spmd

F32 = mybir.dt.float32
BF16 = mybir.dt.bfloat16
I32 = mybir.dt.int32
AF = mybir.ActivationFunctionType
ALU = mybir.AluOpType
AX = mybir.AxisListType

T = 256
NT = T // 128
SEQ = 4096
NMETA = 16
D = 1024
DFF = 4096
EPS = 1e-6
NG = 1 + SEQ // T
import os
DEBUG = bool(int(os.environ.get("KDEBUG", "0")))
KNG = int(os.environ.get("KNG", "1000"))
KSTOP = int(os.environ.get("KSTOP", "3"))
KPHASE = os.environ.get("KPHASE", "all")

ENGS = ("pe", "act", "dve", "pool", "sp")


class Ins:
    __slots__ = ("eng", "fn", "waits", "idx", "needs_inc", "chan", "clock", "ninc", "val", "tag")


class Sched:
    def __init__(self, nc):
        self.nc = nc
        self.lists = {e: [] for e in ENGS}
        self.lastw = {}
        self.readers = {}
        self.clock = {e: {} for e in ENGS}
        self.chan_last = {}
        self.chan_ins = {}

    def _need(self, eng, dep, waits):
        src = dep.chan if dep.chan is not None else dep.eng
        if dep.chan is None and dep.eng == "pe" and eng == "pe":
            return
        if self.clock[eng].get(src, -1) >= dep.idx:
            return
        waits[src] = max(waits.get(src, -1), dep.idx)

    def op(self, eng, fn, reads=(), writes=(), chan=None, extra_deps=(), ninc=1):
        ins = Ins()
        ins.eng, ins.fn, ins.chan, ins.needs_inc, ins.ninc = eng, fn, chan, False, ninc
        ins.tag = getattr(self, "tag", "")
        psr = [k for k in reads if k[:2] == "ps" and k[2:].isdigit()]
        if psr:
            reads = [k for k in reads if k not in psr]
            writes = list(writes) + [k for k in psr if k not in writes]
        deps = []
        for k in reads:
            deps.extend(self.lastw.get(k, ()))
        for k in writes:
            deps.extend(self.lastw.get(k, ()))
            deps.extend(self.readers.get(k, ()))
        deps.extend(extra_deps)
        if chan is not None and chan in self.chan_last:
            deps.append(self.chan_last[chan])
        waits = {}
        for d in deps:
            self._need(eng, d, waits)
        ins.waits = []
        ck = self.clock[eng]
        for src, idx in waits.items():
            prod = self.lists[src][idx] if src in ENGS else self.chan_ins[src][idx]
            prod.needs_inc = True
            ins.waits.append(prod)
            for s, v in prod.clock.items():
                if ck.get(s, -1) < v:
                    ck[s] = v
        if chan is not None:
            lst = self.chan_ins.setdefault(chan, [])
            ins.idx = len(lst)
            lst.append(ins)
            self.chan_last[chan] = ins
            ins.clock = dict(ck)
            ins.clock[chan] = ins.idx
            self.lists[eng].append(ins)
        else:
            ins.idx = len(self.lists[eng])
            ins.clock = dict(ck)
            ins.clock[eng] = ins.idx
            self.lists[eng].append(ins)
        for k in reads:
            self.readers.setdefault(k, []).append(ins)
        me = chan if chan is not None else eng
        for k in writes:
            lst = [w for w in self.lastw.get(k, ()) if (w.chan if w.chan is not None else w.eng) != me]
            lst.append(ins)
            self.lastw[k] = lst
            self.readers[k] = []
        return ins

    def emit(self):
        nc = self.nc
        sems = {}
        for e in ("pe", "act", "dve", "pool"):
            sems[e] = nc.alloc_semaphore("s_" + e)
        for c in self.chan_ins:
            sems[c] = nc.alloc_semaphore("c_" + str(c))
        for e in ENGS:
            r = 0
            for ins in self.lists[e]:
                if ins.chan is None and ins.needs_inc:
                    r += 1
                    ins.val = r
        for c, lst in self.chan_ins.items():
            v = 0
            for ins in lst:
                v += 16 * ins.ninc
                ins.val = v

        def run(e):
            def body(eng):
                for ins in self.lists[e]:
                    for p in ins.waits:
                        src = p.chan if p.chan is not None else p.eng
                        eng.wait_ge(sems[src], p.val)
                    r = ins.fn(eng)
                    rl = r if isinstance(r, (list, tuple)) else [r]
                    if ins.chan is not None:
                        assert len(rl) == ins.ninc
                        for x in rl:
                            x.then_inc(sems[ins.chan], 16)
                    elif ins.needs_inc:
                        rl[-1].then_inc(sems[e], 1)
            return body

        with nc.Block() as block:
            block.tensor(run("pe"))
            block.scalar(run("act"))
            block.vector(run("dve"))
            block.gpsimd(run("pool"))
            block.sync(run("sp"))


class DB:
    def __init__(self, nc, name, shape, dtype, nbuf=2):
        self.bufs = [nc.alloc_sbuf_tensor(f"sb_{name}_{i}", list(shape), dtype) for i in range(nbuf)]
        self.keys = [f"{name}_{i}" for i in range(nbuf)]
        self.i = 0

    def next(self):
        i = self.i
        self.i = (i + 1) % len(self.bufs)
        return self.bufs[i], self.keys[i]


class PsumPool:
    def __init__(self, nc):
        self.banks = [nc.alloc_psum_tensor(f"ps{i}", [128, 512], F32) for i in range(8)]
        self.i = 0
        self.j = 0
        self.live = set()

    def half(self):
        while True:
            u = self.i
            self.i = (u + 1) % 16
            h, b = divmod(u, 8)
            if b not in self.live:
                break
        return self.banks[b][:, h * 256:(h + 1) * 256], [f"ps{b}"]

    def full(self):
        b = self.j
        self.j = (b + 1) % 8
        self.last_full = b
        return self.banks[b][:, :], [f"ps{b}"]


GAMMA = [1.0 - 2.0 ** (-5.0 - h) for h in range(4)]


def host_constants():
    c = {}
    c["ident"] = np.eye(128, dtype=np.float32)
    L = NMETA + SEQ
    pos = np.arange(L, dtype=np.float32)
    inv = (10000.0 ** (-np.arange(64, dtype=np.float32) / 64)).astype(np.float32)
    ang = (pos[:, None] * inv[None, :]).astype(np.float32)
    c["rope"] = np.concatenate([np.cos(ang), np.sin(ang)], axis=1).astype(np.float32)
    s = np.arange(128)[:, None]
    t = np.arange(128)[None, :]
    retD = np.zeros((128, 4, 128), np.float64)
    retG = np.zeros((128, 4), np.float64)
    retKD = np.zeros((128, 4), np.float64)
    retKDm = np.zeros((128, 4), np.float64)
    for h in range(4):
        g = GAMMA[h]
        same = (s // 64) == (t // 64)
        earlier = (s // 64) < (t // 64)
        w = np.where(same, g ** np.abs(t - s), np.where(earlier, g ** (t - s).clip(0), 0.0))
        retD[:, h, :] = w / (g ** (t + 1.0))
        retG[:, h] = (128 ** -0.5) * g ** (np.arange(128) + 1.0)
        retKD[:, h] = g ** (127.0 - np.arange(128))
        retKDm[:16, h] = g ** (15.0 - np.arange(16))
    c["retD"] = retD.astype(np.float32)
    c["retS"] = np.concatenate([retG, retKD, retKDm], axis=1).astype(np.float32)
    c["triu"] = (s <= t).astype(np.float32)
    rs = np.ones((128, T), np.float32)
    rs[:, ::64] = 0.0
    c["restart"] = rs
    c["iota"] = np.broadcast_to(np.arange(T, dtype=np.float32), (128, T)).copy()
    mb = np.zeros((128, 16, 8), np.float32)
    for m in range(16):
        for gl in range(2):
            mb[gl * 64:(gl + 1) * 64, m, (2 * m + gl) % 8] = 1.0
    c["maskB"] = mb
    mc = np.zeros((128, 4, 2, 2), np.float32)
    for mm in range(4):
        for gl in range(2):
            g8 = 2 * mm + gl
            mc[g8 * 16:(g8 + 1) * 16, mm, gl, 0] = 1.0
            mc[g8 * 16:(g8 + 1) * 16, mm, gl, 1] = -1.0
    c["maskC"] = mc
    return c


def host_layouts(inp):
    f = lambda a: np.ascontiguousarray(np.asarray(a, dtype=np.float32))
    o = {}
    o["meta"] = f(inp["meta"])
    o["w_in_ab"] = f(inp["w_in_ab"][0])
    o["w_out_ab"] = f(inp["w_out_ab"][0])
    o["w1"] = f(inp["mlp_w1"])
    o["w2"] = f(inp["mlp_w2"])
    o["w_in_cd"] = f(inp["w_in_cd"][0])
    o["w_out_cd"] = f(inp["w_out_cd"][0])
    o["glu_w"] = f(inp["s5_glu_w"][0])
    ng = np.asarray(inp["norm_g"], np.float32)
    pre = ng[:, [0, 2], :].reshape(2, 2, 8, 128)
    o["pgT"] = f(pre.transpose(3, 0, 1, 2).reshape(128, 32))
    o["postg"] = f(ng[:, [1, 3], :].reshape(4, 1024))
    gn = np.concatenate([np.asarray(inp["ret_gn"][0], np.float32).reshape(8, 128).T,
                         np.asarray(inp["hg_gn"][0], np.float32).reshape(4, 128).T], axis=1)
    o["gnT"] = f(gn)
    fm8 = lambda v: np.asarray(v, np.float32).reshape(8, 128).T
    cw = np.asarray(inp["rg_conv_w"][0], np.float32)
    lru = np.stack([fm8(inp["rg_lam"][0]), fm8(inp["rg_ba"][0]), fm8(inp["rg_bi"][0]), fm8(inp["rg_conv_b"][0]),
                    fm8(cw[0]), fm8(cw[1]), fm8(cw[2]), fm8(cw[3])], axis=2)
    o["lru_s"] = f(lru)
    o["rg_wa"] = f(inp["rg_wa"][0])
    o["rg_wi"] = f(inp["rg_wi"][0])
    hl = np.asarray(inp["hg_lb_logits"], np.float32).reshape(2, 4, 128)
    o["hgl"] = f(hl.transpose(2, 0, 1).reshape(128, 8))
    st = lambda a: np.asarray(a, np.float32).reshape(16, 2, 64).transpose(1, 2, 0).reshape(128, 16)
    ldt = np.repeat(np.asarray(inp["s5_log_dt"][0], np.float32)[:, None], 64, axis=1)
    o["s5s"] = f(np.stack([st(inp["s5_a_re_log"][0]), st(inp["s5_a_im"][0]), st(ldt)], axis=2))
    sb = lambda a: np.asarray(a, np.float32).reshape(16, 2, 64, 16).transpose(1, 2, 0, 3).reshape(128, 16, 16)
    o["s5b"] = f(np.stack([sb(inp["s5_b_re"][0]), sb(inp["s5_b_im"][0])], axis=2))
    sc = lambda a: np.asarray(a, np.float32).reshape(4, 8, 16, 64).transpose(1, 2, 0, 3).reshape(128, 4, 64)
    o["s5c"] = f(np.stack([sc(inp["s5_c_re"][0]), sc(inp["s5_c_im"][0])], axis=2))
    sd = np.asarray(inp["s5_d"][0], np.float32).reshape(4, 128).T
    gb = np.asarray(inp["s5_glu_b"][0], np.float32).reshape(4, 128).T
    o["s5dg"] = f(np.concatenate([sd, gb], axis=1))
    return o


def build_program():
    nc = bass.Bass("TRN2", target_bir_lowering=False)
    S = Sched(nc)
    consts = host_constants()

    def din(name, shape, dt=F32):
        return nc.dram_tensor(name, list(shape), dt, kind="ExternalInput").ap()

    x_d = din("x", [SEQ, D])
    meta_d = din("meta", [NMETA, D])
    wdefs = [("w_in_ab", 1024, 5120), ("w_out_ab", 2048, 1024), ("w1_0", 1024, 4096), ("w2_0", 4096, 1024),
             ("w_in_cd", 1024, 2560), ("w_out_cd", 1024, 1024), ("w1_1", 1024, 4096), ("w2_1", 4096, 1024),
             ("glu_w", 512, 512)]
    w1_d = din("w1", [2, 1024, 4096])
    w2_d = din("w2", [2, 4096, 1024])
    wsrc = {"w_in_ab": din("w_in_ab", [1024, 5120]), "w_out_ab": din("w_out_ab", [2048, 1024]),
            "w1_0": w1_d[0], "w1_1": w1_d[1], "w2_0": w2_d[0], "w2_1": w2_d[1],
            "w_in_cd": din("w_in_cd", [1024, 2560]), "w_out_cd": din("w_out_cd", [1024, 1024]),
            "glu_w": din("glu_w", [512, 512])}
    wscr = {n: (nc.dram_tensor("scr_" + n, [r, c], BF16, kind="Internal").ap() if n == "glu_w" else
                nc.dram_tensor("scr_" + n, [r // 1024, c // 512, 128, 8, 512], BF16, kind="Internal").ap())
            for n, r, c in wdefs}
    small = {}
    for n, shp in [("pgT", [128, 32]), ("postg", [4, 1024]), ("gnT", [128, 12]), ("lru_s", [128, 8, 8]),
                   ("rg_wa", [16, 64, 64]), ("rg_wi", [16, 64, 64]), ("hgl", [128, 8]), ("s5s", [128, 16, 3]),
                   ("s5b", [128, 16, 2, 16]), ("s5c", [128, 4, 2, 64]), ("s5dg", [128, 8])]:
        small[n] = din(n, shp)
    cd = {n: din("c_" + n, list(v.shape)) for n, v in consts.items()}
    out_d = nc.dram_tensor("out", [SEQ, D], F32, kind="ExternalOutput").ap()
    dbg_d = nc.dram_tensor("dbg", [4, NMETA + SEQ, D], F32, kind="ExternalOutput").ap() if DEBUG else None

    def sb(name, shape, dt=F32):
        return nc.alloc_sbuf_tensor("sb_" + name, list(shape), dt)

    def MM(out, lhsT, rhs, start, stop, r, w):
        S.op("pe", lambda e: e.matmul(out, lhsT=lhsT, rhs=rhs, start=start, stop=stop), reads=r, writes=w)

    def TR(out, in_, idn, r, w):
        S.op("pe", lambda e: e.transpose(out, in_, idn), reads=r, writes=w)

    def ACT(out, in_, func, r, w, **kw):
        S.op("act", lambda e: e.activation(out=out, in_=in_, func=func, **kw), reads=r, writes=w)

    def TT(out, a, b, op, r, w, eng="dve"):
        S.op(eng, lambda e: e.tensor_tensor(out=out, in0=a, in1=b, op=op), reads=r, writes=w)

    def TS(out, a, s1, s2, op0, op1, r, w, eng="dve"):
        if s2 is None:
            S.op(eng, lambda e: e.tensor_scalar(out=out, in0=a, scalar1=s1, scalar2=None, op0=op0), reads=r, writes=w)
        else:
            S.op(eng, lambda e: e.tensor_scalar(out=out, in0=a, scalar1=s1, scalar2=s2, op0=op0, op1=op1),
                 reads=r, writes=w)

    def STT(out, a, s, b, op0, op1, r, w, eng="dve"):
        S.op(eng, lambda e: e.scalar_tensor_tensor(out=out, in0=a, scalar=s, in1=b, op0=op0, op1=op1),
             reads=r, writes=w)

    def CP(out, a, r, w, eng="dve"):
        if eng == "act":
            S.op("act", lambda e: e.copy(out=out, in_=a), reads=r, writes=w)
        else:
            S.op(eng, lambda e: e.tensor_copy(out=out, in_=a), reads=r, writes=w)

    def DMA(q, out, in_, chan, r, w):
        return S.op(q, lambda e: e.dma_start(out=out, in_=in_), reads=r, writes=w, chan=chan)

    def MEMSET(ap, val, w, eng="dve"):
        S.op(eng, lambda e: e.memset(ap, val), writes=w)

    PS = PsumPool(nc)
    _rr = [0]

    def ew():
        _rr[0] += 1
        return "pool" if _rr[0] % 3 == 0 else "dve"

    ident_f = sb("ident_f", [128, 128])
    ident = sb("ident", [128, 128], BF16)
    H = [sb(f"H{i}", [128, NT, D]) for i in range(2)]
    hnT = sb("hnT", [128, 8, T], BF16)
    big = sb("big", [128, 32, T], BF16)
    NSLOT = 3
    Wr = [sb(f"W{i}", [128, 8 * 512], BF16) for i in range(NSLOT)]
    wslot = [0]
    pgT = sb("pgT", [128, 32])
    gnT = sb("gnT", [128, 12])
    postg = [sb(f"postg{i}", [128, D]) for i in range(2)]
    epsc = sb("epsc", [128, 2])
    ropeT = DB(nc, "rope", [128, NT, 128], F32, 2)
    retD = sb("retD", [128, 4, 128])
    retS = sb("retS", [128, 12])
    triu = sb("triu", [128, 128])
    restart = sb("restart", [128, T])
    RS = sb("RS", [128, 4, 256])
    RSb = sb("RSb", [128, 4, 256], BF16)
    lru_s = sb("lru_s", [128, 8, 8])
    lru_sp8 = sb("lru_sp8", [128, 8])
    Wab = sb("Wab", [128, 2, 8, 128], BF16)
    lru_carry = sb("lru_carry", [128, 8, 3])
    lru_h = sb("lru_h", [128, 8])
    hg_lb = sb("hg_lb", [128, 8])
    HS = sb("HS", [128, 4, 128])
    HSb = sb("HSb", [128, 4, 128], BF16)
    s5_mag = sb("s5_mag", [128, 16])
    s5_bnd = sb("s5_bnd", [128, 2, 2, 16])
    s5_st = sb("s5_st", [128, 2, 16])
    s5_ini = sb("s5_ini", [128, 2, 16])
    Bblk = sb("Bblk", [128, 2, 16, 128], BF16)
    Cblk = sb("Cblk", [128, 2, 16, 128], BF16)
    Rcs = sb("Rcs", [128, 2, 16, T], BF16)
    s5dg = sb("s5dg", [128, 8])
    gluW = sb("gluW", [128, 4, 512], BF16)

    f32a = DB(nc, "f32a", [128, T + 4], F32, 16)
    bf16a = DB(nc, "bf16a", [128, T], BF16, 4)
    hsb = DB(nc, "hsb", [128, T], F32, 5)
    s5h = DB(nc, "s5h", [128, T], BF16, 8)
    hn_tok = DB(nc, "hn_tok", [128, D], BF16, 1)
    junk_t = sb("junk", [128, D], BF16)
    colsA = DB(nc, "colsA", [128, 16], F32, 8)
    tmpTok = DB(nc, "tmpTok", [128, 512], F32, 4)
    qkr = sb("qkr", [128, NT, 2, 4, 128], BF16)
    v_tok = sb("v_tok", [128, NT, 1024], BF16)
    sg_tok = sb("sg_tok", [128, NT, 1024], BF16)
    qT = DB(nc, "qT", [128, 4, 128], BF16, 2)
    kT = DB(nc, "kT", [128, 4, 128], BF16, 2)
    sTb = DB(nc, "sTb", [128, 4, 128], BF16, 2)
    kdec = DB(nc, "kdec", [128, 4, 128], BF16, 2)
    o_sb = DB(nc, "o_sb", [128, 4, 256], F32, 2)
    ya = DB(nc, "ya", [128, D], BF16, 2)
    assert T == 256
    uf = o_sb.bufs[0]
    UFK = o_sb.keys[0]
    ub = sb("ub", [128, 4, T], BF16)
    ygf = o_sb.bufs[1]
    YGK = o_sb.keys[1]
    ygb = sb("ygb", [128, 4, T], BF16)
    hqT = sb("hqT", [128, 4, T], BF16)
    hkT = sb("hkT", [128, 4, T], BF16)
    hgE = sb("hgE", [128, 3, 4, T // 64])

    BIGK = lambda f: f"big.{f}"
    HNK = "hnT"

    def load(dst_ap, src_ap, key, q="act"):
        DMA(q, dst_ap, src_ap, "ld_small", [], [key])

    load(ident_f[:], cd["ident"], "ident_f")
    CP(ident[:], ident_f[:], ["ident_f"], ["ident"])
    load(pgT[:], small["pgT"], "pgT")
    load(gnT[:], small["gnT"], "gnT")
    load(retD[:], cd["retD"], "retD")
    load(retS[:], cd["retS"], "retS")
    load(triu[:], cd["triu"], "triu")
    load(restart[:], cd["restart"], "restart")
    load(lru_s[:], small["lru_s"], "lru_s")
    load(s5dg[:], small["s5dg"], "s5dg")
    MEMSET(epsc[:, 0:1], EPS, ["epsc"])
    MEMSET(epsc[:, 1:2], 1.0, ["epsc"])
    MEMSET(RS[:], 0.0, ["RS"])
    MEMSET(RSb[:], 0.0, ["RSb"])
    MEMSET(HS[:], 0.0, ["HS"])
    MEMSET(HSb[:], 0.0, ["HSb"])
    MEMSET(lru_carry[:], 0.0, ["lru_carry"])
    MEMSET(lru_h[:], 0.0, ["lru_h"])
    MEMSET(s5_st[:], 0.0, ["s5_st"])

    ACT(lru_sp8[:], lru_s[:, :, 0], AF.Exp, ["lru_s"], ["lru_sp8"], scale=-1.0)
    ACT(lru_sp8[:], lru_sp8[:], AF.Ln, ["lru_sp8", "epsc"], ["lru_sp8"], bias=epsc[:, 1:2])
    TS(lru_sp8[:], lru_sp8[:], -8.0, None, ALU.mult, None, ["lru_sp8"], ["lru_sp8"])
    for wi, nm in enumerate(("rg_wa", "rg_wi")):
        wabf = o_sb.bufs[wi][:].rearrange("p a (c d) -> p (a c) d", d=128)
        wk_ = o_sb.keys[wi]
        MEMSET(wabf, 0.0, [wk_])
        src = small[nm].rearrange("(ct bl) i j -> bl i ct j", bl=2)
        for bl in range(2):
            DMA("act", wabf[bl * 64:(bl + 1) * 64, :, bl * 64:(bl + 1) * 64], src[bl], "ld_small", [], [wk_])
        CP(Wab[:, wi, :, :], wabf, [wk_], ["Wab"])

    hgl = sb("hgl", [128, 8])
    load(hgl[:], small["hgl"], "hgl")
    TT(hg_lb[:, 0:4], hgl[:, 0:4], hgl[:, 4:8], ALU.subtract, ["hgl"], ["hg_lb"])
    ACT(hg_lb[:, 0:4], hg_lb[:, 0:4], AF.Sigmoid, ["hg_lb"], ["hg_lb"])
    TS(hg_lb[:, 4:8], hg_lb[:, 0:4], -1.0, 1.0, ALU.mult, ALU.add, ["hg_lb"], ["hg_lb"])

    s5s = sb("s5s", [128, 16, 3])
    load(s5s[:], small["s5s"], "s5s")
    pp = sb("s5pp", [128, 12, 16])
    PPK = ["s5pp"]
    iota = f32a.bufs[0][:, 0:T]
    IOK = f32a.keys[0]
    DMA("act", iota, cd["iota"], "ld_small", [], [IOK])
    DT_, ARE, AIM, TH, MAG, COS, SIN, NRE, DEN, ZRE, ZIM, TMP = range(12)
    ACT(pp[:, DT_, :], s5s[:, :, 2], AF.Exp, ["s5s"], PPK)
    ACT(pp[:, ARE, :], s5s[:, :, 0], AF.Exp, ["s5s"], PPK)
    TS(pp[:, ARE, :], pp[:, ARE, :], -1.0, None, ALU.mult, None, PPK, PPK)
    CP(pp[:, AIM, :], s5s[:, :, 1], ["s5s"], PPK)
    TT(pp[:, TH, :], pp[:, DT_, :], pp[:, AIM, :], ALU.mult, PPK, PPK)
    TT(pp[:, TMP, :], pp[:, DT_, :], pp[:, ARE, :], ALU.mult, PPK, PPK)
    ACT(pp[:, MAG, :], pp[:, TMP, :], AF.Exp, PPK, PPK)
    CP(s5_mag[:], pp[:, MAG, :], PPK, ["s5_mag"])

    TWO_PI = 2.0 * np.pi
    redi = f32a.bufs[1][:].bitcast(I32)[:, 0:T]
    redf = f32a.bufs[2][:, 0:T]
    RIK, RFK = f32a.keys[1], f32a.keys[2]

    def sincos(out_sin, out_cos, ang_ap, shape_n, keys_r, keys_w):
        ri = redi[:, 0:shape_n]
        rf = redf[:, 0:shape_n]
        for out_ap, shift in ((out_sin, 0.0), (out_cos, 0.5 * np.pi)):
            if out_ap is None:
                continue
            TS(ri, ang_ap, shift, 1.0 / TWO_PI, ALU.add, ALU.mult, keys_r, [RIK])
            STT(rf, ri, -TWO_PI, ang_ap, ALU.mult, ALU.add, [RIK] + keys_r, [RFK])
            if shift != 0.0:
                TS(rf, rf, shift, None, ALU.add, None, [RFK], [RFK])
            TS(rf, rf, 3.1415925, -3.1415925, ALU.min, ALU.max, [RFK], [RFK])
            ACT(out_ap, rf, AF.Sin, [RFK], keys_w)

    sincos(pp[:, SIN, :], pp[:, COS, :], pp[:, TH, :], 16, PPK, PPK)
    ang2 = sb("ang2", [128, 2, 16])
    TS(ang2[:, 0, :], pp[:, TH, :], float(T), None, ALU.mult, None, PPK, ["ang2"])
    TS(ang2[:, 1, :], pp[:, TH, :], float(NMETA), None, ALU.mult, None, PPK, ["ang2"])
    bndt = sb("bndt", [128, 2, 2, 16])
    for fr in range(2):
        sincos(bndt[:, fr, 1, :], bndt[:, fr, 0, :], ang2[:, fr, :], 16, ["ang2"], ["bndt"])
    CP(s5_bnd[:], bndt[:], ["bndt"], ["s5_bnd"])
    angT = f32a.bufs[3][:, 0:T]
    ATK = f32a.keys[3]
    for m in range(16):
        TS(angT, iota, pp[:, TH, m:m + 1], None, ALU.mult, None, PPK + [IOK], [ATK])
        sincos(Rcs[:, 1, m, :], Rcs[:, 0, m, :], angT, T, [ATK], ["Rcs"])
    TT(pp[:, COS, :], pp[:, COS, :], pp[:, MAG, :], ALU.mult, PPK, PPK)
    TT(pp[:, SIN, :], pp[:, SIN, :], pp[:, MAG, :], ALU.mult, PPK, PPK)
    TS(pp[:, NRE, :], pp[:, COS, :], -1.0, None, ALU.add, None, PPK, PPK)
    TT(pp[:, DEN, :], pp[:, ARE, :], pp[:, ARE, :], ALU.mult, PPK, PPK)
    TT(pp[:, TMP, :], pp[:, AIM, :], pp[:, AIM, :], ALU.mult, PPK, PPK)
    TT(pp[:, DEN, :], pp[:, DEN, :], pp[:, TMP, :], ALU.add, PPK, PPK)
    S.op("dve", lambda e: e.reciprocal(out=pp[:, DEN, :], in_=pp[:, DEN, :]), reads=PPK, writes=PPK)
    TT(pp[:, ZRE, :], pp[:, NRE, :], pp[:, ARE, :], ALU.mult, PPK, PPK)
    TT(pp[:, TMP, :], pp[:, SIN, :], pp[:, AIM, :], ALU.mult, PPK, PPK)
    TT(pp[:, ZRE, :], pp[:, ZRE, :], pp[:, TMP, :], ALU.add, PPK, PPK)
    TT(pp[:, ZRE, :], pp[:, ZRE, :], pp[:, DEN, :], ALU.mult, PPK, PPK)
    TT(pp[:, ZIM, :], pp[:, SIN, :], pp[:, ARE, :], ALU.mult, PPK, PPK)
    TT(pp[:, TMP, :], pp[:, NRE, :], pp[:, AIM, :], ALU.mult, PPK, PPK)
    TT(pp[:, ZIM, :], pp[:, ZIM, :], pp[:, TMP, :], ALU.subtract, PPK, PPK)
    TT(pp[:, ZIM, :], pp[:, ZIM, :], pp[:, DEN, :], ALU.mult, PPK, PPK)
    s5b = tmpTok.bufs[0][:].rearrange("p (m r c) -> p m r c", m=16, r=2)
    DMA("act", s5b, small["s5b"], "ld_small", [], [tmpTok.keys[0]])
    bbn = tmpTok.bufs[1][:].rearrange("p (r m c) -> p r m c", r=2, m=16)
    tb = tmpTok.bufs[2][:].rearrange("p (r m c) -> p r m c", r=2, m=16)
    zre_b = pp[:, ZRE, :].unsqueeze(2).to_broadcast([128, 16, 16])
    zim_b = pp[:, ZIM, :].unsqueeze(2).to_broadcast([128, 16, 16])
    TT(tb[:, 0], s5b[:, :, 0, :], zre_b, ALU.mult, PPK + [tmpTok.keys[0]], [tmpTok.keys[2]])
    TT(tb[:, 1], s5b[:, :, 1, :], zim_b, ALU.mult, PPK + [tmpTok.keys[0]], [tmpTok.keys[2]])
    TT(bbn[:, 0], tb[:, 0], tb[:, 1], ALU.subtract, [tmpTok.keys[2]], [tmpTok.keys[1]])
    TT(tb[:, 0], s5b[:, :, 1, :], zre_b, ALU.mult, PPK + [tmpTok.keys[0]], [tmpTok.keys[2]])
    TT(tb[:, 1], s5b[:, :, 0, :], zim_b, ALU.mult, PPK + [tmpTok.keys[0]], [tmpTok.keys[2]])
    TT(bbn[:, 1], tb[:, 0], tb[:, 1], ALU.add, [tmpTok.keys[2]], [tmpTok.keys[1]])
    maskB = sb("maskB", [128, 16, 8])
    load(maskB[:], cd["maskB"], "maskB")
    maskC = sb("maskC", [128, 4, 2, 2])
    load(maskC[:], cd["maskC"], "maskC")
    s5c = tmpTok.bufs[3][:].rearrange("p (a r q) -> p a r q", a=4, r=2)
    DMA("act", s5c, small["s5c"], "ld_small", [], [tmpTok.keys[3]])
    wide = DB(nc, "wide", [128, 128], BF16, 2)
    for ri in range(2):
        for m in range(16):
            wd, wk = wide.next()
            TT(wd[:].rearrange("p (g c) -> p g c", g=8), bbn[:, ri, m, :].unsqueeze(1).to_broadcast([128, 8, 16]),
               maskB[:, m, :].unsqueeze(2).to_broadcast([128, 8, 16]), ALU.mult, [tmpTok.keys[1], "maskB"], [wk])
            ps, pk = PS.half()
            psb = ps.bitcast(BF16)
            TR(psb[:, 0:128], wd[:], ident[:], [wk, "ident"], pk)
            CP(Bblk[:, ri, m, :], psb[:, 0:128], pk, ["Bblk"], eng="act")
    for ri in range(2):
        for m in range(16):
            wd, wk = wide.next()
            TT(wd[:].rearrange("p (g q) -> p g q", g=2), s5c[:, m // 4, ri, :].unsqueeze(1).to_broadcast([128, 2, 64]),
               maskC[:, m % 4, :, ri].unsqueeze(2).to_broadcast([128, 2, 64]), ALU.mult, [tmpTok.keys[3], "maskC"], [wk])
            ps, pk = PS.half()
            psb = ps.bitcast(BF16)
            TR(psb[:, 0:128], wd[:], ident[:], [wk, "ident"], pk)
            CP(Cblk[:, ri, m, :], psb[:, 0:128], pk, ["Cblk"], eng="act")

    cast_engs = ["dve", "act", "pool"]
    ci = 0
    for n, R, C in (wdefs if KPHASE != "pro" else []):
        for rt in range(R // 128):
            for c0 in range(0, C, 2048):
                w = min(2048, C - c0)
                s = wslot[0]
                wslot[0] = (s + 1) % NSLOT
                stg = Wr[s][:].bitcast(F32)
                DMA("sp", stg[:, 0:w], wsrc[n][rt * 128:(rt + 1) * 128, c0:c0 + w], f"w{s}", [], [f"W{s}"])
                bi = ci % 4
                cb = big[:, bi * 8:(bi + 1) * 8, :].rearrange("p a b -> p (a b)")
                ck = [BIGK(bi * 8 + t_) for t_ in range(8)]
                CP(cb[:, 0:w], stg[:, 0:w], [f"W{s}"], ck, eng=cast_engs[ci % 3])
                ci += 1
                if n == "glu_w":
                    dst = wscr[n][rt * 128:(rt + 1) * 128, c0:c0 + w]
                    srcv = cb[:, 0:w]
                else:
                    dst = wscr[n][rt // 8, c0 // 512:(c0 + w) // 512, :, rt % 8, :].rearrange("b p c -> p b c")
                    srcv = cb[:, 0:w].rearrange("p (b c) -> p b c", c=512)
                DMA("pool", dst, srcv, f"st_scr{bi}", ck, [f"scr_{n}.{rt}.{c0 // 2048}"])
    if KPHASE != "pro":
      DMA("sp", gluW[:], wscr["glu_w"].rearrange("(kt p) c -> p kt c", p=128), "ld_glu",
        [f"scr_glu_w.{rt}.0" for rt in range(4)], ["gluW"])

    def load_w(name, k0, c0, ncols=512, nk=8):
        s = wslot[0]
        wslot[0] = (s + 1) % NSLOT
        view = Wr[s][:].rearrange("p (k c) -> p k c", k=8)
        assert nk == 8 and ncols == 512 and k0 % 1024 == 0 and c0 % 512 == 0
        DMA("sp", view[:, 0:nk, 0:ncols], wscr[name][k0 // 1024, c0 // 512],
            f"w{s}", [f"scr_{name}.{k0 // 128 + i_}.{c0 // 2048}" for i_ in range(nk)], [f"W{s}"])
        return view, f"W{s}"

    def rstd_from_ss(ss_ap, rows, dim, ck):
        k = ss_ap.shape[1]
        ACT(ss_ap, ss_ap, AF.Sqrt, [ck, "epsc"], [ck], scale=1.0 / dim, bias=epsc[0:rows, 0:1])
        S.op("dve", lambda e: e.reciprocal(out=ss_ap, in_=ss_ap), reads=[ck], writes=[ck])

    def to_feature(src_tok, rows, ntile, dst, dst_f0, tok0, gain_ap, rkeys, wkeys_fn):
        for q0 in range(0, ntile, 4):
            nq = min(4, ntile - q0)
            ps, pk = PS.half()
            psb = ps.bitcast(BF16)
            for i in range(nq):
                TR(psb[:, i * 128:i * 128 + rows], src_tok[0:rows, (q0 + i) * 128:(q0 + i + 1) * 128],
                   ident[0:rows, 0:rows], rkeys + ["ident"], pk)
            src = psb.rearrange("p (k t) -> p k t", k=4)[:, 0:nq, 0:rows]
            dsl = dst[:, dst_f0 + q0:dst_f0 + q0 + nq, tok0:tok0 + rows]
            wk = [wkeys_fn(dst_f0 + q0 + i) for i in range(nq)]
            if gain_ap is None:
                CP(dsl, src, pk, wk, eng="act")
            else:
                g = gain_ap[:, q0:q0 + nq].unsqueeze(2).to_broadcast([128, nq, rows])
                TT(dsl, src, g, ALU.mult, pk + ["pgT", "gnT"], wk)

    def prenorm(Hc, HK, tiles, gcol):
        for (j, rows) in tiles:
            jb, jk = junk_t, None
            cl, ck = colsA.next()
            ACT(jb[0:rows, :], Hc[0:rows, j, :], AF.Square, [HK], [ck], accum_out=cl[0:rows, 0:1])
            rstd_from_ss(cl[0:rows, 0:1], rows, D, ck)
            hb, hk = hn_tok.next()
            TS(hb[0:rows, :], Hc[0:rows, j, :], cl[0:rows, 0:1], None, ALU.mult, None, [HK, ck], [hk])
            to_feature(hb, rows, 8, hnT, 0, j * 128, pgT[:, gcol:gcol + 8], [hk], lambda f: HNK)

    def postnorm_add(Hc, HK, j, rows, psA, pkA, psB, pkB, gtab, gk):
        cl, ck = colsA.next()
        jb, jk = junk_t, None
        ACT(jb[0:rows, 0:512], psA[0:rows, :], AF.Square, pkA, [ck], accum_out=cl[0:rows, 0:1])
        ACT(jb[0:rows, 512:1024], psB[0:rows, :], AF.Square, pkB, [ck], accum_out=cl[0:rows, 1:2])
        TT(cl[0:rows, 0:1], cl[0:rows, 0:1], cl[0:rows, 1:2], ALU.add, [ck], [ck])
        rstd_from_ss(cl[0:rows, 0:1], rows, D, ck)
        for (ps, pk, c0) in ((psA, pkA, 0), (psB, pkB, 512)):
            tb_, tk = tmpTok.next()
            STT(tb_[0:rows, :], ps[0:rows, :], cl[0:rows, 0:1], gtab[0:rows, c0:c0 + 512], ALU.mult, ALU.mult,
                pk + [ck, gk], [tk])
            TT(Hc[0:rows, j, c0:c0 + 512], Hc[0:rows, j, c0:c0 + 512], tb_[0:rows, :], ALU.add, [HK, tk], [HK],
               eng="pool")

    def proj_token_major(wname, F, srcbuf, src_key_fn, tiles, consume, hook=None):
        banks = {}
        live = set()
        for (j, rows) in tiles:
            a_ = PS.full()
            live.add(PS.last_full)
            b_ = PS.full()
            live.add(PS.last_full)
            banks[j] = (a_, b_)
        nunit = F // 8
        for cc in range(2):
            for u in range(nunit):
                wv, wk = load_w(wname, u * 1024, cc * 512)
                for (j, rows) in tiles:
                    ps, pk = banks[j][cc]
                    for fi in range(8):
                        f = u * 8 + fi
                        MM(ps[0:rows, :], srcbuf[:, f, j * 128:j * 128 + rows], wv[:, fi, :], f == 0, f == F - 1,
                           [src_key_fn(f), wk], pk)
        if hook is not None:
            PS.live = live
            hook()
            PS.live = set()
        for (j, rows) in tiles:
            (psA, pkA), (psB, pkB) = banks[j]
            consume(j, rows, psA, pkA, psB, pkB)

    def load_postg(l):
        for i in range(2):
            DMA("pool", postg[i][:], small["postg"][l * 2 + i:l * 2 + i + 1, :].partition_broadcast(128), "ld_postg",
                [], [f"postg{i}"])

    def mlp(l, Hc, HK, tiles, n, hook=None):
        prenorm(Hc, HK, tiles, (l * 2 + 1) * 8)
        wn1, wn2 = f"w1_{l}", f"w2_{l}"
        for blk in range(8):
            wv, wk = load_w(wn1, 0, blk * 512)
            for ft in range(4):
                f = blk * 4 + ft
                ps, pk = PS.half()
                for kt in range(8):
                    MM(ps[:, 0:n], wv[:, kt, ft * 128:(ft + 1) * 128], hnT[:, kt, 0:n], kt == 0, kt == 7, [HNK, wk], pk)
                tf, tk = f32a.next()
                ACT(tf[:, 0:n], ps[:, 0:n], AF.Relu, pk, [tk])
                TT(big[:, f, 0:n], tf[:, 0:n], tf[:, 0:n], ALU.mult, [tk], [BIGK(f)], eng=ew())
        proj_token_major(wn2, 32, big, BIGK, tiles,
                         lambda j, rows, a, ak, b, bk: postnorm_add(Hc, HK, j, rows, a, ak, b, bk, postg[1], "postg1"),
                         hook=hook)

    def layer0_mixer(Hc, HK, tiles, n, meta_group, rope_ap, rope_key, do_prenorm=True):
        if do_prenorm:
            prenorm(Hc, HK, tiles, 0)
        for qk in range(2):
            wv, wk = load_w("w_in_ab", 0, qk * 512)
            for (j, rows) in tiles:
                ps, pk = PS.full()
                for kt in range(8):
                    MM(ps[0:rows, :], hnT[:, kt, j * 128:j * 128 + rows], wv[:, kt, :], kt == 0, kt == 7, [HNK, wk], pk)
                x3 = ps.rearrange("p (h d) -> p h d", h=4)
                x1 = x3[0:rows, :, 0:64]
                x2 = x3[0:rows, :, 64:128]
                cosb = rope_ap[0:rows, j, 0:64].unsqueeze(1).to_broadcast([rows, 4, 64])
                sinb = rope_ap[0:rows, j, 64:128].unsqueeze(1).to_broadcast([rows, 4, 64])
                t1, k1 = tmpTok.next()
                t2, k2 = tmpTok.next()
                a1 = t1[0:rows, 0:256].rearrange("p (h d) -> p h d", h=4)
                a2 = t2[0:rows, 0:256].rearrange("p (h d) -> p h d", h=4)
                b1 = t1[0:rows, 256:512].rearrange("p (h d) -> p h d", h=4)
                b2 = t2[0:rows, 256:512].rearrange("p (h d) -> p h d", h=4)
                TT(a1, x1, cosb, ALU.mult, pk + [rope_key], [k1])
                TT(a2, x2, sinb, ALU.mult, pk + [rope_key], [k2])
                TT(b1, x1, sinb, ALU.mult, pk + [rope_key], [k1])
                TT(b2, x2, cosb, ALU.mult, pk + [rope_key], [k2])
                TT(qkr[0:rows, j, qk, :, 0:64], a1, a2, ALU.subtract, [k1, k2], [f"qkr{j}"], eng="pool")
                TT(qkr[0:rows, j, qk, :, 64:128], b1, b2, ALU.add, [k1, k2], [f"qkr{j}"], eng="pool")
        for vb in range(2):
            wv, wk = load_w("w_in_ab", 0, 1024 + vb * 512)
            for (j, rows) in tiles:
                ps, pk = PS.full()
                for kt in range(8):
                    MM(ps[0:rows, :], hnT[:, kt, j * 128:j * 128 + rows], wv[:, kt, :], kt == 0, kt == 7, [HNK, wk], pk)
                CP(v_tok[0:rows, j, vb * 512:(vb + 1) * 512], ps[0:rows, :], pk, [f"v{j}"], eng="act")
        for gb in range(2):
            wv, wk = load_w("w_in_ab", 0, 2048 + gb * 512)
            for (j, rows) in tiles:
                ps, pk = PS.full()
                for kt in range(8):
                    MM(ps[0:rows, :], hnT[:, kt, j * 128:j * 128 + rows], wv[:, kt, :], kt == 0, kt == 7, [HNK, wk], pk)
                ACT(sg_tok[0:rows, j, gb * 512:(gb + 1) * 512], ps[0:rows, :], AF.Silu, pk, [f"sg{j}"])
        for (j, rows) in tiles:
            qt, qtk = qT.next()
            kt_, ktk = kT.next()
            for (dst, dk_, qk) in ((qt, qtk, 0), (kt_, ktk, 1)):
                ps, pk = PS.half()
                psb = ps.bitcast(BF16)
                for h in range(4):
                    TR(psb[:, h * 128:h * 128 + rows], qkr[0:rows, j, qk, h, :], ident[0:rows, 0:rows],
                       [f"qkr{j}", "ident"], pk)
                CP(dst[:, :, 0:rows], psb.rearrange("p (h t) -> p h t", h=4)[:, :, 0:rows], pk, [dk_], eng="act")
            ps, pk = PS.full()
            p3 = ps.rearrange("p (h t) -> p h t", h=4)
            for h in range(4):
                MM(p3[0:rows, h, 0:rows], kt_[:, h, 0:rows], qt[:, h, 0:rows], True, True, [qtk, ktk], pk)
            st_, stk = sTb.next()
            TT(st_[0:rows, :, 0:rows], p3[0:rows, :, 0:rows], retD[0:rows, :, 0:rows], ALU.mult, pk + ["retD"], [stk])
            ob, obk = o_sb.next()
            for hp in range(2):
                ps, pk = PS.full()
                for hh in range(2):
                    h = hp * 2 + hh
                    osl = ps[0:rows, hh * 256:(hh + 1) * 256]
                    MM(osl, st_[0:rows, h, 0:rows], v_tok[0:rows, j, h * 256:(h + 1) * 256], True, False,
                       [stk, f"v{j}"], pk)
                    MM(osl, qt[:, h, 0:rows], RSb[:, h, :], False, True, [qtk, "RSb"], pk)
                TT(ob[0:rows, hp * 2:hp * 2 + 2, :], ps.rearrange("p (h e) -> p h e", h=2)[0:rows],
                   retS[0:rows, hp * 2:hp * 2 + 2].unsqueeze(2).to_broadcast([rows, 2, 256]), ALU.mult,
                   pk + ["retS"], [obk])
            kd, kdk = kdec.next()
            kdcol = 8 if meta_group else 4
            TT(kd[0:rows, :, :], qkr[0:rows, j, 1, :, :],
               retS[0:rows, kdcol:kdcol + 4].unsqueeze(2).to_broadcast([rows, 4, 128]), ALU.mult,
               [f"qkr{j}", "retS"], [kdk], eng="pool")
            for hp in range(2):
                ps, pk = PS.full()
                for hh in range(2):
                    h = hp * 2 + hh
                    MM(ps[:, hh * 256:(hh + 1) * 256], kd[0:rows, h, :], v_tok[0:rows, j, h * 256:(h + 1) * 256],
                       True, True, [kdk, f"v{j}"], pk)
                for hh in range(2):
                    h = hp * 2 + hh
                    STT(RS[:, h, :], RS[:, h, :], float(GAMMA[h] ** rows), ps[:, hh * 256:(hh + 1) * 256],
                        ALU.mult, ALU.add, ["RS"] + pk, ["RS"])
            CP(RSb[:], RS[:], ["RS"], ["RSb"], eng="act")
            cl, ck = colsA.next()
            S.op("dve", lambda e, ob=ob, cl=cl, rows=rows: e.reduce_sum(out=cl[0:rows, 0:4], in_=ob[0:rows], axis=AX.X),
                 reads=[obk], writes=[ck])
            TS(cl[0:rows, 0:4], cl[0:rows, 0:4], -1.0 / 256, None, ALU.mult, None, [ck], [ck])
            TT(ob[0:rows], ob[0:rows], cl[0:rows, 0:4].unsqueeze(2).to_broadcast([rows, 4, 256]), ALU.add,
               [obk, ck], [obk])
            jb, jk = junk_t, None
            for h in range(4):
                ACT(jb[0:rows, h * 256:(h + 1) * 256], ob[0:rows, h, :], AF.Square, [obk], [ck],
                    accum_out=cl[0:rows, 4 + h:5 + h])
            rstd_from_ss(cl[0:rows, 4:8], rows, 256, ck)
            yb_, ybk = ya.next()
            for h in range(4):
                STT(yb_[0:rows, h * 256:(h + 1) * 256], ob[0:rows, h, :], cl[0:rows, 4 + h:5 + h],
                    sg_tok[0:rows, j, h * 256:(h + 1) * 256], ALU.mult, ALU.mult, [obk, ck, f"sg{j}"], [ybk])
            to_feature(yb_, rows, 8, big, 0, j * 128, gnT[:, 0:8], [ybk], BIGK)
        for half in range(2):
            hs_list = []
            wv, wk = load_w("w_in_ab", 0, 3072 + half * 512)
            for ct in range(4):
                c = half * 4 + ct
                ps, pk = PS.half()
                for kt in range(8):
                    MM(ps[:, 0:n], wv[:, kt, ct * 128:(ct + 1) * 128], hnT[:, kt, 0:n], kt == 0, kt == 7, [HNK, wk], pk)
                xp, xk = f32a.next()
                CP(xp[:, 0:3], lru_carry[:, c, :], ["lru_carry"], [xk], eng="pool")
                CP(xp[:, 3:3 + n], ps[:, 0:n], pk, [xk], eng="act")
                acc, ak = f32a.next()
                P = lambda i: lru_s[:, c, i:i + 1]
                TS(acc[:, 0:n], xp[:, 0:n], P(4), P(3), ALU.mult, ALU.add, [xk, "lru_s"], [ak])
                STT(acc[:, 0:n], xp[:, 1:1 + n], P(5), acc[:, 0:n], ALU.mult, ALU.add, [xk, ak, "lru_s"], [ak])
                STT(acc[:, 0:n], xp[:, 2:2 + n], P(6), acc[:, 0:n], ALU.mult, ALU.add, [xk, ak, "lru_s"], [ak])
                STT(acc[:, 0:n], xp[:, 3:3 + n], P(7), acc[:, 0:n], ALU.mult, ALU.add, [xk, ak, "lru_s"], [ak])
                CP(lru_carry[:, c, :], xp[:, n:n + 3], [xk], ["lru_carry"], eng="pool")
                xb_, xbk = bf16a.next()
                CP(xb_[:, 0:n], acc[:, 0:n], [ak], [xbk], eng="act")
                gates = []
                for wi in range(2):
                    ps2, pk2 = PS.half()
                    MM(ps2[:, 0:n], Wab[:, wi, c, :], xb_[:, 0:n], True, True, ["Wab", xbk], pk2)
                    gt_, gk_ = f32a.next()
                    ACT(gt_[:, 0:n], ps2[:, 0:n], AF.Sigmoid, pk2 + ["lru_s"], [gk_], bias=lru_s[:, c, 1 + wi:2 + wi])
                    gates.append((gt_, gk_))
                (rg, rk), (ig, ik) = gates
                ACT(rg[:, 0:n], rg[:, 0:n], AF.Exp, [rk, "lru_sp8"], [rk], scale=lru_sp8[:, c:c + 1])
                TT(ig[:, 0:n], ig[:, 0:n], acc[:, 0:n], ALU.mult, [ik, ak], [ik])
                TT(acc[:, 0:n], rg[:, 0:n], rg[:, 0:n], ALU.mult, [rk], [ak], eng="pool")
                ACT(acc[:, 0:n], acc[:, 0:n], AF.Sqrt, [ak, "epsc"], [ak], scale=-1.0, bias=epsc[:, 1:2])
                TT(ig[:, 0:n], ig[:, 0:n], acc[:, 0:n], ALU.mult, [ik, ak], [ik])
                hb_, hbk = hsb.next()
                S.op("dve", lambda e, hb_=hb_, rg=rg, ig=ig, c=c: e.tensor_tensor_scan(
                    out=hb_[:, 0:n], data0=rg[:, 0:n], data1=ig[:, 0:n], initial=lru_h[:, c:c + 1],
                    op0=ALU.mult, op1=ALU.add), reads=[rk, ik, "lru_h"], writes=[hbk])
                CP(lru_h[:, c:c + 1], hb_[:, n - 1:n], [hbk], ["lru_h"], eng="pool")
                hs_list.append((hb_, hbk))
            wv, wk = load_w("w_in_ab", 0, 4096 + half * 512)
            for ct in range(4):
                c = half * 4 + ct
                ps, pk = PS.half()
                for kt in range(8):
                    MM(ps[:, 0:n], wv[:, kt, ct * 128:(ct + 1) * 128], hnT[:, kt, 0:n], kt == 0, kt == 7, [HNK, wk], pk)
                ge, gek = f32a.next()
                ACT(ge[:, 0:n], ps[:, 0:n], AF.Gelu_apprx_tanh, pk, [gek])
                hb_, hbk = hs_list[ct]
                TT(big[:, 8 + c, 0:n], ge[:, 0:n], hb_[:, 0:n], ALU.mult, [gek, hbk], [BIGK(8 + c)], eng=ew())
        proj_token_major("w_out_ab", 16, big, BIGK, tiles,
                         lambda j, rows, a, ak, b, bk: postnorm_add(Hc, HK, j, rows, a, ak, b, bk, postg[0], "postg0"))

    def layer1_mixer(Hc, HK, tiles, n, meta_group, prev_n):
        prenorm(Hc, HK, tiles, 16)
        wv, wk = load_w("w_in_cd", 0, 0)
        for ct in range(4):
            ps, pk = PS.half()
            for kt in range(8):
                MM(ps[:, 0:n], wv[:, kt, ct * 128:(ct + 1) * 128], hnT[:, kt, 0:n], kt == 0, kt == 7, [HNK, wk], pk)
            CP(uf[:, ct, 0:n], ps[:, 0:n], pk, [UFK], eng="act")
            CP(ub[:, ct, 0:n], ps[:, 0:n], pk, [f"ub{ct}"])
        if prev_n is not None:
            fr = 0 if prev_n == T else 1
            cb_ = s5_bnd[:, fr, 0, :]
            sb_ = s5_bnd[:, fr, 1, :]
            q1, qk1 = colsA.next()
            q2, qk2 = colsA.next()
            TT(q1[:, 0:16], s5_st[:, 0, :], cb_, ALU.mult, ["s5_st", "s5_bnd"], [qk1])
            TT(q2[:, 0:16], s5_st[:, 1, :], sb_, ALU.mult, ["s5_st", "s5_bnd"], [qk2])
            TT(s5_ini[:, 0, :], q1[:, 0:16], q2[:, 0:16], ALU.subtract, [qk1, qk2], ["s5_ini"])
            q3, qk3 = colsA.next()
            q4, qk4 = colsA.next()
            TT(q3[:, 0:16], s5_st[:, 0, :], sb_, ALU.mult, ["s5_st", "s5_bnd"], [qk3])
            TT(q4[:, 0:16], s5_st[:, 1, :], cb_, ALU.mult, ["s5_st", "s5_bnd"], [qk4])
            TT(s5_ini[:, 1, :], q3[:, 0:16], q4[:, 0:16], ALU.add, [qk3, qk4], ["s5_ini"])
        else:
            MEMSET(s5_ini[:], 0.0, ["s5_ini"])
        for ct in range(4):
            ms = [ct * 4 + q for q in range(4)]
            W = {}
            for m in ms:
                psr, pkr = PS.half()
                MM(psr[:, 0:n], Bblk[:, 0, m, :], ub[:, ct, 0:n], True, True, ["Bblk", f"ub{ct}"], pkr)
                psi, pki = PS.half()
                MM(psi[:, 0:n], Bblk[:, 1, m, :], ub[:, ct, 0:n], True, True, ["Bblk", f"ub{ct}"], pki)
                W[m] = dict(psr=psr, pkr=pkr, psi=psi, pki=pki, t=[f32a.next() for _ in range(4)],
                            Rc=Rcs[:, 0, m, 0:n], Rs=Rcs[:, 1, m, 0:n])
            for m in ms:
                w_ = W[m]
                (t1, k1), (t2, k2), (t3, k3), (t4, k4) = w_["t"]
                TT(t1[:, 0:n], w_["psr"][:, 0:n], w_["Rc"], ALU.mult, w_["pkr"] + ["Rcs"], [k1])
                TT(t4[:, 0:n], w_["psr"][:, 0:n], w_["Rs"], ALU.mult, w_["pkr"] + ["Rcs"], [k4])
                TT(t2[:, 0:n], w_["psi"][:, 0:n], w_["Rs"], ALU.mult, w_["pki"] + ["Rcs"], [k2])
                TT(t3[:, 0:n], w_["psi"][:, 0:n], w_["Rc"], ALU.mult, w_["pki"] + ["Rcs"], [k3])
            for m in ms:
                (t1, k1), (t2, k2), (t3, k3), (t4, k4) = W[m]["t"]
                TT(t1[:, 0:n], t1[:, 0:n], t2[:, 0:n], ALU.add, [k1, k2], [k1], eng="pool")
                TT(t3[:, 0:n], t3[:, 0:n], t4[:, 0:n], ALU.subtract, [k3, k4], [k3], eng="pool")
            for m in ms:
                (t1, k1), (t2, k2), (t3, k3), (t4, k4) = W[m]["t"]
                magb = s5_mag[:, m:m + 1].to_broadcast([128, n])
                S.op("dve", lambda e, t2=t2, t1=t1, m=m, magb=magb: e.tensor_tensor_scan(
                    out=t2[:, 0:n], data0=magb, data1=t1[:, 0:n], initial=s5_ini[:, 0, m:m + 1],
                    op0=ALU.mult, op1=ALU.add), reads=[k1, "s5_mag", "s5_ini"], writes=[k2])
                S.op("dve", lambda e, t4=t4, t3=t3, m=m, magb=magb: e.tensor_tensor_scan(
                    out=t4[:, 0:n], data0=magb, data1=t3[:, 0:n], initial=s5_ini[:, 1, m:m + 1],
                    op0=ALU.mult, op1=ALU.add), reads=[k3, "s5_mag", "s5_ini"], writes=[k4])
            for m in ms:
                (t1, k1), (t2, k2), (t3, k3), (t4, k4) = W[m]["t"]
                CP(s5_st[:, 0, m:m + 1], t2[:, n - 1:n], [k2], ["s5_st"], eng="pool")
                CP(s5_st[:, 1, m:m + 1], t4[:, n - 1:n], [k4], ["s5_st"], eng="pool")
            for m in ms:
                w_ = W[m]
                (t1, k1), (t2, k2), (t3, k3), (t4, k4) = w_["t"]
                TT(t1[:, 0:n], t2[:, 0:n], w_["Rc"], ALU.mult, [k2, "Rcs"], [k1])
                TT(t3[:, 0:n], t4[:, 0:n], w_["Rs"], ALU.mult, [k4, "Rcs"], [k3])
            for m in ms:
                (t1, k1), (t2, k2), (t3, k3), (t4, k4) = W[m]["t"]
                hr, hrk = s5h.next()
                TT(hr[:, 0:n], t1[:, 0:n], t3[:, 0:n], ALU.subtract, [k1, k3], [hrk], eng="pool")
                W[m]["hr"] = (hr, hrk)
            for m in ms:
                w_ = W[m]
                (t1, k1), (t2, k2), (t3, k3), (t4, k4) = w_["t"]
                TT(t1[:, 0:n], t2[:, 0:n], w_["Rs"], ALU.mult, [k2, "Rcs"], [k1])
                TT(t3[:, 0:n], t4[:, 0:n], w_["Rc"], ALU.mult, [k4, "Rcs"], [k3])
            for m in ms:
                (t1, k1), (t2, k2), (t3, k3), (t4, k4) = W[m]["t"]
                hi, hik = s5h.next()
                TT(hi[:, 0:n], t1[:, 0:n], t3[:, 0:n], ALU.add, [k1, k3], [hik], eng="pool")
                W[m]["hi"] = (hi, hik)
            psy, pky = PS.half()
            for q, m in enumerate(ms):
                hr_, hrk_ = W[m]["hr"]
                hi_, hik_ = W[m]["hi"]
                MM(psy[:, 0:n], Cblk[:, 0, m, :], hr_[:, 0:n], q == 0, False, ["Cblk", hrk_], pky)
                MM(psy[:, 0:n], Cblk[:, 1, m, :], hi_[:, 0:n], False, q == 3, ["Cblk", hik_], pky)
            yt_, ytk = f32a.next()
            STT(yt_[:, 0:n], uf[:, ct, 0:n], s5dg[:, ct:ct + 1], psy[:, 0:n], ALU.mult, ALU.add,
                [UFK, "s5dg"] + pky, [ytk])
            ACT(ygf[:, ct, 0:n], yt_[:, 0:n], AF.Gelu_apprx_tanh, [ytk], [YGK])
            CP(ygb[:, ct, 0:n], ygf[:, ct, 0:n], [YGK], [f"ygb{ct}"], eng="pool")
        for co in range(4):
            ps, pk = PS.half()
            for ci_ in range(4):
                MM(ps[:, 0:n], gluW[:, ci_, co * 128:(co + 1) * 128], ygb[:, ci_, 0:n], ci_ == 0, ci_ == 3,
                   ["gluW", f"ygb{ci_}"], pk)
            sgm, sgk = f32a.next()
            ACT(sgm[:, 0:n], ps[:, 0:n], AF.Sigmoid, pk + ["s5dg"], [sgk], bias=s5dg[:, 4 + co:5 + co])
            TT(big[:, co, 0:n], ygf[:, co, 0:n], sgm[:, 0:n], ALU.mult, [YGK, sgk], [BIGK(co)])
        wq, wqk = load_w("w_in_cd", 0, 512)
        wf, wfk = load_w("w_in_cd", 0, 1024)
        bw = 16 if meta_group else 64
        htiles = [(0, NMETA)] if meta_group else [(c_, 64) for c_ in range(n // 64)]
        nblk = len(htiles)
        mid = bw // 2 - 1
        for h in range(4):
            psq, pkq = PS.half()
            for kt in range(8):
                MM(psq[:, 0:n], wq[:, kt, h * 128:(h + 1) * 128], hnT[:, kt, 0:n], kt == 0, kt == 7, [HNK, wqk], pkq)
            psf, pkf = PS.half()
            for kt in range(8):
                MM(psf[:, 0:n], wf[:, kt, h * 128:(h + 1) * 128], hnT[:, kt, 0:n], kt == 0, kt == 7, [HNK, wfk], pkf)
            ff, fk = f32a.next()
            ACT(ff[:, 0:n], psf[:, 0:n], AF.Sigmoid, pkf, [fk])
            TS(ff[:, 0:n], ff[:, 0:n], hg_lb[:, 4 + h:5 + h], hg_lb[:, h:h + 1], ALU.mult, ALU.add, [fk, "hg_lb"], [fk])
            lf, lk = f32a.next()
            ACT(lf[:, 0:n], ff[:, 0:n], AF.Ln, [fk], [lk])
            TS(ff[:, 0:n], ff[:, 0:n], -1.0, 1.0, ALU.mult, ALU.add, [fk], [fk], eng="pool")
            cum, cmk = f32a.next()
            S.op("dve", lambda e, cum=cum, lf=lf: e.tensor_tensor_scan(
                out=cum[:, 0:n], data0=restart[:, 0:n], data1=lf[:, 0:n], initial=0.0,
                op0=ALU.mult, op1=ALU.add), reads=[lk, "restart"], writes=[cmk])
            c3 = cum[:, 0:n].rearrange("p (j t) -> p j t", t=bw)
            ACT(hgE[:, 0, h, 0:nblk], c3[:, :, mid], AF.Exp, [cmk], ["hgE"])
            ACT(hgE[:, 1, h, 0:nblk], c3[:, :, bw - 1], AF.Exp, [cmk], ["hgE"])
            cm, cmk2 = f32a.next()
            cm3 = cm[:, 0:n].rearrange("p (j t) -> p j t", t=bw)
            TT(cm3, c3, c3[:, :, mid:mid + 1].to_broadcast([128, nblk, bw]), ALU.subtract, [cmk], [cmk2])
            ACT(lf[:, 0:n], cm[:, 0:n], AF.Exp, [cmk2], [lk])
            ACT(cm[:, 0:n], cm[:, 0:n], AF.Exp, [cmk2], [cmk2], scale=-1.0)
            l3 = lf[:, 0:n].rearrange("p (j t) -> p j t", t=bw)
            CP(hgE[:, 2, h, 0:nblk], l3[:, :, bw - 1], [lk], ["hgE"], eng="pool")
            TT(hqT[:, h, 0:n], psq[:, 0:n], lf[:, 0:n], ALU.mult, pkq + [lk], [f"hq{h}"])
            TT(hkT[:, h, 0:n], ff[:, 0:n], cm[:, 0:n], ALU.mult, [fk, cmk2], [f"hk{h}"], eng="pool")
        iv = v_tok[:].rearrange("p j (a c) -> p (j a) c", a=2)
        gv = sg_tok[:].rearrange("p j (a c) -> p (j a) c", a=2)
        IVK = lambda c_: f"v{c_ // 2}"
        GVK = lambda c_: f"sg{c_ // 2}"
        wi_, wik = load_w("w_in_cd", 0, 1536)
        for (c_, rows) in htiles:
            ps, pk = PS.full()
            for kt in range(8):
                MM(ps[0:rows, :], hnT[:, kt, c_ * 64:c_ * 64 + rows], wi_[:, kt, :], kt == 0, kt == 7, [HNK, wik], pk)
            CP(iv[0:rows, c_, :], ps[0:rows, :], pk, [IVK(c_)], eng="act")
        wg_, wgk = load_w("w_in_cd", 0, 2048)
        for (c_, rows) in htiles:
            ps, pk = PS.full()
            for kt in range(8):
                MM(ps[0:rows, :], hnT[:, kt, c_ * 64:c_ * 64 + rows], wg_[:, kt, :], kt == 0, kt == 7, [HNK, wgk], pk)
            ACT(gv[0:rows, c_, :], ps[0:rows, :], AF.Silu, pk, [GVK(c_)])
        HQK = [f"hq{h}" for h in range(4)]
        HKK = [f"hk{h}" for h in range(4)]
        for (j, rows) in htiles:
            t0 = j * 64
            ps, pk = PS.full()
            p3 = ps.rearrange("p (h t) -> p h t", h=4)
            for h in range(4):
                MM(p3[0:rows, h, 0:rows], hkT[:, h, t0:t0 + rows], hqT[:, h, t0:t0 + rows], True, True, HQK + HKK, pk)
            at, atk = sTb.next()
            STT(at[0:rows, :, 0:rows], p3[0:rows, :, 0:rows], 1e30,
                triu[0:rows, 0:rows].unsqueeze(1).to_broadcast([rows, 4, rows]), ALU.min, ALU.mult,
                pk + ["triu"], [atk])
            ps2, pk2 = PS.half()
            psb = ps2.bitcast(BF16)
            for h in range(4):
                TR(psb[0:rows, h * 128:(h + 1) * 128], hkT[:, h, t0:t0 + rows], ident[:, :], HKK + ["ident"], pk2)
            ktk_, ktkk = kdec.next()
            CP(ktk_[0:rows, :, :], psb.rearrange("p (h d) -> p h d", h=4)[0:rows], pk2, [ktkk], eng="act")
            TT(HSb[:], HS[:], hgE[:, 0, :, j:j + 1].to_broadcast([128, 4, 128]), ALU.mult, ["HS", "hgE"], ["HSb"])
            pso, pko = PS.full()
            o3 = pso.rearrange("p (h e) -> p h e", h=4)
            for h in range(4):
                MM(o3[0:rows, h, :], at[0:rows, h, 0:rows], iv[0:rows, j, h * 128:(h + 1) * 128], True, False,
                   [atk, IVK(j)], pko)
                MM(o3[0:rows, h, :], hqT[:, h, t0:t0 + rows], HSb[:, h, :], False, True, HQK + ["HSb"], pko)
            psk, pkk = PS.full()
            k3 = psk.rearrange("p (h e) -> p h e", h=4)
            for h in range(4):
                MM(k3[:, h, :], ktk_[0:rows, h, :], iv[0:rows, j, h * 128:(h + 1) * 128], True, True,
                   [ktkk, IVK(j)], pkk)
            ta, tak = tmpTok.next()
            TT(ta[:].rearrange("p (h e) -> p h e", h=4), k3, hgE[:, 2, :, j:j + 1].to_broadcast([128, 4, 128]),
               ALU.mult, pkk + ["hgE"], [tak])
            TT(HS[:], HS[:], hgE[:, 1, :, j:j + 1].to_broadcast([128, 4, 128]), ALU.mult, ["HS", "hgE"], ["HS"])
            TT(HS[:], HS[:], ta[:].rearrange("p (h e) -> p h e", h=4), ALU.add, ["HS", tak], ["HS"], eng="pool")
            jb, jk = junk_t, None
            cl, ck = colsA.next()
            for h in range(4):
                ACT(jb[0:rows, h * 128:(h + 1) * 128], o3[0:rows, h, :], AF.Square, pko, [ck],
                    accum_out=cl[0:rows, h:h + 1])
            rstd_from_ss(cl[0:rows, 0:4], rows, 128, ck)
            on, onk = tmpTok.next()
            TT(on[0:rows].rearrange("p (h e) -> p h e", h=4), o3[0:rows],
               cl[0:rows, 0:4].unsqueeze(2).to_broadcast([rows, 4, 128]), ALU.mult, pko + [ck], [onk])
            yb_, ybk = ya.next()
            TT(yb_[0:rows, 0:512], on[0:rows, :], gv[0:rows, j, :], ALU.mult, [onk, GVK(j)], [ybk], eng="pool")
            to_feature(yb_, rows, 4, big, 4, t0, gnT[:, 8:12], [ybk], BIGK)
        proj_token_major("w_out_cd", 8, big, BIGK, tiles,
                         lambda j, rows, a, ak, b, bk: postnorm_add(Hc, HK, j, rows, a, ak, b, bk, postg[0], "postg0"))

    last_store = None
    prev_n = None
    NGR = min(NG, KNG) if KPHASE == "all" else 0

    def group_geom(g):
        if g == 0:
            return NMETA, [(0, NMETA)], 0, 0
        f0 = (g - 1) * T
        return T, [(j, 128) for j in range(NT)], NMETA + f0, f0

    rope_bufs = {}

    def issue_loads(g):
        Hc, HK = H[g % 2], f"H{g % 2}"
        n, tiles, pos0, f0 = group_geom(g)
        rb, rkey = ropeT.next()
        rope_bufs[g] = (rb, rkey)
        if g == 0:
            DMA("pool", Hc[0:NMETA, 0, :], meta_d, "ld_x", [], [HK])
            DMA("pool", rb[0:NMETA, 0, :], cd["rope"][0:NMETA, :], "ld_rope", [], [rkey])
        else:
            DMA("pool", Hc[:, :, :], x_d[f0:f0 + T, :].rearrange("(j p) d -> p j d", p=128), "ld_x", [], [HK])
            DMA("pool", rb[:, :, :], cd["rope"][pos0:pos0 + T, :].rearrange("(j p) d -> p j d", p=128), "ld_rope",
                [], [rkey])

    if NGR > 0:
        issue_loads(0)
    for g in range(NGR):
        Hc = H[g % 2]
        HK = f"H{g % 2}"
        meta_group = g == 0
        n, tiles, pos0, f0 = group_geom(g)
        rb, rkey = rope_bufs.pop(g)

        def dbg(slot):
            if DEBUG and (g < 3):
                if meta_group:
                    DMA("pool", dbg_d[slot, 0:NMETA, :], Hc[0:NMETA, 0, :], "st_dbg", [HK], [])
                else:
                    DMA("pool", dbg_d[slot, pos0:pos0 + T, :].rearrange("(j p) d -> p j d", p=128), Hc[:, :, :],
                        "st_dbg", [HK], [])

        load_postg(0)
        layer0_mixer(Hc, HK, tiles, n, meta_group, rb, rkey, do_prenorm=(g == 0 or KSTOP < 3))
        has_next = g + 1 < NGR
        if has_next:
            issue_loads(g + 1)
        dbg(0)
        if KSTOP >= 1:
            mlp(0, Hc, HK, tiles, n)
            dbg(1)
        if KSTOP >= 2:
            load_postg(1)
            layer1_mixer(Hc, HK, tiles, n, meta_group, prev_n)
            dbg(2)
        if KSTOP >= 3:
            hook = None
            if has_next:
                nH, nHK = H[(g + 1) % 2], f"H{(g + 1) % 2}"
                ntiles = group_geom(g + 1)[1]
                hook = lambda nH=nH, nHK=nHK, ntiles=ntiles: prenorm(nH, nHK, ntiles, 0)
            mlp(1, Hc, HK, tiles, n, hook=hook)
            dbg(3)
        if not meta_group:
            last_store = DMA("pool", out_d[f0:f0 + T, :].rearrange("(j p) d -> p j d", p=128), Hc[:, :, :], "st_out",
                             [HK], [])
        prev_n = n
    fin = [d for d in [last_store, S.chan_last.get("st_dbg"), S.chan_last.get("ld_small"), S.chan_last.get("st_scr0")] if d is not None]
    S.op("sp", lambda e: e.nop(), extra_deps=fin)
    print("instr counts", {e: len(S.lists[e]) for e in ENGS}, flush=True)
    S.emit()
    S.stats = {e: max([i.val for i in S.lists[e] if i.chan is None and i.needs_inc] or [0]) for e in ENGS}
    S.stats.update({c: lst[-1].val for c, lst in S.chan_ins.items()})
    print("max sem values", S.stats, flush=True)
    return nc, consts


_CACHE = {}


def kernel(**inputs):
    if "prog" not in _CACHE:
        _CACHE["prog"] = build_program()
    nc, consts = _CACHE["prog"]
    lay = host_layouts(inputs)
    common = dict(lay)
    for n, v in consts.items():
        common["c_" + n] = v
    x = np.asarray(inputs["x"], dtype=np.float32)
    in_maps = []
    for c in range(8):
        m = dict(common)
        m["x"] = np.ascontiguousarray(x[c % 4])
        in_maps.append(m)
    res = run_bass_kernel_spmd(nc, in_maps, core_ids=list(range(8)))
    out = np.stack([np.asarray(res.results[b]["out"], dtype=np.float32) for b in range(4)], axis=0)
    if DEBUG:
        kernel.dbg = [np.asarray(res.results[b]["dbg"]) for b in range(4)]
    return out
```

```python
import numpy as np
import concourse.bass as bass
import concourse.mybir as mybir
from concourse.bass_utils import run_bass_kernel_spmd

F32 = mybir.dt.float32
BF16 = mybir.dt.bfloat16
I32 = mybir.dt.int32
AF = mybir.ActivationFunctionType
ALU = mybir.AluOpType
AX = mybir.AxisListType

T = 256
NT = T // 128
SEQ = 4096
NMETA = 16
D = 1024
DFF = 4096
EPS = 1e-6
NG = 1 + SEQ // T
import os
DEBUG = bool(int(os.environ.get("KDEBUG", "0")))
KNG = int(os.environ.get("KNG", "1000"))
KSTOP = int(os.environ.get("KSTOP", "3"))
KPHASE = os.environ.get("KPHASE", "all")

ENGS = ("pe", "act", "dve", "pool", "sp")


class Ins:
    __slots__ = ("eng", "fn", "waits", "idx", "needs_inc", "chan", "clock", "ninc", "val", "tag")


class Sched:
    def __init__(self, nc):
        self.nc = nc
        self.lists = {e: [] for e in ENGS}
        self.lastw = {}
        self.readers = {}
        self.clock = {e: {} for e in ENGS}
        self.chan_last = {}
        self.chan_ins = {}

    def _need(self, eng, dep, waits):
        src = dep.chan if dep.chan is not None else dep.eng
        if dep.chan is None and dep.eng == "pe" and eng == "pe":
            return
        if self.clock[eng].get(src, -1) >= dep.idx:
            return
        waits[src] = max(waits.get(src, -1), dep.idx)

    def op(self, eng, fn, reads=(), writes=(), chan=None, extra_deps=(), ninc=1):
        ins = Ins()
        ins.eng, ins.fn, ins.chan, ins.needs_inc, ins.ninc = eng, fn, chan, False, ninc
        ins.tag = getattr(self, "tag", "")
        psr = [k for k in reads if k[:2] == "ps" and k[2:].isdigit()]
        if psr:
            reads = [k for k in reads if k not in psr]
            writes = list(writes) + [k for k in psr if k not in writes]
        deps = []
        for k in reads:
            deps.extend(self.lastw.get(k, ()))
        for k in writes:
            deps.extend(self.lastw.get(k, ()))
            deps.extend(self.readers.get(k, ()))
        deps.extend(extra_deps)
        if chan is not None and chan in self.chan_last:
            deps.append(self.chan_last[chan])
        waits = {}
        for d in deps:
            self._need(eng, d, waits)
        ins.waits = []
        ck = self.clock[eng]
        for src, idx in waits.items():
            prod = self.lists[src][idx] if src in ENGS else self.chan_ins[src][idx]
            prod.needs_inc = True
            ins.waits.append(prod)
            for s, v in prod.clock.items():
                if ck.get(s, -1) < v:
                    ck[s] = v
        if chan is not None:
            lst = self.chan_ins.setdefault(chan, [])
            ins.idx = len(lst)
            lst.append(ins)
            self.chan_last[chan] = ins
            ins.clock = dict(ck)
            ins.clock[chan] = ins.idx
            self.lists[eng].append(ins)
        else:
            ins.idx = len(self.lists[eng])
            ins.clock = dict(ck)
            ins.clock[eng] = ins.idx
            self.lists[eng].append(ins)
        for k in reads:
            self.readers.setdefault(k, []).append(ins)
        me = chan if chan is not None else eng
        for k in writes:
            lst = [w for w in self.lastw.get(k, ()) if (w.chan if w.chan is not None else w.eng) != me]
            lst.append(ins)
            self.lastw[k] = lst
            self.readers[k] = []
        return ins

    def emit(self):
        nc = self.nc
        sems = {}
        for e in ("pe", "act", "dve", "pool"):
            sems[e] = nc.alloc_semaphore("s_" + e)
        for c in self.chan_ins:
            sems[c] = nc.alloc_semaphore("c_" + str(c))
        for e in ENGS:
            r = 0
            for ins in self.lists[e]:
                if ins.chan is None and ins.needs_inc:
                    r += 1
                    ins.val = r
        for c, lst in self.chan_ins.items():
            v = 0
            for ins in lst:
                v += 16 * ins.ninc
                ins.val = v

        def run(e):
            def body(eng):
                for ins in self.lists[e]:
                    for p in ins.waits:
                        src = p.chan if p.chan is not None else p.eng
                        eng.wait_ge(sems[src], p.val)
                    r = ins.fn(eng)
                    rl = r if isinstance(r, (list, tuple)) else [r]
                    if ins.chan is not None:
                        assert len(rl) == ins.ninc
                        for x in rl:
                            x.then_inc(sems[ins.chan], 16)
                    elif ins.needs_inc:
                        rl[-1].then_inc(sems[e], 1)
            return body

        with nc.Block() as block:
            block.tensor(run("pe"))
            block.scalar(run("act"))
            block.vector(run("dve"))
            block.gpsimd(run("pool"))
            block.sync(run("sp"))


class DB:
    def __init__(self, nc, name, shape, dtype, nbuf=2):
        self.bufs = [nc.alloc_sbuf_tensor(f"sb_{name}_{i}", list(shape), dtype) for i in range(nbuf)]
        self.keys = [f"{name}_{i}" for i in range(nbuf)]
        self.i = 0

    def next(self):
        i = self.i
        self.i = (i + 1) % len(self.bufs)
        return self.bufs[i], self.keys[i]


class PsumPool:
    def __init__(self, nc):
        self.banks = [nc.alloc_psum_tensor(f"ps{i}", [128, 512], F32) for i in range(8)]
        self.i = 0
        self.j = 0
        self.live = set()

    def half(self):
        while True:
            u = self.i
            self.i = (u + 1) % 16
            h, b = divmod(u, 8)
            if b not in self.live:
                break
        return self.banks[b][:, h * 256:(h + 1) * 256], [f"ps{b}"]

    def full(self):
        b = self.j
        self.j = (b + 1) % 8
        self.last_full = b
        return self.banks[b][:, :], [f"ps{b}"]


GAMMA = [1.0 - 2.0 ** (-5.0 - h) for h in range(4)]


def host_constants():
    c = {}
    c["ident"] = np.eye(128, dtype=np.float32)
    L = NMETA + SEQ
    pos = np.arange(L, dtype=np.float32)
    inv = (10000.0 ** (-np.arange(64, dtype=np.float32) / 64)).astype(np.float32)
    ang = (pos[:, None] * inv[None, :]).astype(np.float32)
    c["rope"] = np.concatenate([np.cos(ang), np.sin(ang)], axis=1).astype(np.float32)
    s = np.arange(128)[:, None]
    t = np.arange(128)[None, :]
    retD = np.zeros((128, 4, 128), np.float64)
    retG = np.zeros((128, 4), np.float64)
    retKD = np.zeros((128, 4), np.float64)
    retKDm = np.zeros((128, 4), np.float64)
    for h in range(4):
        g = GAMMA[h]
        same = (s // 64) == (t // 64)
        earlier = (s // 64) < (t // 64)
        w = np.where(same, g ** np.abs(t - s), np.where(earlier, g ** (t - s).clip(0), 0.0))
        retD[:, h, :] = w / (g ** (t + 1.0))
        retG[:, h] = (128 ** -0.5) * g ** (np.arange(128) + 1.0)
        retKD[:, h] = g ** (127.0 - np.arange(128))
        retKDm[:16, h] = g ** (15.0 - np.arange(16))
    c["retD"] = retD.astype(np.float32)
    c["retS"] = np.concatenate([retG, retKD, retKDm], axis=1).astype(np.float32)
    c["triu"] = (s <= t).astype(np.float32)
    rs = np.ones((128, T), np.float32)
    rs[:, ::64] = 0.0
    c["restart"] = rs
    c["iota"] = np.broadcast_to(np.arange(T, dtype=np.float32), (128, T)).copy()
    mb = np.zeros((128, 16, 8), np.float32)
    for m in range(16):
        for gl in range(2):
            mb[gl * 64:(gl + 1) * 64, m, (2 * m + gl) % 8] = 1.0
    c["maskB"] = mb
    mc = np.zeros((128, 4, 2, 2), np.float32)
    for mm in range(4):
        for gl in range(2):
            g8 = 2 * mm + gl
            mc[g8 * 16:(g8 + 1) * 16, mm, gl, 0] = 1.0
            mc[g8 * 16:(g8 + 1) * 16, mm, gl, 1] = -1.0
    c["maskC"] = mc
    return c


def host_layouts(inp):
    f = lambda a: np.ascontiguousarray(np.asarray(a, dtype=np.float32))
    o = {}
    o["meta"] = f(inp["meta"])
    o["w_in_ab"] = f(inp["w_in_ab"][0])
    o["w_out_ab"] = f(inp["w_out_ab"][0])
    o["w1"] = f(inp["mlp_w1"])
    o["w2"] = f(inp["mlp_w2"])
    o["w_in_cd"] = f(inp["w_in_cd"][0])
    o["w_out_cd"] = f(inp["w_out_cd"][0])
    o["glu_w"] = f(inp["s5_glu_w"][0])
    ng = np.asarray(inp["norm_g"], np.float32)
    pre = ng[:, [0, 2], :].reshape(2, 2, 8, 128)
    o["pgT"] = f(pre.transpose(3, 0, 1, 2).reshape(128, 32))
    o["postg"] = f(ng[:, [1, 3], :].reshape(4, 1024))
    gn = np.concatenate([np.asarray(inp["ret_gn"][0], np.float32).reshape(8, 128).T,
                         np.asarray(inp["hg_gn"][0], np.float32).reshape(4, 128).T], axis=1)
    o["gnT"] = f(gn)
    fm8 = lambda v: np.asarray(v, np.float32).reshape(8, 128).T
    cw = np.asarray(inp["rg_conv_w"][0], np.float32)
    lru = np.stack([fm8(inp["rg_lam"][0]), fm8(inp["rg_ba"][0]), fm8(inp["rg_bi"][0]), fm8(inp["rg_conv_b"][0]),
                    fm8(cw[0]), fm8(cw[1]), fm8(cw[2]), fm8(cw[3])], axis=2)
    o["lru_s"] = f(lru)
    o["rg_wa"] = f(inp["rg_wa"][0])
    o["rg_wi"] = f(inp["rg_wi"][0])
    hl = np.asarray(inp["hg_lb_logits"], np.float32).reshape(2, 4, 128)
    o["hgl"] = f(hl.transpose(2, 0, 1).reshape(128, 8))
    st = lambda a: np.asarray(a, np.float32).reshape(16, 2, 64).transpose(1, 2, 0).reshape(128, 16)
    ldt = np.repeat(np.asarray(inp["s5_log_dt"][0], np.float32)[:, None], 64, axis=1)
    o["s5s"] = f(np.stack([st(inp["s5_a_re_log"][0]), st(inp["s5_a_im"][0]), st(ldt)], axis=2))
    sb = lambda a: np.asarray(a, np.float32).reshape(16, 2, 64, 16).transpose(1, 2, 0, 3).reshape(128, 16, 16)
    o["s5b"] = f(np.stack([sb(inp["s5_b_re"][0]), sb(inp["s5_b_im"][0])], axis=2))
    sc = lambda a: np.asarray(a, np.float32).reshape(4, 8, 16, 64).transpose(1, 2, 0, 3).reshape(128, 4, 64)
    o["s5c"] = f(np.stack([sc(inp["s5_c_re"][0]), sc(inp["s5_c_im"][0])], axis=2))
    sd = np.asarray(inp["s5_d"][0], np.float32).reshape(4, 128).T
    gb = np.asarray(inp["s5_glu_b"][0], np.float32).reshape(4, 128).T
    o["s5dg"] = f(np.concatenate([sd, gb], axis=1))
    return o


def build_program():
    nc = bass.Bass("TRN2", target_bir_lowering=False)
    S = Sched(nc)
    consts = host_constants()

    def din(name, shape, dt=F32):
        return nc.dram_tensor(name, list(shape), dt, kind="ExternalInput").ap()

    x_d = din("x", [SEQ, D])
    meta_d = din("meta", [NMETA, D])
    wdefs = [("w_in_ab", 1024, 5120), ("w_out_ab", 2048, 1024), ("w1_0", 1024, 4096), ("w2_0", 4096, 1024),
             ("w_in_cd", 1024, 2560), ("w_out_cd", 1024, 1024), ("w1_1", 1024, 4096), ("w2_1", 4096, 1024),
             ("glu_w", 512, 512)]
    w1_d = din("w1", [2, 1024, 4096])
    w2_d = din("w2", [2, 4096, 1024])
    wsrc = {"w_in_ab": din("w_in_ab", [1024, 5120]), "w_out_ab": din("w_out_ab", [2048, 1024]),
            "w1_0": w1_d[0], "w1_1": w1_d[1], "w2_0": w2_d[0], "w2_1": w2_d[1],
            "w_in_cd": din("w_in_cd", [1024, 2560]), "w_out_cd": din("w_out_cd", [1024, 1024]),
            "glu_w": din("glu_w", [512, 512])}
    wscr = {n: (nc.dram_tensor("scr_" + n, [r, c], BF16, kind="Internal").ap() if n == "glu_w" else
                nc.dram_tensor("scr_" + n, [r // 1024, c // 512, 128, 8, 512], BF16, kind="Internal").ap())
            for n, r, c in wdefs}
    small = {}
    for n, shp in [("pgT", [128, 32]), ("postg", [4, 1024]), ("gnT", [128, 12]), ("lru_s", [128, 8, 8]),
                   ("rg_wa", [16, 64, 64]), ("rg_wi", [16, 64, 64]), ("hgl", [128, 8]), ("s5s", [128, 16, 3]),
                   ("s5b", [128, 16, 2, 16]), ("s5c", [128, 4, 2, 64]), ("s5dg", [128, 8])]:
        small[n] = din(n, shp)
    cd = {n: din("c_" + n, list(v.shape)) for n, v in consts.items()}
    out_d = nc.dram_tensor("out", [SEQ, D], F32, kind="ExternalOutput").ap()
    dbg_d = nc.dram_tensor("dbg", [4, NMETA + SEQ, D], F32, kind="ExternalOutput").ap() if DEBUG else None

    def sb(name, shape, dt=F32):
        return nc.alloc_sbuf_tensor("sb_" + name, list(shape), dt)

    def MM(out, lhsT, rhs, start, stop, r, w):
        S.op("pe", lambda e: e.matmul(out, lhsT=lhsT, rhs=rhs, start=start, stop=stop), reads=r, writes=w)

    def TR(out, in_, idn, r, w):
        S.op("pe", lambda e: e.transpose(out, in_, idn), reads=r, writes=w)

    def ACT(out, in_, func, r, w, **kw):
        S.op("act", lambda e: e.activation(out=out, in_=in_, func=func, **kw), reads=r, writes=w)

    def TT(out, a, b, op, r, w, eng="dve"):
        S.op(eng, lambda e: e.tensor_tensor(out=out, in0=a, in1=b, op=op), reads=r, writes=w)

    def TS(out, a, s1, s2, op0, op1, r, w, eng="dve"):
        if s2 is None:
            S.op(eng, lambda e: e.tensor_scalar(out=out, in0=a, scalar1=s1, scalar2=None, op0=op0), reads=r, writes=w)
        else:
            S.op(eng, lambda e: e.tensor_scalar(out=out, in0=a, scalar1=s1, scalar2=s2, op0=op0, op1=op1),
                 reads=r, writes=w)

    def STT(out, a, s, b, op0, op1, r, w, eng="dve"):
        S.op(eng, lambda e: e.scalar_tensor_tensor(out=out, in0=a, scalar=s, in1=b, op0=op0, op1=op1),
             reads=r, writes=w)

    def CP(out, a, r, w, eng="dve"):
        if eng == "act":
            S.op("act", lambda e: e.copy(out=out, in_=a), reads=r, writes=w)
        else:
            S.op(eng, lambda e: e.tensor_copy(out=out, in_=a), reads=r, writes=w)

    def DMA(q, out, in_, chan, r, w):
        return S.op(q, lambda e: e.dma_start(out=out, in_=in_), reads=r, writes=w, chan=chan)

    def MEMSET(ap, val, w, eng="dve"):
        S.op(eng, lambda e: e.memset(ap, val), writes=w)

    PS = PsumPool(nc)
    _rr = [0]

    def ew():
        _rr[0] += 1
        return "pool" if _rr[0] % 3 == 0 else "dve"

    ident_f = sb("ident_f", [128, 128])
    ident = sb("ident", [128, 128], BF16)
    H = [sb(f"H{i}", [128, NT, D]) for i in range(2)]
    hnT = sb("hnT", [128, 8, T], BF16)
    big = sb("big", [128, 32, T], BF16)
    NSLOT = 3
    Wr = [sb(f"W{i}", [128, 8 * 512], BF16) for i in range(NSLOT)]
    wslot = [0]
    pgT = sb("pgT", [128, 32])
    gnT = sb("gnT", [128, 12])
    postg = [sb(f"postg{i}", [128, D]) for i in range(2)]
    epsc = sb("epsc", [128, 2])
    ropeT = DB(nc, "rope", [128, NT, 128], F32, 2)
    retD = sb("retD", [128, 4, 128])
    retS = sb("retS", [128, 12])
    triu = sb("triu", [128, 128])
    restart = sb("restart", [128, T])
    RS = sb("RS", [128, 4, 256])
    RSb = sb("RSb", [128, 4, 256], BF16)
    lru_s = sb("lru_s", [128, 8, 8])
    lru_sp8 = sb("lru_sp8", [128, 8])
    Wab = sb("Wab", [128, 2, 8, 128], BF16)
    lru_carry = sb("lru_carry", [128, 8, 3])
    lru_h = sb("lru_h", [128, 8])
    hg_lb = sb("hg_lb", [128, 8])
    HS = sb("HS", [128, 4, 128])
    HSb = sb("HSb", [128, 4, 128], BF16)
    s5_mag = sb("s5_mag", [128, 16])
    s5_bnd = sb("s5_bnd", [128, 2, 2, 16])
    s5_st = sb("s5_st", [128, 2, 16])
    s5_ini = sb("s5_ini", [128, 2, 16])
    Bblk = sb("Bblk", [128, 2, 16, 128], BF16)
    Cblk = sb("Cblk", [128, 2, 16, 128], BF16)
    Rcs = sb("Rcs", [128, 2, 16, T], BF16)
    s5dg = sb("s5dg", [128, 8])
    gluW = sb("gluW", [128, 4, 512], BF16)

    f32a = DB(nc, "f32a", [128, T + 4], F32, 16)
    bf16a = DB(nc, "bf16a", [128, T], BF16, 4)
    hsb = DB(nc, "hsb", [128, T], F32, 5)
    s5h = DB(nc, "s5h", [128, T], BF16, 8)
    hn_tok = DB(nc, "hn_tok", [128, D], BF16, 1)
    junk_t = sb("junk", [128, D], BF16)
    colsA = DB(nc, "colsA", [128, 16], F32, 8)
    tmpTok = DB(nc, "tmpTok", [128, 512], F32, 4)
    qkr = sb("qkr", [128, NT, 2, 4, 128], BF16)
    v_tok = sb("v_tok", [128, NT, 1024], BF16)
    sg_tok = sb("sg_tok", [128, NT, 1024], BF16)
    qT = DB(nc, "qT", [128, 4, 128], BF16, 2)
    kT = DB(nc, "kT", [128, 4, 128], BF16, 2)
    sTb = DB(nc, "sTb", [128, 4, 128], BF16, 2)
    kdec = DB(nc, "kdec", [128, 4, 128], BF16, 2)
    o_sb = DB(nc, "o_sb", [128, 4, 256], F32, 2)
    ya = DB(nc, "ya", [128, D], BF16, 2)
    assert T == 256
    uf = o_sb.bufs[0]
    UFK = o_sb.keys[0]
    ub = sb("ub", [128, 4, T], BF16)
    ygf = o_sb.bufs[1]
    YGK = o_sb.keys[1]
    ygb = sb("ygb", [128, 4, T], BF16)
    hqT = sb("hqT", [128, 4, T], BF16)
    hkT = sb("hkT", [128, 4, T], BF16)
    hgE = sb("hgE", [128, 3, 4, T // 64])

    BIGK = lambda f: f"big.{f}"
    HNK = "hnT"

    def load(dst_ap, src_ap, key, q="act"):
        DMA(q, dst_ap, src_ap, "ld_small", [], [key])

    load(ident_f[:], cd["ident"], "ident_f")
    CP(ident[:], ident_f[:], ["ident_f"], ["ident"])
    load(pgT[:], small["pgT"], "pgT")
    load(gnT[:], small["gnT"], "gnT")
    load(retD[:], cd["retD"], "retD")
    load(retS[:], cd["retS"], "retS")
    load(triu[:], cd["triu"], "triu")
    load(restart[:], cd["restart"], "restart")
    load(lru_s[:], small["lru_s"], "lru_s")
    load(s5dg[:], small["s5dg"], "s5dg")
    MEMSET(epsc[:, 0:1], EPS, ["epsc"])
    MEMSET(epsc[:, 1:2], 1.0, ["epsc"])
    MEMSET(RS[:], 0.0, ["RS"])
    MEMSET(RSb[:], 0.0, ["RSb"])
    MEMSET(HS[:], 0.0, ["HS"])
    MEMSET(HSb[:], 0.0, ["HSb"])
    MEMSET(lru_carry[:], 0.0, ["lru_carry"])
    MEMSET(lru_h[:], 0.0, ["lru_h"])
    MEMSET(s5_st[:], 0.0, ["s5_st"])

    ACT(lru_sp8[:], lru_s[:, :, 0], AF.Exp, ["lru_s"], ["lru_sp8"], scale=-1.0)
    ACT(lru_sp8[:], lru_sp8[:], AF.Ln, ["lru_sp8", "epsc"], ["lru_sp8"], bias=epsc[:, 1:2])
    TS(lru_sp8[:], lru_sp8[:], -8.0, None, ALU.mult, None, ["lru_sp8"], ["lru_sp8"])
    for wi, nm in enumerate(("rg_wa", "rg_wi")):
        wabf = o_sb.bufs[wi][:].rearrange("p a (c d) -> p (a c) d", d=128)
        wk_ = o_sb.keys[wi]
        MEMSET(wabf, 0.0, [wk_])
        src = small[nm].rearrange("(ct bl) i j -> bl i ct j", bl=2)
        for bl in range(2):
            DMA("act", wabf[bl * 64:(bl + 1) * 64, :, bl * 64:(bl + 1) * 64], src[bl], "ld_small", [], [wk_])
        CP(Wab[:, wi, :, :], wabf, [wk_], ["Wab"])

    hgl = sb("hgl", [128, 8])
    load(hgl[:], small["hgl"], "hgl")
    TT(hg_lb[:, 0:4], hgl[:, 0:4], hgl[:, 4:8], ALU.subtract, ["hgl"], ["hg_lb"])
    ACT(hg_lb[:, 0:4], hg_lb[:, 0:4], AF.Sigmoid, ["hg_lb"], ["hg_lb"])
    TS(hg_lb[:, 4:8], hg_lb[:, 0:4], -1.0, 1.0, ALU.mult, ALU.add, ["hg_lb"], ["hg_lb"])

    s5s = sb("s5s", [128, 16, 3])
    load(s5s[:], small["s5s"], "s5s")
    pp = sb("s5pp", [128, 12, 16])
    PPK = ["s5pp"]
    iota = f32a.bufs[0][:, 0:T]
    IOK = f32a.keys[0]
    DMA("act", iota, cd["iota"], "ld_small", [], [IOK])
    DT_, ARE, AIM, TH, MAG, COS, SIN, NRE, DEN, ZRE, ZIM, TMP = range(12)
    ACT(pp[:, DT_, :], s5s[:, :, 2], AF.Exp, ["s5s"], PPK)
    ACT(pp[:, ARE, :], s5s[:, :, 0], AF.Exp, ["s5s"], PPK)
    TS(pp[:, ARE, :], pp[:, ARE, :], -1.0, None, ALU.mult, None, PPK, PPK)
    CP(pp[:, AIM, :], s5s[:, :, 1], ["s5s"], PPK)
    TT(pp[:, TH, :], pp[:, DT_, :], pp[:, AIM, :], ALU.mult, PPK, PPK)
    TT(pp[:, TMP, :], pp[:, DT_, :], pp[:, ARE, :], ALU.mult, PPK, PPK)
    ACT(pp[:, MAG, :], pp[:, TMP, :], AF.Exp, PPK, PPK)
    CP(s5_mag[:], pp[:, MAG, :], PPK, ["s5_mag"])

    TWO_PI = 2.0 * np.pi
    redi = f32a.bufs[1][:].bitcast(I32)[:, 0:T]
    redf = f32a.bufs[2][:, 0:T]
    RIK, RFK = f32a.keys[1], f32a.keys[2]

    def sincos(out_sin, out_cos, ang_ap, shape_n, keys_r, keys_w):
        ri = redi[:, 0:shape_n]
        rf = redf[:, 0:shape_n]
        for out_ap, shift in ((out_sin, 0.0), (out_cos, 0.5 * np.pi)):
            if out_ap is None:
                continue
            TS(ri, ang_ap, shift, 1.0 / TWO_PI, ALU.add, ALU.mult, keys_r, [RIK])
            STT(rf, ri, -TWO_PI, ang_ap, ALU.mult, ALU.add, [RIK] + keys_r, [RFK])
            if shift != 0.0:
                TS(rf, rf, shift, None, ALU.add, None, [RFK], [RFK])
            TS(rf, rf, 3.1415925, -3.1415925, ALU.min, ALU.max, [RFK], [RFK])
            ACT(out_ap, rf, AF.Sin, [RFK], keys_w)

    sincos(pp[:, SIN, :], pp[:, COS, :], pp[:, TH, :], 16, PPK, PPK)
    ang2 = sb("ang2", [128, 2, 16])
    TS(ang2[:, 0, :], pp[:, TH, :], float(T), None, ALU.mult, None, PPK, ["ang2"])
    TS(ang2[:, 1, :], pp[:, TH, :], float(NMETA), None, ALU.mult, None, PPK, ["ang2"])
    bndt = sb("bndt", [128, 2, 2, 16])
    for fr in range(2):
        sincos(bndt[:, fr, 1, :], bndt[:, fr, 0, :], ang2[:, fr, :], 16, ["ang2"], ["bndt"])
    CP(s5_bnd[:], bndt[:], ["bndt"], ["s5_bnd"])
    angT = f32a.bufs[3][:, 0:T]
    ATK = f32a.keys[3]
    for m in range(16):
        TS(angT, iota, pp[:, TH, m:m + 1], None, ALU.mult, None, PPK + [IOK], [ATK])
        sincos(Rcs[:, 1, m, :], Rcs[:, 0, m, :], angT, T, [ATK], ["Rcs"])
    TT(pp[:, COS, :], pp[:, COS, :], pp[:, MAG, :], ALU.mult, PPK, PPK)
    TT(pp[:, SIN, :], pp[:, SIN, :], pp[:, MAG, :], ALU.mult, PPK, PPK)
    TS(pp[:, NRE, :], pp[:, COS, :], -1.0, None, ALU.add, None, PPK, PPK)
    TT(pp[:, DEN, :], pp[:, ARE, :], pp[:, ARE, :], ALU.mult, PPK, PPK)
    TT(pp[:, TMP, :], pp[:, AIM, :], pp[:, AIM, :], ALU.mult, PPK, PPK)
    TT(pp[:, DEN, :], pp[:, DEN, :], pp[:, TMP, :], ALU.add, PPK, PPK)
    S.op("dve", lambda e: e.reciprocal(out=pp[:, DEN, :], in_=pp[:, DEN, :]), reads=PPK, writes=PPK)
    TT(pp[:, ZRE, :], pp[:, NRE, :], pp[:, ARE, :], ALU.mult, PPK, PPK)
    TT(pp[:, TMP, :], pp[:, SIN, :], pp[:, AIM, :], ALU.mult, PPK, PPK)
    TT(pp[:, ZRE, :], pp[:, ZRE, :], pp[:, TMP, :], ALU.add, PPK, PPK)
    TT(pp[:, ZRE, :], pp[:, ZRE, :], pp[:, DEN, :], ALU.mult, PPK, PPK)
    TT(pp[:, ZIM, :], pp[:, SIN, :], pp[:, ARE, :], ALU.mult, PPK, PPK)
    TT(pp[:, TMP, :], pp[:, NRE, :], pp[:, AIM, :], ALU.mult, PPK, PPK)
    TT(pp[:, ZIM, :], pp[:, ZIM, :], pp[:, TMP, :], ALU.subtract, PPK, PPK)
    TT(pp[:, ZIM, :], pp[:, ZIM, :], pp[:, DEN, :], ALU.mult, PPK, PPK)
    s5b = tmpTok.bufs[0][:].rearrange("p (m r c) -> p m r c", m=16, r=2)
    DMA("act", s5b, small["s5b"], "ld_small", [], [tmpTok.keys[0]])
    bbn = tmpTok.bufs[1][:].rearrange("p (r m c) -> p r m c", r=2, m=16)
    tb = tmpTok.bufs[2][:].rearrange("p (r m c) -> p r m c", r=2, m=16)
    zre_b = pp[:, ZRE, :].unsqueeze(2).to_broadcast([128, 16, 16])
    zim_b = pp[:, ZIM, :].unsqueeze(2).to_broadcast([128, 16, 16])
    TT(tb[:, 0], s5b[:, :, 0, :], zre_b, ALU.mult, PPK + [tmpTok.keys[0]], [tmpTok.keys[2]])
    TT(tb[:, 1], s5b[:, :, 1, :], zim_b, ALU.mult, PPK + [tmpTok.keys[0]], [tmpTok.keys[2]])
    TT(bbn[:, 0], tb[:, 0], tb[:, 1], ALU.subtract, [tmpTok.keys[2]], [tmpTok.keys[1]])
    TT(tb[:, 0], s5b[:, :, 1, :], zre_b, ALU.mult, PPK + [tmpTok.keys[0]], [tmpTok.keys[2]])
    TT(tb[:, 1], s5b[:, :, 0, :], zim_b, ALU.mult, PPK + [tmpTok.keys[0]], [tmpTok.keys[2]])
    TT(bbn[:, 1], tb[:, 0], tb[:, 1], ALU.add, [tmpTok.keys[2]], [tmpTok.keys[1]])
    maskB = sb("maskB", [128, 16, 8])
    load(maskB[:], cd["maskB"], "maskB")
    maskC = sb("maskC", [128, 4, 2, 2])
    load(maskC[:], cd["maskC"], "maskC")
    s5c = tmpTok.bufs[3][:].rearrange("p (a r q) -> p a r q", a=4, r=2)
    DMA("act", s5c, small["s5c"], "ld_small", [], [tmpTok.keys[3]])
    wide = DB(nc, "wide", [128, 128], BF16, 2)
    for ri in range(2):
        for m in range(16):
            wd, wk = wide.next()
            TT(wd[:].rearrange("p (g c) -> p g c", g=8), bbn[:, ri, m, :].unsqueeze(1).to_broadcast([128, 8, 16]),
               maskB[:, m, :].unsqueeze(2).to_broadcast([128, 8, 16]), ALU.mult, [tmpTok.keys[1], "maskB"], [wk])
            ps, pk = PS.half()
            psb = ps.bitcast(BF16)
            TR(psb[:, 0:128], wd[:], ident[:], [wk, "ident"], pk)
            CP(Bblk[:, ri, m, :], psb[:, 0:128], pk, ["Bblk"], eng="act")
    for ri in range(2):
        for m in range(16):
            wd, wk = wide.next()
            TT(wd[:].rearrange("p (g q) -> p g q", g=2), s5c[:, m // 4, ri, :].unsqueeze(1).to_broadcast([128, 2, 64]),
               maskC[:, m % 4, :, ri].unsqueeze(2).to_broadcast([128, 2, 64]), ALU.mult, [tmpTok.keys[3], "maskC"], [wk])
            ps, pk = PS.half()
            psb = ps.bitcast(BF16)
            TR(psb[:, 0:128], wd[:], ident[:], [wk, "ident"], pk)
            CP(Cblk[:, ri, m, :], psb[:, 0:128], pk, ["Cblk"], eng="act")

    cast_engs = ["dve", "act", "pool"]
    ci = 0
    for n, R, C in (wdefs if KPHASE != "pro" else []):
        for rt in range(R // 128):
            for c0 in range(0, C, 2048):
                w = min(2048, C - c0)
                s = wslot[0]
                wslot[0] = (s + 1) % NSLOT
                stg = Wr[s][:].bitcast(F32)
                DMA("sp", stg[:, 0:w], wsrc[n][rt * 128:(rt + 1) * 128, c0:c0 + w], f"w{s}", [], [f"W{s}"])
                bi = ci % 4
                cb = big[:, bi * 8:(bi + 1) * 8, :].rearrange("p a b -> p (a b)")
                ck = [BIGK(bi * 8 + t_) for t_ in range(8)]
                CP(cb[:, 0:w], stg[:, 0:w], [f"W{s}"], ck, eng=cast_engs[ci % 3])
                ci += 1
                if n == "glu_w":
                    dst = wscr[n][rt * 128:(rt + 1) * 128, c0:c0 + w]
                    srcv = cb[:, 0:w]
                else:
                    dst = wscr[n][rt // 8, c0 // 512:(c0 + w) // 512, :, rt % 8, :].rearrange("b p c -> p b c")
                    srcv = cb[:, 0:w].rearrange("p (b c) -> p b c", c=512)
                DMA("pool", dst, srcv, f"st_scr{bi}", ck, [f"scr_{n}.{rt}.{c0 // 2048}"])
    if KPHASE != "pro":
      DMA("sp", gluW[:], wscr["glu_w"].rearrange("(kt p) c -> p kt c", p=128), "ld_glu",
        [f"scr_glu_w.{rt}.0" for rt in range(4)], ["gluW"])

    def load_w(name, k0, c0, ncols=512, nk=8):
        s = wslot[0]
        wslot[0] = (s + 1) % NSLOT
        view = Wr[s][:].rearrange("p (k c) -> p k c", k=8)
        assert nk == 8 and ncols == 512 and k0 % 1024 == 0 and c0 % 512 == 0
        DMA("sp", view[:, 0:nk, 0:ncols], wscr[name][k0 // 1024, c0 // 512],
            f"w{s}", [f"scr_{name}.{k0 // 128 + i_}.{c0 // 2048}" for i_ in range(nk)], [f"W{s}"])
        return view, f"W{s}"

    def rstd_from_ss(ss_ap, rows, dim, ck):
        k = ss_ap.shape[1]
        ACT(ss_ap, ss_ap, AF.Sqrt, [ck, "epsc"], [ck], scale=1.0 / dim, bias=epsc[0:rows, 0:1])
        S.op("dve", lambda e: e.reciprocal(out=ss_ap, in_=ss_ap), reads=[ck], writes=[ck])

    def to_feature(src_tok, rows, ntile, dst, dst_f0, tok0, gain_ap, rkeys, wkeys_fn):
        for q0 in range(0, ntile, 4):
            nq = min(4, ntile - q0)
            ps, pk = PS.half()
            psb = ps.bitcast(BF16)
            for i in range(nq):
                TR(psb[:, i * 128:i * 128 + rows], src_tok[0:rows, (q0 + i) * 128:(q0 + i + 1) * 128],
                   ident[0:rows, 0:rows], rkeys + ["ident"], pk)
            src = psb.rearrange("p (k t) -> p k t", k=4)[:, 0:nq, 0:rows]
            dsl = dst[:, dst_f0 + q0:dst_f0 + q0 + nq, tok0:tok0 + rows]
            wk = [wkeys_fn(dst_f0 + q0 + i) for i in range(nq)]
            if gain_ap is None:
                CP(dsl, src, pk, wk, eng="act")
            else:
                g = gain_ap[:, q0:q0 + nq].unsqueeze(2).to_broadcast([128, nq, rows])
                TT(dsl, src, g, ALU.mult, pk + ["pgT", "gnT"], wk)

    def prenorm(Hc, HK, tiles, gcol):
        for (j, rows) in tiles:
            jb, jk = junk_t, None
            cl, ck = colsA.next()
            ACT(jb[0:rows, :], Hc[0:rows, j, :], AF.Square, [HK], [ck], accum_out=cl[0:rows, 0:1])
            rstd_from_ss(cl[0:rows, 0:1], rows, D, ck)
            hb, hk = hn_tok.next()
            TS(hb[0:rows, :], Hc[0:rows, j, :], cl[0:rows, 0:1], None, ALU.mult, None, [HK, ck], [hk])
            to_feature(hb, rows, 8, hnT, 0, j * 128, pgT[:, gcol:gcol + 8], [hk], lambda f: HNK)

    def postnorm_add(Hc, HK, j, rows, psA, pkA, psB, pkB, gtab, gk):
        cl, ck = colsA.next()
        jb, jk = junk_t, None
        ACT(jb[0:rows, 0:512], psA[0:rows, :], AF.Square, pkA, [ck], accum_out=cl[0:rows, 0:1])
        ACT(jb[0:rows, 512:1024], psB[0:rows, :], AF.Square, pkB, [ck], accum_out=cl[0:rows, 1:2])
        TT(cl[0:rows, 0:1], cl[0:rows, 0:1], cl[0:rows, 1:2], ALU.add, [ck], [ck])
        rstd_from_ss(cl[0:rows, 0:1], rows, D, ck)
        for (ps, pk, c0) in ((psA, pkA, 0), (psB, pkB, 512)):
            tb_, tk = tmpTok.next()
            STT(tb_[0:rows, :], ps[0:rows, :], cl[0:rows, 0:1], gtab[0:rows, c0:c0 + 512], ALU.mult, ALU.mult,
                pk + [ck, gk], [tk])
            TT(Hc[0:rows, j, c0:c0 + 512], Hc[0:rows, j, c0:c0 + 512], tb_[0:rows, :], ALU.add, [HK, tk], [HK],
               eng="pool")

    def proj_token_major(wname, F, srcbuf, src_key_fn, tiles, consume, hook=None):
        banks = {}
        live = set()
        for (j, rows) in tiles:
            a_ = PS.full()
            live.add(PS.last_full)
            b_ = PS.full()
            live.add(PS.last_full)
            banks[j] = (a_, b_)
        nunit = F // 8
        for cc in range(2):
            for u in range(nunit):
                wv, wk = load_w(wname, u * 1024, cc * 512)
                for (j, rows) in tiles:
                    ps, pk = banks[j][cc]
                    for fi in range(8):
                        f = u * 8 + fi
                        MM(ps[0:rows, :], srcbuf[:, f, j * 128:j * 128 + rows], wv[:, fi, :], f == 0, f == F - 1,
                           [src_key_fn(f), wk], pk)
        if hook is not None:
            PS.live = live
            hook()
            PS.live = set()
        for (j, rows) in tiles:
            (psA, pkA), (psB, pkB) = banks[j]
            consume(j, rows, psA, pkA, psB, pkB)

    def load_postg(l):
        for i in range(2):
            DMA("pool", postg[i][:], small["postg"][l * 2 + i:l * 2 + i + 1, :].partition_broadcast(128), "ld_postg",
                [], [f"postg{i}"])

    def mlp(l, Hc, HK, tiles, n, hook=None):
        prenorm(Hc, HK, tiles, (l * 2 + 1) * 8)
        wn1, wn2 = f"w1_{l}", f"w2_{l}"
        for blk in range(8):
            wv, wk = load_w(wn1, 0, blk * 512)
            for ft in range(4):
                f = blk * 4 + ft
                ps, pk = PS.half()
                for kt in range(8):
                    MM(ps[:, 0:n], wv[:, kt, ft * 128:(ft + 1) * 128], hnT[:, kt, 0:n], kt == 0, kt == 7, [HNK, wk], pk)
                tf, tk = f32a.next()
                ACT(tf[:, 0:n], ps[:, 0:n], AF.Relu, pk, [tk])
                TT(big[:, f, 0:n], tf[:, 0:n], tf[:, 0:n], ALU.mult, [tk], [BIGK(f)], eng=ew())
        proj_token_major(wn2, 32, big, BIGK, tiles,
                         lambda j, rows, a, ak, b, bk: postnorm_add(Hc, HK, j, rows, a, ak, b, bk, postg[1], "postg1"),
                         hook=hook)

    def layer0_mixer(Hc, HK, tiles, n, meta_group, rope_ap, rope_key, do_prenorm=True):
        if do_prenorm:
            prenorm(Hc, HK, tiles, 0)
        for qk in range(2):
            wv, wk = load_w("w_in_ab", 0, qk * 512)
            for (j, rows) in tiles:
                ps, pk = PS.full()
                for kt in range(8):
                    MM(ps[0:rows, :], hnT[:, kt, j * 128:j * 128 + rows], wv[:, kt, :], kt == 0, kt == 7, [HNK, wk], pk)
                x3 = ps.rearrange("p (h d) -> p h d", h=4)
                x1 = x3[0:rows, :, 0:64]
                x2 = x3[0:rows, :, 64:128]
                cosb = rope_ap[0:rows, j, 0:64].unsqueeze(1).to_broadcast([rows, 4, 64])
                sinb = rope_ap[0:rows, j, 64:128].unsqueeze(1).to_broadcast([rows, 4, 64])
                t1, k1 = tmpTok.next()
                t2, k2 = tmpTok.next()
                a1 = t1[0:rows, 0:256].rearrange("p (h d) -> p h d", h=4)
                a2 = t2[0:rows, 0:256].rearrange("p (h d) -> p h d", h=4)
                b1 = t1[0:rows, 256:512].rearrange("p (h d) -> p h d", h=4)
                b2 = t2[0:rows, 256:512].rearrange("p (h d) -> p h d", h=4)
                TT(a1, x1, cosb, ALU.mult, pk + [rope_key], [k1])
                TT(a2, x2, sinb, ALU.mult, pk + [rope_key], [k2])
                TT(b1, x1, sinb, ALU.mult, pk + [rope_key], [k1])
                TT(b2, x2, cosb, ALU.mult, pk + [rope_key], [k2])
                TT(qkr[0:rows, j, qk, :, 0:64], a1, a2, ALU.subtract, [k1, k2], [f"qkr{j}"], eng="pool")
                TT(qkr[0:rows, j, qk, :, 64:128], b1, b2, ALU.add, [k1, k2], [f"qkr{j}"], eng="pool")
        for vb in range(2):
            wv, wk = load_w("w_in_ab", 0, 1024 + vb * 512)
            for (j, rows) in tiles:
                ps, pk = PS.full()
                for kt in range(8):
                    MM(ps[0:rows, :], hnT[:, kt, j * 128:j * 128 + rows], wv[:, kt, :], kt == 0, kt == 7, [HNK, wk], pk)
                CP(v_tok[0:rows, j, vb * 512:(vb + 1) * 512], ps[0:rows, :], pk, [f"v{j}"], eng="act")
        for gb in range(2):
            wv, wk = load_w("w_in_ab", 0, 2048 + gb * 512)
            for (j, rows) in tiles:
                ps, pk = PS.full()
                for kt in range(8):
                    MM(ps[0:rows, :], hnT[:, kt, j * 128:j * 128 + rows], wv[:, kt, :], kt == 0, kt == 7, [HNK, wk], pk)
                ACT(sg_tok[0:rows, j, gb * 512:(gb + 1) * 512], ps[0:rows, :], AF.Silu, pk, [f"sg{j}"])
        for (j, rows) in tiles:
            qt, qtk = qT.next()
            kt_, ktk = kT.next()
            for (dst, dk_, qk) in ((qt, qtk, 0), (kt_, ktk, 1)):
                ps, pk = PS.half()
                psb = ps.bitcast(BF16)
                for h in range(4):
                    TR(psb[:, h * 128:h * 128 + rows], qkr[0:rows, j, qk, h, :], ident[0:rows, 0:rows],
                       [f"qkr{j}", "ident"], pk)
                CP(dst[:, :, 0:rows], psb.rearrange("p (h t) -> p h t", h=4)[:, :, 0:rows], pk, [dk_], eng="act")
            ps, pk = PS.full()
            p3 = ps.rearrange("p (h t) -> p h t", h=4)
            for h in range(4):
                MM(p3[0:rows, h, 0:rows], kt_[:, h, 0:rows], qt[:, h, 0:rows], True, True, [qtk, ktk], pk)
            st_, stk = sTb.next()
            TT(st_[0:rows, :, 0:rows], p3[0:rows, :, 0:rows], retD[0:rows, :, 0:rows], ALU.mult, pk + ["retD"], [stk])
            ob, obk = o_sb.next()
            for hp in range(2):
                ps, pk = PS.full()
                for hh in range(2):
                    h = hp * 2 + hh
                    osl = ps[0:rows, hh * 256:(hh + 1) * 256]
                    MM(osl, st_[0:rows, h, 0:rows], v_tok[0:rows, j, h * 256:(h + 1) * 256], True, False,
                       [stk, f"v{j}"], pk)
                    MM(osl, qt[:, h, 0:rows], RSb[:, h, :], False, True, [qtk, "RSb"], pk)
                TT(ob[0:rows, hp * 2:hp * 2 + 2, :], ps.rearrange("p (h e) -> p h e", h=2)[0:rows],
                   retS[0:rows, hp * 2:hp * 2 + 2].unsqueeze(2).to_broadcast([rows, 2, 256]), ALU.mult,
                   pk + ["retS"], [obk])
            kd, kdk = kdec.next()
            kdcol = 8 if meta_group else 4
            TT(kd[0:rows, :, :], qkr[0:rows, j, 1, :, :],
               retS[0:rows, kdcol:kdcol + 4].unsqueeze(2).to_broadcast([rows, 4, 128]), ALU.mult,
               [f"qkr{j}", "retS"], [kdk], eng="pool")
            for hp in range(2):
                ps, pk = PS.full()
                for hh in range(2):
                    h = hp * 2 + hh
                    MM(ps[:, hh * 256:(hh + 1) * 256], kd[0:rows, h, :], v_tok[0:rows, j, h * 256:(h + 1) * 256],
                       True, True, [kdk, f"v{j}"], pk)
                for hh in range(2):
                    h = hp * 2 + hh
                    STT(RS[:, h, :], RS[:, h, :], float(GAMMA[h] ** rows), ps[:, hh * 256:(hh + 1) * 256],
                        ALU.mult, ALU.add, ["RS"] + pk, ["RS"])
            CP(RSb[:], RS[:], ["RS"], ["RSb"], eng="act")
            cl, ck = colsA.next()
            S.op("dve", lambda e, ob=ob, cl=cl, rows=rows: e.reduce_sum(out=cl[0:rows, 0:4], in_=ob[0:rows], axis=AX.X),
                 reads=[obk], writes=[ck])
            TS(cl[0:rows, 0:4], cl[0:rows, 0:4], -1.0 / 256, None, ALU.mult, None, [ck], [ck])
            TT(ob[0:rows], ob[0:rows], cl[0:rows, 0:4].unsqueeze(2).to_broadcast([rows, 4, 256]), ALU.add,
               [obk, ck], [obk])
            jb, jk = junk_t, None
            for h in range(4):
                ACT(jb[0:rows, h * 256:(h + 1) * 256], ob[0:rows, h, :], AF.Square, [obk], [ck],
                    accum_out=cl[0:rows, 4 + h:5 + h])
            rstd_from_ss(cl[0:rows, 4:8], rows, 256, ck)
            yb_, ybk = ya.next()
            for h in range(4):
                STT(yb_[0:rows, h * 256:(h + 1) * 256], ob[0:rows, h, :], cl[0:rows, 4 + h:5 + h],
                    sg_tok[0:rows, j, h * 256:(h + 1) * 256], ALU.mult, ALU.mult, [obk, ck, f"sg{j}"], [ybk])
            to_feature(yb_, rows, 8, big, 0, j * 128, gnT[:, 0:8], [ybk], BIGK)
        for half in range(2):
            hs_list = []
            wv, wk = load_w("w_in_ab", 0, 3072 + half * 512)
            L = {}
            for ct in range(4):
                c = half * 4 + ct
                ps, pk = PS.half()
                for kt in range(8):
                    MM(ps[:, 0:n], wv[:, kt, ct * 128:(ct + 1) * 128], hnT[:, kt, 0:n], kt == 0, kt == 7, [HNK, wk], pk)
                xp, xk = f32a.next()
                CP(xp[:, 0:3], lru_carry[:, c, :], ["lru_carry"], [xk], eng="pool")
                CP(xp[:, 3:3 + n], ps[:, 0:n], pk, [xk], eng="act")
                L[ct] = dict(xp=xp, xk=xk)
            for ct in range(4):
                c = half * 4 + ct
                xp, xk = L[ct]["xp"], L[ct]["xk"]
                acc, ak = f32a.next()
                P = lambda i, c=c: lru_s[:, c, i:i + 1]
                TS(acc[:, 0:n], xp[:, 0:n], P(4), P(3), ALU.mult, ALU.add, [xk, "lru_s"], [ak])
                STT(acc[:, 0:n], xp[:, 1:1 + n], P(5), acc[:, 0:n], ALU.mult, ALU.add, [xk, ak, "lru_s"], [ak])
                STT(acc[:, 0:n], xp[:, 2:2 + n], P(6), acc[:, 0:n], ALU.mult, ALU.add, [xk, ak, "lru_s"], [ak])
                STT(acc[:, 0:n], xp[:, 3:3 + n], P(7), acc[:, 0:n], ALU.mult, ALU.add, [xk, ak, "lru_s"], [ak])
                CP(lru_carry[:, c, :], xp[:, n:n + 3], [xk], ["lru_carry"], eng="pool")
                L[ct].update(acc=acc, ak=ak)
            for ct in range(4):
                xb_, xbk = bf16a.next()
                CP(xb_[:, 0:n], L[ct]["acc"][:, 0:n], [L[ct]["ak"]], [xbk], eng="act")
                L[ct].update(xb=xb_, xbk=xbk)
            for ct in range(4):
                c = half * 4 + ct
                gates = []
                for wi in range(2):
                    ps2, pk2 = PS.half()
                    MM(ps2[:, 0:n], Wab[:, wi, c, :], L[ct]["xb"][:, 0:n], True, True, ["Wab", L[ct]["xbk"]], pk2)
                    gt_, gk_ = f32a.next()
                    ACT(gt_[:, 0:n], ps2[:, 0:n], AF.Sigmoid, pk2 + ["lru_s"], [gk_], bias=lru_s[:, c, 1 + wi:2 + wi])
                    gates.append((gt_, gk_))
                L[ct].update(gates=gates)
            for ct in range(4):
                c = half * 4 + ct
                (rg, rk), (ig, ik) = L[ct]["gates"]
                ACT(rg[:, 0:n], rg[:, 0:n], AF.Exp, [rk, "lru_sp8"], [rk], scale=lru_sp8[:, c:c + 1])
            for ct in range(4):
                (rg, rk), (ig, ik) = L[ct]["gates"]
                acc, ak = L[ct]["acc"], L[ct]["ak"]
                TT(ig[:, 0:n], ig[:, 0:n], acc[:, 0:n], ALU.mult, [ik, ak], [ik])
                TT(acc[:, 0:n], rg[:, 0:n], rg[:, 0:n], ALU.mult, [rk], [ak], eng="pool")
            for ct in range(4):
                acc, ak = L[ct]["acc"], L[ct]["ak"]
                ACT(acc[:, 0:n], acc[:, 0:n], AF.Sqrt, [ak, "epsc"], [ak], scale=-1.0, bias=epsc[:, 1:2])
            for ct in range(4):
                c = half * 4 + ct
                (rg, rk), (ig, ik) = L[ct]["gates"]
                acc, ak = L[ct]["acc"], L[ct]["ak"]
                TT(ig[:, 0:n], ig[:, 0:n], acc[:, 0:n], ALU.mult, [ik, ak], [ik])
                hb_, hbk = hsb.next()
                S.op("dve", lambda e, hb_=hb_, rg=rg, ig=ig, c=c: e.tensor_tensor_scan(
                    out=hb_[:, 0:n], data0=rg[:, 0:n], data1=ig[:, 0:n], initial=lru_h[:, c:c + 1],
                    op0=ALU.mult, op1=ALU.add), reads=[rk, ik, "lru_h"], writes=[hbk])
                CP(lru_h[:, c:c + 1], hb_[:, n - 1:n], [hbk], ["lru_h"], eng="pool")
                hs_list.append((hb_, hbk))
            wv, wk = load_w("w_in_ab", 0, 4096 + half * 512)
            for ct in range(4):
                c = half * 4 + ct
                ps, pk = PS.half()
                for kt in range(8):
                    MM(ps[:, 0:n], wv[:, kt, ct * 128:(ct + 1) * 128], hnT[:, kt, 0:n], kt == 0, kt == 7, [HNK, wk], pk)
                ge, gek = f32a.next()
                ACT(ge[:, 0:n], ps[:, 0:n], AF.Gelu_apprx_tanh, pk, [gek])
                hb_, hbk = hs_list[ct]
                TT(big[:, 8 + c, 0:n], ge[:, 0:n], hb_[:, 0:n], ALU.mult, [gek, hbk], [BIGK(8 + c)], eng=ew())
        proj_token_major("w_out_ab", 16, big, BIGK, tiles,
                         lambda j, rows, a, ak, b, bk: postnorm_add(Hc, HK, j, rows, a, ak, b, bk, postg[0], "postg0"))

    def layer1_mixer(Hc, HK, tiles, n, meta_group, prev_n):
        prenorm(Hc, HK, tiles, 16)
        wv, wk = load_w("w_in_cd", 0, 0)
        for ct in range(4):
            ps, pk = PS.half()
            for kt in range(8):
                MM(ps[:, 0:n], wv[:, kt, ct * 128:(ct + 1) * 128], hnT[:, kt, 0:n], kt == 0, kt == 7, [HNK, wk], pk)
            CP(uf[:, ct, 0:n], ps[:, 0:n], pk, [UFK], eng="act")
            CP(ub[:, ct, 0:n], ps[:, 0:n], pk, [f"ub{ct}"])
        if prev_n is not None:
            fr = 0 if prev_n == T else 1
            cb_ = s5_bnd[:, fr, 0, :]
            sb_ = s5_bnd[:, fr, 1, :]
            q1, qk1 = colsA.next()
            q2, qk2 = colsA.next()
            TT(q1[:, 0:16], s5_st[:, 0, :], cb_, ALU.mult, ["s5_st", "s5_bnd"], [qk1])
            TT(q2[:, 0:16], s5_st[:, 1, :], sb_, ALU.mult, ["s5_st", "s5_bnd"], [qk2])
            TT(s5_ini[:, 0, :], q1[:, 0:16], q2[:, 0:16], ALU.subtract, [qk1, qk2], ["s5_ini"])
            q3, qk3 = colsA.next()
            q4, qk4 = colsA.next()
            TT(q3[:, 0:16], s5_st[:, 0, :], sb_, ALU.mult, ["s5_st", "s5_bnd"], [qk3])
            TT(q4[:, 0:16], s5_st[:, 1, :], cb_, ALU.mult, ["s5_st", "s5_bnd"], [qk4])
            TT(s5_ini[:, 1, :], q3[:, 0:16], q4[:, 0:16], ALU.add, [qk3, qk4], ["s5_ini"])
        else:
            MEMSET(s5_ini[:], 0.0, ["s5_ini"])
        for ct in range(4):
            ms = [ct * 4 + q for q in range(4)]
            W = {}
            for m in ms:
                psr, pkr = PS.half()
                MM(psr[:, 0:n], Bblk[:, 0, m, :], ub[:, ct, 0:n], True, True, ["Bblk", f"ub{ct}"], pkr)
                psi, pki = PS.half()
                MM(psi[:, 0:n], Bblk[:, 1, m, :], ub[:, ct, 0:n], True, True, ["Bblk", f"ub{ct}"], pki)
                W[m] = dict(psr=psr, pkr=pkr, psi=psi, pki=pki, t=[f32a.next() for _ in range(4)],
                            Rc=Rcs[:, 0, m, 0:n], Rs=Rcs[:, 1, m, 0:n])
            for m in ms:
                w_ = W[m]
                (t1, k1), (t2, k2), (t3, k3), (t4, k4) = w_["t"]
                TT(t1[:, 0:n], w_["psr"][:, 0:n], w_["Rc"], ALU.mult, w_["pkr"] + ["Rcs"], [k1])
                TT(t4[:, 0:n], w_["psr"][:, 0:n], w_["Rs"], ALU.mult, w_["pkr"] + ["Rcs"], [k4])
                TT(t2[:, 0:n], w_["psi"][:, 0:n], w_["Rs"], ALU.mult, w_["pki"] + ["Rcs"], [k2])
                TT(t3[:, 0:n], w_["psi"][:, 0:n], w_["Rc"], ALU.mult, w_["pki"] + ["Rcs"], [k3])
            for m in ms:
                (t1, k1), (t2, k2), (t3, k3), (t4, k4) = W[m]["t"]
                TT(t1[:, 0:n], t1[:, 0:n], t2[:, 0:n], ALU.add, [k1, k2], [k1], eng="pool")
                TT(t3[:, 0:n], t3[:, 0:n], t4[:, 0:n], ALU.subtract, [k3, k4], [k3], eng="pool")
            for m in ms:
                (t1, k1), (t2, k2), (t3, k3), (t4, k4) = W[m]["t"]
                magb = s5_mag[:, m:m + 1].to_broadcast([128, n])
                S.op("dve", lambda e, t2=t2, t1=t1, m=m, magb=magb: e.tensor_tensor_scan(
                    out=t2[:, 0:n], data0=magb, data1=t1[:, 0:n], initial=s5_ini[:, 0, m:m + 1],
                    op0=ALU.mult, op1=ALU.add), reads=[k1, "s5_mag", "s5_ini"], writes=[k2])
                S.op("dve", lambda e, t4=t4, t3=t3, m=m, magb=magb: e.tensor_tensor_scan(
                    out=t4[:, 0:n], data0=magb, data1=t3[:, 0:n], initial=s5_ini[:, 1, m:m + 1],
                    op0=ALU.mult, op1=ALU.add), reads=[k3, "s5_mag", "s5_ini"], writes=[k4])
            for m in ms:
                (t1, k1), (t2, k2), (t3, k3), (t4, k4) = W[m]["t"]
                CP(s5_st[:, 0, m:m + 1], t2[:, n - 1:n], [k2], ["s5_st"], eng="pool")
                CP(s5_st[:, 1, m:m + 1], t4[:, n - 1:n], [k4], ["s5_st"], eng="pool")
            for m in ms:
                w_ = W[m]
                (t1, k1), (t2, k2), (t3, k3), (t4, k4) = w_["t"]
                TT(t1[:, 0:n], t2[:, 0:n], w_["Rc"], ALU.mult, [k2, "Rcs"], [k1])
                TT(t3[:, 0:n], t4[:, 0:n], w_["Rs"], ALU.mult, [k4, "Rcs"], [k3])
            for m in ms:
                (t1, k1), (t2, k2), (t3, k3), (t4, k4) = W[m]["t"]
                hr, hrk = s5h.next()
                TT(hr[:, 0:n], t1[:, 0:n], t3[:, 0:n], ALU.subtract, [k1, k3], [hrk], eng="pool")
                W[m]["hr"] = (hr, hrk)
            for m in ms:
                w_ = W[m]
                (t1, k1), (t2, k2), (t3, k3), (t4, k4) = w_["t"]
                TT(t1[:, 0:n], t2[:, 0:n], w_["Rs"], ALU.mult, [k2, "Rcs"], [k1])
                TT(t3[:, 0:n], t4[:, 0:n], w_["Rc"], ALU.mult, [k4, "Rcs"], [k3])
            for m in ms:
                (t1, k1), (t2, k2), (t3, k3), (t4, k4) = W[m]["t"]
                hi, hik = s5h.next()
                TT(hi[:, 0:n], t1[:, 0:n], t3[:, 0:n], ALU.add, [k1, k3], [hik], eng="pool")
                W[m]["hi"] = (hi, hik)
            psy, pky = PS.half()
            for q, m in enumerate(ms):
                hr_, hrk_ = W[m]["hr"]
                hi_, hik_ = W[m]["hi"]
                MM(psy[:, 0:n], Cblk[:, 0, m, :], hr_[:, 0:n], q == 0, False, ["Cblk", hrk_], pky)
                MM(psy[:, 0:n], Cblk[:, 1, m, :], hi_[:, 0:n], False, q == 3, ["Cblk", hik_], pky)
            yt_, ytk = f32a.next()
            STT(yt_[:, 0:n], uf[:, ct, 0:n], s5dg[:, ct:ct + 1], psy[:, 0:n], ALU.mult, ALU.add,
                [UFK, "s5dg"] + pky, [ytk])
            ACT(ygf[:, ct, 0:n], yt_[:, 0:n], AF.Gelu_apprx_tanh, [ytk], [YGK])
            CP(ygb[:, ct, 0:n], ygf[:, ct, 0:n], [YGK], [f"ygb{ct}"], eng="pool")
        for co in range(4):
            ps, pk = PS.half()
            for ci_ in range(4):
                MM(ps[:, 0:n], gluW[:, ci_, co * 128:(co + 1) * 128], ygb[:, ci_, 0:n], ci_ == 0, ci_ == 3,
                   ["gluW", f"ygb{ci_}"], pk)
            sgm, sgk = f32a.next()
            ACT(sgm[:, 0:n], ps[:, 0:n], AF.Sigmoid, pk + ["s5dg"], [sgk], bias=s5dg[:, 4 + co:5 + co])
            TT(big[:, co, 0:n], ygf[:, co, 0:n], sgm[:, 0:n], ALU.mult, [YGK, sgk], [BIGK(co)])
        wq, wqk = load_w("w_in_cd", 0, 512)
        wf, wfk = load_w("w_in_cd", 0, 1024)
        bw = 16 if meta_group else 64
        htiles = [(0, NMETA)] if meta_group else [(c_, 64) for c_ in range(n // 64)]
        nblk = len(htiles)
        mid = bw // 2 - 1
        for h in range(4):
            psq, pkq = PS.half()
            for kt in range(8):
                MM(psq[:, 0:n], wq[:, kt, h * 128:(h + 1) * 128], hnT[:, kt, 0:n], kt == 0, kt == 7, [HNK, wqk], pkq)
            psf, pkf = PS.half()
            for kt in range(8):
                MM(psf[:, 0:n], wf[:, kt, h * 128:(h + 1) * 128], hnT[:, kt, 0:n], kt == 0, kt == 7, [HNK, wfk], pkf)
            ff, fk = f32a.next()
            ACT(ff[:, 0:n], psf[:, 0:n], AF.Sigmoid, pkf, [fk])
            TS(ff[:, 0:n], ff[:, 0:n], hg_lb[:, 4 + h:5 + h], hg_lb[:, h:h + 1], ALU.mult, ALU.add, [fk, "hg_lb"], [fk])
            lf, lk = f32a.next()
            ACT(lf[:, 0:n], ff[:, 0:n], AF.Ln, [fk], [lk])
            TS(ff[:, 0:n], ff[:, 0:n], -1.0, 1.0, ALU.mult, ALU.add, [fk], [fk], eng="pool")
            cum, cmk = f32a.next()
            S.op("dve", lambda e, cum=cum, lf=lf: e.tensor_tensor_scan(
                out=cum[:, 0:n], data0=restart[:, 0:n], data1=lf[:, 0:n], initial=0.0,
                op0=ALU.mult, op1=ALU.add), reads=[lk, "restart"], writes=[cmk])
            c3 = cum[:, 0:n].rearrange("p (j t) -> p j t", t=bw)
            ACT(hgE[:, 0, h, 0:nblk], c3[:, :, mid], AF.Exp, [cmk], ["hgE"])
            ACT(hgE[:, 1, h, 0:nblk], c3[:, :, bw - 1], AF.Exp, [cmk], ["hgE"])
            cm, cmk2 = f32a.next()
            cm3 = cm[:, 0:n].rearrange("p (j t) -> p j t", t=bw)
            TT(cm3, c3, c3[:, :, mid:mid + 1].to_broadcast([128, nblk, bw]), ALU.subtract, [cmk], [cmk2])
            ACT(lf[:, 0:n], cm[:, 0:n], AF.Exp, [cmk2], [lk])
            ACT(cm[:, 0:n], cm[:, 0:n], AF.Exp, [cmk2], [cmk2], scale=-1.0)
            l3 = lf[:, 0:n].rearrange("p (j t) -> p j t", t=bw)
            CP(hgE[:, 2, h, 0:nblk], l3[:, :, bw - 1], [lk], ["hgE"], eng="pool")
            TT(hqT[:, h, 0:n], psq[:, 0:n], lf[:, 0:n], ALU.mult, pkq + [lk], [f"hq{h}"])
            TT(hkT[:, h, 0:n], ff[:, 0:n], cm[:, 0:n], ALU.mult, [fk, cmk2], [f"hk{h}"], eng="pool")
        iv = v_tok[:].rearrange("p j (a c) -> p (j a) c", a=2)
        gv = sg_tok[:].rearrange("p j (a c) -> p (j a) c", a=2)
        IVK = lambda c_: f"v{c_ // 2}"
        GVK = lambda c_: f"sg{c_ // 2}"
        wi_, wik = load_w("w_in_cd", 0, 1536)
        for (c_, rows) in htiles:
            ps, pk = PS.full()
            for kt in range(8):
                MM(ps[0:rows, :], hnT[:, kt, c_ * 64:c_ * 64 + rows], wi_[:, kt, :], kt == 0, kt == 7, [HNK, wik], pk)
            CP(iv[0:rows, c_, :], ps[0:rows, :], pk, [IVK(c_)], eng="act")
        wg_, wgk = load_w("w_in_cd", 0, 2048)
        for (c_, rows) in htiles:
            ps, pk = PS.full()
            for kt in range(8):
                MM(ps[0:rows, :], hnT[:, kt, c_ * 64:c_ * 64 + rows], wg_[:, kt, :], kt == 0, kt == 7, [HNK, wgk], pk)
            ACT(gv[0:rows, c_, :], ps[0:rows, :], AF.Silu, pk, [GVK(c_)])
        HQK = [f"hq{h}" for h in range(4)]
        HKK = [f"hk{h}" for h in range(4)]
        for (j, rows) in htiles:
            t0 = j * 64
            ps, pk = PS.full()
            p3 = ps.rearrange("p (h t) -> p h t", h=4)
            for h in range(4):
                MM(p3[0:rows, h, 0:rows], hkT[:, h, t0:t0 + rows], hqT[:, h, t0:t0 + rows], True, True, HQK + HKK, pk)
            at, atk = sTb.next()
            STT(at[0:rows, :, 0:rows], p3[0:rows, :, 0:rows], 1e30,
                triu[0:rows, 0:rows].unsqueeze(1).to_broadcast([rows, 4, rows]), ALU.min, ALU.mult,
                pk + ["triu"], [atk])
            ps2, pk2 = PS.half()
            psb = ps2.bitcast(BF16)
            for h in range(4):
                TR(psb[0:rows, h * 128:(h + 1) * 128], hkT[:, h, t0:t0 + rows], ident[:, :], HKK + ["ident"], pk2)
            ktk_, ktkk = kdec.next()
            CP(ktk_[0:rows, :, :], psb.rearrange("p (h d) -> p h d", h=4)[0:rows], pk2, [ktkk], eng="act")
            TT(HSb[:], HS[:], hgE[:, 0, :, j:j + 1].to_broadcast([128, 4, 128]), ALU.mult, ["HS", "hgE"], ["HSb"])
            pso, pko = PS.full()
            o3 = pso.rearrange("p (h e) -> p h e", h=4)
            for h in range(4):
                MM(o3[0:rows, h, :], at[0:rows, h, 0:rows], iv[0:rows, j, h * 128:(h + 1) * 128], True, False,
                   [atk, IVK(j)], pko)
                MM(o3[0:rows, h, :], hqT[:, h, t0:t0 + rows], HSb[:, h, :], False, True, HQK + ["HSb"], pko)
            psk, pkk = PS.full()
            k3 = psk.rearrange("p (h e) -> p h e", h=4)
            for h in range(4):
                MM(k3[:, h, :], ktk_[0:rows, h, :], iv[0:rows, j, h * 128:(h + 1) * 128], True, True,
                   [ktkk, IVK(j)], pkk)
            ta, tak = tmpTok.next()
            TT(ta[:].rearrange("p (h e) -> p h e", h=4), k3, hgE[:, 2, :, j:j + 1].to_broadcast([128, 4, 128]),
               ALU.mult, pkk + ["hgE"], [tak])
            TT(HS[:], HS[:], hgE[:, 1, :, j:j + 1].to_broadcast([128, 4, 128]), ALU.mult, ["HS", "hgE"], ["HS"])
            TT(HS[:], HS[:], ta[:].rearrange("p (h e) -> p h e", h=4), ALU.add, ["HS", tak], ["HS"], eng="pool")
            jb, jk = junk_t, None
            cl, ck = colsA.next()
            for h in range(4):
                ACT(jb[0:rows, h * 128:(h + 1) * 128], o3[0:rows, h, :], AF.Square, pko, [ck],
                    accum_out=cl[0:rows, h:h + 1])
            rstd_from_ss(cl[0:rows, 0:4], rows, 128, ck)
            on, onk = tmpTok.next()
            TT(on[0:rows].rearrange("p (h e) -> p h e", h=4), o3[0:rows],
               cl[0:rows, 0:4].unsqueeze(2).to_broadcast([rows, 4, 128]), ALU.mult, pko + [ck], [onk])
            yb_, ybk = ya.next()
            TT(yb_[0:rows, 0:512], on[0:rows, :], gv[0:rows, j, :], ALU.mult, [onk, GVK(j)], [ybk], eng="pool")
            to_feature(yb_, rows, 4, big, 4, t0, gnT[:, 8:12], [ybk], BIGK)
        proj_token_major("w_out_cd", 8, big, BIGK, tiles,
                         lambda j, rows, a, ak, b, bk: postnorm_add(Hc, HK, j, rows, a, ak, b, bk, postg[0], "postg0"))

    last_store = None
    prev_n = None
    NGR = min(NG, KNG) if KPHASE == "all" else 0

    def group_geom(g):
        if g == 0:
            return NMETA, [(0, NMETA)], 0, 0
        f0 = (g - 1) * T
        return T, [(j, 128) for j in range(NT)], NMETA + f0, f0

    rope_bufs = {}

    def issue_loads(g):
        Hc, HK = H[g % 2], f"H{g % 2}"
        n, tiles, pos0, f0 = group_geom(g)
        rb, rkey = ropeT.next()
        rope_bufs[g] = (rb, rkey)
        if g == 0:
            DMA("pool", Hc[0:NMETA, 0, :], meta_d, "ld_x", [], [HK])
            DMA("pool", rb[0:NMETA, 0, :], cd["rope"][0:NMETA, :], "ld_rope", [], [rkey])
        else:
            DMA("pool", Hc[:, :, :], x_d[f0:f0 + T, :].rearrange("(j p) d -> p j d", p=128), "ld_x", [], [HK])
            DMA("pool", rb[:, :, :], cd["rope"][pos0:pos0 + T, :].rearrange("(j p) d -> p j d", p=128), "ld_rope",
                [], [rkey])

    if NGR > 0:
        issue_loads(0)
    for g in range(NGR):
        Hc = H[g % 2]
        HK = f"H{g % 2}"
        meta_group = g == 0
        n, tiles, pos0, f0 = group_geom(g)
        rb, rkey = rope_bufs.pop(g)

        def dbg(slot):
            if DEBUG and (g < 3):
                if meta_group:
                    DMA("pool", dbg_d[slot, 0:NMETA, :], Hc[0:NMETA, 0, :], "st_dbg", [HK], [])
                else:
                    DMA("pool", dbg_d[slot, pos0:pos0 + T, :].rearrange("(j p) d -> p j d", p=128), Hc[:, :, :],
                        "st_dbg", [HK], [])

        load_postg(0)
        layer0_mixer(Hc, HK, tiles, n, meta_group, rb, rkey, do_prenorm=(g == 0 or KSTOP < 3))
        has_next = g + 1 < NGR
        if has_next:
            issue_loads(g + 1)
        dbg(0)
        if KSTOP >= 1:
            mlp(0, Hc, HK, tiles, n)
            dbg(1)
        if KSTOP >= 2:
            load_postg(1)
            layer1_mixer(Hc, HK, tiles, n, meta_group, prev_n)
            dbg(2)
        if KSTOP >= 3:
            hook = None
            if has_next:
                nH, nHK = H[(g + 1) % 2], f"H{(g + 1) % 2}"
                ntiles = group_geom(g + 1)[1]
                hook = lambda nH=nH, nHK=nHK, ntiles=ntiles: prenorm(nH, nHK, ntiles, 0)
            mlp(1, Hc, HK, tiles, n, hook=hook)
            dbg(3)
        if not meta_group:
            last_store = DMA("pool", out_d[f0:f0 + T, :].rearrange("(j p) d -> p j d", p=128), Hc[:, :, :], "st_out",
                             [HK], [])
        prev_n = n
    fin = [d for d in [last_store, S.chan_last.get("st_dbg"), S.chan_last.get("ld_small"), S.chan_last.get("st_scr0")] if d is not None]
    S.op("sp", lambda e: e.nop(), extra_deps=fin)
    print("instr counts", {e: len(S.lists[e]) for e in ENGS}, flush=True)
    S.emit()
    S.stats = {e: max([i.val for i in S.lists[e] if i.chan is None and i.needs_inc] or [0]) for e in ENGS}
    S.stats.update({c: lst[-1].val for c, lst in S.chan_ins.items()})
    print("max sem values", S.stats, flush=True)
    return nc, consts


_CACHE = {}


def kernel(**inputs):
    if "prog" not in _CACHE:
        _CACHE["prog"] = build_program()
    nc, consts = _CACHE["prog"]
    lay = host_layouts(inputs)
    common = dict(lay)
    for n, v in consts.items():
        common["c_" + n] = v
    x = np.asarray(inputs["x"], dtype=np.float32)
    in_maps = []
    for c in range(8):
        m = dict(common)
        m["x"] = np.ascontiguousarray(x[c % 4])
        in_maps.append(m)
    res = run_bass_kernel_spmd(nc, in_maps, core_ids=list(range(8)))
    out = np.stack([np.asarray(res.results[b]["out"], dtype=np.float32) for b in range(4)], axis=0)
    if DEBUG:
        kernel.dbg = [np.asarray(res.results[b]["dbg"]) for b in range(4)]
    return out
```

```python
import numpy as np
import concourse.bass as bass
import concourse.mybir as mybir
from concourse.bass_utils import run_bass_kernel_spmd

F32 = mybir.dt.float32
BF16 = mybir.dt.bfloat16
I32 = mybir.dt.int32
AF = mybir.ActivationFunctionType
ALU = mybir.AluOpType
AX = mybir.AxisListType

T = 256
NT = T // 128
SEQ = 4096
NMETA = 16
D = 1024
DFF = 4096
EPS = 1e-6
NG = 1 + SEQ // T
import os
DEBUG = bool(int(os.environ.get("KDEBUG", "0")))
KNG = int(os.environ.get("KNG", "1000"))
KSTOP = int(os.environ.get("KSTOP", "3"))
KPHASE = os.environ.get("KPHASE", "all")

ENGS = ("pe", "act", "dve", "pool", "sp")


class Ins:
    __slots__ = ("eng", "fn", "waits", "idx", "needs_inc", "chan", "clock", "ninc", "val", "tag")


class Sched:
    def __init__(self, nc):
        self.nc = nc
        self.lists = {e: [] for e in ENGS}
        self.lastw = {}
        self.readers = {}
        self.clock = {e: {} for e in ENGS}
        self.chan_last = {}
        self.chan_ins = {}

    def _need(self, eng, dep, waits):
        src = dep.chan if dep.chan is not None else dep.eng
        if dep.chan is None and dep.eng == "pe" and eng == "pe":
            return
        if self.clock[eng].get(src, -1) >= dep.idx:
            return
        waits[src] = max(waits.get(src, -1), dep.idx)

    def op(self, eng, fn, reads=(), writes=(), chan=None, extra_deps=(), ninc=1):
        ins = Ins()
        ins.eng, ins.fn, ins.chan, ins.needs_inc, ins.ninc = eng, fn, chan, False, ninc
        ins.tag = getattr(self, "tag", "")
        psr = [k for k in reads if k[:2] == "ps" and k[2:].isdigit()]
        if psr:
            reads = [k for k in reads if k not in psr]
            writes = list(writes) + [k for k in psr if k not in writes]
        deps = []
        for k in reads:
            deps.extend(self.lastw.get(k, ()))
        for k in writes:
            deps.extend(self.lastw.get(k, ()))
            deps.extend(self.readers.get(k, ()))
        deps.extend(extra_deps)
        if chan is not None and chan in self.chan_last:
            deps.append(self.chan_last[chan])
        waits = {}
        for d in deps:
            self._need(eng, d, waits)
        ins.waits = []
        ck = self.clock[eng]
        for src, idx in waits.items():
            prod = self.lists[src][idx] if src in ENGS else self.chan_ins[src][idx]
            prod.needs_inc = True
            ins.waits.append(prod)
            for s, v in prod.clock.items():
                if ck.get(s, -1) < v:
                    ck[s] = v
        if chan is not None:
            lst = self.chan_ins.setdefault(chan, [])
            ins.idx = len(lst)
            lst.append(ins)
            self.chan_last[chan] = ins
            ins.clock = dict(ck)
            ins.clock[chan] = ins.idx
            self.lists[eng].append(ins)
        else:
            ins.idx = len(self.lists[eng])
            ins.clock = dict(ck)
            ins.clock[eng] = ins.idx
            self.lists[eng].append(ins)
        for k in reads:
            self.readers.setdefault(k, []).append(ins)
        me = chan if chan is not None else eng
        for k in writes:
            lst = [w for w in self.lastw.get(k, ()) if (w.chan if w.chan is not None else w.eng) != me]
            lst.append(ins)
            self.lastw[k] = lst
            self.readers[k] = []
        return ins

    def emit(self):
        nc = self.nc
        sems = {}
        for e in ("pe", "act", "dve", "pool"):
            sems[e] = nc.alloc_semaphore("s_" + e)
        for c in self.chan_ins:
            sems[c] = nc.alloc_semaphore("c_" + str(c))
        for e in ENGS:
            r = 0
            for ins in self.lists[e]:
                if ins.chan is None and ins.needs_inc:
                    r += 1
                    ins.val = r
        for c, lst in self.chan_ins.items():
            v = 0
            for ins in lst:
                v += 16 * ins.ninc
                ins.val = v

        def run(e):
            def body(eng):
                for ins in self.lists[e]:
                    for p in ins.waits:
                        src = p.chan if p.chan is not None else p.eng
                        eng.wait_ge(sems[src], p.val)
                    r = ins.fn(eng)
                    rl = r if isinstance(r, (list, tuple)) else [r]
                    if ins.chan is not None:
                        assert len(rl) == ins.ninc
                        for x in rl:
                            x.then_inc(sems[ins.chan], 16)
                    elif ins.needs_inc:
                        rl[-1].then_inc(sems[e], 1)
            return body

        with nc.Block() as block:
            block.tensor(run("pe"))
            block.scalar(run("act"))
            block.vector(run("dve"))
            block.gpsimd(run("pool"))
            block.sync(run("sp"))


class DB:
    def __init__(self, nc, name, shape, dtype, nbuf=2):
        self.bufs = [nc.alloc_sbuf_tensor(f"sb_{name}_{i}", list(shape), dtype) for i in range(nbuf)]
        self.keys = [f"{name}_{i}" for i in range(nbuf)]
        self.i = 0

    def next(self):
        i = self.i
        self.i = (i + 1) % len(self.bufs)
        return self.bufs[i], self.keys[i]


class PsumPool:
    def __init__(self, nc):
        self.banks = [nc.alloc_psum_tensor(f"ps{i}", [128, 512], F32) for i in range(8)]
        self.i = 0
        self.j = 0
        self.live = set()

    def half(self):
        while True:
            u = self.i
            self.i = (u + 1) % 16
            h, b = divmod(u, 8)
            if b not in self.live:
                break
        return self.banks[b][:, h * 256:(h + 1) * 256], [f"ps{b}"]

    def full(self):
        b = self.j
        self.j = (b + 1) % 8
        self.last_full = b
        return self.banks[b][:, :], [f"ps{b}"]


GAMMA = [1.0 - 2.0 ** (-5.0 - h) for h in range(4)]


def host_constants():
    c = {}
    c["ident"] = np.eye(128, dtype=np.float32)
    L = NMETA + SEQ
    pos = np.arange(L, dtype=np.float32)
    inv = (10000.0 ** (-np.arange(64, dtype=np.float32) / 64)).astype(np.float32)
    ang = (pos[:, None] * inv[None, :]).astype(np.float32)
    c["rope"] = np.concatenate([np.cos(ang), np.sin(ang)], axis=1).astype(np.float32)
    s = np.arange(128)[:, None]
    t = np.arange(128)[None, :]
    retD = np.zeros((128, 4, 128), np.float64)
    retG = np.zeros((128, 4), np.float64)
    retKD = np.zeros((128, 4), np.float64)
    retKDm = np.zeros((128, 4), np.float64)
    for h in range(4):
        g = GAMMA[h]
        same = (s // 64) == (t // 64)
        earlier = (s // 64) < (t // 64)
        w = np.where(same, g ** np.abs(t - s), np.where(earlier, g ** (t - s).clip(0), 0.0))
        retD[:, h, :] = w / (g ** (t + 1.0))
        retG[:, h] = (128 ** -0.5) * g ** (np.arange(128) + 1.0)
        retKD[:, h] = g ** (127.0 - np.arange(128))
        retKDm[:16, h] = g ** (15.0 - np.arange(16))
    c["retD"] = retD.astype(np.float32)
    c["retS"] = np.concatenate([retG, retKD, retKDm], axis=1).astype(np.float32)
    c["triu"] = (s <= t).astype(np.float32)
    rs = np.ones((128, T), np.float32)
    rs[:, ::64] = 0.0
    c["restart"] = rs
    c["iota"] = np.broadcast_to(np.arange(T, dtype=np.float32), (128, T)).copy()
    mb = np.zeros((128, 16, 8), np.float32)
    for m in range(16):
        for gl in range(2):
            mb[gl * 64:(gl + 1) * 64, m, (2 * m + gl) % 8] = 1.0
    c["maskB"] = mb
    mc = np.zeros((128, 4, 2, 2), np.float32)
    for mm in range(4):
        for gl in range(2):
            g8 = 2 * mm + gl
            mc[g8 * 16:(g8 + 1) * 16, mm, gl, 0] = 1.0
            mc[g8 * 16:(g8 + 1) * 16, mm, gl, 1] = -1.0
    c["maskC"] = mc
    return c


def host_layouts(inp):
    f = lambda a: np.ascontiguousarray(np.asarray(a, dtype=np.float32))
    o = {}
    o["meta"] = f(inp["meta"])
    o["w_in_ab"] = f(inp["w_in_ab"][0])
    o["w_out_ab"] = f(inp["w_out_ab"][0])
    o["w1"] = f(inp["mlp_w1"])
    o["w2"] = f(inp["mlp_w2"])
    o["w_in_cd"] = f(inp["w_in_cd"][0])
    o["w_out_cd"] = f(inp["w_out_cd"][0])
    o["glu_w"] = f(inp["s5_glu_w"][0])
    ng = np.asarray(inp["norm_g"], np.float32)
    pre = ng[:, [0, 2], :].reshape(2, 2, 8, 128)
    o["pgT"] = f(pre.transpose(3, 0, 1, 2).reshape(128, 32))
    o["postg"] = f(ng[:, [1, 3], :].reshape(4, 1024))
    gn = np.concatenate([np.asarray(inp["ret_gn"][0], np.float32).reshape(8, 128).T,
                         np.asarray(inp["hg_gn"][0], np.float32).reshape(4, 128).T], axis=1)
    o["gnT"] = f(gn)
    fm8 = lambda v: np.asarray(v, np.float32).reshape(8, 128).T
    cw = np.asarray(inp["rg_conv_w"][0], np.float32)
    lru = np.stack([fm8(inp["rg_lam"][0]), fm8(inp["rg_ba"][0]), fm8(inp["rg_bi"][0]), fm8(inp["rg_conv_b"][0]),
                    fm8(cw[0]), fm8(cw[1]), fm8(cw[2]), fm8(cw[3])], axis=2)
    o["lru_s"] = f(lru)
    o["rg_wa"] = f(inp["rg_wa"][0])
    o["rg_wi"] = f(inp["rg_wi"][0])
    hl = np.asarray(inp["hg_lb_logits"], np.float32).reshape(2, 4, 128)
    o["hgl"] = f(hl.transpose(2, 0, 1).reshape(128, 8))
    st = lambda a: np.asarray(a, np.float32).reshape(16, 2, 64).transpose(1, 2, 0).reshape(128, 16)
    ldt = np.repeat(np.asarray(inp["s5_log_dt"][0], np.float32)[:, None], 64, axis=1)
    o["s5s"] = f(np.stack([st(inp["s5_a_re_log"][0]), st(inp["s5_a_im"][0]), st(ldt)], axis=2))
    sb = lambda a: np.asarray(a, np.float32).reshape(16, 2, 64, 16).transpose(1, 2, 0, 3).reshape(128, 16, 16)
    o["s5b"] = f(np.stack([sb(inp["s5_b_re"][0]), sb(inp["s5_b_im"][0])], axis=2))
    sc = lambda a: np.asarray(a, np.float32).reshape(4, 8, 16, 64).transpose(1, 2, 0, 3).reshape(128, 4, 64)
    o["s5c"] = f(np.stack([sc(inp["s5_c_re"][0]), sc(inp["s5_c_im"][0])], axis=2))
    sd = np.asarray(inp["s5_d"][0], np.float32).reshape(4, 128).T
    gb = np.asarray(inp["s5_glu_b"][0], np.float32).reshape(4, 128).T
    o["s5dg"] = f(np.concatenate([sd, gb], axis=1))
    return o


def build_program():
    nc = bass.Bass("TRN2", target_bir_lowering=False)
    S = Sched(nc)
    consts = host_constants()

    def din(name, shape, dt=F32):
        return nc.dram_tensor(name, list(shape), dt, kind="ExternalInput").ap()

    x_d = din("x", [SEQ, D])
    meta_d = din("meta", [NMETA, D])
    wdefs = [("w_in_ab", 1024, 5120), ("w_out_ab", 2048, 1024), ("w1_0", 1024, 4096), ("w2_0", 4096, 1024),
             ("w_in_cd", 1024, 2560), ("w_out_cd", 1024, 1024), ("w1_1", 1024, 4096), ("w2_1", 4096, 1024),
             ("glu_w", 512, 512)]
    w1_d = din("w1", [2, 1024, 4096])
    w2_d = din("w2", [2, 4096, 1024])
    wsrc = {"w_in_ab": din("w_in_ab", [1024, 5120]), "w_out_ab": din("w_out_ab", [2048, 1024]),
            "w1_0": w1_d[0], "w1_1": w1_d[1], "w2_0": w2_d[0], "w2_1": w2_d[1],
            "w_in_cd": din("w_in_cd", [1024, 2560]), "w_out_cd": din("w_out_cd", [1024, 1024]),
            "glu_w": din("glu_w", [512, 512])}
    wscr = {n: (nc.dram_tensor("scr_" + n, [r, c], BF16, kind="Internal").ap() if n == "glu_w" else
                nc.dram_tensor("scr_" + n, [r // 1024, c // 512, 128, 8, 512], BF16, kind="Internal").ap())
            for n, r, c in wdefs}
    small = {}
    for n, shp in [("pgT", [128, 32]), ("postg", [4, 1024]), ("gnT", [128, 12]), ("lru_s", [128, 8, 8]),
                   ("rg_wa", [16, 64, 64]), ("rg_wi", [16, 64, 64]), ("hgl", [128, 8]), ("s5s", [128, 16, 3]),
                   ("s5b", [128, 16, 2, 16]), ("s5c", [128, 4, 2, 64]), ("s5dg", [128, 8])]:
        small[n] = din(n, shp)
    cd = {n: din("c_" + n, list(v.shape)) for n, v in consts.items()}
    out_d = nc.dram_tensor("out", [SEQ, D], F32, kind="ExternalOutput").ap()
    dbg_d = nc.dram_tensor("dbg", [4, NMETA + SEQ, D], F32, kind="ExternalOutput").ap() if DEBUG else None

    def sb(name, shape, dt=F32):
        return nc.alloc_sbuf_tensor("sb_" + name, list(shape), dt)

    def MM(out, lhsT, rhs, start, stop, r, w):
        S.op("pe", lambda e: e.matmul(out, lhsT=lhsT, rhs=rhs, start=start, stop=stop), reads=r, writes=w)

    def TR(out, in_, idn, r, w):
        S.op("pe", lambda e: e.transpose(out, in_, idn), reads=r, writes=w)

    def ACT(out, in_, func, r, w, **kw):
        S.op("act", lambda e: e.activation(out=out, in_=in_, func=func, **kw), reads=r, writes=w)

    def TT(out, a, b, op, r, w, eng="dve"):
        S.op(eng, lambda e: e.tensor_tensor(out=out, in0=a, in1=b, op=op), reads=r, writes=w)

    def TS(out, a, s1, s2, op0, op1, r, w, eng="dve"):
        if s2 is None:
            S.op(eng, lambda e: e.tensor_scalar(out=out, in0=a, scalar1=s1, scalar2=None, op0=op0), reads=r, writes=w)
        else:
            S.op(eng, lambda e: e.tensor_scalar(out=out, in0=a, scalar1=s1, scalar2=s2, op0=op0, op1=op1),
                 reads=r, writes=w)

    def STT(out, a, s, b, op0, op1, r, w, eng="dve"):
        S.op(eng, lambda e: e.scalar_tensor_tensor(out=out, in0=a, scalar=s, in1=b, op0=op0, op1=op1),
             reads=r, writes=w)

    def CP(out, a, r, w, eng="dve"):
        if eng == "act":
            S.op("act", lambda e: e.copy(out=out, in_=a), reads=r, writes=w)
        else:
            S.op(eng, lambda e: e.tensor_copy(out=out, in_=a), reads=r, writes=w)

    def DMA(q, out, in_, chan, r, w):
        return S.op(q, lambda e: e.dma_start(out=out, in_=in_), reads=r, writes=w, chan=chan)

    def MEMSET(ap, val, w, eng="dve"):
        S.op(eng, lambda e: e.memset(ap, val), writes=w)

    PS = PsumPool(nc)
    _rr = [0]

    def ew():
        _rr[0] += 1
        return "pool" if _rr[0] % 3 == 0 else "dve"

    ident_f = sb("ident_f", [128, 128])
    ident = sb("ident", [128, 128], BF16)
    H = [sb(f"H{i}", [128, NT, D]) for i in range(2)]
    hnT = sb("hnT", [128, 8, T], BF16)
    big = sb("big", [128, 32, T], BF16)
    NSLOT = 3
    Wr = [sb(f"W{i}", [128, 8 * 512], BF16) for i in range(NSLOT)]
    wslot = [0]
    pgT = sb("pgT", [128, 32])
    gnT = sb("gnT", [128, 12])
    postg = [sb(f"postg{i}", [128, D]) for i in range(2)]
    epsc = sb("epsc", [128, 2])
    ropeT = DB(nc, "rope", [128, NT, 128], F32, 2)
    retD = sb("retD", [128, 4, 128])
    retS = sb("retS", [128, 12])
    triu = sb("triu", [128, 128])
    restart = sb("restart", [128, T])
    RS = sb("RS", [128, 4, 256])
    RSb = sb("RSb", [128, 4, 256], BF16)
    lru_s = sb("lru_s", [128, 8, 8])
    lru_sp8 = sb("lru_sp8", [128, 8])
    Wab = sb("Wab", [128, 2, 8, 128], BF16)
    lru_carry = sb("lru_carry", [128, 8, 3])
    lru_h = sb("lru_h", [128, 8])
    hg_lb = sb("hg_lb", [128, 8])
    HS = sb("HS", [128, 4, 128])
    HSb = sb("HSb", [128, 4, 128], BF16)
    s5_mag = sb("s5_mag", [128, 16])
    s5_bnd = sb("s5_bnd", [128, 2, 2, 16])
    s5_st = sb("s5_st", [128, 2, 16])
    s5_ini = sb("s5_ini", [128, 2, 16])
    Bblk = sb("Bblk", [128, 2, 16, 128], BF16)
    Cblk = sb("Cblk", [128, 2, 16, 128], BF16)
    Rcs = sb("Rcs", [128, 2, 16, T], BF16)
    s5dg = sb("s5dg", [128, 8])
    gluW = sb("gluW", [128, 4, 512], BF16)

    f32a = DB(nc, "f32a", [128, T + 4], F32, 16)
    bf16a = DB(nc, "bf16a", [128, T], BF16, 4)
    hsb = DB(nc, "hsb", [128, T], F32, 5)
    s5h = DB(nc, "s5h", [128, T], BF16, 8)
    hn_tok = DB(nc, "hn_tok", [128, D], BF16, 1)
    junk_t = sb("junk", [128, D], BF16)
    colsA = DB(nc, "colsA", [128, 16], F32, 8)
    tmpTok = DB(nc, "tmpTok", [128, 512], F32, 4)
    qkr = sb("qkr", [128, NT, 2, 4, 128], BF16)
    v_tok = sb("v_tok", [128, NT, 1024], BF16)
    sg_tok = sb("sg_tok", [128, NT, 1024], BF16)
    qT = DB(nc, "qT", [128, 4, 128], BF16, 2)
    kT = DB(nc, "kT", [128, 4, 128], BF16, 2)
    sTb = DB(nc, "sTb", [128, 4, 128], BF16, 2)
    kdec = DB(nc, "kdec", [128, 4, 128], BF16, 2)
    o_sb = DB(nc, "o_sb", [128, 4, 256], F32, 2)
    ya = DB(nc, "ya", [128, D], BF16, 2)
    assert T == 256
    uf = o_sb.bufs[0]
    UFK = o_sb.keys[0]
    ub = sb("ub", [128, 4, T], BF16)
    ygf = o_sb.bufs[1]
    YGK = o_sb.keys[1]
    ygb = sb("ygb", [128, 4, T], BF16)
    hqT = sb("hqT", [128, 4, T], BF16)
    hkT = sb("hkT", [128, 4, T], BF16)
    hgE = sb("hgE", [128, 3, 4, T // 64])

    BIGK = lambda f: f"big.{f}"
    HNK = "hnT"

    def load(dst_ap, src_ap, key, q="act"):
        DMA(q, dst_ap, src_ap, "ld_small", [], [key])

    load(ident_f[:], cd["ident"], "ident_f")
    CP(ident[:], ident_f[:], ["ident_f"], ["ident"])
    load(pgT[:], small["pgT"], "pgT")
    load(gnT[:], small["gnT"], "gnT")
    load(retD[:], cd["retD"], "retD")
    load(retS[:], cd["retS"], "retS")
    load(triu[:], cd["triu"], "triu")
    load(restart[:], cd["restart"], "restart")
    load(lru_s[:], small["lru_s"], "lru_s")
    load(s5dg[:], small["s5dg"], "s5dg")
    MEMSET(epsc[:, 0:1], EPS, ["epsc"])
    MEMSET(epsc[:, 1:2], 1.0, ["epsc"])
    MEMSET(RS[:], 0.0, ["RS"])
    MEMSET(RSb[:], 0.0, ["RSb"])
    MEMSET(HS[:], 0.0, ["HS"])
    MEMSET(HSb[:], 0.0, ["HSb"])
    MEMSET(lru_carry[:], 0.0, ["lru_carry"])
    MEMSET(lru_h[:], 0.0, ["lru_h"])
    MEMSET(s5_st[:], 0.0, ["s5_st"])

    ACT(lru_sp8[:], lru_s[:, :, 0], AF.Exp, ["lru_s"], ["lru_sp8"], scale=-1.0)
    ACT(lru_sp8[:], lru_sp8[:], AF.Ln, ["lru_sp8", "epsc"], ["lru_sp8"], bias=epsc[:, 1:2])
    TS(lru_sp8[:], lru_sp8[:], -8.0, None, ALU.mult, None, ["lru_sp8"], ["lru_sp8"])
    for wi, nm in enumerate(("rg_wa", "rg_wi")):
        wabf = o_sb.bufs[wi][:].rearrange("p a (c d) -> p (a c) d", d=128)
        wk_ = o_sb.keys[wi]
        MEMSET(wabf, 0.0, [wk_])
        src = small[nm].rearrange("(ct bl) i j -> bl i ct j", bl=2)
        for bl in range(2):
            DMA("act", wabf[bl * 64:(bl + 1) * 64, :, bl * 64:(bl + 1) * 64], src[bl], "ld_small", [], [wk_])
        CP(Wab[:, wi, :, :], wabf, [wk_], ["Wab"])

    hgl = sb("hgl", [128, 8])
    load(hgl[:], small["hgl"], "hgl")
    TT(hg_lb[:, 0:4], hgl[:, 0:4], hgl[:, 4:8], ALU.subtract, ["hgl"], ["hg_lb"])
    ACT(hg_lb[:, 0:4], hg_lb[:, 0:4], AF.Sigmoid, ["hg_lb"], ["hg_lb"])
    TS(hg_lb[:, 4:8], hg_lb[:, 0:4], -1.0, 1.0, ALU.mult, ALU.add, ["hg_lb"], ["hg_lb"])

    s5s = sb("s5s", [128, 16, 3])
    load(s5s[:], small["s5s"], "s5s")
    pp = sb("s5pp", [128, 12, 16])
    PPK = ["s5pp"]
    iota = f32a.bufs[0][:, 0:T]
    IOK = f32a.keys[0]
    DMA("act", iota, cd["iota"], "ld_small", [], [IOK])
    DT_, ARE, AIM, TH, MAG, COS, SIN, NRE, DEN, ZRE, ZIM, TMP = range(12)
    ACT(pp[:, DT_, :], s5s[:, :, 2], AF.Exp, ["s5s"], PPK)
    ACT(pp[:, ARE, :], s5s[:, :, 0], AF.Exp, ["s5s"], PPK)
    TS(pp[:, ARE, :], pp[:, ARE, :], -1.0, None, ALU.mult, None, PPK, PPK)
    CP(pp[:, AIM, :], s5s[:, :, 1], ["s5s"], PPK)
    TT(pp[:, TH, :], pp[:, DT_, :], pp[:, AIM, :], ALU.mult, PPK, PPK)
    TT(pp[:, TMP, :], pp[:, DT_, :], pp[:, ARE, :], ALU.mult, PPK, PPK)
    ACT(pp[:, MAG, :], pp[:, TMP, :], AF.Exp, PPK, PPK)
    CP(s5_mag[:], pp[:, MAG, :], PPK, ["s5_mag"])

    TWO_PI = 2.0 * np.pi
    redi = f32a.bufs[1][:].bitcast(I32)[:, 0:T]
    redf = f32a.bufs[2][:, 0:T]
    RIK, RFK = f32a.keys[1], f32a.keys[2]

    def sincos(out_sin, out_cos, ang_ap, shape_n, keys_r, keys_w):
        ri = redi[:, 0:shape_n]
        rf = redf[:, 0:shape_n]
        for out_ap, shift in ((out_sin, 0.0), (out_cos, 0.5 * np.pi)):
            if out_ap is None:
                continue
            TS(ri, ang_ap, shift, 1.0 / TWO_PI, ALU.add, ALU.mult, keys_r, [RIK])
            STT(rf, ri, -TWO_PI, ang_ap, ALU.mult, ALU.add, [RIK] + keys_r, [RFK])
            if shift != 0.0:
                TS(rf, rf, shift, None, ALU.add, None, [RFK], [RFK])
            TS(rf, rf, 3.1415925, -3.1415925, ALU.min, ALU.max, [RFK], [RFK])
            ACT(out_ap, rf, AF.Sin, [RFK], keys_w)

    sincos(pp[:, SIN, :], pp[:, COS, :], pp[:, TH, :], 16, PPK, PPK)
    ang2 = sb("ang2", [128, 2, 16])
    TS(ang2[:, 0, :], pp[:, TH, :], float(T), None, ALU.mult, None, PPK, ["ang2"])
    TS(ang2[:, 1, :], pp[:, TH, :], float(NMETA), None, ALU.mult, None, PPK, ["ang2"])
    bndt = sb("bndt", [128, 2, 2, 16])
    for fr in range(2):
        sincos(bndt[:, fr, 1, :], bndt[:, fr, 0, :], ang2[:, fr, :], 16, ["ang2"], ["bndt"])
    CP(s5_bnd[:], bndt[:], ["bndt"], ["s5_bnd"])
    angT = f32a.bufs[3][:, 0:T]
    ATK = f32a.keys[3]
    for m in range(16):
        TS(angT, iota, pp[:, TH, m:m + 1], None, ALU.mult, None, PPK + [IOK], [ATK])
        sincos(Rcs[:, 1, m, :], Rcs[:, 0, m, :], angT, T, [ATK], ["Rcs"])
    TT(pp[:, COS, :], pp[:, COS, :], pp[:, MAG, :], ALU.mult, PPK, PPK)
    TT(pp[:, SIN, :], pp[:, SIN, :], pp[:, MAG, :], ALU.mult, PPK, PPK)
    TS(pp[:, NRE, :], pp[:, COS, :], -1.0, None, ALU.add, None, PPK, PPK)
    TT(pp[:, DEN, :], pp[:, ARE, :], pp[:, ARE, :], ALU.mult, PPK, PPK)
    TT(pp[:, TMP, :], pp[:, AIM, :], pp[:, AIM, :], ALU.mult, PPK, PPK)
    TT(pp[:, DEN, :], pp[:, DEN, :], pp[:, TMP, :], ALU.add, PPK, PPK)
    S.op("dve", lambda e: e.reciprocal(out=pp[:, DEN, :], in_=pp[:, DEN, :]), reads=PPK, writes=PPK)
    TT(pp[:, ZRE, :], pp[:, NRE, :], pp[:, ARE, :], ALU.mult, PPK, PPK)
    TT(pp[:, TMP, :], pp[:, SIN, :], pp[:, AIM, :], ALU.mult, PPK, PPK)
    TT(pp[:, ZRE, :], pp[:, ZRE, :], pp[:, TMP, :], ALU.add, PPK, PPK)
    TT(pp[:, ZRE, :], pp[:, ZRE, :], pp[:, DEN, :], ALU.mult, PPK, PPK)
    TT(pp[:, ZIM, :], pp[:, SIN, :], pp[:, ARE, :], ALU.mult, PPK, PPK)
    TT(pp[:, TMP, :], pp[:, NRE, :], pp[:, AIM, :], ALU.mult, PPK, PPK)
    TT(pp[:, ZIM, :], pp[:, ZIM, :], pp[:, TMP, :], ALU.subtract, PPK, PPK)
    TT(pp[:, ZIM, :], pp[:, ZIM, :], pp[:, DEN, :], ALU.mult, PPK, PPK)
    s5b = tmpTok.bufs[0][:].rearrange("p (m r c) -> p m r c", m=16, r=2)
    DMA("act", s5b, small["s5b"], "ld_small", [], [tmpTok.keys[0]])
    bbn = tmpTok.bufs[1][:].rearrange("p (r m c) -> p r m c", r=2, m=16)
    tb = tmpTok.bufs[2][:].rearrange("p (r m c) -> p r m c", r=2, m=16)
    zre_b = pp[:, ZRE, :].unsqueeze(2).to_broadcast([128, 16, 16])
    zim_b = pp[:, ZIM, :].unsqueeze(2).to_broadcast([128, 16, 16])
    TT(tb[:, 0], s5b[:, :, 0, :], zre_b, ALU.mult, PPK + [tmpTok.keys[0]], [tmpTok.keys[2]])
    TT(tb[:, 1], s5b[:, :, 1, :], zim_b, ALU.mult, PPK + [tmpTok.keys[0]], [tmpTok.keys[2]])
    TT(bbn[:, 0], tb[:, 0], tb[:, 1], ALU.subtract, [tmpTok.keys[2]], [tmpTok.keys[1]])
    TT(tb[:, 0], s5b[:, :, 1, :], zre_b, ALU.mult, PPK + [tmpTok.keys[0]], [tmpTok.keys[2]])
    TT(tb[:, 1], s5b[:, :, 0, :], zim_b, ALU.mult, PPK + [tmpTok.keys[0]], [tmpTok.keys[2]])
    TT(bbn[:, 1], tb[:, 0], tb[:, 1], ALU.add, [tmpTok.keys[2]], [tmpTok.keys[1]])
    maskB = sb("maskB", [128, 16, 8])
    load(maskB[:], cd["maskB"], "maskB")
    maskC = sb("maskC", [128, 4, 2, 2])
    load(maskC[:], cd["maskC"], "maskC")
    s5c = tmpTok.bufs[3][:].rearrange("p (a r q) -> p a r q", a=4, r=2)
    DMA("act", s5c, small["s5c"], "ld_small", [], [tmpTok.keys[3]])
    wide = DB(nc, "wide", [128, 128], BF16, 2)
    for ri in range(2):
        for m in range(16):
            wd, wk = wide.next()
            TT(wd[:].rearrange("p (g c) -> p g c", g=8), bbn[:, ri, m, :].unsqueeze(1).to_broadcast([128, 8, 16]),
               maskB[:, m, :].unsqueeze(2).to_broadcast([128, 8, 16]), ALU.mult, [tmpTok.keys[1], "maskB"], [wk])
            ps, pk = PS.half()
            psb = ps.bitcast(BF16)
            TR(psb[:, 0:128], wd[:], ident[:], [wk, "ident"], pk)
            CP(Bblk[:, ri, m, :], psb[:, 0:128], pk, ["Bblk"], eng="act")
    for ri in range(2):
        for m in range(16):
            wd, wk = wide.next()
            TT(wd[:].rearrange("p (g q) -> p g q", g=2), s5c[:, m // 4, ri, :].unsqueeze(1).to_broadcast([128, 2, 64]),
               maskC[:, m % 4, :, ri].unsqueeze(2).to_broadcast([128, 2, 64]), ALU.mult, [tmpTok.keys[3], "maskC"], [wk])
            ps, pk = PS.half()
            psb = ps.bitcast(BF16)
            TR(psb[:, 0:128], wd[:], ident[:], [wk, "ident"], pk)
            CP(Cblk[:, ri, m, :], psb[:, 0:128], pk, ["Cblk"], eng="act")

    cast_engs = ["dve", "act", "pool"]
    ci = 0
    for n, R, C in (wdefs if KPHASE != "pro" else []):
        for rt in range(R // 128):
            for c0 in range(0, C, 2048):
                w = min(2048, C - c0)
                s = wslot[0]
                wslot[0] = (s + 1) % NSLOT
                stg = Wr[s][:].bitcast(F32)
                DMA("sp", stg[:, 0:w], wsrc[n][rt * 128:(rt + 1) * 128, c0:c0 + w], f"w{s}", [], [f"W{s}"])
                bi = ci % 4
                cb = big[:, bi * 8:(bi + 1) * 8, :].rearrange("p a b -> p (a b)")
                ck = [BIGK(bi * 8 + t_) for t_ in range(8)]
                CP(cb[:, 0:w], stg[:, 0:w], [f"W{s}"], ck, eng=cast_engs[ci % 3])
                ci += 1
                if n == "glu_w":
                    dst = wscr[n][rt * 128:(rt + 1) * 128, c0:c0 + w]
                    srcv = cb[:, 0:w]
                else:
                    dst = wscr[n][rt // 8, c0 // 512:(c0 + w) // 512, :, rt % 8, :].rearrange("b p c -> p b c")
                    srcv = cb[:, 0:w].rearrange("p (b c) -> p b c", c=512)
                DMA("pool", dst, srcv, f"st_scr{bi}", ck, [f"scr_{n}.{rt}.{c0 // 2048}"])
    if KPHASE != "pro":
      DMA("sp", gluW[:], wscr["glu_w"].rearrange("(kt p) c -> p kt c", p=128), "ld_glu",
        [f"scr_glu_w.{rt}.0" for rt in range(4)], ["gluW"])

    def load_w(name, k0, c0, ncols=512, nk=8):
        s = wslot[0]
        wslot[0] = (s + 1) % NSLOT
        view = Wr[s][:].rearrange("p (k c) -> p k c", k=8)
        assert nk == 8 and ncols == 512 and k0 % 1024 == 0 and c0 % 512 == 0
        DMA("sp", view[:, 0:nk, 0:ncols], wscr[name][k0 // 1024, c0 // 512],
            f"w{s}", [f"scr_{name}.{k0 // 128 + i_}.{c0 // 2048}" for i_ in range(nk)], [f"W{s}"])
        return view, f"W{s}"

    def rstd_from_ss(ss_ap, rows, dim, ck):
        k = ss_ap.shape[1]
        ACT(ss_ap, ss_ap, AF.Sqrt, [ck, "epsc"], [ck], scale=1.0 / dim, bias=epsc[0:rows, 0:1])
        S.op("dve", lambda e: e.reciprocal(out=ss_ap, in_=ss_ap), reads=[ck], writes=[ck])

    def to_feature(src_tok, rows, ntile, dst, dst_f0, tok0, gain_ap, rkeys, wkeys_fn):
        for q0 in range(0, ntile, 4):
            nq = min(4, ntile - q0)
            ps, pk = PS.half()
            psb = ps.bitcast(BF16)
            for i in range(nq):
                TR(psb[:, i * 128:i * 128 + rows], src_tok[0:rows, (q0 + i) * 128:(q0 + i + 1) * 128],
                   ident[0:rows, 0:rows], rkeys + ["ident"], pk)
            src = psb.rearrange("p (k t) -> p k t", k=4)[:, 0:nq, 0:rows]
            dsl = dst[:, dst_f0 + q0:dst_f0 + q0 + nq, tok0:tok0 + rows]
            wk = [wkeys_fn(dst_f0 + q0 + i) for i in range(nq)]
            if gain_ap is None:
                CP(dsl, src, pk, wk, eng="act")
            else:
                g = gain_ap[:, q0:q0 + nq].unsqueeze(2).to_broadcast([128, nq, rows])
                TT(dsl, src, g, ALU.mult, pk + ["pgT", "gnT"], wk)

    def prenorm(Hc, HK, tiles, gcol):
        for (j, rows) in tiles:
            jb, jk = junk_t, None
            cl, ck = colsA.next()
            ACT(jb[0:rows, :], Hc[0:rows, j, :], AF.Square, [HK], [ck], accum_out=cl[0:rows, 0:1])
            rstd_from_ss(cl[0:rows, 0:1], rows, D, ck)
            hb, hk = hn_tok.next()
            TS(hb[0:rows, :], Hc[0:rows, j, :], cl[0:rows, 0:1], None, ALU.mult, None, [HK, ck], [hk])
            to_feature(hb, rows, 8, hnT, 0, j * 128, pgT[:, gcol:gcol + 8], [hk], lambda f: HNK)

    def postnorm_add(Hc, HK, j, rows, psA, pkA, psB, pkB, gtab, gk):
        cl, ck = colsA.next()
        jb, jk = junk_t, None
        ACT(jb[0:rows, 0:512], psA[0:rows, :], AF.Square, pkA, [ck], accum_out=cl[0:rows, 0:1])
        ACT(jb[0:rows, 512:1024], psB[0:rows, :], AF.Square, pkB, [ck], accum_out=cl[0:rows, 1:2])
        TT(cl[0:rows, 0:1], cl[0:rows, 0:1], cl[0:rows, 1:2], ALU.add, [ck], [ck])
        rstd_from_ss(cl[0:rows, 0:1], rows, D, ck)
        for (ps, pk, c0) in ((psA, pkA, 0), (psB, pkB, 512)):
            tb_, tk = tmpTok.next()
            STT(tb_[0:rows, :], ps[0:rows, :], cl[0:rows, 0:1], gtab[0:rows, c0:c0 + 512], ALU.mult, ALU.mult,
                pk + [ck, gk], [tk])
            TT(Hc[0:rows, j, c0:c0 + 512], Hc[0:rows, j, c0:c0 + 512], tb_[0:rows, :], ALU.add, [HK, tk], [HK],
               eng="pool")

    def proj_token_major(wname, F, srcbuf, src_key_fn, tiles, consume, hook=None):
        banks = {}
        live = set()
        for (j, rows) in tiles:
            a_ = PS.full()
            live.add(PS.last_full)
            b_ = PS.full()
            live.add(PS.last_full)
            banks[j] = (a_, b_)
        nunit = F // 8
        for cc in range(2):
            for u in range(nunit):
                wv, wk = load_w(wname, u * 1024, cc * 512)
                for (j, rows) in tiles:
                    ps, pk = banks[j][cc]
                    for fi in range(8):
                        f = u * 8 + fi
                        MM(ps[0:rows, :], srcbuf[:, f, j * 128:j * 128 + rows], wv[:, fi, :], f == 0, f == F - 1,
                           [src_key_fn(f), wk], pk)
        if hook is not None:
            PS.live = live
            hook()
            PS.live = set()
        for (j, rows) in tiles:
            (psA, pkA), (psB, pkB) = banks[j]
            consume(j, rows, psA, pkA, psB, pkB)

    def load_postg(l):
        for i in range(2):
            DMA("pool", postg[i][:], small["postg"][l * 2 + i:l * 2 + i + 1, :].partition_broadcast(128), "ld_postg",
                [], [f"postg{i}"])

    def mlp(l, Hc, HK, tiles, n, hook=None):
        prenorm(Hc, HK, tiles, (l * 2 + 1) * 8)
        wn1, wn2 = f"w1_{l}", f"w2_{l}"
        for blk in range(8):
            wv, wk = load_w(wn1, 0, blk * 512)
            for ft in range(4):
                f = blk * 4 + ft
                ps, pk = PS.half()
                for kt in range(8):
                    MM(ps[:, 0:n], wv[:, kt, ft * 128:(ft + 1) * 128], hnT[:, kt, 0:n], kt == 0, kt == 7, [HNK, wk], pk)
                tf, tk = f32a.next()
                ACT(tf[:, 0:n], ps[:, 0:n], AF.Relu, pk, [tk])
                TT(big[:, f, 0:n], tf[:, 0:n], tf[:, 0:n], ALU.mult, [tk], [BIGK(f)], eng=ew())
        proj_token_major(wn2, 32, big, BIGK, tiles,
                         lambda j, rows, a, ak, b, bk: postnorm_add(Hc, HK, j, rows, a, ak, b, bk, postg[1], "postg1"),
                         hook=hook)

    def layer0_mixer(Hc, HK, tiles, n, meta_group, rope_ap, rope_key, do_prenorm=True):
        if do_prenorm:
            prenorm(Hc, HK, tiles, 0)
        for qk in range(2):
            wv, wk = load_w("w_in_ab", 0, qk * 512)
            for (j, rows) in tiles:
                ps, pk = PS.full()
                for kt in range(8):
                    MM(ps[0:rows, :], hnT[:, kt, j * 128:j * 128 + rows], wv[:, kt, :], kt == 0, kt == 7, [HNK, wk], pk)
                x3 = ps.rearrange("p (h d) -> p h d", h=4)
                x1 = x3[0:rows, :, 0:64]
                x2 = x3[0:rows, :, 64:128]
                cosb = rope_ap[0:rows, j, 0:64].unsqueeze(1).to_broadcast([rows, 4, 64])
                sinb = rope_ap[0:rows, j, 64:128].unsqueeze(1).to_broadcast([rows, 4, 64])
                t1, k1 = tmpTok.next()
                t2, k2 = tmpTok.next()
                a1 = t1[0:rows, 0:256].rearrange("p (h d) -> p h d", h=4)
                a2 = t2[0:rows, 0:256].rearrange("p (h d) -> p h d", h=4)
                b1 = t1[0:rows, 256:512].rearrange("p (h d) -> p h d", h=4)
                b2 = t2[0:rows, 256:512].rearrange("p (h d) -> p h d", h=4)
                TT(a1, x1, cosb, ALU.mult, pk + [rope_key], [k1])
                TT(a2, x2, sinb, ALU.mult, pk + [rope_key], [k2])
                TT(b1, x1, sinb, ALU.mult, pk + [rope_key], [k1])
                TT(b2, x2, cosb, ALU.mult, pk + [rope_key], [k2])
                TT(qkr[0:rows, j, qk, :, 0:64], a1, a2, ALU.subtract, [k1, k2], [f"qkr{j}"], eng="pool")
                TT(qkr[0:rows, j, qk, :, 64:128], b1, b2, ALU.add, [k1, k2], [f"qkr{j}"], eng="pool")
        for vb in range(2):
            wv, wk = load_w("w_in_ab", 0, 1024 + vb * 512)
            for (j, rows) in tiles:
                ps, pk = PS.full()
                for kt in range(8):
                    MM(ps[0:rows, :], hnT[:, kt, j * 128:j * 128 + rows], wv[:, kt, :], kt == 0, kt == 7, [HNK, wk], pk)
                CP(v_tok[0:rows, j, vb * 512:(vb + 1) * 512], ps[0:rows, :], pk, [f"v{j}"], eng="act")
        for gb in range(2):
            wv, wk = load_w("w_in_ab", 0, 2048 + gb * 512)
            for (j, rows) in tiles:
                ps, pk = PS.full()
                for kt in range(8):
                    MM(ps[0:rows, :], hnT[:, kt, j * 128:j * 128 + rows], wv[:, kt, :], kt == 0, kt == 7, [HNK, wk], pk)
                ACT(sg_tok[0:rows, j, gb * 512:(gb + 1) * 512], ps[0:rows, :], AF.Silu, pk, [f"sg{j}"])
        for (j, rows) in tiles:
            qt, qtk = qT.next()
            kt_, ktk = kT.next()
            for (dst, dk_, qk) in ((qt, qtk, 0), (kt_, ktk, 1)):
                ps, pk = PS.half()
                psb = ps.bitcast(BF16)
                for h in range(4):
                    TR(psb[:, h * 128:h * 128 + rows], qkr[0:rows, j, qk, h, :], ident[0:rows, 0:rows],
                       [f"qkr{j}", "ident"], pk)
                CP(dst[:, :, 0:rows], psb.rearrange("p (h t) -> p h t", h=4)[:, :, 0:rows], pk, [dk_], eng="act")
            ps, pk = PS.full()
            p3 = ps.rearrange("p (h t) -> p h t", h=4)
            for h in range(4):
                MM(p3[0:rows, h, 0:rows], kt_[:, h, 0:rows], qt[:, h, 0:rows], True, True, [qtk, ktk], pk)
            st_, stk = sTb.next()
            TT(st_[0:rows, :, 0:rows], p3[0:rows, :, 0:rows], retD[0:rows, :, 0:rows], ALU.mult, pk + ["retD"], [stk])
            ob, obk = o_sb.next()
            for hp in range(2):
                ps, pk = PS.full()
                for hh in range(2):
                    h = hp * 2 + hh
                    osl = ps[0:rows, hh * 256:(hh + 1) * 256]
                    MM(osl, st_[0:rows, h, 0:rows], v_tok[0:rows, j, h * 256:(h + 1) * 256], True, False,
                       [stk, f"v{j}"], pk)
                    MM(osl, qt[:, h, 0:rows], RSb[:, h, :], False, True, [qtk, "RSb"], pk)
                TT(ob[0:rows, hp * 2:hp * 2 + 2, :], ps.rearrange("p (h e) -> p h e", h=2)[0:rows],
                   retS[0:rows, hp * 2:hp * 2 + 2].unsqueeze(2).to_broadcast([rows, 2, 256]), ALU.mult,
                   pk + ["retS"], [obk])
            kd, kdk = kdec.next()
            kdcol = 8 if meta_group else 4
            TT(kd[0:rows, :, :], qkr[0:rows, j, 1, :, :],
               retS[0:rows, kdcol:kdcol + 4].unsqueeze(2).to_broadcast([rows, 4, 128]), ALU.mult,
               [f"qkr{j}", "retS"], [kdk], eng="pool")
            for hp in range(2):
                ps, pk = PS.full()
                for hh in range(2):
                    h = hp * 2 + hh
                    MM(ps[:, hh * 256:(hh + 1) * 256], kd[0:rows, h, :], v_tok[0:rows, j, h * 256:(h + 1) * 256],
                       True, True, [kdk, f"v{j}"], pk)
                for hh in range(2):
                    h = hp * 2 + hh
                    STT(RS[:, h, :], RS[:, h, :], float(GAMMA[h] ** rows), ps[:, hh * 256:(hh + 1) * 256],
                        ALU.mult, ALU.add, ["RS"] + pk, ["RS"])
            CP(RSb[:], RS[:], ["RS"], ["RSb"], eng="act")
            cl, ck = colsA.next()
            S.op("dve", lambda e, ob=ob, cl=cl, rows=rows: e.reduce_sum(out=cl[0:rows, 0:4], in_=ob[0:rows], axis=AX.X),
                 reads=[obk], writes=[ck])
            TS(cl[0:rows, 0:4], cl[0:rows, 0:4], -1.0 / 256, None, ALU.mult, None, [ck], [ck])
            TT(ob[0:rows], ob[0:rows], cl[0:rows, 0:4].unsqueeze(2).to_broadcast([rows, 4, 256]), ALU.add,
               [obk, ck], [obk])
            jb, jk = junk_t, None
            for h in range(4):
                ACT(jb[0:rows, h * 256:(h + 1) * 256], ob[0:rows, h, :], AF.Square, [obk], [ck],
                    accum_out=cl[0:rows, 4 + h:5 + h])
            rstd_from_ss(cl[0:rows, 4:8], rows, 256, ck)
            yb_, ybk = ya.next()
            for h in range(4):
                STT(yb_[0:rows, h * 256:(h + 1) * 256], ob[0:rows, h, :], cl[0:rows, 4 + h:5 + h],
                    sg_tok[0:rows, j, h * 256:(h + 1) * 256], ALU.mult, ALU.mult, [obk, ck, f"sg{j}"], [ybk])
            to_feature(yb_, rows, 8, big, 0, j * 128, gnT[:, 0:8], [ybk], BIGK)
        for half in range(2):
            hs_list = []
            wv, wk = load_w("w_in_ab", 0, 3072 + half * 512)
            L = {}
            for ct in range(4):
                c = half * 4 + ct
                ps, pk = PS.half()
                for kt in range(8):
                    MM(ps[:, 0:n], wv[:, kt, ct * 128:(ct + 1) * 128], hnT[:, kt, 0:n], kt == 0, kt == 7, [HNK, wk], pk)
                xp, xk = f32a.next()
                CP(xp[:, 0:3], lru_carry[:, c, :], ["lru_carry"], [xk], eng="pool")
                CP(xp[:, 3:3 + n], ps[:, 0:n], pk, [xk], eng="act")
                L[ct] = dict(xp=xp, xk=xk)
            for ct in range(4):
                c = half * 4 + ct
                xp, xk = L[ct]["xp"], L[ct]["xk"]
                acc, ak = f32a.next()
                P = lambda i, c=c: lru_s[:, c, i:i + 1]
                TS(acc[:, 0:n], xp[:, 0:n], P(4), P(3), ALU.mult, ALU.add, [xk, "lru_s"], [ak])
                STT(acc[:, 0:n], xp[:, 1:1 + n], P(5), acc[:, 0:n], ALU.mult, ALU.add, [xk, ak, "lru_s"], [ak])
                STT(acc[:, 0:n], xp[:, 2:2 + n], P(6), acc[:, 0:n], ALU.mult, ALU.add, [xk, ak, "lru_s"], [ak])
                STT(acc[:, 0:n], xp[:, 3:3 + n], P(7), acc[:, 0:n], ALU.mult, ALU.add, [xk, ak, "lru_s"], [ak])
                CP(lru_carry[:, c, :], xp[:, n:n + 3], [xk], ["lru_carry"], eng="pool")
                L[ct].update(acc=acc, ak=ak)
            for ct in range(4):
                xb_, xbk = bf16a.next()
                CP(xb_[:, 0:n], L[ct]["acc"][:, 0:n], [L[ct]["ak"]], [xbk], eng="act")
                L[ct].update(xb=xb_, xbk=xbk)
            for ct in range(4):
                c = half * 4 + ct
                gates = []
                for wi in range(2):
                    ps2, pk2 = PS.half()
                    MM(ps2[:, 0:n], Wab[:, wi, c, :], L[ct]["xb"][:, 0:n], True, True, ["Wab", L[ct]["xbk"]], pk2)
                    gt_, gk_ = f32a.next()
                    ACT(gt_[:, 0:n], ps2[:, 0:n], AF.Sigmoid, pk2 + ["lru_s"], [gk_], bias=lru_s[:, c, 1 + wi:2 + wi])
                    gates.append((gt_, gk_))
                L[ct].update(gates=gates)
            for ct in range(4):
                c = half * 4 + ct
                (rg, rk), (ig, ik) = L[ct]["gates"]
                ACT(rg[:, 0:n], rg[:, 0:n], AF.Exp, [rk, "lru_sp8"], [rk], scale=lru_sp8[:, c:c + 1])
            for ct in range(4):
                (rg, rk), (ig, ik) = L[ct]["gates"]
                acc, ak = L[ct]["acc"], L[ct]["ak"]
                TT(ig[:, 0:n], ig[:, 0:n], acc[:, 0:n], ALU.mult, [ik, ak], [ik])
                TT(acc[:, 0:n], rg[:, 0:n], rg[:, 0:n], ALU.mult, [rk], [ak], eng="pool")
            for ct in range(4):
                acc, ak = L[ct]["acc"], L[ct]["ak"]
                ACT(acc[:, 0:n], acc[:, 0:n], AF.Sqrt, [ak, "epsc"], [ak], scale=-1.0, bias=epsc[:, 1:2])
            for ct in range(4):
                c = half * 4 + ct
                (rg, rk), (ig, ik) = L[ct]["gates"]
                acc, ak = L[ct]["acc"], L[ct]["ak"]
                TT(ig[:, 0:n], ig[:, 0:n], acc[:, 0:n], ALU.mult, [ik, ak], [ik])
                hb_, hbk = hsb.next()
                S.op("dve", lambda e, hb_=hb_, rg=rg, ig=ig, c=c: e.tensor_tensor_scan(
                    out=hb_[:, 0:n], data0=rg[:, 0:n], data1=ig[:, 0:n], initial=lru_h[:, c:c + 1],
                    op0=ALU.mult, op1=ALU.add), reads=[rk, ik, "lru_h"], writes=[hbk])
                CP(lru_h[:, c:c + 1], hb_[:, n - 1:n], [hbk], ["lru_h"], eng="pool")
                hs_list.append((hb_, hbk))
            wv, wk = load_w("w_in_ab", 0, 4096 + half * 512)
            for ct in range(4):
                c = half * 4 + ct
                ps, pk = PS.half()
                for kt in range(8):
                    MM(ps[:, 0:n], wv[:, kt, ct * 128:(ct + 1) * 128], hnT[:, kt, 0:n], kt == 0, kt == 7, [HNK, wk], pk)
                ge, gek = f32a.next()
                ACT(ge[:, 0:n], ps[:, 0:n], AF.Gelu_apprx_tanh, pk, [gek])
                hb_, hbk = hs_list[ct]
                TT(big[:, 8 + c, 0:n], ge[:, 0:n], hb_[:, 0:n], ALU.mult, [gek, hbk], [BIGK(8 + c)], eng=ew())
        proj_token_major("w_out_ab", 16, big, BIGK, tiles,
                         lambda j, rows, a, ak, b, bk: postnorm_add(Hc, HK, j, rows, a, ak, b, bk, postg[0], "postg0"))

    def layer1_mixer(Hc, HK, tiles, n, meta_group, prev_n):
        prenorm(Hc, HK, tiles, 16)
        wv, wk = load_w("w_in_cd", 0, 0)
        for ct in range(4):
            ps, pk = PS.half()
            for kt in range(8):
                MM(ps[:, 0:n], wv[:, kt, ct * 128:(ct + 1) * 128], hnT[:, kt, 0:n], kt == 0, kt == 7, [HNK, wk], pk)
            CP(uf[:, ct, 0:n], ps[:, 0:n], pk, [UFK], eng="act")
            CP(ub[:, ct, 0:n], ps[:, 0:n], pk, [f"ub{ct}"])
        if prev_n is not None:
            fr = 0 if prev_n == T else 1
            cb_ = s5_bnd[:, fr, 0, :]
            sb_ = s5_bnd[:, fr, 1, :]
            q1, qk1 = colsA.next()
            q2, qk2 = colsA.next()
            TT(q1[:, 0:16], s5_st[:, 0, :], cb_, ALU.mult, ["s5_st", "s5_bnd"], [qk1])
            TT(q2[:, 0:16], s5_st[:, 1, :], sb_, ALU.mult, ["s5_st", "s5_bnd"], [qk2])
            TT(s5_ini[:, 0, :], q1[:, 0:16], q2[:, 0:16], ALU.subtract, [qk1, qk2], ["s5_ini"])
            q3, qk3 = colsA.next()
            q4, qk4 = colsA.next()
            TT(q3[:, 0:16], s5_st[:, 0, :], sb_, ALU.mult, ["s5_st", "s5_bnd"], [qk3])
            TT(q4[:, 0:16], s5_st[:, 1, :], cb_, ALU.mult, ["s5_st", "s5_bnd"], [qk4])
            TT(s5_ini[:, 1, :], q3[:, 0:16], q4[:, 0:16], ALU.add, [qk3, qk4], ["s5_ini"])
        else:
            MEMSET(s5_ini[:], 0.0, ["s5_ini"])
        for ct in range(4):
            ms = [ct * 4 + q for q in range(4)]
            W = {}
            for m in ms:
                psr, pkr = PS.half()
                MM(psr[:, 0:n], Bblk[:, 0, m, :], ub[:, ct, 0:n], True, True, ["Bblk", f"ub{ct}"], pkr)
                psi, pki = PS.half()
                MM(psi[:, 0:n], Bblk[:, 1, m, :], ub[:, ct, 0:n], True, True, ["Bblk", f"ub{ct}"], pki)
                W[m] = dict(psr=psr, pkr=pkr, psi=psi, pki=pki, t=[f32a.next() for _ in range(4)],
                            Rc=Rcs[:, 0, m, 0:n], Rs=Rcs[:, 1, m, 0:n])
            for m in ms:
                w_ = W[m]
                (t1, k1), (t2, k2), (t3, k3), (t4, k4) = w_["t"]
                TT(t1[:, 0:n], w_["psr"][:, 0:n], w_["Rc"], ALU.mult, w_["pkr"] + ["Rcs"], [k1])
                TT(t4[:, 0:n], w_["psr"][:, 0:n], w_["Rs"], ALU.mult, w_["pkr"] + ["Rcs"], [k4])
                TT(t2[:, 0:n], w_["psi"][:, 0:n], w_["Rs"], ALU.mult, w_["pki"] + ["Rcs"], [k2])
                TT(t3[:, 0:n], w_["psi"][:, 0:n], w_["Rc"], ALU.mult, w_["pki"] + ["Rcs"], [k3])
            for m in ms:
                (t1, k1), (t2, k2), (t3, k3), (t4, k4) = W[m]["t"]
                TT(t1[:, 0:n], t1[:, 0:n], t2[:, 0:n], ALU.add, [k1, k2], [k1], eng="pool")
                TT(t3[:, 0:n], t3[:, 0:n], t4[:, 0:n], ALU.subtract, [k3, k4], [k3], eng="pool")
            for m in ms:
                (t1, k1), (t2, k2), (t3, k3), (t4, k4) = W[m]["t"]
                magb = s5_mag[:, m:m + 1].to_broadcast([128, n])
                S.op("dve", lambda e, t2=t2, t1=t1, m=m, magb=magb: e.tensor_tensor_scan(
                    out=t2[:, 0:n], data0=magb, data1=t1[:, 0:n], initial=s5_ini[:, 0, m:m + 1],
                    op0=ALU.mult, op1=ALU.add), reads=[k1, "s5_mag", "s5_ini"], writes=[k2])
                S.op("dve", lambda e, t4=t4, t3=t3, m=m, magb=magb: e.tensor_tensor_scan(
                    out=t4[:, 0:n], data0=magb, data1=t3[:, 0:n], initial=s5_ini[:, 1, m:m + 1],
                    op0=ALU.mult, op1=ALU.add), reads=[k3, "s5_mag", "s5_ini"], writes=[k4])
            for m in ms:
                (t1, k1), (t2, k2), (t3, k3), (t4, k4) = W[m]["t"]
                CP(s5_st[:, 0, m:m + 1], t2[:, n - 1:n], [k2], ["s5_st"], eng="pool")
                CP(s5_st[:, 1, m:m + 1], t4[:, n - 1:n], [k4], ["s5_st"], eng="pool")
            for m in ms:
                w_ = W[m]
                (t1, k1), (t2, k2), (t3, k3), (t4, k4) = w_["t"]
                hr, hrk = s5h.next()
                hi, hik = s5h.next()
                b1 = t1[:].bitcast(BF16)
                b3 = t3[:].bitcast(BF16)
                TT(hr[:, 0:n], t2[:, 0:n], w_["Rc"], ALU.mult, [k2, "Rcs"], [hrk])
                STT(b1[:, 0:n], t4[:, 0:n], -1.0, w_["Rs"], ALU.mult, ALU.mult, [k4, "Rcs"], [k1])
                TT(hi[:, 0:n], t2[:, 0:n], w_["Rs"], ALU.mult, [k2, "Rcs"], [hik])
                TT(b3[:, 0:n], t4[:, 0:n], w_["Rc"], ALU.mult, [k4, "Rcs"], [k3])
                w_["prods"] = [(0, hr, hrk), (0, b1, k1), (1, hi, hik), (1, b3, k3)]
            psy, pky = PS.half()
            for q, m in enumerate(ms):
                for pi, (ri, buf, key) in enumerate(W[m]["prods"]):
                    MM(psy[:, 0:n], Cblk[:, ri, m, :], buf[:, 0:n], q == 0 and pi == 0, q == 3 and pi == 3,
                       ["Cblk", key], pky)
            yt_, ytk = f32a.next()
            STT(yt_[:, 0:n], uf[:, ct, 0:n], s5dg[:, ct:ct + 1], psy[:, 0:n], ALU.mult, ALU.add,
                [UFK, "s5dg"] + pky, [ytk])
            ACT(ygf[:, ct, 0:n], yt_[:, 0:n], AF.Gelu_apprx_tanh, [ytk], [YGK])
            CP(ygb[:, ct, 0:n], ygf[:, ct, 0:n], [YGK], [f"ygb{ct}"], eng="pool")
        for co in range(4):
            ps, pk = PS.half()
            for ci_ in range(4):
                MM(ps[:, 0:n], gluW[:, ci_, co * 128:(co + 1) * 128], ygb[:, ci_, 0:n], ci_ == 0, ci_ == 3,
                   ["gluW", f"ygb{ci_}"], pk)
            sgm, sgk = f32a.next()
            ACT(sgm[:, 0:n], ps[:, 0:n], AF.Sigmoid, pk + ["s5dg"], [sgk], bias=s5dg[:, 4 + co:5 + co])
            TT(big[:, co, 0:n], ygf[:, co, 0:n], sgm[:, 0:n], ALU.mult, [YGK, sgk], [BIGK(co)])
        wq, wqk = load_w("w_in_cd", 0, 512)
        wf, wfk = load_w("w_in_cd", 0, 1024)
        bw = 16 if meta_group else 64
        htiles = [(0, NMETA)] if meta_group else [(c_, 64) for c_ in range(n // 64)]
        nblk = len(htiles)
        mid = bw // 2 - 1
        for h in range(4):
            psq, pkq = PS.half()
            for kt in range(8):
                MM(psq[:, 0:n], wq[:, kt, h * 128:(h + 1) * 128], hnT[:, kt, 0:n], kt == 0, kt == 7, [HNK, wqk], pkq)
            psf, pkf = PS.half()
            for kt in range(8):
                MM(psf[:, 0:n], wf[:, kt, h * 128:(h + 1) * 128], hnT[:, kt, 0:n], kt == 0, kt == 7, [HNK, wfk], pkf)
            ff, fk = f32a.next()
            ACT(ff[:, 0:n], psf[:, 0:n], AF.Sigmoid, pkf, [fk])
            TS(ff[:, 0:n], ff[:, 0:n], hg_lb[:, 4 + h:5 + h], hg_lb[:, h:h + 1], ALU.mult, ALU.add, [fk, "hg_lb"], [fk])
            lf, lk = f32a.next()
            ACT(lf[:, 0:n], ff[:, 0:n], AF.Ln, [fk], [lk])
            TS(ff[:, 0:n], ff[:, 0:n], -1.0, 1.0, ALU.mult, ALU.add, [fk], [fk], eng="pool")
            cum, cmk = f32a.next()
            S.op("dve", lambda e, cum=cum, lf=lf: e.tensor_tensor_scan(
                out=cum[:, 0:n], data0=restart[:, 0:n], data1=lf[:, 0:n], initial=0.0,
                op0=ALU.mult, op1=ALU.add), reads=[lk, "restart"], writes=[cmk])
            c3 = cum[:, 0:n].rearrange("p (j t) -> p j t", t=bw)
            ACT(hgE[:, 0, h, 0:nblk], c3[:, :, mid], AF.Exp, [cmk], ["hgE"])
            ACT(hgE[:, 1, h, 0:nblk], c3[:, :, bw - 1], AF.Exp, [cmk], ["hgE"])
            cm, cmk2 = f32a.next()
            cm3 = cm[:, 0:n].rearrange("p (j t) -> p j t", t=bw)
            TT(cm3, c3, c3[:, :, mid:mid + 1].to_broadcast([128, nblk, bw]), ALU.subtract, [cmk], [cmk2])
            ACT(lf[:, 0:n], cm[:, 0:n], AF.Exp, [cmk2], [lk])
            ACT(cm[:, 0:n], cm[:, 0:n], AF.Exp, [cmk2], [cmk2], scale=-1.0)
            l3 = lf[:, 0:n].rearrange("p (j t) -> p j t", t=bw)
            CP(hgE[:, 2, h, 0:nblk], l3[:, :, bw - 1], [lk], ["hgE"], eng="pool")
            TT(hqT[:, h, 0:n], psq[:, 0:n], lf[:, 0:n], ALU.mult, pkq + [lk], [f"hq{h}"])
            TT(hkT[:, h, 0:n], ff[:, 0:n], cm[:, 0:n], ALU.mult, [fk, cmk2], [f"hk{h}"], eng="pool")
        iv = v_tok[:].rearrange("p j (a c) -> p (j a) c", a=2)
        gv = sg_tok[:].rearrange("p j (a c) -> p (j a) c", a=2)
        IVK = lambda c_: f"v{c_ // 2}"
        GVK = lambda c_: f"sg{c_ // 2}"
        wi_, wik = load_w("w_in_cd", 0, 1536)
        for (c_, rows) in htiles:
            ps, pk = PS.full()
            for kt in range(8):
                MM(ps[0:rows, :], hnT[:, kt, c_ * 64:c_ * 64 + rows], wi_[:, kt, :], kt == 0, kt == 7, [HNK, wik], pk)
            CP(iv[0:rows, c_, :], ps[0:rows, :], pk, [IVK(c_)], eng="act")
        wg_, wgk = load_w("w_in_cd", 0, 2048)
        for (c_, rows) in htiles:
            ps, pk = PS.full()
            for kt in range(8):
                MM(ps[0:rows, :], hnT[:, kt, c_ * 64:c_ * 64 + rows], wg_[:, kt, :], kt == 0, kt == 7, [HNK, wgk], pk)
            ACT(gv[0:rows, c_, :], ps[0:rows, :], AF.Silu, pk, [GVK(c_)])
        HQK = [f"hq{h}" for h in range(4)]
        HKK = [f"hk{h}" for h in range(4)]
        for (j, rows) in htiles:
            t0 = j * 64
            ps, pk = PS.full()
            p3 = ps.rearrange("p (h t) -> p h t", h=4)
            for h in range(4):
                MM(p3[0:rows, h, 0:rows], hkT[:, h, t0:t0 + rows], hqT[:, h, t0:t0 + rows], True, True, HQK + HKK, pk)
            at, atk = sTb.next()
            STT(at[0:rows, :, 0:rows], p3[0:rows, :, 0:rows], 1e30,
                triu[0:rows, 0:rows].unsqueeze(1).to_broadcast([rows, 4, rows]), ALU.min, ALU.mult,
                pk + ["triu"], [atk])
            ps2, pk2 = PS.half()
            psb = ps2.bitcast(BF16)
            for h in range(4):
                TR(psb[0:rows, h * 128:(h + 1) * 128], hkT[:, h, t0:t0 + rows], ident[:, :], HKK + ["ident"], pk2)
            ktk_, ktkk = kdec.next()
            CP(ktk_[0:rows, :, :], psb.rearrange("p (h d) -> p h d", h=4)[0:rows], pk2, [ktkk], eng="act")
            TT(HSb[:], HS[:], hgE[:, 0, :, j:j + 1].to_broadcast([128, 4, 128]), ALU.mult, ["HS", "hgE"], ["HSb"])
            pso, pko = PS.full()
            o3 = pso.rearrange("p (h e) -> p h e", h=4)
            for h in range(4):
                MM(o3[0:rows, h, :], at[0:rows, h, 0:rows], iv[0:rows, j, h * 128:(h + 1) * 128], True, False,
                   [atk, IVK(j)], pko)
                MM(o3[0:rows, h, :], hqT[:, h, t0:t0 + rows], HSb[:, h, :], False, True, HQK + ["HSb"], pko)
            psk, pkk = PS.full()
            k3 = psk.rearrange("p (h e) -> p h e", h=4)
            for h in range(4):
                MM(k3[:, h, :], ktk_[0:rows, h, :], iv[0:rows, j, h * 128:(h + 1) * 128], True, True,
                   [ktkk, IVK(j)], pkk)
            ta, tak = tmpTok.next()
            TT(ta[:].rearrange("p (h e) -> p h e", h=4), k3, hgE[:, 2, :, j:j + 1].to_broadcast([128, 4, 128]),
               ALU.mult, pkk + ["hgE"], [tak])
            TT(HS[:], HS[:], hgE[:, 1, :, j:j + 1].to_broadcast([128, 4, 128]), ALU.mult, ["HS", "hgE"], ["HS"])
            TT(HS[:], HS[:], ta[:].rearrange("p (h e) -> p h e", h=4), ALU.add, ["HS", tak], ["HS"], eng="pool")
            jb, jk = junk_t, None
            cl, ck = colsA.next()
            for h in range(4):
                ACT(jb[0:rows, h * 128:(h + 1) * 128], o3[0:rows, h, :], AF.Square, pko, [ck],
                    accum_out=cl[0:rows, h:h + 1])
            rstd_from_ss(cl[0:rows, 0:4], rows, 128, ck)
            on, onk = tmpTok.next()
            TT(on[0:rows].rearrange("p (h e) -> p h e", h=4), o3[0:rows],
               cl[0:rows, 0:4].unsqueeze(2).to_broadcast([rows, 4, 128]), ALU.mult, pko + [ck], [onk])
            yb_, ybk = ya.next()
            TT(yb_[0:rows, 0:512], on[0:rows, :], gv[0:rows, j, :], ALU.mult, [onk, GVK(j)], [ybk], eng="pool")
            to_feature(yb_, rows, 4, big, 4, t0, gnT[:, 8:12], [ybk], BIGK)
        proj_token_major("w_out_cd", 8, big, BIGK, tiles,
                         lambda j, rows, a, ak, b, bk: postnorm_add(Hc, HK, j, rows, a, ak, b, bk, postg[0], "postg0"))

    last_store = None
    prev_n = None
    NGR = min(NG, KNG) if KPHASE == "all" else 0

    def group_geom(g):
        if g == 0:
            return NMETA, [(0, NMETA)], 0, 0
        f0 = (g - 1) * T
        return T, [(j, 128) for j in range(NT)], NMETA + f0, f0

    rope_bufs = {}

    def issue_loads(g):
        Hc, HK = H[g % 2], f"H{g % 2}"
        n, tiles, pos0, f0 = group_geom(g)
        rb, rkey = ropeT.next()
        rope_bufs[g] = (rb, rkey)
        if g == 0:
            DMA("pool", Hc[0:NMETA, 0, :], meta_d, "ld_x", [], [HK])
            DMA("pool", rb[0:NMETA, 0, :], cd["rope"][0:NMETA, :], "ld_rope", [], [rkey])
        else:
            DMA("pool", Hc[:, :, :], x_d[f0:f0 + T, :].rearrange("(j p) d -> p j d", p=128), "ld_x", [], [HK])
            DMA("pool", rb[:, :, :], cd["rope"][pos0:pos0 + T, :].rearrange("(j p) d -> p j d", p=128), "ld_rope",
                [], [rkey])

    if NGR > 0:
        issue_loads(0)
    for g in range(NGR):
        Hc = H[g % 2]
        HK = f"H{g % 2}"
        meta_group = g == 0
        n, tiles, pos0, f0 = group_geom(g)
        rb, rkey = rope_bufs.pop(g)

        def dbg(slot):
            if DEBUG and (g < 3):
                if meta_group:
                    DMA("pool", dbg_d[slot, 0:NMETA, :], Hc[0:NMETA, 0, :], "st_dbg", [HK], [])
                else:
                    DMA("pool", dbg_d[slot, pos0:pos0 + T, :].rearrange("(j p) d -> p j d", p=128), Hc[:, :, :],
                        "st_dbg", [HK], [])

        load_postg(0)
        layer0_mixer(Hc, HK, tiles, n, meta_group, rb, rkey, do_prenorm=(g == 0 or KSTOP < 3))
        has_next = g + 1 < NGR
        if has_next:
            issue_loads(g + 1)
        dbg(0)
        if KSTOP >= 1:
            mlp(0, Hc, HK, tiles, n)
            dbg(1)
        if KSTOP >= 2:
            load_postg(1)
            layer1_mixer(Hc, HK, tiles, n, meta_group, prev_n)
            dbg(2)
        if KSTOP >= 3:
            hook = None
            if has_next:
                nH, nHK = H[(g + 1) % 2], f"H{(g + 1) % 2}"
                ntiles = group_geom(g + 1)[1]
                hook = lambda nH=nH, nHK=nHK, ntiles=ntiles: prenorm(nH, nHK, ntiles, 0)
            mlp(1, Hc, HK, tiles, n, hook=hook)
            dbg(3)
        if not meta_group:
            last_store = DMA("pool", out_d[f0:f0 + T, :].rearrange("(j p) d -> p j d", p=128), Hc[:, :, :], "st_out",
                             [HK], [])
        prev_n = n
    fin = [d for d in [last_store, S.chan_last.get("st_dbg"), S.chan_last.get("ld_small"), S.chan_last.get("st_scr0")] if d is not None]
    S.op("sp", lambda e: e.nop(), extra_deps=fin)
    print("instr counts", {e: len(S.lists[e]) for e in ENGS}, flush=True)
    S.emit()
    S.stats = {e: max([i.val for i in S.lists[e] if i.chan is None and i.needs_inc] or [0]) for e in ENGS}
    S.stats.update({c: lst[-1].val for c, lst in S.chan_ins.items()})
    print("max sem values", S.stats, flush=True)
    return nc, consts


_CACHE = {}


def kernel(**inputs):
    if "prog" not in _CACHE:
        _CACHE["prog"] = build_program()
    nc, consts = _CACHE["prog"]
    lay = host_layouts(inputs)
    common = dict(lay)
    for n, v in consts.items():
        common["c_" + n] = v
    x = np.asarray(inputs["x"], dtype=np.float32)
    in_maps = []
    for c in range(8):
        m = dict(common)
        m["x"] = np.ascontiguousarray(x[c % 4])
        in_maps.append(m)
    res = run_bass_kernel_spmd(nc, in_maps, core_ids=list(range(8)))
    out = np.stack([np.asarray(res.results[b]["out"], dtype=np.float32) for b in range(4)], axis=0)
    if DEBUG:
        kernel.dbg = [np.asarray(res.results[b]["dbg"]) for b in range(4)]
    return out
```

```python
import numpy as np
import concourse.bass as bass
import concourse.mybir as mybir
from concourse.bass_utils import run_bass_kernel_spmd

F32 = mybir.dt.float32
BF16 = mybir.dt.bfloat16
I32 = mybir.dt.int32
AF = mybir.ActivationFunctionType
ALU = mybir.AluOpType
AX = mybir.AxisListType

T = 256
NT = T // 128
SEQ = 4096
NMETA = 16
D = 1024
DFF = 4096
EPS = 1e-6
NG = 1 + SEQ // T
import os
DEBUG = bool(int(os.environ.get("KDEBUG", "0")))
KNG = int(os.environ.get("KNG", "1000"))
KSTOP = int(os.environ.get("KSTOP", "3"))
KPHASE = os.environ.get("KPHASE", "all")

ENGS = ("pe", "act", "dve", "pool", "sp")


class Ins:
    __slots__ = ("eng", "fn", "waits", "idx", "needs_inc", "chan", "clock", "ninc", "val", "tag")


class Sched:
    def __init__(self, nc):
        self.nc = nc
        self.lists = {e: [] for e in ENGS}
        self.lastw = {}
        self.readers = {}
        self.clock = {e: {} for e in ENGS}
        self.chan_last = {}
        self.chan_ins = {}

    def _need(self, eng, dep, waits):
        src = dep.chan if dep.chan is not None else dep.eng
        if dep.chan is None and dep.eng == "pe" and eng == "pe":
            return
        if self.clock[eng].get(src, -1) >= dep.idx:
            return
        waits[src] = max(waits.get(src, -1), dep.idx)

    def op(self, eng, fn, reads=(), writes=(), chan=None, extra_deps=(), ninc=1):
        ins = Ins()
        ins.eng, ins.fn, ins.chan, ins.needs_inc, ins.ninc = eng, fn, chan, False, ninc
        ins.tag = getattr(self, "tag", "")
        psr = [k for k in reads if k[:2] == "ps" and k[2:].isdigit()]
        if psr:
            reads = [k for k in reads if k not in psr]
            writes = list(writes) + [k for k in psr if k not in writes]
        deps = []
        for k in reads:
            deps.extend(self.lastw.get(k, ()))
        for k in writes:
            deps.extend(self.lastw.get(k, ()))
            deps.extend(self.readers.get(k, ()))
        deps.extend(extra_deps)
        if chan is not None and chan in self.chan_last:
            deps.append(self.chan_last[chan])
        waits = {}
        for d in deps:
            self._need(eng, d, waits)
        ins.waits = []
        ck = self.clock[eng]
        for src, idx in waits.items():
            prod = self.lists[src][idx] if src in ENGS else self.chan_ins[src][idx]
            prod.needs_inc = True
            ins.waits.append(prod)
            for s, v in prod.clock.items():
                if ck.get(s, -1) < v:
                    ck[s] = v
        if chan is not None:
            lst = self.chan_ins.setdefault(chan, [])
            ins.idx = len(lst)
            lst.append(ins)
            self.chan_last[chan] = ins
            ins.clock = dict(ck)
            ins.clock[chan] = ins.idx
            self.lists[eng].append(ins)
        else:
            ins.idx = len(self.lists[eng])
            ins.clock = dict(ck)
            ins.clock[eng] = ins.idx
            self.lists[eng].append(ins)
        for k in reads:
            self.readers.setdefault(k, []).append(ins)
        me = chan if chan is not None else eng
        for k in writes:
            lst = [w for w in self.lastw.get(k, ()) if (w.chan if w.chan is not None else w.eng) != me]
            lst.append(ins)
            self.lastw[k] = lst
            self.readers[k] = []
        return ins

    def emit(self):
        nc = self.nc
        sems = {}
        for e in ("pe", "act", "dve", "pool"):
            sems[e] = nc.alloc_semaphore("s_" + e)
        for c in self.chan_ins:
            sems[c] = nc.alloc_semaphore("c_" + str(c))
        for e in ENGS:
            r = 0
            for ins in self.lists[e]:
                if ins.chan is None and ins.needs_inc:
                    r += 1
                    ins.val = r
        for c, lst in self.chan_ins.items():
            v = 0
            for ins in lst:
                v += 16 * ins.ninc
                ins.val = v

        def run(e):
            def body(eng):
                for ins in self.lists[e]:
                    for p in ins.waits:
                        src = p.chan if p.chan is not None else p.eng
                        eng.wait_ge(sems[src], p.val)
                    r = ins.fn(eng)
                    rl = r if isinstance(r, (list, tuple)) else [r]
                    if ins.chan is not None:
                        assert len(rl) == ins.ninc
                        for x in rl:
                            x.then_inc(sems[ins.chan], 16)
                    elif ins.needs_inc:
                        rl[-1].then_inc(sems[e], 1)
            return body

        with nc.Block() as block:
            block.tensor(run("pe"))
            block.scalar(run("act"))
            block.vector(run("dve"))
            block.gpsimd(run("pool"))
            block.sync(run("sp"))


class DB:
    def __init__(self, nc, name, shape, dtype, nbuf=2):
        self.bufs = [nc.alloc_sbuf_tensor(f"sb_{name}_{i}", list(shape), dtype) for i in range(nbuf)]
        self.keys = [f"{name}_{i}" for i in range(nbuf)]
        self.i = 0

    def next(self):
        i = self.i
        self.i = (i + 1) % len(self.bufs)
        return self.bufs[i], self.keys[i]


class PsumPool:
    def __init__(self, nc):
        self.banks = [nc.alloc_psum_tensor(f"ps{i}", [128, 512], F32) for i in range(8)]
        self.i = 0
        self.j = 0
        self.live = set()

    def half(self):
        while True:
            u = self.i
            self.i = (u + 1) % 16
            h, b = divmod(u, 8)
            if b not in self.live:
                break
        return self.banks[b][:, h * 256:(h + 1) * 256], [f"ps{b}"]

    def full(self):
        b = self.j
        self.j = (b + 1) % 8
        self.last_full = b
        return self.banks[b][:, :], [f"ps{b}"]


GAMMA = [1.0 - 2.0 ** (-5.0 - h) for h in range(4)]


def host_constants():
    c = {}
    c["ident"] = np.eye(128, dtype=np.float32)
    L = NMETA + SEQ
    pos = np.arange(L, dtype=np.float32)
    inv = (10000.0 ** (-np.arange(64, dtype=np.float32) / 64)).astype(np.float32)
    ang = (pos[:, None] * inv[None, :]).astype(np.float32)
    c["rope"] = np.concatenate([np.cos(ang), np.sin(ang)], axis=1).astype(np.float32)
    s = np.arange(128)[:, None]
    t = np.arange(128)[None, :]
    retD = np.zeros((128, 4, 128), np.float64)
    retG = np.zeros((128, 4), np.float64)
    retKD = np.zeros((128, 4), np.float64)
    retKDm = np.zeros((128, 4), np.float64)
    for h in range(4):
        g = GAMMA[h]
        same = (s // 64) == (t // 64)
        earlier = (s // 64) < (t // 64)
        w = np.where(same, g ** np.abs(t - s), np.where(earlier, g ** (t - s).clip(0), 0.0))
        retD[:, h, :] = w / (g ** (t + 1.0))
        retG[:, h] = (128 ** -0.5) * g ** (np.arange(128) + 1.0)
        retKD[:, h] = g ** (127.0 - np.arange(128))
        retKDm[:16, h] = g ** (15.0 - np.arange(16))
    c["retD"] = retD.astype(np.float32)
    c["retS"] = np.concatenate([retG, retKD, retKDm], axis=1).astype(np.float32)
    c["triu"] = (s <= t).astype(np.float32)
    rs = np.ones((128, T), np.float32)
    rs[:, ::64] = 0.0
    c["restart"] = rs
    c["iota"] = np.broadcast_to(np.arange(T, dtype=np.float32), (128, T)).copy()
    mb = np.zeros((128, 16, 8), np.float32)
    for m in range(16):
        for gl in range(2):
            mb[gl * 64:(gl + 1) * 64, m, (2 * m + gl) % 8] = 1.0
    c["maskB"] = mb
    mc = np.zeros((128, 4, 2, 2), np.float32)
    for mm in range(4):
        for gl in range(2):
            g8 = 2 * mm + gl
            mc[g8 * 16:(g8 + 1) * 16, mm, gl, 0] = 1.0
            mc[g8 * 16:(g8 + 1) * 16, mm, gl, 1] = -1.0
    c["maskC"] = mc
    return c


def host_layouts(inp):
    f = lambda a: np.ascontiguousarray(np.asarray(a, dtype=np.float32))
    o = {}
    o["meta"] = f(inp["meta"])
    o["w_in_ab"] = f(inp["w_in_ab"][0])
    o["w_out_ab"] = f(inp["w_out_ab"][0])
    o["w1"] = f(inp["mlp_w1"])
    o["w2"] = f(inp["mlp_w2"])
    o["w_in_cd"] = f(inp["w_in_cd"][0])
    o["w_out_cd"] = f(inp["w_out_cd"][0])
    o["glu_w"] = f(inp["s5_glu_w"][0])
    ng = np.asarray(inp["norm_g"], np.float32)
    pre = ng[:, [0, 2], :].reshape(2, 2, 8, 128)
    o["pgT"] = f(pre.transpose(3, 0, 1, 2).reshape(128, 32))
    o["postg"] = f(ng[:, [1, 3], :].reshape(4, 1024))
    gn = np.concatenate([np.asarray(inp["ret_gn"][0], np.float32).reshape(8, 128).T,
                         np.asarray(inp["hg_gn"][0], np.float32).reshape(4, 128).T], axis=1)
    o["gnT"] = f(gn)
    fm8 = lambda v: np.asarray(v, np.float32).reshape(8, 128).T
    cw = np.asarray(inp["rg_conv_w"][0], np.float32)
    lru = np.stack([fm8(inp["rg_lam"][0]), fm8(inp["rg_ba"][0]), fm8(inp["rg_bi"][0]), fm8(inp["rg_conv_b"][0]),
                    fm8(cw[0]), fm8(cw[1]), fm8(cw[2]), fm8(cw[3])], axis=2)
    o["lru_s"] = f(lru)
    o["rg_wa"] = f(inp["rg_wa"][0])
    o["rg_wi"] = f(inp["rg_wi"][0])
    hl = np.asarray(inp["hg_lb_logits"], np.float32).reshape(2, 4, 128)
    o["hgl"] = f(hl.transpose(2, 0, 1).reshape(128, 8))
    st = lambda a: np.asarray(a, np.float32).reshape(16, 2, 64).transpose(1, 2, 0).reshape(128, 16)
    ldt = np.repeat(np.asarray(inp["s5_log_dt"][0], np.float32)[:, None], 64, axis=1)
    o["s5s"] = f(np.stack([st(inp["s5_a_re_log"][0]), st(inp["s5_a_im"][0]), st(ldt)], axis=2))
    sb = lambda a: np.asarray(a, np.float32).reshape(16, 2, 64, 16).transpose(1, 2, 0, 3).reshape(128, 16, 16)
    o["s5b"] = f(np.stack([sb(inp["s5_b_re"][0]), sb(inp["s5_b_im"][0])], axis=2))
    sc = lambda a: np.asarray(a, np.float32).reshape(4, 8, 16, 64).transpose(1, 2, 0, 3).reshape(128, 4, 64)
    o["s5c"] = f(np.stack([sc(inp["s5_c_re"][0]), sc(inp["s5_c_im"][0])], axis=2))
    sd = np.asarray(inp["s5_d"][0], np.float32).reshape(4, 128).T
    gb = np.asarray(inp["s5_glu_b"][0], np.float32).reshape(4, 128).T
    o["s5dg"] = f(np.concatenate([sd, gb], axis=1))
    return o


def build_program():
    nc = bass.Bass("TRN2", target_bir_lowering=False)
    S = Sched(nc)
    consts = host_constants()

    def din(name, shape, dt=F32):
        return nc.dram_tensor(name, list(shape), dt, kind="ExternalInput").ap()

    x_d = din("x", [SEQ, D])
    meta_d = din("meta", [NMETA, D])
    wdefs = [("w_in_ab", 1024, 5120), ("w_out_ab", 2048, 1024), ("w1_0", 1024, 4096), ("w2_0", 4096, 1024),
             ("w_in_cd", 1024, 2560), ("w_out_cd", 1024, 1024), ("w1_1", 1024, 4096), ("w2_1", 4096, 1024),
             ("glu_w", 512, 512)]
    w1_d = din("w1", [2, 1024, 4096])
    w2_d = din("w2", [2, 4096, 1024])
    wsrc = {"w_in_ab": din("w_in_ab", [1024, 5120]), "w_out_ab": din("w_out_ab", [2048, 1024]),
            "w1_0": w1_d[0], "w1_1": w1_d[1], "w2_0": w2_d[0], "w2_1": w2_d[1],
            "w_in_cd": din("w_in_cd", [1024, 2560]), "w_out_cd": din("w_out_cd", [1024, 1024]),
            "glu_w": din("glu_w", [512, 512])}
    wscr = {n: (nc.dram_tensor("scr_" + n, [r, c], BF16, kind="Internal").ap() if n == "glu_w" else
                nc.dram_tensor("scr_" + n, [r // 1024, c // 512, 128, 8, 512], BF16, kind="Internal").ap())
            for n, r, c in wdefs}
    small = {}
    for n, shp in [("pgT", [128, 32]), ("postg", [4, 1024]), ("gnT", [128, 12]), ("lru_s", [128, 8, 8]),
                   ("rg_wa", [16, 64, 64]), ("rg_wi", [16, 64, 64]), ("hgl", [128, 8]), ("s5s", [128, 16, 3]),
                   ("s5b", [128, 16, 2, 16]), ("s5c", [128, 4, 2, 64]), ("s5dg", [128, 8])]:
        small[n] = din(n, shp)
    cd = {n: din("c_" + n, list(v.shape)) for n, v in consts.items()}
    out_d = nc.dram_tensor("out", [SEQ, D], F32, kind="ExternalOutput").ap()
    dbg_d = nc.dram_tensor("dbg", [4, NMETA + SEQ, D], F32, kind="ExternalOutput").ap() if DEBUG else None

    def sb(name, shape, dt=F32):
        return nc.alloc_sbuf_tensor("sb_" + name, list(shape), dt)

    def MM(out, lhsT, rhs, start, stop, r, w):
        S.op("pe", lambda e: e.matmul(out, lhsT=lhsT, rhs=rhs, start=start, stop=stop), reads=r, writes=w)

    def TR(out, in_, idn, r, w):
        S.op("pe", lambda e: e.transpose(out, in_, idn), reads=r, writes=w)

    def ACT(out, in_, func, r, w, **kw):
        S.op("act", lambda e: e.activation(out=out, in_=in_, func=func, **kw), reads=r, writes=w)

    def TT(out, a, b, op, r, w, eng="dve"):
        S.op(eng, lambda e: e.tensor_tensor(out=out, in0=a, in1=b, op=op), reads=r, writes=w)

    def TS(out, a, s1, s2, op0, op1, r, w, eng="dve"):
        if s2 is None:
            S.op(eng, lambda e: e.tensor_scalar(out=out, in0=a, scalar1=s1, scalar2=None, op0=op0), reads=r, writes=w)
        else:
            S.op(eng, lambda e: e.tensor_scalar(out=out, in0=a, scalar1=s1, scalar2=s2, op0=op0, op1=op1),
                 reads=r, writes=w)

    def STT(out, a, s, b, op0, op1, r, w, eng="dve"):
        S.op(eng, lambda e: e.scalar_tensor_tensor(out=out, in0=a, scalar=s, in1=b, op0=op0, op1=op1),
             reads=r, writes=w)

    def CP(out, a, r, w, eng="dve"):
        if eng == "act":
            S.op("act", lambda e: e.copy(out=out, in_=a), reads=r, writes=w)
        else:
            S.op(eng, lambda e: e.tensor_copy(out=out, in_=a), reads=r, writes=w)

    def DMA(q, out, in_, chan, r, w):
        return S.op(q, lambda e: e.dma_start(out=out, in_=in_), reads=r, writes=w, chan=chan)

    def MEMSET(ap, val, w, eng="dve"):
        S.op(eng, lambda e: e.memset(ap, val), writes=w)

    PS = PsumPool(nc)
    _rr = [0]

    def ew():
        _rr[0] += 1
        return "pool" if _rr[0] % 3 == 0 else "dve"

    ident_f = sb("ident_f", [128, 128])
    ident = sb("ident", [128, 128], BF16)
    H = [sb(f"H{i}", [128, NT, D]) for i in range(2)]
    hnT = sb("hnT", [128, 8, T], BF16)
    big = sb("big", [128, 32, T], BF16)
    NSLOT = 3
    Wr = [sb(f"W{i}", [128, 8 * 512], BF16) for i in range(NSLOT)]
    wslot = [0]
    pgT = sb("pgT", [128, 32])
    gnT = sb("gnT", [128, 12])
    postg = [sb(f"postg{i}", [128, D]) for i in range(2)]
    epsc = sb("epsc", [128, 2])
    ropeT = DB(nc, "rope", [128, NT, 128], F32, 2)
    retD = sb("retD", [128, 4, 128])
    retS = sb("retS", [128, 12])
    triu = sb("triu", [128, 128])
    restart = sb("restart", [128, T])
    RS = sb("RS", [128, 4, 256])
    RSb = sb("RSb", [128, 4, 256], BF16)
    lru_s = sb("lru_s", [128, 8, 8])
    lru_sp8 = sb("lru_sp8", [128, 8])
    Wab = sb("Wab", [128, 2, 8, 128], BF16)
    lru_carry = sb("lru_carry", [128, 8, 3])
    lru_h = sb("lru_h", [128, 8])
    hg_lb = sb("hg_lb", [128, 8])
    HS = sb("HS", [128, 4, 128])
    HSb = sb("HSb", [128, 4, 128], BF16)
    s5_mag = sb("s5_mag", [128, 16])
    s5_bnd = sb("s5_bnd", [128, 2, 2, 16])
    s5_st = sb("s5_st", [128, 2, 16])
    s5_ini = sb("s5_ini", [128, 2, 16])
    Bblk = sb("Bblk", [128, 2, 16, 128], BF16)
    Cblk = sb("Cblk", [128, 2, 16, 128], BF16)
    Rcs = sb("Rcs", [128, 2, 16, T], BF16)
    s5dg = sb("s5dg", [128, 8])
    gluW = sb("gluW", [128, 4, 512], BF16)

    f32a = DB(nc, "f32a", [128, T + 4], F32, 16)
    bf16a = DB(nc, "bf16a", [128, T], BF16, 4)
    hsb = DB(nc, "hsb", [128, T], F32, 5)
    s5h = DB(nc, "s5h", [128, T], BF16, 8)
    hn_tok = DB(nc, "hn_tok", [128, D], BF16, 1)
    junk_t = sb("junk", [128, D], BF16)
    colsA = DB(nc, "colsA", [128, 16], F32, 8)
    tmpTok = DB(nc, "tmpTok", [128, 512], F32, 4)
    qkr = sb("qkr", [128, NT, 2, 4, 128], BF16)
    v_tok = sb("v_tok", [128, NT, 1024], BF16)
    sg_tok = sb("sg_tok", [128, NT, 1024], BF16)
    qT = DB(nc, "qT", [128, 4, 128], BF16, 2)
    kT = DB(nc, "kT", [128, 4, 128], BF16, 2)
    sTb = DB(nc, "sTb", [128, 4, 128], BF16, 2)
    kdec = DB(nc, "kdec", [128, 4, 128], BF16, 2)
    o_sb = DB(nc, "o_sb", [128, 4, 256], F32, 2)
    ya = DB(nc, "ya", [128, D], BF16, 2)
    assert T == 256
    uf = o_sb.bufs[0]
    UFK = o_sb.keys[0]
    ub = sb("ub", [128, 4, T], BF16)
    ygf = o_sb.bufs[1]
    YGK = o_sb.keys[1]
    ygb = sb("ygb", [128, 4, T], BF16)
    hqT = sb("hqT", [128, 4, T], BF16)
    hkT = sb("hkT", [128, 4, T], BF16)
    hgE = sb("hgE", [128, 3, 4, T // 64])

    BIGK = lambda f: f"big.{f}"
    HNK = "hnT"

    def load(dst_ap, src_ap, key, q="act"):
        DMA(q, dst_ap, src_ap, "ld_small", [], [key])

    load(ident_f[:], cd["ident"], "ident_f")
    CP(ident[:], ident_f[:], ["ident_f"], ["ident"])
    load(pgT[:], small["pgT"], "pgT")
    load(gnT[:], small["gnT"], "gnT")
    load(retD[:], cd["retD"], "retD")
    load(retS[:], cd["retS"], "retS")
    load(triu[:], cd["triu"], "triu")
    load(restart[:], cd["restart"], "restart")
    load(lru_s[:], small["lru_s"], "lru_s")
    load(s5dg[:], small["s5dg"], "s5dg")
    MEMSET(epsc[:, 0:1], EPS, ["epsc"])
    MEMSET(epsc[:, 1:2], 1.0, ["epsc"])
    MEMSET(RS[:], 0.0, ["RS"])
    MEMSET(RSb[:], 0.0, ["RSb"])
    MEMSET(HS[:], 0.0, ["HS"])
    MEMSET(HSb[:], 0.0, ["HSb"])
    MEMSET(lru_carry[:], 0.0, ["lru_carry"])
    MEMSET(lru_h[:], 0.0, ["lru_h"])
    MEMSET(s5_st[:], 0.0, ["s5_st"])

    ACT(lru_sp8[:], lru_s[:, :, 0], AF.Exp, ["lru_s"], ["lru_sp8"], scale=-1.0)
    ACT(lru_sp8[:], lru_sp8[:], AF.Ln, ["lru_sp8", "epsc"], ["lru_sp8"], bias=epsc[:, 1:2])
    TS(lru_sp8[:], lru_sp8[:], -8.0, None, ALU.mult, None, ["lru_sp8"], ["lru_sp8"])
    for wi, nm in enumerate(("rg_wa", "rg_wi")):
        wabf = o_sb.bufs[wi][:].rearrange("p a (c d) -> p (a c) d", d=128)
        wk_ = o_sb.keys[wi]
        MEMSET(wabf, 0.0, [wk_])
        src = small[nm].rearrange("(ct bl) i j -> bl i ct j", bl=2)
        for bl in range(2):
            DMA("act", wabf[bl * 64:(bl + 1) * 64, :, bl * 64:(bl + 1) * 64], src[bl], "ld_small", [], [wk_])
        CP(Wab[:, wi, :, :], wabf, [wk_], ["Wab"])

    hgl = sb("hgl", [128, 8])
    load(hgl[:], small["hgl"], "hgl")
    TT(hg_lb[:, 0:4], hgl[:, 0:4], hgl[:, 4:8], ALU.subtract, ["hgl"], ["hg_lb"])
    ACT(hg_lb[:, 0:4], hg_lb[:, 0:4], AF.Sigmoid, ["hg_lb"], ["hg_lb"])
    TS(hg_lb[:, 4:8], hg_lb[:, 0:4], -1.0, 1.0, ALU.mult, ALU.add, ["hg_lb"], ["hg_lb"])

    s5s = sb("s5s", [128, 16, 3])
    load(s5s[:], small["s5s"], "s5s")
    pp = sb("s5pp", [128, 12, 16])
    PPK = ["s5pp"]
    iota = f32a.bufs[0][:, 0:T]
    IOK = f32a.keys[0]
    DMA("act", iota, cd["iota"], "ld_small", [], [IOK])
    DT_, ARE, AIM, TH, MAG, COS, SIN, NRE, DEN, ZRE, ZIM, TMP = range(12)
    ACT(pp[:, DT_, :], s5s[:, :, 2], AF.Exp, ["s5s"], PPK)
    ACT(pp[:, ARE, :], s5s[:, :, 0], AF.Exp, ["s5s"], PPK)
    TS(pp[:, ARE, :], pp[:, ARE, :], -1.0, None, ALU.mult, None, PPK, PPK)
    CP(pp[:, AIM, :], s5s[:, :, 1], ["s5s"], PPK)
    TT(pp[:, TH, :], pp[:, DT_, :], pp[:, AIM, :], ALU.mult, PPK, PPK)
    TT(pp[:, TMP, :], pp[:, DT_, :], pp[:, ARE, :], ALU.mult, PPK, PPK)
    ACT(pp[:, MAG, :], pp[:, TMP, :], AF.Exp, PPK, PPK)
    CP(s5_mag[:], pp[:, MAG, :], PPK, ["s5_mag"])

    TWO_PI = 2.0 * np.pi
    redi = f32a.bufs[1][:].bitcast(I32)[:, 0:T]
    redf = f32a.bufs[2][:, 0:T]
    RIK, RFK = f32a.keys[1], f32a.keys[2]

    def sincos(out_sin, out_cos, ang_ap, shape_n, keys_r, keys_w):
        ri = redi[:, 0:shape_n]
        rf = redf[:, 0:shape_n]
        for out_ap, shift in ((out_sin, 0.0), (out_cos, 0.5 * np.pi)):
            if out_ap is None:
                continue
            TS(ri, ang_ap, shift, 1.0 / TWO_PI, ALU.add, ALU.mult, keys_r, [RIK])
            STT(rf, ri, -TWO_PI, ang_ap, ALU.mult, ALU.add, [RIK] + keys_r, [RFK])
            if shift != 0.0:
                TS(rf, rf, shift, None, ALU.add, None, [RFK], [RFK])
            TS(rf, rf, 3.1415925, -3.1415925, ALU.min, ALU.max, [RFK], [RFK])
            ACT(out_ap, rf, AF.Sin, [RFK], keys_w)

    sincos(pp[:, SIN, :], pp[:, COS, :], pp[:, TH, :], 16, PPK, PPK)
    ang2 = sb("ang2", [128, 2, 16])
    TS(ang2[:, 0, :], pp[:, TH, :], float(T), None, ALU.mult, None, PPK, ["ang2"])
    TS(ang2[:, 1, :], pp[:, TH, :], float(NMETA), None, ALU.mult, None, PPK, ["ang2"])
    bndt = sb("bndt", [128, 2, 2, 16])
    for fr in range(2):
        sincos(bndt[:, fr, 1, :], bndt[:, fr, 0, :], ang2[:, fr, :], 16, ["ang2"], ["bndt"])
    CP(s5_bnd[:], bndt[:], ["bndt"], ["s5_bnd"])
    angT = f32a.bufs[3][:, 0:T]
    ATK = f32a.keys[3]
    for m in range(16):
        TS(angT, iota, pp[:, TH, m:m + 1], None, ALU.mult, None, PPK + [IOK], [ATK])
        sincos(Rcs[:, 1, m, :], Rcs[:, 0, m, :], angT, T, [ATK], ["Rcs"])
    TT(pp[:, COS, :], pp[:, COS, :], pp[:, MAG, :], ALU.mult, PPK, PPK)
    TT(pp[:, SIN, :], pp[:, SIN, :], pp[:, MAG, :], ALU.mult, PPK, PPK)
    TS(pp[:, NRE, :], pp[:, COS, :], -1.0, None, ALU.add, None, PPK, PPK)
    TT(pp[:, DEN, :], pp[:, ARE, :], pp[:, ARE, :], ALU.mult, PPK, PPK)
    TT(pp[:, TMP, :], pp[:, AIM, :], pp[:, AIM, :], ALU.mult, PPK, PPK)
    TT(pp[:, DEN, :], pp[:, DEN, :], pp[:, TMP, :], ALU.add, PPK, PPK)
    S.op("dve", lambda e: e.reciprocal(out=pp[:, DEN, :], in_=pp[:, DEN, :]), reads=PPK, writes=PPK)
    TT(pp[:, ZRE, :], pp[:, NRE, :], pp[:, ARE, :], ALU.mult, PPK, PPK)
    TT(pp[:, TMP, :], pp[:, SIN, :], pp[:, AIM, :], ALU.mult, PPK, PPK)
    TT(pp[:, ZRE, :], pp[:, ZRE, :], pp[:, TMP, :], ALU.add, PPK, PPK)
    TT(pp[:, ZRE, :], pp[:, ZRE, :], pp[:, DEN, :], ALU.mult, PPK, PPK)
    TT(pp[:, ZIM, :], pp[:, SIN, :], pp[:, ARE, :], ALU.mult, PPK, PPK)
    TT(pp[:, TMP, :], pp[:, NRE, :], pp[:, AIM, :], ALU.mult, PPK, PPK)
    TT(pp[:, ZIM, :], pp[:, ZIM, :], pp[:, TMP, :], ALU.subtract, PPK, PPK)
    TT(pp[:, ZIM, :], pp[:, ZIM, :], pp[:, DEN, :], ALU.mult, PPK, PPK)
    s5b = tmpTok.bufs[0][:].rearrange("p (m r c) -> p m r c", m=16, r=2)
    DMA("act", s5b, small["s5b"], "ld_small", [], [tmpTok.keys[0]])
    bbn = tmpTok.bufs[1][:].rearrange("p (r m c) -> p r m c", r=2, m=16)
    tb = tmpTok.bufs[2][:].rearrange("p (r m c) -> p r m c", r=2, m=16)
    zre_b = pp[:, ZRE, :].unsqueeze(2).to_broadcast([128, 16, 16])
    zim_b = pp[:, ZIM, :].unsqueeze(2).to_broadcast([128, 16, 16])
    TT(tb[:, 0], s5b[:, :, 0, :], zre_b, ALU.mult, PPK + [tmpTok.keys[0]], [tmpTok.keys[2]])
    TT(tb[:, 1], s5b[:, :, 1, :], zim_b, ALU.mult, PPK + [tmpTok.keys[0]], [tmpTok.keys[2]])
    TT(bbn[:, 0], tb[:, 0], tb[:, 1], ALU.subtract, [tmpTok.keys[2]], [tmpTok.keys[1]])
    TT(tb[:, 0], s5b[:, :, 1, :], zre_b, ALU.mult, PPK + [tmpTok.keys[0]], [tmpTok.keys[2]])
    TT(tb[:, 1], s5b[:, :, 0, :], zim_b, ALU.mult, PPK + [tmpTok.keys[0]], [tmpTok.keys[2]])
    TT(bbn[:, 1], tb[:, 0], tb[:, 1], ALU.add, [tmpTok.keys[2]], [tmpTok.keys[1]])
    maskB = sb("maskB", [128, 16, 8])
    load(maskB[:], cd["maskB"], "maskB")
    maskC = sb("maskC", [128, 4, 2, 2])
    load(maskC[:], cd["maskC"], "maskC")
    s5c = tmpTok.bufs[3][:].rearrange("p (a r q) -> p a r q", a=4, r=2)
    DMA("act", s5c, small["s5c"], "ld_small", [], [tmpTok.keys[3]])
    wide = DB(nc, "wide", [128, 128], BF16, 2)
    for ri in range(2):
        for m in range(16):
            wd, wk = wide.next()
            TT(wd[:].rearrange("p (g c) -> p g c", g=8), bbn[:, ri, m, :].unsqueeze(1).to_broadcast([128, 8, 16]),
               maskB[:, m, :].unsqueeze(2).to_broadcast([128, 8, 16]), ALU.mult, [tmpTok.keys[1], "maskB"], [wk])
            ps, pk = PS.half()
            psb = ps.bitcast(BF16)
            TR(psb[:, 0:128], wd[:], ident[:], [wk, "ident"], pk)
            CP(Bblk[:, ri, m, :], psb[:, 0:128], pk, ["Bblk"], eng="act")
    for ri in range(2):
        for m in range(16):
            wd, wk = wide.next()
            TT(wd[:].rearrange("p (g q) -> p g q", g=2), s5c[:, m // 4, ri, :].unsqueeze(1).to_broadcast([128, 2, 64]),
               maskC[:, m % 4, :, ri].unsqueeze(2).to_broadcast([128, 2, 64]), ALU.mult, [tmpTok.keys[3], "maskC"], [wk])
            ps, pk = PS.half()
            psb = ps.bitcast(BF16)
            TR(psb[:, 0:128], wd[:], ident[:], [wk, "ident"], pk)
            CP(Cblk[:, ri, m, :], psb[:, 0:128], pk, ["Cblk"], eng="act")

    cast_engs = ["dve", "act", "pool"]
    ci = 0
    for n, R, C in (wdefs if KPHASE != "pro" else []):
        for rt in range(R // 128):
            for c0 in range(0, C, 2048):
                w = min(2048, C - c0)
                s = wslot[0]
                wslot[0] = (s + 1) % NSLOT
                stg = Wr[s][:].bitcast(F32)
                DMA("sp", stg[:, 0:w], wsrc[n][rt * 128:(rt + 1) * 128, c0:c0 + w], f"w{s}", [], [f"W{s}"])
                bi = ci % 4
                cb = big[:, bi * 8:(bi + 1) * 8, :].rearrange("p a b -> p (a b)")
                ck = [BIGK(bi * 8 + t_) for t_ in range(8)]
                CP(cb[:, 0:w], stg[:, 0:w], [f"W{s}"], ck, eng=cast_engs[ci % 3])
                ci += 1
                if n == "glu_w":
                    dst = wscr[n][rt * 128:(rt + 1) * 128, c0:c0 + w]
                    srcv = cb[:, 0:w]
                else:
                    dst = wscr[n][rt // 8, c0 // 512:(c0 + w) // 512, :, rt % 8, :].rearrange("b p c -> p b c")
                    srcv = cb[:, 0:w].rearrange("p (b c) -> p b c", c=512)
                DMA("pool", dst, srcv, f"st_scr{bi}", ck, [f"scr_{n}.{rt}.{c0 // 2048}"])
    if KPHASE != "pro":
      DMA("sp", gluW[:], wscr["glu_w"].rearrange("(kt p) c -> p kt c", p=128), "ld_glu",
        [f"scr_glu_w.{rt}.0" for rt in range(4)], ["gluW"])

    def load_w(name, k0, c0, ncols=512, nk=8):
        s = wslot[0]
        wslot[0] = (s + 1) % NSLOT
        view = Wr[s][:].rearrange("p (k c) -> p k c", k=8)
        assert nk == 8 and ncols == 512 and k0 % 1024 == 0 and c0 % 512 == 0
        DMA("sp", view[:, 0:nk, 0:ncols], wscr[name][k0 // 1024, c0 // 512],
            f"w{s}", [f"scr_{name}.{k0 // 128 + i_}.{c0 // 2048}" for i_ in range(nk)], [f"W{s}"])
        return view, f"W{s}"

    def rstd_from_ss(ss_ap, rows, dim, ck):
        k = ss_ap.shape[1]
        ACT(ss_ap, ss_ap, AF.Sqrt, [ck, "epsc"], [ck], scale=1.0 / dim, bias=epsc[0:rows, 0:1])
        S.op("dve", lambda e: e.reciprocal(out=ss_ap, in_=ss_ap), reads=[ck], writes=[ck])

    def to_feature(src_tok, rows, ntile, dst, dst_f0, tok0, gain_ap, rkeys, wkeys_fn):
        for q0 in range(0, ntile, 4):
            nq = min(4, ntile - q0)
            ps, pk = PS.half()
            psb = ps.bitcast(BF16)
            for i in range(nq):
                TR(psb[:, i * 128:i * 128 + rows], src_tok[0:rows, (q0 + i) * 128:(q0 + i + 1) * 128],
                   ident[0:rows, 0:rows], rkeys + ["ident"], pk)
            src = psb.rearrange("p (k t) -> p k t", k=4)[:, 0:nq, 0:rows]
            dsl = dst[:, dst_f0 + q0:dst_f0 + q0 + nq, tok0:tok0 + rows]
            wk = [wkeys_fn(dst_f0 + q0 + i) for i in range(nq)]
            if gain_ap is None:
                CP(dsl, src, pk, wk, eng="act")
            else:
                g = gain_ap[:, q0:q0 + nq].unsqueeze(2).to_broadcast([128, nq, rows])
                TT(dsl, src, g, ALU.mult, pk + ["pgT", "gnT"], wk)

    def prenorm(Hc, HK, tiles, gcol):
        for (j, rows) in tiles:
            jb, jk = junk_t, None
            cl, ck = colsA.next()
            ACT(jb[0:rows, :], Hc[0:rows, j, :], AF.Square, [HK], [ck], accum_out=cl[0:rows, 0:1])
            rstd_from_ss(cl[0:rows, 0:1], rows, D, ck)
            hb, hk = hn_tok.next()
            TS(hb[0:rows, :], Hc[0:rows, j, :], cl[0:rows, 0:1], None, ALU.mult, None, [HK, ck], [hk])
            to_feature(hb, rows, 8, hnT, 0, j * 128, pgT[:, gcol:gcol + 8], [hk], lambda f: HNK)

    def postnorm_add(Hc, HK, j, rows, psA, pkA, psB, pkB, gtab, gk):
        cl, ck = colsA.next()
        jb, jk = junk_t, None
        ACT(jb[0:rows, 0:512], psA[0:rows, :], AF.Square, pkA, [ck], accum_out=cl[0:rows, 0:1])
        ACT(jb[0:rows, 512:1024], psB[0:rows, :], AF.Square, pkB, [ck], accum_out=cl[0:rows, 1:2])
        TT(cl[0:rows, 0:1], cl[0:rows, 0:1], cl[0:rows, 1:2], ALU.add, [ck], [ck])
        rstd_from_ss(cl[0:rows, 0:1], rows, D, ck)
        for (ps, pk, c0) in ((psA, pkA, 0), (psB, pkB, 512)):
            tb_, tk = tmpTok.next()
            STT(tb_[0:rows, :], ps[0:rows, :], cl[0:rows, 0:1], gtab[0:rows, c0:c0 + 512], ALU.mult, ALU.mult,
                pk + [ck, gk], [tk])
            TT(Hc[0:rows, j, c0:c0 + 512], Hc[0:rows, j, c0:c0 + 512], tb_[0:rows, :], ALU.add, [HK, tk], [HK],
               eng="pool")

    def proj_token_major(wname, F, srcbuf, src_key_fn, tiles, consume, hook=None):
        banks = {}
        live = set()
        for (j, rows) in tiles:
            a_ = PS.full()
            live.add(PS.last_full)
            b_ = PS.full()
            live.add(PS.last_full)
            banks[j] = (a_, b_)
        nunit = F // 8
        for cc in range(2):
            for u in range(nunit):
                wv, wk = load_w(wname, u * 1024, cc * 512)
                for (j, rows) in tiles:
                    ps, pk = banks[j][cc]
                    for fi in range(8):
                        f = u * 8 + fi
                        MM(ps[0:rows, :], srcbuf[:, f, j * 128:j * 128 + rows], wv[:, fi, :], f == 0, f == F - 1,
                           [src_key_fn(f), wk], pk)
        if hook is not None:
            PS.live = live
            hook()
            PS.live = set()
        for (j, rows) in tiles:
            (psA, pkA), (psB, pkB) = banks[j]
            consume(j, rows, psA, pkA, psB, pkB)

    def load_postg(l):
        for i in range(2):
            DMA("pool", postg[i][:], small["postg"][l * 2 + i:l * 2 + i + 1, :].partition_broadcast(128), "ld_postg",
                [], [f"postg{i}"])

    def mlp(l, Hc, HK, tiles, n, hook=None):
        prenorm(Hc, HK, tiles, (l * 2 + 1) * 8)
        wn1, wn2 = f"w1_{l}", f"w2_{l}"
        for blk in range(8):
            wv, wk = load_w(wn1, 0, blk * 512)
            for ft in range(4):
                f = blk * 4 + ft
                ps, pk = PS.half()
                for kt in range(8):
                    MM(ps[:, 0:n], wv[:, kt, ft * 128:(ft + 1) * 128], hnT[:, kt, 0:n], kt == 0, kt == 7, [HNK, wk], pk)
                tf, tk = f32a.next()
                ACT(tf[:, 0:n], ps[:, 0:n], AF.Relu, pk, [tk])
                TT(big[:, f, 0:n], tf[:, 0:n], tf[:, 0:n], ALU.mult, [tk], [BIGK(f)], eng=ew())
        proj_token_major(wn2, 32, big, BIGK, tiles,
                         lambda j, rows, a, ak, b, bk: postnorm_add(Hc, HK, j, rows, a, ak, b, bk, postg[1], "postg1"),
                         hook=hook)

    def layer0_mixer(Hc, HK, tiles, n, meta_group, rope_ap, rope_key, do_prenorm=True):
        if do_prenorm:
            prenorm(Hc, HK, tiles, 0)
        for qk in range(2):
            wv, wk = load_w("w_in_ab", 0, qk * 512)
            for (j, rows) in tiles:
                ps, pk = PS.full()
                for kt in range(8):
                    MM(ps[0:rows, :], hnT[:, kt, j * 128:j * 128 + rows], wv[:, kt, :], kt == 0, kt == 7, [HNK, wk], pk)
                x3 = ps.rearrange("p (h d) -> p h d", h=4)
                x1 = x3[0:rows, :, 0:64]
                x2 = x3[0:rows, :, 64:128]
                cosb = rope_ap[0:rows, j, 0:64].unsqueeze(1).to_broadcast([rows, 4, 64])
                sinb = rope_ap[0:rows, j, 64:128].unsqueeze(1).to_broadcast([rows, 4, 64])
                t1, k1 = tmpTok.next()
                t2, k2 = tmpTok.next()
                a1 = t1[0:rows, 0:256].rearrange("p (h d) -> p h d", h=4)
                a2 = t2[0:rows, 0:256].rearrange("p (h d) -> p h d", h=4)
                b1 = t1[0:rows, 256:512].rearrange("p (h d) -> p h d", h=4)
                b2 = t2[0:rows, 256:512].rearrange("p (h d) -> p h d", h=4)
                TT(a1, x1, cosb, ALU.mult, pk + [rope_key], [k1])
                TT(a2, x2, sinb, ALU.mult, pk + [rope_key], [k2])
                TT(b1, x1, sinb, ALU.mult, pk + [rope_key], [k1])
                TT(b2, x2, cosb, ALU.mult, pk + [rope_key], [k2])
                TT(qkr[0:rows, j, qk, :, 0:64], a1, a2, ALU.subtract, [k1, k2], [f"qkr{j}"], eng="pool")
                TT(qkr[0:rows, j, qk, :, 64:128], b1, b2, ALU.add, [k1, k2], [f"qkr{j}"], eng="pool")
        for vb in range(2):
            wv, wk = load_w("w_in_ab", 0, 1024 + vb * 512)
            for (j, rows) in tiles:
                ps, pk = PS.full()
                for kt in range(8):
                    MM(ps[0:rows, :], hnT[:, kt, j * 128:j * 128 + rows], wv[:, kt, :], kt == 0, kt == 7, [HNK, wk], pk)
                CP(v_tok[0:rows, j, vb * 512:(vb + 1) * 512], ps[0:rows, :], pk, [f"v{j}"], eng="act")
        for gb in range(2):
            wv, wk = load_w("w_in_ab", 0, 2048 + gb * 512)
            for (j, rows) in tiles:
                ps, pk = PS.full()
                for kt in range(8):
                    MM(ps[0:rows, :], hnT[:, kt, j * 128:j * 128 + rows], wv[:, kt, :], kt == 0, kt == 7, [HNK, wk], pk)
                ACT(sg_tok[0:rows, j, gb * 512:(gb + 1) * 512], ps[0:rows, :], AF.Silu, pk, [f"sg{j}"])
        for (j, rows) in tiles:
            qt, qtk = qT.next()
            kt_, ktk = kT.next()
            for (dst, dk_, qk) in ((qt, qtk, 0), (kt_, ktk, 1)):
                ps, pk = PS.half()
                psb = ps.bitcast(BF16)
                for h in range(4):
                    TR(psb[:, h * 128:h * 128 + rows], qkr[0:rows, j, qk, h, :], ident[0:rows, 0:rows],
                       [f"qkr{j}", "ident"], pk)
                CP(dst[:, :, 0:rows], psb.rearrange("p (h t) -> p h t", h=4)[:, :, 0:rows], pk, [dk_], eng="act")
            ps, pk = PS.full()
            p3 = ps.rearrange("p (h t) -> p h t", h=4)
            for h in range(4):
                MM(p3[0:rows, h, 0:rows], kt_[:, h, 0:rows], qt[:, h, 0:rows], True, True, [qtk, ktk], pk)
            st_, stk = sTb.next()
            TT(st_[0:rows, :, 0:rows], p3[0:rows, :, 0:rows], retD[0:rows, :, 0:rows], ALU.mult, pk + ["retD"], [stk])
            ob, obk = o_sb.next()
            for hp in range(2):
                ps, pk = PS.full()
                for hh in range(2):
                    h = hp * 2 + hh
                    osl = ps[0:rows, hh * 256:(hh + 1) * 256]
                    MM(osl, st_[0:rows, h, 0:rows], v_tok[0:rows, j, h * 256:(h + 1) * 256], True, False,
                       [stk, f"v{j}"], pk)
                    MM(osl, qt[:, h, 0:rows], RSb[:, h, :], False, True, [qtk, "RSb"], pk)
                TT(ob[0:rows, hp * 2:hp * 2 + 2, :], ps.rearrange("p (h e) -> p h e", h=2)[0:rows],
                   retS[0:rows, hp * 2:hp * 2 + 2].unsqueeze(2).to_broadcast([rows, 2, 256]), ALU.mult,
                   pk + ["retS"], [obk])
            kd, kdk = kdec.next()
            kdcol = 8 if meta_group else 4
            TT(kd[0:rows, :, :], qkr[0:rows, j, 1, :, :],
               retS[0:rows, kdcol:kdcol + 4].unsqueeze(2).to_broadcast([rows, 4, 128]), ALU.mult,
               [f"qkr{j}", "retS"], [kdk], eng="pool")
            for hp in range(2):
                ps, pk = PS.full()
                for hh in range(2):
                    h = hp * 2 + hh
                    MM(ps[:, hh * 256:(hh + 1) * 256], kd[0:rows, h, :], v_tok[0:rows, j, h * 256:(h + 1) * 256],
                       True, True, [kdk, f"v{j}"], pk)
                for hh in range(2):
                    h = hp * 2 + hh
                    STT(RS[:, h, :], RS[:, h, :], float(GAMMA[h] ** rows), ps[:, hh * 256:(hh + 1) * 256],
                        ALU.mult, ALU.add, ["RS"] + pk, ["RS"])
            CP(RSb[:], RS[:], ["RS"], ["RSb"], eng="act")
            cl, ck = colsA.next()
            S.op("dve", lambda e, ob=ob, cl=cl, rows=rows: e.reduce_sum(out=cl[0:rows, 0:4], in_=ob[0:rows], axis=AX.X),
                 reads=[obk], writes=[ck])
            TS(cl[0:rows, 0:4], cl[0:rows, 0:4], -1.0 / 256, None, ALU.mult, None, [ck], [ck])
            TT(ob[0:rows], ob[0:rows], cl[0:rows, 0:4].unsqueeze(2).to_broadcast([rows, 4, 256]), ALU.add,
               [obk, ck], [obk])
            jb, jk = junk_t, None
            for h in range(4):
                ACT(jb[0:rows, h * 256:(h + 1) * 256], ob[0:rows, h, :], AF.Square, [obk], [ck],
                    accum_out=cl[0:rows, 4 + h:5 + h])
            rstd_from_ss(cl[0:rows, 4:8], rows, 256, ck)
            yb_, ybk = ya.next()
            for h in range(4):
                STT(yb_[0:rows, h * 256:(h + 1) * 256], ob[0:rows, h, :], cl[0:rows, 4 + h:5 + h],
                    sg_tok[0:rows, j, h * 256:(h + 1) * 256], ALU.mult, ALU.mult, [obk, ck, f"sg{j}"], [ybk])
            to_feature(yb_, rows, 8, big, 0, j * 128, gnT[:, 0:8], [ybk], BIGK)
        for half in range(2):
            hs_list = []
            wv, wk = load_w("w_in_ab", 0, 3072 + half * 512)
            L = {}
            for ct in range(4):
                c = half * 4 + ct
                ps, pk = PS.half()
                for kt in range(8):
                    MM(ps[:, 0:n], wv[:, kt, ct * 128:(ct + 1) * 128], hnT[:, kt, 0:n], kt == 0, kt == 7, [HNK, wk], pk)
                xp, xk = f32a.next()
                CP(xp[:, 0:3], lru_carry[:, c, :], ["lru_carry"], [xk], eng="pool")
                CP(xp[:, 3:3 + n], ps[:, 0:n], pk, [xk], eng="act")
                L[ct] = dict(xp=xp, xk=xk)
            for ct in range(4):
                c = half * 4 + ct
                xp, xk = L[ct]["xp"], L[ct]["xk"]
                acc, ak = f32a.next()
                P = lambda i, c=c: lru_s[:, c, i:i + 1]
                TS(acc[:, 0:n], xp[:, 0:n], P(4), P(3), ALU.mult, ALU.add, [xk, "lru_s"], [ak])
                STT(acc[:, 0:n], xp[:, 1:1 + n], P(5), acc[:, 0:n], ALU.mult, ALU.add, [xk, ak, "lru_s"], [ak])
                STT(acc[:, 0:n], xp[:, 2:2 + n], P(6), acc[:, 0:n], ALU.mult, ALU.add, [xk, ak, "lru_s"], [ak])
                STT(acc[:, 0:n], xp[:, 3:3 + n], P(7), acc[:, 0:n], ALU.mult, ALU.add, [xk, ak, "lru_s"], [ak])
                CP(lru_carry[:, c, :], xp[:, n:n + 3], [xk], ["lru_carry"], eng="pool")
                L[ct].update(acc=acc, ak=ak)
            for ct in range(4):
                xb_, xbk = bf16a.next()
                CP(xb_[:, 0:n], L[ct]["acc"][:, 0:n], [L[ct]["ak"]], [xbk], eng="act")
                L[ct].update(xb=xb_, xbk=xbk)
            for ct in range(4):
                c = half * 4 + ct
                gates = []
                for wi in range(2):
                    ps2, pk2 = PS.half()
                    MM(ps2[:, 0:n], Wab[:, wi, c, :], L[ct]["xb"][:, 0:n], True, True, ["Wab", L[ct]["xbk"]], pk2)
                    gt_, gk_ = f32a.next()
                    ACT(gt_[:, 0:n], ps2[:, 0:n], AF.Sigmoid, pk2 + ["lru_s"], [gk_], bias=lru_s[:, c, 1 + wi:2 + wi])
                    gates.append((gt_, gk_))
                L[ct].update(gates=gates)
            for ct in range(4):
                c = half * 4 + ct
                (rg, rk), (ig, ik) = L[ct]["gates"]
                ACT(rg[:, 0:n], rg[:, 0:n], AF.Exp, [rk, "lru_sp8"], [rk], scale=lru_sp8[:, c:c + 1])
            for ct in range(4):
                (rg, rk), (ig, ik) = L[ct]["gates"]
                acc, ak = L[ct]["acc"], L[ct]["ak"]
                TT(ig[:, 0:n], ig[:, 0:n], acc[:, 0:n], ALU.mult, [ik, ak], [ik])
                TT(acc[:, 0:n], rg[:, 0:n], rg[:, 0:n], ALU.mult, [rk], [ak], eng="pool")
            for ct in range(4):
                acc, ak = L[ct]["acc"], L[ct]["ak"]
                ACT(acc[:, 0:n], acc[:, 0:n], AF.Sqrt, [ak, "epsc"], [ak], scale=-1.0, bias=epsc[:, 1:2])
            for ct in range(4):
                c = half * 4 + ct
                (rg, rk), (ig, ik) = L[ct]["gates"]
                acc, ak = L[ct]["acc"], L[ct]["ak"]
                TT(ig[:, 0:n], ig[:, 0:n], acc[:, 0:n], ALU.mult, [ik, ak], [ik])
                hb_, hbk = hsb.next()
                S.op("dve", lambda e, hb_=hb_, rg=rg, ig=ig, c=c: e.tensor_tensor_scan(
                    out=hb_[:, 0:n], data0=rg[:, 0:n], data1=ig[:, 0:n], initial=lru_h[:, c:c + 1],
                    op0=ALU.mult, op1=ALU.add), reads=[rk, ik, "lru_h"], writes=[hbk])
                CP(lru_h[:, c:c + 1], hb_[:, n - 1:n], [hbk], ["lru_h"], eng="pool")
                hs_list.append((hb_, hbk))
            wv, wk = load_w("w_in_ab", 0, 4096 + half * 512)
            for ct in range(4):
                c = half * 4 + ct
                ps, pk = PS.half()
                for kt in range(8):
                    MM(ps[:, 0:n], wv[:, kt, ct * 128:(ct + 1) * 128], hnT[:, kt, 0:n], kt == 0, kt == 7, [HNK, wk], pk)
                ge, gek = f32a.next()
                ACT(ge[:, 0:n], ps[:, 0:n], AF.Gelu_apprx_tanh, pk, [gek])
                hb_, hbk = hs_list[ct]
                TT(big[:, 8 + c, 0:n], ge[:, 0:n], hb_[:, 0:n], ALU.mult, [gek, hbk], [BIGK(8 + c)], eng=ew())
        proj_token_major("w_out_ab", 16, big, BIGK, tiles,
                         lambda j, rows, a, ak, b, bk: postnorm_add(Hc, HK, j, rows, a, ak, b, bk, postg[0], "postg0"))

    def layer1_mixer(Hc, HK, tiles, n, meta_group, prev_n):
        prenorm(Hc, HK, tiles, 16)
        wv, wk = load_w("w_in_cd", 0, 0)
        for ct in range(4):
            ps, pk = PS.half()
            for kt in range(8):
                MM(ps[:, 0:n], wv[:, kt, ct * 128:(ct + 1) * 128], hnT[:, kt, 0:n], kt == 0, kt == 7, [HNK, wk], pk)
            CP(uf[:, ct, 0:n], ps[:, 0:n], pk, [UFK], eng="act")
            CP(ub[:, ct, 0:n], ps[:, 0:n], pk, [f"ub{ct}"])
        if prev_n is not None:
            fr = 0 if prev_n == T else 1
            cb_ = s5_bnd[:, fr, 0, :]
            sb_ = s5_bnd[:, fr, 1, :]
            q1, qk1 = colsA.next()
            q2, qk2 = colsA.next()
            TT(q1[:, 0:16], s5_st[:, 0, :], cb_, ALU.mult, ["s5_st", "s5_bnd"], [qk1])
            TT(q2[:, 0:16], s5_st[:, 1, :], sb_, ALU.mult, ["s5_st", "s5_bnd"], [qk2])
            TT(s5_ini[:, 0, :], q1[:, 0:16], q2[:, 0:16], ALU.subtract, [qk1, qk2], ["s5_ini"])
            q3, qk3 = colsA.next()
            q4, qk4 = colsA.next()
            TT(q3[:, 0:16], s5_st[:, 0, :], sb_, ALU.mult, ["s5_st", "s5_bnd"], [qk3])
            TT(q4[:, 0:16], s5_st[:, 1, :], cb_, ALU.mult, ["s5_st", "s5_bnd"], [qk4])
            TT(s5_ini[:, 1, :], q3[:, 0:16], q4[:, 0:16], ALU.add, [qk3, qk4], ["s5_ini"])
        else:
            MEMSET(s5_ini[:], 0.0, ["s5_ini"])
        for ct in range(4):
            ms = [ct * 4 + q for q in range(4)]
            W = {}
            for m in ms:
                psr, pkr = PS.half()
                MM(psr[:, 0:n], Bblk[:, 0, m, :], ub[:, ct, 0:n], True, True, ["Bblk", f"ub{ct}"], pkr)
                psi, pki = PS.half()
                MM(psi[:, 0:n], Bblk[:, 1, m, :], ub[:, ct, 0:n], True, True, ["Bblk", f"ub{ct}"], pki)
                W[m] = dict(psr=psr, pkr=pkr, psi=psi, pki=pki, t=[f32a.next() for _ in range(4)],
                            Rc=Rcs[:, 0, m, 0:n], Rs=Rcs[:, 1, m, 0:n])
            for m in ms:
                w_ = W[m]
                (t1, k1), (t2, k2), (t3, k3), (t4, k4) = w_["t"]
                TT(t1[:, 0:n], w_["psr"][:, 0:n], w_["Rc"], ALU.mult, w_["pkr"] + ["Rcs"], [k1])
                TT(t4[:, 0:n], w_["psr"][:, 0:n], w_["Rs"], ALU.mult, w_["pkr"] + ["Rcs"], [k4])
                TT(t2[:, 0:n], w_["psi"][:, 0:n], w_["Rs"], ALU.mult, w_["pki"] + ["Rcs"], [k2])
                TT(t3[:, 0:n], w_["psi"][:, 0:n], w_["Rc"], ALU.mult, w_["pki"] + ["Rcs"], [k3])
            for m in ms:
                (t1, k1), (t2, k2), (t3, k3), (t4, k4) = W[m]["t"]
                TT(t1[:, 0:n], t1[:, 0:n], t2[:, 0:n], ALU.add, [k1, k2], [k1], eng="pool")
                TT(t3[:, 0:n], t3[:, 0:n], t4[:, 0:n], ALU.subtract, [k3, k4], [k3], eng="pool")
            for m in ms:
                (t1, k1), (t2, k2), (t3, k3), (t4, k4) = W[m]["t"]
                magb = s5_mag[:, m:m + 1].to_broadcast([128, n])
                S.op("dve", lambda e, t2=t2, t1=t1, m=m, magb=magb: e.tensor_tensor_scan(
                    out=t2[:, 0:n], data0=magb, data1=t1[:, 0:n], initial=s5_ini[:, 0, m:m + 1],
                    op0=ALU.mult, op1=ALU.add), reads=[k1, "s5_mag", "s5_ini"], writes=[k2])
                S.op("dve", lambda e, t4=t4, t3=t3, m=m, magb=magb: e.tensor_tensor_scan(
                    out=t4[:, 0:n], data0=magb, data1=t3[:, 0:n], initial=s5_ini[:, 1, m:m + 1],
                    op0=ALU.mult, op1=ALU.add), reads=[k3, "s5_mag", "s5_ini"], writes=[k4])
            for m in ms:
                (t1, k1), (t2, k2), (t3, k3), (t4, k4) = W[m]["t"]
                CP(s5_st[:, 0, m:m + 1], t2[:, n - 1:n], [k2], ["s5_st"], eng="pool")
                CP(s5_st[:, 1, m:m + 1], t4[:, n - 1:n], [k4], ["s5_st"], eng="pool")
            for m in ms:
                w_ = W[m]
                (t1, k1), (t2, k2), (t3, k3), (t4, k4) = w_["t"]
                hr, hrk = s5h.next()
                hi, hik = s5h.next()
                b1 = t1[:].bitcast(BF16)
                b3 = t3[:].bitcast(BF16)
                TT(hr[:, 0:n], t2[:, 0:n], w_["Rc"], ALU.mult, [k2, "Rcs"], [hrk])
                STT(b1[:, 0:n], t4[:, 0:n], -1.0, w_["Rs"], ALU.mult, ALU.mult, [k4, "Rcs"], [k1])
                TT(hi[:, 0:n], t2[:, 0:n], w_["Rs"], ALU.mult, [k2, "Rcs"], [hik])
                TT(b3[:, 0:n], t4[:, 0:n], w_["Rc"], ALU.mult, [k4, "Rcs"], [k3])
                w_["prods"] = [(0, hr, hrk), (0, b1, k1), (1, hi, hik), (1, b3, k3)]
            psy, pky = PS.half()
            for q, m in enumerate(ms):
                for pi, (ri, buf, key) in enumerate(W[m]["prods"]):
                    MM(psy[:, 0:n], Cblk[:, ri, m, :], buf[:, 0:n], q == 0 and pi == 0, q == 3 and pi == 3,
                       ["Cblk", key], pky)
            yt_, ytk = f32a.next()
            STT(yt_[:, 0:n], uf[:, ct, 0:n], s5dg[:, ct:ct + 1], psy[:, 0:n], ALU.mult, ALU.add,
                [UFK, "s5dg"] + pky, [ytk])
            ACT(ygf[:, ct, 0:n], yt_[:, 0:n], AF.Gelu_apprx_tanh, [ytk], [YGK])
            CP(ygb[:, ct, 0:n], ygf[:, ct, 0:n], [YGK], [f"ygb{ct}"], eng="pool")
        for co in range(4):
            ps, pk = PS.half()
            for ci_ in range(4):
                MM(ps[:, 0:n], gluW[:, ci_, co * 128:(co + 1) * 128], ygb[:, ci_, 0:n], ci_ == 0, ci_ == 3,
                   ["gluW", f"ygb{ci_}"], pk)
            sgm, sgk = f32a.next()
            ACT(sgm[:, 0:n], ps[:, 0:n], AF.Sigmoid, pk + ["s5dg"], [sgk], bias=s5dg[:, 4 + co:5 + co])
            TT(big[:, co, 0:n], ygf[:, co, 0:n], sgm[:, 0:n], ALU.mult, [YGK, sgk], [BIGK(co)])
        wq, wqk = load_w("w_in_cd", 0, 512)
        wf, wfk = load_w("w_in_cd", 0, 1024)
        bw = 16 if meta_group else 64
        htiles = [(0, NMETA)] if meta_group else [(c_, 64) for c_ in range(n // 64)]
        nblk = len(htiles)
        mid = bw // 2 - 1
        G = {}
        for h in range(4):
            psf, pkf = PS.half()
            for kt in range(8):
                MM(psf[:, 0:n], wf[:, kt, h * 128:(h + 1) * 128], hnT[:, kt, 0:n], kt == 0, kt == 7, [HNK, wfk], pkf)
            G[h] = dict(psf=psf, pkf=pkf)
        for h in range(4):
            ff, fk = f32a.next()
            ACT(ff[:, 0:n], G[h]["psf"][:, 0:n], AF.Sigmoid, G[h]["pkf"], [fk])
            TS(ff[:, 0:n], ff[:, 0:n], hg_lb[:, 4 + h:5 + h], hg_lb[:, h:h + 1], ALU.mult, ALU.add, [fk, "hg_lb"], [fk])
            G[h].update(ff=ff, fk=fk)
        for h in range(4):
            ff, fk = G[h]["ff"], G[h]["fk"]
            lf, lk = f32a.next()
            ACT(lf[:, 0:n], ff[:, 0:n], AF.Ln, [fk], [lk])
            TS(ff[:, 0:n], ff[:, 0:n], -1.0, 1.0, ALU.mult, ALU.add, [fk], [fk], eng="pool")
            G[h].update(lf=lf, lk=lk)
        for h in range(4):
            lf, lk = G[h]["lf"], G[h]["lk"]
            cum, cmk = f32a.next()
            S.op("dve", lambda e, cum=cum, lf=lf: e.tensor_tensor_scan(
                out=cum[:, 0:n], data0=restart[:, 0:n], data1=lf[:, 0:n], initial=0.0,
                op0=ALU.mult, op1=ALU.add), reads=[lk, "restart"], writes=[cmk])
            G[h].update(cum=cum, cmk=cmk)
        for h in range(4):
            cum, cmk = G[h]["cum"], G[h]["cmk"]
            c3 = cum[:, 0:n].rearrange("p (j t) -> p j t", t=bw)
            ACT(hgE[:, 0, h, 0:nblk], c3[:, :, mid], AF.Exp, [cmk], ["hgE"])
            ACT(hgE[:, 1, h, 0:nblk], c3[:, :, bw - 1], AF.Exp, [cmk], ["hgE"])
            cm, cmk2 = f32a.next()
            cm3 = cm[:, 0:n].rearrange("p (j t) -> p j t", t=bw)
            TT(cm3, c3, c3[:, :, mid:mid + 1].to_broadcast([128, nblk, bw]), ALU.subtract, [cmk], [cmk2])
            G[h].update(cm=cm, cmk2=cmk2)
        for h in range(4):
            lf, lk, cm, cmk2 = G[h]["lf"], G[h]["lk"], G[h]["cm"], G[h]["cmk2"]
            ACT(lf[:, 0:n], cm[:, 0:n], AF.Exp, [cmk2], [lk])
            ACT(cm[:, 0:n], cm[:, 0:n], AF.Exp, [cmk2], [cmk2], scale=-1.0)
        for h in range(4):
            psq, pkq = PS.half()
            for kt in range(8):
                MM(psq[:, 0:n], wq[:, kt, h * 128:(h + 1) * 128], hnT[:, kt, 0:n], kt == 0, kt == 7, [HNK, wqk], pkq)
            G[h].update(psq=psq, pkq=pkq)
        for h in range(4):
            ff, fk, lf, lk, cm, cmk2 = G[h]["ff"], G[h]["fk"], G[h]["lf"], G[h]["lk"], G[h]["cm"], G[h]["cmk2"]
            l3 = lf[:, 0:n].rearrange("p (j t) -> p j t", t=bw)
            CP(hgE[:, 2, h, 0:nblk], l3[:, :, bw - 1], [lk], ["hgE"], eng="pool")
            TT(hqT[:, h, 0:n], G[h]["psq"][:, 0:n], lf[:, 0:n], ALU.mult, G[h]["pkq"] + [lk], [f"hq{h}"])
            TT(hkT[:, h, 0:n], ff[:, 0:n], cm[:, 0:n], ALU.mult, [fk, cmk2], [f"hk{h}"], eng="pool")
        iv = v_tok[:].rearrange("p j (a c) -> p (j a) c", a=2)
        gv = sg_tok[:].rearrange("p j (a c) -> p (j a) c", a=2)
        IVK = lambda c_: f"v{c_ // 2}"
        GVK = lambda c_: f"sg{c_ // 2}"
        wi_, wik = load_w("w_in_cd", 0, 1536)
        for (c_, rows) in htiles:
            ps, pk = PS.full()
            for kt in range(8):
                MM(ps[0:rows, :], hnT[:, kt, c_ * 64:c_ * 64 + rows], wi_[:, kt, :], kt == 0, kt == 7, [HNK, wik], pk)
            CP(iv[0:rows, c_, :], ps[0:rows, :], pk, [IVK(c_)], eng="act")
        wg_, wgk = load_w("w_in_cd", 0, 2048)
        for (c_, rows) in htiles:
            ps, pk = PS.full()
            for kt in range(8):
                MM(ps[0:rows, :], hnT[:, kt, c_ * 64:c_ * 64 + rows], wg_[:, kt, :], kt == 0, kt == 7, [HNK, wgk], pk)
            ACT(gv[0:rows, c_, :], ps[0:rows, :], AF.Silu, pk, [GVK(c_)])
        HQK = [f"hq{h}" for h in range(4)]
        HKK = [f"hk{h}" for h in range(4)]
        for (j, rows) in htiles:
            t0 = j * 64
            ps, pk = PS.full()
            p3 = ps.rearrange("p (h t) -> p h t", h=4)
            for h in range(4):
                MM(p3[0:rows, h, 0:rows], hkT[:, h, t0:t0 + rows], hqT[:, h, t0:t0 + rows], True, True, HQK + HKK, pk)
            at, atk = sTb.next()
            STT(at[0:rows, :, 0:rows], p3[0:rows, :, 0:rows], 1e30,
                triu[0:rows, 0:rows].unsqueeze(1).to_broadcast([rows, 4, rows]), ALU.min, ALU.mult,
                pk + ["triu"], [atk])
            ps2, pk2 = PS.half()
            psb = ps2.bitcast(BF16)
            for h in range(4):
                TR(psb[0:rows, h * 128:(h + 1) * 128], hkT[:, h, t0:t0 + rows], ident[:, :], HKK + ["ident"], pk2)
            ktk_, ktkk = kdec.next()
            CP(ktk_[0:rows, :, :], psb.rearrange("p (h d) -> p h d", h=4)[0:rows], pk2, [ktkk], eng="act")
            TT(HSb[:], HS[:], hgE[:, 0, :, j:j + 1].to_broadcast([128, 4, 128]), ALU.mult, ["HS", "hgE"], ["HSb"])
            pso, pko = PS.full()
            o3 = pso.rearrange("p (h e) -> p h e", h=4)
            for h in range(4):
                MM(o3[0:rows, h, :], at[0:rows, h, 0:rows], iv[0:rows, j, h * 128:(h + 1) * 128], True, False,
                   [atk, IVK(j)], pko)
                MM(o3[0:rows, h, :], hqT[:, h, t0:t0 + rows], HSb[:, h, :], False, True, HQK + ["HSb"], pko)
            psk, pkk = PS.full()
            k3 = psk.rearrange("p (h e) -> p h e", h=4)
            for h in range(4):
                MM(k3[:, h, :], ktk_[0:rows, h, :], iv[0:rows, j, h * 128:(h + 1) * 128], True, True,
                   [ktkk, IVK(j)], pkk)
            ta, tak = tmpTok.next()
            TT(ta[:].rearrange("p (h e) -> p h e", h=4), k3, hgE[:, 2, :, j:j + 1].to_broadcast([128, 4, 128]),
               ALU.mult, pkk + ["hgE"], [tak])
            TT(HS[:], HS[:], hgE[:, 1, :, j:j + 1].to_broadcast([128, 4, 128]), ALU.mult, ["HS", "hgE"], ["HS"])
            TT(HS[:], HS[:], ta[:].rearrange("p (h e) -> p h e", h=4), ALU.add, ["HS", tak], ["HS"], eng="pool")
            jb, jk = junk_t, None
            cl, ck = colsA.next()
            for h in range(4):
                ACT(jb[0:rows, h * 128:(h + 1) * 128], o3[0:rows, h, :], AF.Square, pko, [ck],
                    accum_out=cl[0:rows, h:h + 1])
            rstd_from_ss(cl[0:rows, 0:4], rows, 128, ck)
            on, onk = tmpTok.next()
            TT(on[0:rows].rearrange("p (h e) -> p h e", h=4), o3[0:rows],
               cl[0:rows, 0:4].unsqueeze(2).to_broadcast([rows, 4, 128]), ALU.mult, pko + [ck], [onk])
            yb_, ybk = ya.next()
            TT(yb_[0:rows, 0:512], on[0:rows, :], gv[0:rows, j, :], ALU.mult, [onk, GVK(j)], [ybk], eng="pool")
            to_feature(yb_, rows, 4, big, 4, t0, gnT[:, 8:12], [ybk], BIGK)
        proj_token_major("w_out_cd", 8, big, BIGK, tiles,
                         lambda j, rows, a, ak, b, bk: postnorm_add(Hc, HK, j, rows, a, ak, b, bk, postg[0], "postg0"))

    last_store = None
    prev_n = None
    NGR = min(NG, KNG) if KPHASE == "all" else 0

    def group_geom(g):
        if g == 0:
            return NMETA, [(0, NMETA)], 0, 0
        f0 = (g - 1) * T
        return T, [(j, 128) for j in range(NT)], NMETA + f0, f0

    rope_bufs = {}

    def issue_loads(g):
        Hc, HK = H[g % 2], f"H{g % 2}"
        n, tiles, pos0, f0 = group_geom(g)
        rb, rkey = ropeT.next()
        rope_bufs[g] = (rb, rkey)
        if g == 0:
            DMA("pool", Hc[0:NMETA, 0, :], meta_d, "ld_x", [], [HK])
            DMA("pool", rb[0:NMETA, 0, :], cd["rope"][0:NMETA, :], "ld_rope", [], [rkey])
        else:
            DMA("pool", Hc[:, :, :], x_d[f0:f0 + T, :].rearrange("(j p) d -> p j d", p=128), "ld_x", [], [HK])
            DMA("pool", rb[:, :, :], cd["rope"][pos0:pos0 + T, :].rearrange("(j p) d -> p j d", p=128), "ld_rope",
                [], [rkey])

    if NGR > 0:
        issue_loads(0)
    for g in range(NGR):
        Hc = H[g % 2]
        HK = f"H{g % 2}"
        meta_group = g == 0
        n, tiles, pos0, f0 = group_geom(g)
        rb, rkey = rope_bufs.pop(g)

        def dbg(slot):
            if DEBUG and (g < 3):
                if meta_group:
                    DMA("pool", dbg_d[slot, 0:NMETA, :], Hc[0:NMETA, 0, :], "st_dbg", [HK], [])
                else:
                    DMA("pool", dbg_d[slot, pos0:pos0 + T, :].rearrange("(j p) d -> p j d", p=128), Hc[:, :, :],
                        "st_dbg", [HK], [])

        load_postg(0)
        layer0_mixer(Hc, HK, tiles, n, meta_group, rb, rkey, do_prenorm=(g == 0 or KSTOP < 3))
        has_next = g + 1 < NGR
        if has_next:
            issue_loads(g + 1)
        dbg(0)
        if KSTOP >= 1:
            mlp(0, Hc, HK, tiles, n)
            dbg(1)
        if KSTOP >= 2:
            load_postg(1)
            layer1_mixer(Hc, HK, tiles, n, meta_group, prev_n)
            dbg(2)
        if KSTOP >= 3:
            hook = None
            if has_next:
                nH, nHK = H[(g + 1) % 2], f"H{(g + 1) % 2}"
                ntiles = group_geom(g + 1)[1]
                hook = lambda nH=nH, nHK=nHK, ntiles=ntiles: prenorm(nH, nHK, ntiles, 0)
            mlp(1, Hc, HK, tiles, n, hook=hook)
            dbg(3)
        if not meta_group:
            last_store = DMA("pool", out_d[f0:f0 + T, :].rearrange("(j p) d -> p j d", p=128), Hc[:, :, :], "st_out",
                             [HK], [])
        prev_n = n
    fin = [d for d in [last_store, S.chan_last.get("st_dbg"), S.chan_last.get("ld_small"), S.chan_last.get("st_scr0")] if d is not None]
    S.op("sp", lambda e: e.nop(), extra_deps=fin)
    print("instr counts", {e: len(S.lists[e]) for e in ENGS}, flush=True)
    S.emit()
    S.stats = {e: max([i.val for i in S.lists[e] if i.chan is None and i.needs_inc] or [0]) for e in ENGS}
    S.stats.update({c: lst[-1].val for c, lst in S.chan_ins.items()})
    print("max sem values", S.stats, flush=True)
    return nc, consts


_CACHE = {}


def kernel(**inputs):
    if "prog" not in _CACHE:
        _CACHE["prog"] = build_program()
    nc, consts = _CACHE["prog"]
    lay = host_layouts(inputs)
    common = dict(lay)
    for n, v in consts.items():
        common["c_" + n] = v
    x = np.asarray(inputs["x"], dtype=np.float32)
    in_maps = []
    for c in range(8):
        m = dict(common)
        m["x"] = np.ascontiguousarray(x[c % 4])
        in_maps.append(m)
    res = run_bass_kernel_spmd(nc, in_maps, core_ids=list(range(8)))
    out = np.stack([np.asarray(res.results[b]["out"], dtype=np.float32) for b in range(4)], axis=0)
    if DEBUG:
        kernel.dbg = [np.asarray(res.results[b]["dbg"]) for b in range(4)]
    return out
```

```python
import numpy as np
import concourse.bass as bass
import concourse.mybir as mybir
from concourse.bass_utils import run_bass_kernel_spmd

F32 = mybir.dt.float32
BF16 = mybir.dt.bfloat16
I32 = mybir.dt.int32
AF = mybir.ActivationFunctionType
ALU = mybir.AluOpType
AX = mybir.AxisListType

T = 256
NT = T // 128
SEQ = 4096
NMETA = 16
D = 1024
DFF = 4096
EPS = 1e-6
NG = 1 + SEQ // T
import os
DEBUG = bool(int(os.environ.get("KDEBUG", "0")))
KNG = int(os.environ.get("KNG", "1000"))
KSTOP = int(os.environ.get("KSTOP", "3"))
KPHASE = os.environ.get("KPHASE", "all")

ENGS = ("pe", "act", "dve", "pool", "sp")


class Ins:
    __slots__ = ("eng", "fn", "waits", "idx", "needs_inc", "chan", "clock", "ninc", "val", "tag")


class Sched:
    def __init__(self, nc):
        self.nc = nc
        self.lists = {e: [] for e in ENGS}
        self.lastw = {}
        self.readers = {}
        self.clock = {e: {} for e in ENGS}
        self.chan_last = {}
        self.chan_ins = {}

    def _need(self, eng, dep, waits):
        src = dep.chan if dep.chan is not None else dep.eng
        if dep.chan is None and dep.eng == "pe" and eng == "pe":
            return
        if self.clock[eng].get(src, -1) >= dep.idx:
            return
        waits[src] = max(waits.get(src, -1), dep.idx)

    def op(self, eng, fn, reads=(), writes=(), chan=None, extra_deps=(), ninc=1):
        ins = Ins()
        ins.eng, ins.fn, ins.chan, ins.needs_inc, ins.ninc = eng, fn, chan, False, ninc
        ins.tag = getattr(self, "tag", "")
        psr = [k for k in reads if k[:2] == "ps" and k[2:].isdigit()]
        if psr:
            reads = [k for k in reads if k not in psr]
            writes = list(writes) + [k for k in psr if k not in writes]
        deps = []
        for k in reads:
            deps.extend(self.lastw.get(k, ()))
        for k in writes:
            deps.extend(self.lastw.get(k, ()))
            deps.extend(self.readers.get(k, ()))
        deps.extend(extra_deps)
        if chan is not None and chan in self.chan_last:
            deps.append(self.chan_last[chan])
        waits = {}
        for d in deps:
            self._need(eng, d, waits)
        ins.waits = []
        ck = self.clock[eng]
        for src, idx in waits.items():
            prod = self.lists[src][idx] if src in ENGS else self.chan_ins[src][idx]
            prod.needs_inc = True
            ins.waits.append(prod)
            for s, v in prod.clock.items():
                if ck.get(s, -1) < v:
                    ck[s] = v
        if chan is not None:
            lst = self.chan_ins.setdefault(chan, [])
            ins.idx = len(lst)
            lst.append(ins)
            self.chan_last[chan] = ins
            ins.clock = dict(ck)
            ins.clock[chan] = ins.idx
            self.lists[eng].append(ins)
        else:
            ins.idx = len(self.lists[eng])
            ins.clock = dict(ck)
            ins.clock[eng] = ins.idx
            self.lists[eng].append(ins)
        for k in reads:
            self.readers.setdefault(k, []).append(ins)
        me = chan if chan is not None else eng
        for k in writes:
            lst = [w for w in self.lastw.get(k, ()) if (w.chan if w.chan is not None else w.eng) != me]
            lst.append(ins)
            self.lastw[k] = lst
            self.readers[k] = []
        return ins

    def emit(self):
        nc = self.nc
        sems = {}
        for e in ("pe", "act", "dve", "pool"):
            sems[e] = nc.alloc_semaphore("s_" + e)
        for c in self.chan_ins:
            sems[c] = nc.alloc_semaphore("c_" + str(c))
        for e in ENGS:
            r = 0
            for ins in self.lists[e]:
                if ins.chan is None and ins.needs_inc:
                    r += 1
                    ins.val = r
        for c, lst in self.chan_ins.items():
            v = 0
            for ins in lst:
                v += 16 * ins.ninc
                ins.val = v

        def run(e):
            def body(eng):
                for ins in self.lists[e]:
                    for p in ins.waits:
                        src = p.chan if p.chan is not None else p.eng
                        eng.wait_ge(sems[src], p.val)
                    r = ins.fn(eng)
                    rl = r if isinstance(r, (list, tuple)) else [r]
                    if ins.chan is not None:
                        assert len(rl) == ins.ninc
                        for x in rl:
                            x.then_inc(sems[ins.chan], 16)
                    elif ins.needs_inc:
                        rl[-1].then_inc(sems[e], 1)
            return body

        with nc.Block() as block:
            block.tensor(run("pe"))
            block.scalar(run("act"))
            block.vector(run("dve"))
            block.gpsimd(run("pool"))
            block.sync(run("sp"))


class DB:
    def __init__(self, nc, name, shape, dtype, nbuf=2):
        self.bufs = [nc.alloc_sbuf_tensor(f"sb_{name}_{i}", list(shape), dtype) for i in range(nbuf)]
        self.keys = [f"{name}_{i}" for i in range(nbuf)]
        self.i = 0

    def next(self):
        i = self.i
        self.i = (i + 1) % len(self.bufs)
        return self.bufs[i], self.keys[i]


class PsumPool:
    def __init__(self, nc):
        self.banks = [nc.alloc_psum_tensor(f"ps{i}", [128, 512], F32) for i in range(8)]
        self.i = 0
        self.j = 0
        self.live = set()

    def half(self):
        while True:
            u = self.i
            self.i = (u + 1) % 16
            h, b = divmod(u, 8)
            if b not in self.live:
                break
        return self.banks[b][:, h * 256:(h + 1) * 256], [f"ps{b}"]

    def full(self):
        b = self.j
        self.j = (b + 1) % 8
        self.last_full = b
        return self.banks[b][:, :], [f"ps{b}"]


GAMMA = [1.0 - 2.0 ** (-5.0 - h) for h in range(4)]


def host_constants():
    c = {}
    c["ident"] = np.eye(128, dtype=np.float32)
    L = NMETA + SEQ
    pos = np.arange(L, dtype=np.float32)
    inv = (10000.0 ** (-np.arange(64, dtype=np.float32) / 64)).astype(np.float32)
    ang = (pos[:, None] * inv[None, :]).astype(np.float32)
    c["rope"] = np.concatenate([np.cos(ang), np.sin(ang)], axis=1).astype(np.float32)
    s = np.arange(128)[:, None]
    t = np.arange(128)[None, :]
    retD = np.zeros((128, 4, 128), np.float64)
    retG = np.zeros((128, 4), np.float64)
    retKD = np.zeros((128, 4), np.float64)
    retKDm = np.zeros((128, 4), np.float64)
    for h in range(4):
        g = GAMMA[h]
        same = (s // 64) == (t // 64)
        earlier = (s // 64) < (t // 64)
        w = np.where(same, g ** np.abs(t - s), np.where(earlier, g ** (t - s).clip(0), 0.0))
        retD[:, h, :] = w / (g ** (t + 1.0))
        retG[:, h] = (128 ** -0.5) * g ** (np.arange(128) + 1.0)
        retKD[:, h] = g ** (127.0 - np.arange(128))
        retKDm[:16, h] = g ** (15.0 - np.arange(16))
    c["retD"] = retD.astype(np.float32)
    c["retS"] = np.concatenate([retG, retKD, retKDm], axis=1).astype(np.float32)
    c["triu"] = (s <= t).astype(np.float32)
    rs = np.ones((128, T), np.float32)
    rs[:, ::64] = 0.0
    c["restart"] = rs
    c["iota"] = np.broadcast_to(np.arange(T, dtype=np.float32), (128, T)).copy()
    mb = np.zeros((128, 16, 8), np.float32)
    for m in range(16):
        for gl in range(2):
            mb[gl * 64:(gl + 1) * 64, m, (2 * m + gl) % 8] = 1.0
    c["maskB"] = mb
    mc = np.zeros((128, 4, 2, 2), np.float32)
    for mm in range(4):
        for gl in range(2):
            g8 = 2 * mm + gl
            mc[g8 * 16:(g8 + 1) * 16, mm, gl, 0] = 1.0
            mc[g8 * 16:(g8 + 1) * 16, mm, gl, 1] = -1.0
    c["maskC"] = mc
    return c


def host_layouts(inp):
    f = lambda a: np.ascontiguousarray(np.asarray(a, dtype=np.float32))
    o = {}
    o["meta"] = f(inp["meta"])
    o["w_in_ab"] = f(inp["w_in_ab"][0])
    o["w_out_ab"] = f(inp["w_out_ab"][0])
    o["w1"] = f(inp["mlp_w1"])
    o["w2"] = f(inp["mlp_w2"])
    o["w_in_cd"] = f(inp["w_in_cd"][0])
    o["w_out_cd"] = f(inp["w_out_cd"][0])
    o["glu_w"] = f(inp["s5_glu_w"][0])
    ng = np.asarray(inp["norm_g"], np.float32)
    pre = ng[:, [0, 2], :].reshape(2, 2, 8, 128)
    o["pgT"] = f(pre.transpose(3, 0, 1, 2).reshape(128, 32))
    o["postg"] = f(ng[:, [1, 3], :].reshape(4, 1024))
    gn = np.concatenate([np.asarray(inp["ret_gn"][0], np.float32).reshape(8, 128).T,
                         np.asarray(inp["hg_gn"][0], np.float32).reshape(4, 128).T], axis=1)
    o["gnT"] = f(gn)
    fm8 = lambda v: np.asarray(v, np.float32).reshape(8, 128).T
    cw = np.asarray(inp["rg_conv_w"][0], np.float32)
    lru = np.stack([fm8(inp["rg_lam"][0]), fm8(inp["rg_ba"][0]), fm8(inp["rg_bi"][0]), fm8(inp["rg_conv_b"][0]),
                    fm8(cw[0]), fm8(cw[1]), fm8(cw[2]), fm8(cw[3])], axis=2)
    o["lru_s"] = f(lru)
    o["rg_wa"] = f(inp["rg_wa"][0])
    o["rg_wi"] = f(inp["rg_wi"][0])
    hl = np.asarray(inp["hg_lb_logits"], np.float32).reshape(2, 4, 128)
    o["hgl"] = f(hl.transpose(2, 0, 1).reshape(128, 8))
    st = lambda a: np.asarray(a, np.float32).reshape(16, 2, 64).transpose(1, 2, 0).reshape(128, 16)
    ldt = np.repeat(np.asarray(inp["s5_log_dt"][0], np.float32)[:, None], 64, axis=1)
    o["s5s"] = f(np.stack([st(inp["s5_a_re_log"][0]), st(inp["s5_a_im"][0]), st(ldt)], axis=2))
    sb = lambda a: np.asarray(a, np.float32).reshape(16, 2, 64, 16).transpose(1, 2, 0, 3).reshape(128, 16, 16)
    o["s5b"] = f(np.stack([sb(inp["s5_b_re"][0]), sb(inp["s5_b_im"][0])], axis=2))
    sc = lambda a: np.asarray(a, np.float32).reshape(4, 8, 16, 64).transpose(1, 2, 0, 3).reshape(128, 4, 64)
    o["s5c"] = f(np.stack([sc(inp["s5_c_re"][0]), sc(inp["s5_c_im"][0])], axis=2))
    sd = np.asarray(inp["s5_d"][0], np.float32).reshape(4, 128).T
    gb = np.asarray(inp["s5_glu_b"][0], np.float32).reshape(4, 128).T
    o["s5dg"] = f(np.concatenate([sd, gb], axis=1))
    return o


def build_program():
    nc = bass.Bass("TRN2", target_bir_lowering=False)
    S = Sched(nc)
    consts = host_constants()

    def din(name, shape, dt=F32):
        return nc.dram_tensor(name, list(shape), dt, kind="ExternalInput").ap()

    x_d = din("x", [SEQ, D])
    meta_d = din("meta", [NMETA, D])
    wdefs = [("w_in_ab", 1024, 5120), ("w_out_ab", 2048, 1024), ("w1_0", 1024, 4096), ("w2_0", 4096, 1024),
             ("w_in_cd", 1024, 2560), ("w_out_cd", 1024, 1024), ("w1_1", 1024, 4096), ("w2_1", 4096, 1024),
             ("glu_w", 512, 512)]
    w1_d = din("w1", [2, 1024, 4096])
    w2_d = din("w2", [2, 4096, 1024])
    wsrc = {"w_in_ab": din("w_in_ab", [1024, 5120]), "w_out_ab": din("w_out_ab", [2048, 1024]),
            "w1_0": w1_d[0], "w1_1": w1_d[1], "w2_0": w2_d[0], "w2_1": w2_d[1],
            "w_in_cd": din("w_in_cd", [1024, 2560]), "w_out_cd": din("w_out_cd", [1024, 1024]),
            "glu_w": din("glu_w", [512, 512])}
    wscr = {n: (nc.dram_tensor("scr_" + n, [r, c], BF16, kind="Internal").ap() if n == "glu_w" else
                nc.dram_tensor("scr_" + n, [r // 1024, c // 512, 128, 8, 512], BF16, kind="Internal").ap())
            for n, r, c in wdefs}
    small = {}
    for n, shp in [("pgT", [128, 32]), ("postg", [4, 1024]), ("gnT", [128, 12]), ("lru_s", [128, 8, 8]),
                   ("rg_wa", [16, 64, 64]), ("rg_wi", [16, 64, 64]), ("hgl", [128, 8]), ("s5s", [128, 16, 3]),
                   ("s5b", [128, 16, 2, 16]), ("s5c", [128, 4, 2, 64]), ("s5dg", [128, 8])]:
        small[n] = din(n, shp)
    cd = {n: din("c_" + n, list(v.shape)) for n, v in consts.items()}
    out_d = nc.dram_tensor("out", [SEQ, D], F32, kind="ExternalOutput").ap()
    dbg_d = nc.dram_tensor("dbg", [4, NMETA + SEQ, D], F32, kind="ExternalOutput").ap() if DEBUG else None

    def sb(name, shape, dt=F32):
        return nc.alloc_sbuf_tensor("sb_" + name, list(shape), dt)

    def MM(out, lhsT, rhs, start, stop, r, w):
        S.op("pe", lambda e: e.matmul(out, lhsT=lhsT, rhs=rhs, start=start, stop=stop), reads=r, writes=w)

    def TR(out, in_, idn, r, w):
        S.op("pe", lambda e: e.transpose(out, in_, idn), reads=r, writes=w)

    def ACT(out, in_, func, r, w, **kw):
        S.op("act", lambda e: e.activation(out=out, in_=in_, func=func, **kw), reads=r, writes=w)

    def TT(out, a, b, op, r, w, eng="dve"):
        S.op(eng, lambda e: e.tensor_tensor(out=out, in0=a, in1=b, op=op), reads=r, writes=w)

    def TS(out, a, s1, s2, op0, op1, r, w, eng="dve"):
        if s2 is None:
            S.op(eng, lambda e: e.tensor_scalar(out=out, in0=a, scalar1=s1, scalar2=None, op0=op0), reads=r, writes=w)
        else:
            S.op(eng, lambda e: e.tensor_scalar(out=out, in0=a, scalar1=s1, scalar2=s2, op0=op0, op1=op1),
                 reads=r, writes=w)

    def STT(out, a, s, b, op0, op1, r, w, eng="dve"):
        S.op(eng, lambda e: e.scalar_tensor_tensor(out=out, in0=a, scalar=s, in1=b, op0=op0, op1=op1),
             reads=r, writes=w)

    def CP(out, a, r, w, eng="dve"):
        if eng == "act":
            S.op("act", lambda e: e.copy(out=out, in_=a), reads=r, writes=w)
        else:
            S.op(eng, lambda e: e.tensor_copy(out=out, in_=a), reads=r, writes=w)

    def DMA(q, out, in_, chan, r, w):
        return S.op(q, lambda e: e.dma_start(out=out, in_=in_), reads=r, writes=w, chan=chan)

    def MEMSET(ap, val, w, eng="dve"):
        S.op(eng, lambda e: e.memset(ap, val), writes=w)

    PS = PsumPool(nc)
    _rr = [0]

    def ew():
        _rr[0] += 1
        return "pool" if _rr[0] % 3 == 0 else "dve"

    ident_f = sb("ident_f", [128, 128])
    ident = sb("ident", [128, 128], BF16)
    H = [sb(f"H{i}", [128, NT, D]) for i in range(2)]
    hnT = sb("hnT", [128, 8, T], BF16)
    big = sb("big", [128, 32, T], BF16)
    NSLOT = 3
    Wr = [sb(f"W{i}", [128, 8 * 512], BF16) for i in range(NSLOT)]
    wslot = [0]
    pgT = sb("pgT", [128, 32])
    gnT = sb("gnT", [128, 12])
    postg = [sb(f"postg{i}", [128, D]) for i in range(2)]
    epsc = sb("epsc", [128, 2])
    ropeT = DB(nc, "rope", [128, NT, 128], F32, 2)
    retD = sb("retD", [128, 4, 128])
    retS = sb("retS", [128, 12])
    triu = sb("triu", [128, 128])
    restart = sb("restart", [128, T])
    RS = sb("RS", [128, 4, 256])
    RSb = sb("RSb", [128, 4, 256], BF16)
    lru_s = sb("lru_s", [128, 8, 8])
    lru_sp8 = sb("lru_sp8", [128, 8])
    Wab = sb("Wab", [128, 2, 8, 128], BF16)
    lru_carry = sb("lru_carry", [128, 8, 3])
    lru_h = sb("lru_h", [128, 8])
    hg_lb = sb("hg_lb", [128, 8])
    HS = sb("HS", [128, 4, 128])
    HSb = sb("HSb", [128, 4, 128], BF16)
    s5_mag = sb("s5_mag", [128, 16])
    s5_bnd = sb("s5_bnd", [128, 2, 2, 16])
    s5_st = sb("s5_st", [128, 2, 16])
    s5_ini = sb("s5_ini", [128, 2, 16])
    Bblk = sb("Bblk", [128, 2, 16, 128], BF16)
    Cblk = sb("Cblk", [128, 2, 16, 128], BF16)
    Rcs = sb("Rcs", [128, 2, 16, T], BF16)
    s5dg = sb("s5dg", [128, 8])
    gluW = sb("gluW", [128, 4, 512], BF16)

    f32a = DB(nc, "f32a", [128, T + 4], F32, 16)
    bf16a = DB(nc, "bf16a", [128, T], BF16, 4)
    hsb = DB(nc, "hsb", [128, T], F32, 5)
    s5h = DB(nc, "s5h", [128, T], BF16, 8)
    hn_tok = DB(nc, "hn_tok", [128, D], BF16, 1)
    junk_t = sb("junk", [128, D], BF16)
    colsA = DB(nc, "colsA", [128, 16], F32, 8)
    tmpTok = DB(nc, "tmpTok", [128, 512], F32, 4)
    qkr = sb("qkr", [128, NT, 2, 4, 128], BF16)
    v_tok = sb("v_tok", [128, NT, 1024], BF16)
    sg_tok = sb("sg_tok", [128, NT, 1024], BF16)
    qT = DB(nc, "qT", [128, 4, 128], BF16, 2)
    kT = DB(nc, "kT", [128, 4, 128], BF16, 2)
    sTb = DB(nc, "sTb", [128, 4, 128], BF16, 2)
    kdec = DB(nc, "kdec", [128, 4, 128], BF16, 2)
    o_sb = DB(nc, "o_sb", [128, 4, 256], F32, 2)
    ya = DB(nc, "ya", [128, D], BF16, 2)
    assert T == 256
    uf = o_sb.bufs[0]
    UFK = o_sb.keys[0]
    ub = sb("ub", [128, 4, T], BF16)
    ygf = o_sb.bufs[1]
    YGK = o_sb.keys[1]
    ygb = sb("ygb", [128, 4, T], BF16)
    hqT = sb("hqT", [128, 4, T], BF16)
    hkT = sb("hkT", [128, 4, T], BF16)
    hgE = sb("hgE", [128, 3, 4, T // 64])

    BIGK = lambda f: f"big.{f}"
    HNK = "hnT"

    def load(dst_ap, src_ap, key, q="act"):
        DMA(q, dst_ap, src_ap, "ld_small", [], [key])

    load(ident_f[:], cd["ident"], "ident_f")
    CP(ident[:], ident_f[:], ["ident_f"], ["ident"])
    load(pgT[:], small["pgT"], "pgT")
    load(gnT[:], small["gnT"], "gnT")
    load(retD[:], cd["retD"], "retD")
    load(retS[:], cd["retS"], "retS")
    load(triu[:], cd["triu"], "triu")
    load(restart[:], cd["restart"], "restart")
    load(lru_s[:], small["lru_s"], "lru_s")
    load(s5dg[:], small["s5dg"], "s5dg")
    MEMSET(epsc[:, 0:1], EPS, ["epsc"])
    MEMSET(epsc[:, 1:2], 1.0, ["epsc"])
    MEMSET(RS[:], 0.0, ["RS"])
    MEMSET(RSb[:], 0.0, ["RSb"])
    MEMSET(HS[:], 0.0, ["HS"])
    MEMSET(HSb[:], 0.0, ["HSb"])
    MEMSET(lru_carry[:], 0.0, ["lru_carry"])
    MEMSET(lru_h[:], 0.0, ["lru_h"])
    MEMSET(s5_st[:], 0.0, ["s5_st"])

    ACT(lru_sp8[:], lru_s[:, :, 0], AF.Exp, ["lru_s"], ["lru_sp8"], scale=-1.0)
    ACT(lru_sp8[:], lru_sp8[:], AF.Ln, ["lru_sp8", "epsc"], ["lru_sp8"], bias=epsc[:, 1:2])
    TS(lru_sp8[:], lru_sp8[:], -8.0, None, ALU.mult, None, ["lru_sp8"], ["lru_sp8"])
    for wi, nm in enumerate(("rg_wa", "rg_wi")):
        wabf = o_sb.bufs[wi][:].rearrange("p a (c d) -> p (a c) d", d=128)
        wk_ = o_sb.keys[wi]
        MEMSET(wabf, 0.0, [wk_])
        src = small[nm].rearrange("(ct bl) i j -> bl i ct j", bl=2)
        for bl in range(2):
            DMA("act", wabf[bl * 64:(bl + 1) * 64, :, bl * 64:(bl + 1) * 64], src[bl], "ld_small", [], [wk_])
        CP(Wab[:, wi, :, :], wabf, [wk_], ["Wab"])

    hgl = sb("hgl", [128, 8])
    load(hgl[:], small["hgl"], "hgl")
    TT(hg_lb[:, 0:4], hgl[:, 0:4], hgl[:, 4:8], ALU.subtract, ["hgl"], ["hg_lb"])
    ACT(hg_lb[:, 0:4], hg_lb[:, 0:4], AF.Sigmoid, ["hg_lb"], ["hg_lb"])
    TS(hg_lb[:, 4:8], hg_lb[:, 0:4], -1.0, 1.0, ALU.mult, ALU.add, ["hg_lb"], ["hg_lb"])

    s5s = sb("s5s", [128, 16, 3])
    load(s5s[:], small["s5s"], "s5s")
    pp = sb("s5pp", [128, 12, 16])
    PPK = ["s5pp"]
    iota = f32a.bufs[0][:, 0:T]
    IOK = f32a.keys[0]
    DMA("act", iota, cd["iota"], "ld_small", [], [IOK])
    DT_, ARE, AIM, TH, MAG, COS, SIN, NRE, DEN, ZRE, ZIM, TMP = range(12)
    ACT(pp[:, DT_, :], s5s[:, :, 2], AF.Exp, ["s5s"], PPK)
    ACT(pp[:, ARE, :], s5s[:, :, 0], AF.Exp, ["s5s"], PPK)
    TS(pp[:, ARE, :], pp[:, ARE, :], -1.0, None, ALU.mult, None, PPK, PPK)
    CP(pp[:, AIM, :], s5s[:, :, 1], ["s5s"], PPK)
    TT(pp[:, TH, :], pp[:, DT_, :], pp[:, AIM, :], ALU.mult, PPK, PPK)
    TT(pp[:, TMP, :], pp[:, DT_, :], pp[:, ARE, :], ALU.mult, PPK, PPK)
    ACT(pp[:, MAG, :], pp[:, TMP, :], AF.Exp, PPK, PPK)
    CP(s5_mag[:], pp[:, MAG, :], PPK, ["s5_mag"])

    TWO_PI = 2.0 * np.pi
    redi = f32a.bufs[1][:].bitcast(I32)[:, 0:T]
    redf = f32a.bufs[2][:, 0:T]
    RIK, RFK = f32a.keys[1], f32a.keys[2]

    def sincos(out_sin, out_cos, ang_ap, shape_n, keys_r, keys_w):
        ri = redi[:, 0:shape_n]
        rf = redf[:, 0:shape_n]
        for out_ap, shift in ((out_sin, 0.0), (out_cos, 0.5 * np.pi)):
            if out_ap is None:
                continue
            TS(ri, ang_ap, shift, 1.0 / TWO_PI, ALU.add, ALU.mult, keys_r, [RIK])
            STT(rf, ri, -TWO_PI, ang_ap, ALU.mult, ALU.add, [RIK] + keys_r, [RFK])
            if shift != 0.0:
                TS(rf, rf, shift, None, ALU.add, None, [RFK], [RFK])
            TS(rf, rf, 3.1415925, -3.1415925, ALU.min, ALU.max, [RFK], [RFK])
            ACT(out_ap, rf, AF.Sin, [RFK], keys_w)

    sincos(pp[:, SIN, :], pp[:, COS, :], pp[:, TH, :], 16, PPK, PPK)
    ang2 = sb("ang2", [128, 2, 16])
    TS(ang2[:, 0, :], pp[:, TH, :], float(T), None, ALU.mult, None, PPK, ["ang2"])
    TS(ang2[:, 1, :], pp[:, TH, :], float(NMETA), None, ALU.mult, None, PPK, ["ang2"])
    bndt = sb("bndt", [128, 2, 2, 16])
    for fr in range(2):
        sincos(bndt[:, fr, 1, :], bndt[:, fr, 0, :], ang2[:, fr, :], 16, ["ang2"], ["bndt"])
    CP(s5_bnd[:], bndt[:], ["bndt"], ["s5_bnd"])
    angT = f32a.bufs[3][:, 0:T]
    ATK = f32a.keys[3]
    for m in range(16):
        TS(angT, iota, pp[:, TH, m:m + 1], None, ALU.mult, None, PPK + [IOK], [ATK])
        sincos(Rcs[:, 1, m, :], Rcs[:, 0, m, :], angT, T, [ATK], ["Rcs"])
    TT(pp[:, COS, :], pp[:, COS, :], pp[:, MAG, :], ALU.mult, PPK, PPK)
    TT(pp[:, SIN, :], pp[:, SIN, :], pp[:, MAG, :], ALU.mult, PPK, PPK)
    TS(pp[:, NRE, :], pp[:, COS, :], -1.0, None, ALU.add, None, PPK, PPK)
    TT(pp[:, DEN, :], pp[:, ARE, :], pp[:, ARE, :], ALU.mult, PPK, PPK)
    TT(pp[:, TMP, :], pp[:, AIM, :], pp[:, AIM, :], ALU.mult, PPK, PPK)
    TT(pp[:, DEN, :], pp[:, DEN, :], pp[:, TMP, :], ALU.add, PPK, PPK)
    S.op("dve", lambda e: e.reciprocal(out=pp[:, DEN, :], in_=pp[:, DEN, :]), reads=PPK, writes=PPK)
    TT(pp[:, ZRE, :], pp[:, NRE, :], pp[:, ARE, :], ALU.mult, PPK, PPK)
    TT(pp[:, TMP, :], pp[:, SIN, :], pp[:, AIM, :], ALU.mult, PPK, PPK)
    TT(pp[:, ZRE, :], pp[:, ZRE, :], pp[:, TMP, :], ALU.add, PPK, PPK)
    TT(pp[:, ZRE, :], pp[:, ZRE, :], pp[:, DEN, :], ALU.mult, PPK, PPK)
    TT(pp[:, ZIM, :], pp[:, SIN, :], pp[:, ARE, :], ALU.mult, PPK, PPK)
    TT(pp[:, TMP, :], pp[:, NRE, :], pp[:, AIM, :], ALU.mult, PPK, PPK)
    TT(pp[:, ZIM, :], pp[:, ZIM, :], pp[:, TMP, :], ALU.subtract, PPK, PPK)
    TT(pp[:, ZIM, :], pp[:, ZIM, :], pp[:, DEN, :], ALU.mult, PPK, PPK)
    s5b = tmpTok.bufs[0][:].rearrange("p (m r c) -> p m r c", m=16, r=2)
    DMA("act", s5b, small["s5b"], "ld_small", [], [tmpTok.keys[0]])
    bbn = tmpTok.bufs[1][:].rearrange("p (r m c) -> p r m c", r=2, m=16)
    tb = tmpTok.bufs[2][:].rearrange("p (r m c) -> p r m c", r=2, m=16)
    zre_b = pp[:, ZRE, :].unsqueeze(2).to_broadcast([128, 16, 16])
    zim_b = pp[:, ZIM, :].unsqueeze(2).to_broadcast([128, 16, 16])
    TT(tb[:, 0], s5b[:, :, 0, :], zre_b, ALU.mult, PPK + [tmpTok.keys[0]], [tmpTok.keys[2]])
    TT(tb[:, 1], s5b[:, :, 1, :], zim_b, ALU.mult, PPK + [tmpTok.keys[0]], [tmpTok.keys[2]])
    TT(bbn[:, 0], tb[:, 0], tb[:, 1], ALU.subtract, [tmpTok.keys[2]], [tmpTok.keys[1]])
    TT(tb[:, 0], s5b[:, :, 1, :], zre_b, ALU.mult, PPK + [tmpTok.keys[0]], [tmpTok.keys[2]])
    TT(tb[:, 1], s5b[:, :, 0, :], zim_b, ALU.mult, PPK + [tmpTok.keys[0]], [tmpTok.keys[2]])
    TT(bbn[:, 1], tb[:, 0], tb[:, 1], ALU.add, [tmpTok.keys[2]], [tmpTok.keys[1]])
    maskB = sb("maskB", [128, 16, 8])
    load(maskB[:], cd["maskB"], "maskB")
    maskC = sb("maskC", [128, 4, 2, 2])
    load(maskC[:], cd["maskC"], "maskC")
    s5c = tmpTok.bufs[3][:].rearrange("p (a r q) -> p a r q", a=4, r=2)
    DMA("act", s5c, small["s5c"], "ld_small", [], [tmpTok.keys[3]])
    wide = DB(nc, "wide", [128, 128], BF16, 2)
    for ri in range(2):
        for m in range(16):
            wd, wk = wide.next()
            TT(wd[:].rearrange("p (g c) -> p g c", g=8), bbn[:, ri, m, :].unsqueeze(1).to_broadcast([128, 8, 16]),
               maskB[:, m, :].unsqueeze(2).to_broadcast([128, 8, 16]), ALU.mult, [tmpTok.keys[1], "maskB"], [wk])
            ps, pk = PS.half()
            psb = ps.bitcast(BF16)
            TR(psb[:, 0:128], wd[:], ident[:], [wk, "ident"], pk)
            CP(Bblk[:, ri, m, :], psb[:, 0:128], pk, ["Bblk"], eng="act")
    for ri in range(2):
        for m in range(16):
            wd, wk = wide.next()
            TT(wd[:].rearrange("p (g q) -> p g q", g=2), s5c[:, m // 4, ri, :].unsqueeze(1).to_broadcast([128, 2, 64]),
               maskC[:, m % 4, :, ri].unsqueeze(2).to_broadcast([128, 2, 64]), ALU.mult, [tmpTok.keys[3], "maskC"], [wk])
            ps, pk = PS.half()
            psb = ps.bitcast(BF16)
            TR(psb[:, 0:128], wd[:], ident[:], [wk, "ident"], pk)
            CP(Cblk[:, ri, m, :], psb[:, 0:128], pk, ["Cblk"], eng="act")

    cast_engs = ["dve", "act", "pool"]
    ci = 0
    for n, R, C in (wdefs if KPHASE != "pro" else []):
        for rt in range(R // 128):
            for c0 in range(0, C, 2048):
                w = min(2048, C - c0)
                s = wslot[0]
                wslot[0] = (s + 1) % NSLOT
                stg = Wr[s][:].bitcast(F32)
                DMA("sp", stg[:, 0:w], wsrc[n][rt * 128:(rt + 1) * 128, c0:c0 + w], f"w{s}", [], [f"W{s}"])
                bi = ci % 4
                cb = big[:, bi * 8:(bi + 1) * 8, :].rearrange("p a b -> p (a b)")
                ck = [BIGK(bi * 8 + t_) for t_ in range(8)]
                CP(cb[:, 0:w], stg[:, 0:w], [f"W{s}"], ck, eng=cast_engs[ci % 3])
                ci += 1
                if n == "glu_w":
                    dst = wscr[n][rt * 128:(rt + 1) * 128, c0:c0 + w]
                    srcv = cb[:, 0:w]
                else:
                    dst = wscr[n][rt // 8, c0 // 512:(c0 + w) // 512, :, rt % 8, :].rearrange("b p c -> p b c")
                    srcv = cb[:, 0:w].rearrange("p (b c) -> p b c", c=512)
                DMA("pool", dst, srcv, f"st_scr{bi}", ck, [f"scr_{n}.{rt}.{c0 // 2048}"])
    if KPHASE != "pro":
      DMA("sp", gluW[:], wscr["glu_w"].rearrange("(kt p) c -> p kt c", p=128), "ld_glu",
        [f"scr_glu_w.{rt}.0" for rt in range(4)], ["gluW"])

    def load_w(name, k0, c0, ncols=512, nk=8):
        s = wslot[0]
        wslot[0] = (s + 1) % NSLOT
        view = Wr[s][:].rearrange("p (k c) -> p k c", k=8)
        assert nk == 8 and ncols == 512 and k0 % 1024 == 0 and c0 % 512 == 0
        DMA("sp", view[:, 0:nk, 0:ncols], wscr[name][k0 // 1024, c0 // 512],
            f"w{s}", [f"scr_{name}.{k0 // 128 + i_}.{c0 // 2048}" for i_ in range(nk)], [f"W{s}"])
        return view, f"W{s}"

    def rstd_from_ss(ss_ap, rows, dim, ck):
        k = ss_ap.shape[1]
        ACT(ss_ap, ss_ap, AF.Sqrt, [ck, "epsc"], [ck], scale=1.0 / dim, bias=epsc[0:rows, 0:1])
        S.op("dve", lambda e: e.reciprocal(out=ss_ap, in_=ss_ap), reads=[ck], writes=[ck])

    def to_feature(src_tok, rows, ntile, dst, dst_f0, tok0, gain_ap, rkeys, wkeys_fn):
        for q0 in range(0, ntile, 4):
            nq = min(4, ntile - q0)
            ps, pk = PS.half()
            psb = ps.bitcast(BF16)
            for i in range(nq):
                TR(psb[:, i * 128:i * 128 + rows], src_tok[0:rows, (q0 + i) * 128:(q0 + i + 1) * 128],
                   ident[0:rows, 0:rows], rkeys + ["ident"], pk)
            src = psb.rearrange("p (k t) -> p k t", k=4)[:, 0:nq, 0:rows]
            dsl = dst[:, dst_f0 + q0:dst_f0 + q0 + nq, tok0:tok0 + rows]
            wk = [wkeys_fn(dst_f0 + q0 + i) for i in range(nq)]
            if gain_ap is None:
                CP(dsl, src, pk, wk, eng="act")
            else:
                g = gain_ap[:, q0:q0 + nq].unsqueeze(2).to_broadcast([128, nq, rows])
                TT(dsl, src, g, ALU.mult, pk + ["pgT", "gnT"], wk)

    def prenorm(Hc, HK, tiles, gcol):
        st = []
        for (j, rows) in tiles:
            cl, ck = colsA.next()
            ACT(junk_t[0:rows, :], Hc[0:rows, j, :], AF.Square, [HK], [ck, "junk_pre"], accum_out=cl[0:rows, 0:1])
            st.append((j, rows, cl, ck))
        for (j, rows, cl, ck) in st:
            ACT(cl[0:rows, 0:1], cl[0:rows, 0:1], AF.Sqrt, [ck, "epsc"], [ck], scale=1.0 / D, bias=epsc[0:rows, 0:1])
        hbs = []
        for (j, rows, cl, ck) in st:
            S.op("dve", lambda e, ap=cl[0:rows, 0:1]: e.reciprocal(out=ap, in_=ap), reads=[ck], writes=[ck])
            hb, hk = ya.next()
            TS(hb[0:rows, :], Hc[0:rows, j, :], cl[0:rows, 0:1], None, ALU.mult, None, [HK, ck], [hk])
            hbs.append((hb, hk))
        for (j, rows, cl, ck), (hb, hk) in zip(st, hbs):
            to_feature(hb, rows, 8, hnT, 0, j * 128, pgT[:, gcol:gcol + 8], [hk], lambda f: HNK)

    def postnorm_add(Hc, HK, j, rows, psA, pkA, psB, pkB, gtab, gk):
        cl, ck = colsA.next()
        jb, jk = junk_t, None
        ACT(jb[0:rows, 0:512], psA[0:rows, :], AF.Square, pkA, [ck], accum_out=cl[0:rows, 0:1])
        ACT(jb[0:rows, 512:1024], psB[0:rows, :], AF.Square, pkB, [ck], accum_out=cl[0:rows, 1:2])
        TT(cl[0:rows, 0:1], cl[0:rows, 0:1], cl[0:rows, 1:2], ALU.add, [ck], [ck])
        rstd_from_ss(cl[0:rows, 0:1], rows, D, ck)
        for (ps, pk, c0) in ((psA, pkA, 0), (psB, pkB, 512)):
            tb_, tk = tmpTok.next()
            STT(tb_[0:rows, :], ps[0:rows, :], cl[0:rows, 0:1], gtab[0:rows, c0:c0 + 512], ALU.mult, ALU.mult,
                pk + [ck, gk], [tk])
            TT(Hc[0:rows, j, c0:c0 + 512], Hc[0:rows, j, c0:c0 + 512], tb_[0:rows, :], ALU.add, [HK, tk], [HK],
               eng="pool")

    def proj_token_major(wname, F, srcbuf, src_key_fn, tiles, consume, hook=None):
        banks = {}
        live = set()
        for (j, rows) in tiles:
            a_ = PS.full()
            live.add(PS.last_full)
            b_ = PS.full()
            live.add(PS.last_full)
            banks[j] = (a_, b_)
        nunit = F // 8
        for cc in range(2):
            for u in range(nunit):
                wv, wk = load_w(wname, u * 1024, cc * 512)
                for (j, rows) in tiles:
                    ps, pk = banks[j][cc]
                    for fi in range(8):
                        f = u * 8 + fi
                        MM(ps[0:rows, :], srcbuf[:, f, j * 128:j * 128 + rows], wv[:, fi, :], f == 0, f == F - 1,
                           [src_key_fn(f), wk], pk)
        if hook is not None:
            PS.live = live
            hook()
            PS.live = set()
        for (j, rows) in tiles:
            (psA, pkA), (psB, pkB) = banks[j]
            consume(j, rows, psA, pkA, psB, pkB)

    def load_postg(l):
        for i in range(2):
            DMA("pool", postg[i][:], small["postg"][l * 2 + i:l * 2 + i + 1, :].partition_broadcast(128), "ld_postg",
                [], [f"postg{i}"])

    def mlp(l, Hc, HK, tiles, n, hook=None):
        prenorm(Hc, HK, tiles, (l * 2 + 1) * 8)
        wn1, wn2 = f"w1_{l}", f"w2_{l}"
        for blk in range(8):
            wv, wk = load_w(wn1, 0, blk * 512)
            for ft in range(4):
                f = blk * 4 + ft
                ps, pk = PS.half()
                for kt in range(8):
                    MM(ps[:, 0:n], wv[:, kt, ft * 128:(ft + 1) * 128], hnT[:, kt, 0:n], kt == 0, kt == 7, [HNK, wk], pk)
                tf, tk = f32a.next()
                ACT(tf[:, 0:n], ps[:, 0:n], AF.Relu, pk, [tk])
                TT(big[:, f, 0:n], tf[:, 0:n], tf[:, 0:n], ALU.mult, [tk], [BIGK(f)], eng=ew())
        proj_token_major(wn2, 32, big, BIGK, tiles,
                         lambda j, rows, a, ak, b, bk: postnorm_add(Hc, HK, j, rows, a, ak, b, bk, postg[1], "postg1"),
                         hook=hook)

    def layer0_mixer(Hc, HK, tiles, n, meta_group, rope_ap, rope_key, do_prenorm=True):
        if do_prenorm:
            prenorm(Hc, HK, tiles, 0)
        for qk in range(2):
            wv, wk = load_w("w_in_ab", 0, qk * 512)
            for (j, rows) in tiles:
                ps, pk = PS.full()
                for kt in range(8):
                    MM(ps[0:rows, :], hnT[:, kt, j * 128:j * 128 + rows], wv[:, kt, :], kt == 0, kt == 7, [HNK, wk], pk)
                x3 = ps.rearrange("p (h d) -> p h d", h=4)
                x1 = x3[0:rows, :, 0:64]
                x2 = x3[0:rows, :, 64:128]
                cosb = rope_ap[0:rows, j, 0:64].unsqueeze(1).to_broadcast([rows, 4, 64])
                sinb = rope_ap[0:rows, j, 64:128].unsqueeze(1).to_broadcast([rows, 4, 64])
                t1, k1 = tmpTok.next()
                t2, k2 = tmpTok.next()
                a1 = t1[0:rows, 0:256].rearrange("p (h d) -> p h d", h=4)
                a2 = t2[0:rows, 0:256].rearrange("p (h d) -> p h d", h=4)
                b1 = t1[0:rows, 256:512].rearrange("p (h d) -> p h d", h=4)
                b2 = t2[0:rows, 256:512].rearrange("p (h d) -> p h d", h=4)
                TT(a1, x1, cosb, ALU.mult, pk + [rope_key], [k1])
                TT(a2, x2, sinb, ALU.mult, pk + [rope_key], [k2])
                TT(b1, x1, sinb, ALU.mult, pk + [rope_key], [k1])
                TT(b2, x2, cosb, ALU.mult, pk + [rope_key], [k2])
                TT(qkr[0:rows, j, qk, :, 0:64], a1, a2, ALU.subtract, [k1, k2], [f"qkr{j}"], eng="pool")
                TT(qkr[0:rows, j, qk, :, 64:128], b1, b2, ALU.add, [k1, k2], [f"qkr{j}"], eng="pool")
        for vb in range(2):
            wv, wk = load_w("w_in_ab", 0, 1024 + vb * 512)
            for (j, rows) in tiles:
                ps, pk = PS.full()
                for kt in range(8):
                    MM(ps[0:rows, :], hnT[:, kt, j * 128:j * 128 + rows], wv[:, kt, :], kt == 0, kt == 7, [HNK, wk], pk)
                CP(v_tok[0:rows, j, vb * 512:(vb + 1) * 512], ps[0:rows, :], pk, [f"v{j}"], eng="act")
        for gb in range(2):
            wv, wk = load_w("w_in_ab", 0, 2048 + gb * 512)
            for (j, rows) in tiles:
                ps, pk = PS.full()
                for kt in range(8):
                    MM(ps[0:rows, :], hnT[:, kt, j * 128:j * 128 + rows], wv[:, kt, :], kt == 0, kt == 7, [HNK, wk], pk)
                ACT(sg_tok[0:rows, j, gb * 512:(gb + 1) * 512], ps[0:rows, :], AF.Silu, pk, [f"sg{j}"])
        for (j, rows) in tiles:
            qt, qtk = qT.next()
            kt_, ktk = kT.next()
            for (dst, dk_, qk) in ((qt, qtk, 0), (kt_, ktk, 1)):
                ps, pk = PS.half()
                psb = ps.bitcast(BF16)
                for h in range(4):
                    TR(psb[:, h * 128:h * 128 + rows], qkr[0:rows, j, qk, h, :], ident[0:rows, 0:rows],
                       [f"qkr{j}", "ident"], pk)
                CP(dst[:, :, 0:rows], psb.rearrange("p (h t) -> p h t", h=4)[:, :, 0:rows], pk, [dk_], eng="act")
            ps, pk = PS.full()
            p3 = ps.rearrange("p (h t) -> p h t", h=4)
            for h in range(4):
                MM(p3[0:rows, h, 0:rows], kt_[:, h, 0:rows], qt[:, h, 0:rows], True, True, [qtk, ktk], pk)
            st_, stk = sTb.next()
            TT(st_[0:rows, :, 0:rows], p3[0:rows, :, 0:rows], retD[0:rows, :, 0:rows], ALU.mult, pk + ["retD"], [stk])
            ob, obk = o_sb.next()
            for hp in range(2):
                ps, pk = PS.full()
                for hh in range(2):
                    h = hp * 2 + hh
                    osl = ps[0:rows, hh * 256:(hh + 1) * 256]
                    MM(osl, st_[0:rows, h, 0:rows], v_tok[0:rows, j, h * 256:(h + 1) * 256], True, False,
                       [stk, f"v{j}"], pk)
                    MM(osl, qt[:, h, 0:rows], RSb[:, h, :], False, True, [qtk, "RSb"], pk)
                TT(ob[0:rows, hp * 2:hp * 2 + 2, :], ps.rearrange("p (h e) -> p h e", h=2)[0:rows],
                   retS[0:rows, hp * 2:hp * 2 + 2].unsqueeze(2).to_broadcast([rows, 2, 256]), ALU.mult,
                   pk + ["retS"], [obk])
            kd, kdk = kdec.next()
            kdcol = 8 if meta_group else 4
            TT(kd[0:rows, :, :], qkr[0:rows, j, 1, :, :],
               retS[0:rows, kdcol:kdcol + 4].unsqueeze(2).to_broadcast([rows, 4, 128]), ALU.mult,
               [f"qkr{j}", "retS"], [kdk], eng="pool")
            for hp in range(2):
                ps, pk = PS.full()
                for hh in range(2):
                    h = hp * 2 + hh
                    MM(ps[:, hh * 256:(hh + 1) * 256], kd[0:rows, h, :], v_tok[0:rows, j, h * 256:(h + 1) * 256],
                       True, True, [kdk, f"v{j}"], pk)
                for hh in range(2):
                    h = hp * 2 + hh
                    STT(RS[:, h, :], RS[:, h, :], float(GAMMA[h] ** rows), ps[:, hh * 256:(hh + 1) * 256],
                        ALU.mult, ALU.add, ["RS"] + pk, ["RS"])
            CP(RSb[:], RS[:], ["RS"], ["RSb"], eng="act")
            cl, ck = colsA.next()
            S.op("dve", lambda e, ob=ob, cl=cl, rows=rows: e.reduce_sum(out=cl[0:rows, 0:4], in_=ob[0:rows], axis=AX.X),
                 reads=[obk], writes=[ck])
            TS(cl[0:rows, 0:4], cl[0:rows, 0:4], -1.0 / 256, None, ALU.mult, None, [ck], [ck])
            TT(ob[0:rows], ob[0:rows], cl[0:rows, 0:4].unsqueeze(2).to_broadcast([rows, 4, 256]), ALU.add,
               [obk, ck], [obk])
            jb, jk = junk_t, None
            for h in range(4):
                ACT(jb[0:rows, h * 256:(h + 1) * 256], ob[0:rows, h, :], AF.Square, [obk], [ck],
                    accum_out=cl[0:rows, 4 + h:5 + h])
            rstd_from_ss(cl[0:rows, 4:8], rows, 256, ck)
            yb_, ybk = ya.next()
            for h in range(4):
                STT(yb_[0:rows, h * 256:(h + 1) * 256], ob[0:rows, h, :], cl[0:rows, 4 + h:5 + h],
                    sg_tok[0:rows, j, h * 256:(h + 1) * 256], ALU.mult, ALU.mult, [obk, ck, f"sg{j}"], [ybk])
            to_feature(yb_, rows, 8, big, 0, j * 128, gnT[:, 0:8], [ybk], BIGK)
        for half in range(2):
            hs_list = []
            wv, wk = load_w("w_in_ab", 0, 3072 + half * 512)
            L = {}
            for ct in range(4):
                c = half * 4 + ct
                ps, pk = PS.half()
                for kt in range(8):
                    MM(ps[:, 0:n], wv[:, kt, ct * 128:(ct + 1) * 128], hnT[:, kt, 0:n], kt == 0, kt == 7, [HNK, wk], pk)
                xp, xk = f32a.next()
                CP(xp[:, 0:3], lru_carry[:, c, :], ["lru_carry"], [xk], eng="pool")
                CP(xp[:, 3:3 + n], ps[:, 0:n], pk, [xk], eng="act")
                L[ct] = dict(xp=xp, xk=xk)
            for ct in range(4):
                c = half * 4 + ct
                xp, xk = L[ct]["xp"], L[ct]["xk"]
                acc, ak = f32a.next()
                P = lambda i, c=c: lru_s[:, c, i:i + 1]
                TS(acc[:, 0:n], xp[:, 0:n], P(4), P(3), ALU.mult, ALU.add, [xk, "lru_s"], [ak])
                STT(acc[:, 0:n], xp[:, 1:1 + n], P(5), acc[:, 0:n], ALU.mult, ALU.add, [xk, ak, "lru_s"], [ak])
                STT(acc[:, 0:n], xp[:, 2:2 + n], P(6), acc[:, 0:n], ALU.mult, ALU.add, [xk, ak, "lru_s"], [ak])
                STT(acc[:, 0:n], xp[:, 3:3 + n], P(7), acc[:, 0:n], ALU.mult, ALU.add, [xk, ak, "lru_s"], [ak])
                CP(lru_carry[:, c, :], xp[:, n:n + 3], [xk], ["lru_carry"], eng="pool")
                L[ct].update(acc=acc, ak=ak)
            for ct in range(4):
                xb_, xbk = bf16a.next()
                CP(xb_[:, 0:n], L[ct]["acc"][:, 0:n], [L[ct]["ak"]], [xbk], eng="act")
                L[ct].update(xb=xb_, xbk=xbk)
            for ct in range(4):
                c = half * 4 + ct
                gates = []
                for wi in range(2):
                    ps2, pk2 = PS.half()
                    MM(ps2[:, 0:n], Wab[:, wi, c, :], L[ct]["xb"][:, 0:n], True, True, ["Wab", L[ct]["xbk"]], pk2)
                    gt_, gk_ = f32a.next()
                    ACT(gt_[:, 0:n], ps2[:, 0:n], AF.Sigmoid, pk2 + ["lru_s"], [gk_], bias=lru_s[:, c, 1 + wi:2 + wi])
                    gates.append((gt_, gk_))
                L[ct].update(gates=gates)
            for ct in range(4):
                c = half * 4 + ct
                (rg, rk), (ig, ik) = L[ct]["gates"]
                ACT(rg[:, 0:n], rg[:, 0:n], AF.Exp, [rk, "lru_sp8"], [rk], scale=lru_sp8[:, c:c + 1])
            for ct in range(4):
                (rg, rk), (ig, ik) = L[ct]["gates"]
                acc, ak = L[ct]["acc"], L[ct]["ak"]
                TT(ig[:, 0:n], ig[:, 0:n], acc[:, 0:n], ALU.mult, [ik, ak], [ik])
                TT(acc[:, 0:n], rg[:, 0:n], rg[:, 0:n], ALU.mult, [rk], [ak], eng="pool")
            for ct in range(4):
                acc, ak = L[ct]["acc"], L[ct]["ak"]
                ACT(acc[:, 0:n], acc[:, 0:n], AF.Sqrt, [ak, "epsc"], [ak], scale=-1.0, bias=epsc[:, 1:2])
            for ct in range(4):
                c = half * 4 + ct
                (rg, rk), (ig, ik) = L[ct]["gates"]
                acc, ak = L[ct]["acc"], L[ct]["ak"]
                TT(ig[:, 0:n], ig[:, 0:n], acc[:, 0:n], ALU.mult, [ik, ak], [ik])
                hb_, hbk = hsb.next()
                S.op("dve", lambda e, hb_=hb_, rg=rg, ig=ig, c=c: e.tensor_tensor_scan(
                    out=hb_[:, 0:n], data0=rg[:, 0:n], data1=ig[:, 0:n], initial=lru_h[:, c:c + 1],
                    op0=ALU.mult, op1=ALU.add), reads=[rk, ik, "lru_h"], writes=[hbk])
                CP(lru_h[:, c:c + 1], hb_[:, n - 1:n], [hbk], ["lru_h"], eng="pool")
                hs_list.append((hb_, hbk))
            wv, wk = load_w("w_in_ab", 0, 4096 + half * 512)
            for ct in range(4):
                c = half * 4 + ct
                ps, pk = PS.half()
                for kt in range(8):
                    MM(ps[:, 0:n], wv[:, kt, ct * 128:(ct + 1) * 128], hnT[:, kt, 0:n], kt == 0, kt == 7, [HNK, wk], pk)
                ge, gek = f32a.next()
                ACT(ge[:, 0:n], ps[:, 0:n], AF.Gelu_apprx_tanh, pk, [gek])
                hb_, hbk = hs_list[ct]
                TT(big[:, 8 + c, 0:n], ge[:, 0:n], hb_[:, 0:n], ALU.mult, [gek, hbk], [BIGK(8 + c)], eng=ew())
        proj_token_major("w_out_ab", 16, big, BIGK, tiles,
                         lambda j, rows, a, ak, b, bk: postnorm_add(Hc, HK, j, rows, a, ak, b, bk, postg[0], "postg0"))

    def layer1_mixer(Hc, HK, tiles, n, meta_group, prev_n):
        prenorm(Hc, HK, tiles, 16)
        wv, wk = load_w("w_in_cd", 0, 0)
        for ct in range(4):
            ps, pk = PS.half()
            for kt in range(8):
                MM(ps[:, 0:n], wv[:, kt, ct * 128:(ct + 1) * 128], hnT[:, kt, 0:n], kt == 0, kt == 7, [HNK, wk], pk)
            CP(uf[:, ct, 0:n], ps[:, 0:n], pk, [UFK], eng="act")
            CP(ub[:, ct, 0:n], ps[:, 0:n], pk, [f"ub{ct}"])
        if prev_n is not None:
            fr = 0 if prev_n == T else 1
            cb_ = s5_bnd[:, fr, 0, :]
            sb_ = s5_bnd[:, fr, 1, :]
            q1, qk1 = colsA.next()
            q2, qk2 = colsA.next()
            TT(q1[:, 0:16], s5_st[:, 0, :], cb_, ALU.mult, ["s5_st", "s5_bnd"], [qk1])
            TT(q2[:, 0:16], s5_st[:, 1, :], sb_, ALU.mult, ["s5_st", "s5_bnd"], [qk2])
            TT(s5_ini[:, 0, :], q1[:, 0:16], q2[:, 0:16], ALU.subtract, [qk1, qk2], ["s5_ini"])
            q3, qk3 = colsA.next()
            q4, qk4 = colsA.next()
            TT(q3[:, 0:16], s5_st[:, 0, :], sb_, ALU.mult, ["s5_st", "s5_bnd"], [qk3])
            TT(q4[:, 0:16], s5_st[:, 1, :], cb_, ALU.mult, ["s5_st", "s5_bnd"], [qk4])
            TT(s5_ini[:, 1, :], q3[:, 0:16], q4[:, 0:16], ALU.add, [qk3, qk4], ["s5_ini"])
        else:
            MEMSET(s5_ini[:], 0.0, ["s5_ini"])
        for ct in range(4):
            ms = [ct * 4 + q for q in range(4)]
            W = {}
            for m in ms:
                psr, pkr = PS.half()
                MM(psr[:, 0:n], Bblk[:, 0, m, :], ub[:, ct, 0:n], True, True, ["Bblk", f"ub{ct}"], pkr)
                psi, pki = PS.half()
                MM(psi[:, 0:n], Bblk[:, 1, m, :], ub[:, ct, 0:n], True, True, ["Bblk", f"ub{ct}"], pki)
                W[m] = dict(psr=psr, pkr=pkr, psi=psi, pki=pki, t=[f32a.next() for _ in range(4)],
                            Rc=Rcs[:, 0, m, 0:n], Rs=Rcs[:, 1, m, 0:n])
            for m in ms:
                w_ = W[m]
                (t1, k1), (t2, k2), (t3, k3), (t4, k4) = w_["t"]
                TT(t1[:, 0:n], w_["psr"][:, 0:n], w_["Rc"], ALU.mult, w_["pkr"] + ["Rcs"], [k1])
                TT(t4[:, 0:n], w_["psr"][:, 0:n], w_["Rs"], ALU.mult, w_["pkr"] + ["Rcs"], [k4])
                TT(t2[:, 0:n], w_["psi"][:, 0:n], w_["Rs"], ALU.mult, w_["pki"] + ["Rcs"], [k2])
                TT(t3[:, 0:n], w_["psi"][:, 0:n], w_["Rc"], ALU.mult, w_["pki"] + ["Rcs"], [k3])
            for m in ms:
                (t1, k1), (t2, k2), (t3, k3), (t4, k4) = W[m]["t"]
                TT(t1[:, 0:n], t1[:, 0:n], t2[:, 0:n], ALU.add, [k1, k2], [k1], eng="pool")
                TT(t3[:, 0:n], t3[:, 0:n], t4[:, 0:n], ALU.subtract, [k3, k4], [k3], eng="pool")
            for m in ms:
                (t1, k1), (t2, k2), (t3, k3), (t4, k4) = W[m]["t"]
                magb = s5_mag[:, m:m + 1].to_broadcast([128, n])
                S.op("dve", lambda e, t2=t2, t1=t1, m=m, magb=magb: e.tensor_tensor_scan(
                    out=t2[:, 0:n], data0=magb, data1=t1[:, 0:n], initial=s5_ini[:, 0, m:m + 1],
                    op0=ALU.mult, op1=ALU.add), reads=[k1, "s5_mag", "s5_ini"], writes=[k2])
                S.op("dve", lambda e, t4=t4, t3=t3, m=m, magb=magb: e.tensor_tensor_scan(
                    out=t4[:, 0:n], data0=magb, data1=t3[:, 0:n], initial=s5_ini[:, 1, m:m + 1],
                    op0=ALU.mult, op1=ALU.add), reads=[k3, "s5_mag", "s5_ini"], writes=[k4])
            for m in ms:
                (t1, k1), (t2, k2), (t3, k3), (t4, k4) = W[m]["t"]
                CP(s5_st[:, 0, m:m + 1], t2[:, n - 1:n], [k2], ["s5_st"], eng="pool")
                CP(s5_st[:, 1, m:m + 1], t4[:, n - 1:n], [k4], ["s5_st"], eng="pool")
            for m in ms:
                w_ = W[m]
                (t1, k1), (t2, k2), (t3, k3), (t4, k4) = w_["t"]
                hr, hrk = s5h.next()
                hi, hik = s5h.next()
                b1 = t1[:].bitcast(BF16)
                b3 = t3[:].bitcast(BF16)
                TT(hr[:, 0:n], t2[:, 0:n], w_["Rc"], ALU.mult, [k2, "Rcs"], [hrk])
                STT(b1[:, 0:n], t4[:, 0:n], -1.0, w_["Rs"], ALU.mult, ALU.mult, [k4, "Rcs"], [k1])
                TT(hi[:, 0:n], t2[:, 0:n], w_["Rs"], ALU.mult, [k2, "Rcs"], [hik])
                TT(b3[:, 0:n], t4[:, 0:n], w_["Rc"], ALU.mult, [k4, "Rcs"], [k3])
                w_["prods"] = [(0, hr, hrk), (0, b1, k1), (1, hi, hik), (1, b3, k3)]
            psy, pky = PS.half()
            for q, m in enumerate(ms):
                for pi, (ri, buf, key) in enumerate(W[m]["prods"]):
                    MM(psy[:, 0:n], Cblk[:, ri, m, :], buf[:, 0:n], q == 0 and pi == 0, q == 3 and pi == 3,
                       ["Cblk", key], pky)
            yt_, ytk = f32a.next()
            STT(yt_[:, 0:n], uf[:, ct, 0:n], s5dg[:, ct:ct + 1], psy[:, 0:n], ALU.mult, ALU.add,
                [UFK, "s5dg"] + pky, [ytk])
            ACT(ygf[:, ct, 0:n], yt_[:, 0:n], AF.Gelu_apprx_tanh, [ytk], [YGK])
            CP(ygb[:, ct, 0:n], ygf[:, ct, 0:n], [YGK], [f"ygb{ct}"], eng="pool")
        for co in range(4):
            ps, pk = PS.half()
            for ci_ in range(4):
                MM(ps[:, 0:n], gluW[:, ci_, co * 128:(co + 1) * 128], ygb[:, ci_, 0:n], ci_ == 0, ci_ == 3,
                   ["gluW", f"ygb{ci_}"], pk)
            sgm, sgk = f32a.next()
            ACT(sgm[:, 0:n], ps[:, 0:n], AF.Sigmoid, pk + ["s5dg"], [sgk], bias=s5dg[:, 4 + co:5 + co])
            TT(big[:, co, 0:n], ygf[:, co, 0:n], sgm[:, 0:n], ALU.mult, [YGK, sgk], [BIGK(co)])
        wq, wqk = load_w("w_in_cd", 0, 512)
        wf, wfk = load_w("w_in_cd", 0, 1024)
        bw = 16 if meta_group else 64
        htiles = [(0, NMETA)] if meta_group else [(c_, 64) for c_ in range(n // 64)]
        nblk = len(htiles)
        mid = bw // 2 - 1
        G = {}
        for h in range(4):
            psf, pkf = PS.half()
            for kt in range(8):
                MM(psf[:, 0:n], wf[:, kt, h * 128:(h + 1) * 128], hnT[:, kt, 0:n], kt == 0, kt == 7, [HNK, wfk], pkf)
            G[h] = dict(psf=psf, pkf=pkf)
        for h in range(4):
            ff, fk = f32a.next()
            ACT(ff[:, 0:n], G[h]["psf"][:, 0:n], AF.Sigmoid, G[h]["pkf"], [fk])
            TS(ff[:, 0:n], ff[:, 0:n], hg_lb[:, 4 + h:5 + h], hg_lb[:, h:h + 1], ALU.mult, ALU.add, [fk, "hg_lb"], [fk])
            G[h].update(ff=ff, fk=fk)
        for h in range(4):
            ff, fk = G[h]["ff"], G[h]["fk"]
            lf, lk = f32a.next()
            ACT(lf[:, 0:n], ff[:, 0:n], AF.Ln, [fk], [lk])
            TS(ff[:, 0:n], ff[:, 0:n], -1.0, 1.0, ALU.mult, ALU.add, [fk], [fk], eng="pool")
            G[h].update(lf=lf, lk=lk)
        for h in range(4):
            lf, lk = G[h]["lf"], G[h]["lk"]
            cum, cmk = f32a.next()
            S.op("dve", lambda e, cum=cum, lf=lf: e.tensor_tensor_scan(
                out=cum[:, 0:n], data0=restart[:, 0:n], data1=lf[:, 0:n], initial=0.0,
                op0=ALU.mult, op1=ALU.add), reads=[lk, "restart"], writes=[cmk])
            G[h].update(cum=cum, cmk=cmk)
        for h in range(4):
            cum, cmk = G[h]["cum"], G[h]["cmk"]
            c3 = cum[:, 0:n].rearrange("p (j t) -> p j t", t=bw)
            ACT(hgE[:, 0, h, 0:nblk], c3[:, :, mid], AF.Exp, [cmk], ["hgE"])
            ACT(hgE[:, 1, h, 0:nblk], c3[:, :, bw - 1], AF.Exp, [cmk], ["hgE"])
            cm, cmk2 = f32a.next()
            cm3 = cm[:, 0:n].rearrange("p (j t) -> p j t", t=bw)
            TT(cm3, c3, c3[:, :, mid:mid + 1].to_broadcast([128, nblk, bw]), ALU.subtract, [cmk], [cmk2])
            G[h].update(cm=cm, cmk2=cmk2)
        for h in range(4):
            lf, lk, cm, cmk2 = G[h]["lf"], G[h]["lk"], G[h]["cm"], G[h]["cmk2"]
            ACT(lf[:, 0:n], cm[:, 0:n], AF.Exp, [cmk2], [lk])
            ACT(cm[:, 0:n], cm[:, 0:n], AF.Exp, [cmk2], [cmk2], scale=-1.0)
        for h in range(4):
            psq, pkq = PS.half()
            for kt in range(8):
                MM(psq[:, 0:n], wq[:, kt, h * 128:(h + 1) * 128], hnT[:, kt, 0:n], kt == 0, kt == 7, [HNK, wqk], pkq)
            G[h].update(psq=psq, pkq=pkq)
        for h in range(4):
            ff, fk, lf, lk, cm, cmk2 = G[h]["ff"], G[h]["fk"], G[h]["lf"], G[h]["lk"], G[h]["cm"], G[h]["cmk2"]
            l3 = lf[:, 0:n].rearrange("p (j t) -> p j t", t=bw)
            CP(hgE[:, 2, h, 0:nblk], l3[:, :, bw - 1], [lk], ["hgE"], eng="pool")
            TT(hqT[:, h, 0:n], G[h]["psq"][:, 0:n], lf[:, 0:n], ALU.mult, G[h]["pkq"] + [lk], [f"hq{h}"])
            TT(hkT[:, h, 0:n], ff[:, 0:n], cm[:, 0:n], ALU.mult, [fk, cmk2], [f"hk{h}"], eng="pool")
        iv = v_tok[:].rearrange("p j (a c) -> p (j a) c", a=2)
        gv = sg_tok[:].rearrange("p j (a c) -> p (j a) c", a=2)
        IVK = lambda c_: f"v{c_ // 2}"
        GVK = lambda c_: f"sg{c_ // 2}"
        wi_, wik = load_w("w_in_cd", 0, 1536)
        for (c_, rows) in htiles:
            ps, pk = PS.full()
            for kt in range(8):
                MM(ps[0:rows, :], hnT[:, kt, c_ * 64:c_ * 64 + rows], wi_[:, kt, :], kt == 0, kt == 7, [HNK, wik], pk)
            CP(iv[0:rows, c_, :], ps[0:rows, :], pk, [IVK(c_)], eng="act")
        wg_, wgk = load_w("w_in_cd", 0, 2048)
        for (c_, rows) in htiles:
            ps, pk = PS.full()
            for kt in range(8):
                MM(ps[0:rows, :], hnT[:, kt, c_ * 64:c_ * 64 + rows], wg_[:, kt, :], kt == 0, kt == 7, [HNK, wgk], pk)
            ACT(gv[0:rows, c_, :], ps[0:rows, :], AF.Silu, pk, [GVK(c_)])
        HQK = [f"hq{h}" for h in range(4)]
        HKK = [f"hk{h}" for h in range(4)]
        for (j, rows) in htiles:
            t0 = j * 64
            ps, pk = PS.full()
            p3 = ps.rearrange("p (h t) -> p h t", h=4)
            for h in range(4):
                MM(p3[0:rows, h, 0:rows], hkT[:, h, t0:t0 + rows], hqT[:, h, t0:t0 + rows], True, True, HQK + HKK, pk)
            at, atk = sTb.next()
            STT(at[0:rows, :, 0:rows], p3[0:rows, :, 0:rows], 1e30,
                triu[0:rows, 0:rows].unsqueeze(1).to_broadcast([rows, 4, rows]), ALU.min, ALU.mult,
                pk + ["triu"], [atk])
            ps2, pk2 = PS.half()
            psb = ps2.bitcast(BF16)
            for h in range(4):
                TR(psb[0:rows, h * 128:(h + 1) * 128], hkT[:, h, t0:t0 + rows], ident[:, :], HKK + ["ident"], pk2)
            ktk_, ktkk = kdec.next()
            CP(ktk_[0:rows, :, :], psb.rearrange("p (h d) -> p h d", h=4)[0:rows], pk2, [ktkk], eng="act")
            TT(HSb[:], HS[:], hgE[:, 0, :, j:j + 1].to_broadcast([128, 4, 128]), ALU.mult, ["HS", "hgE"], ["HSb"])
            pso, pko = PS.full()
            o3 = pso.rearrange("p (h e) -> p h e", h=4)
            for h in range(4):
                MM(o3[0:rows, h, :], at[0:rows, h, 0:rows], iv[0:rows, j, h * 128:(h + 1) * 128], True, False,
                   [atk, IVK(j)], pko)
                MM(o3[0:rows, h, :], hqT[:, h, t0:t0 + rows], HSb[:, h, :], False, True, HQK + ["HSb"], pko)
            psk, pkk = PS.full()
            k3 = psk.rearrange("p (h e) -> p h e", h=4)
            for h in range(4):
                MM(k3[:, h, :], ktk_[0:rows, h, :], iv[0:rows, j, h * 128:(h + 1) * 128], True, True,
                   [ktkk, IVK(j)], pkk)
            ta, tak = tmpTok.next()
            TT(ta[:].rearrange("p (h e) -> p h e", h=4), k3, hgE[:, 2, :, j:j + 1].to_broadcast([128, 4, 128]),
               ALU.mult, pkk + ["hgE"], [tak])
            TT(HS[:], HS[:], hgE[:, 1, :, j:j + 1].to_broadcast([128, 4, 128]), ALU.mult, ["HS", "hgE"], ["HS"])
            TT(HS[:], HS[:], ta[:].rearrange("p (h e) -> p h e", h=4), ALU.add, ["HS", tak], ["HS"], eng="pool")
            jb, jk = junk_t, None
            cl, ck = colsA.next()
            for h in range(4):
                ACT(jb[0:rows, h * 128:(h + 1) * 128], o3[0:rows, h, :], AF.Square, pko, [ck],
                    accum_out=cl[0:rows, h:h + 1])
            rstd_from_ss(cl[0:rows, 0:4], rows, 128, ck)
            on, onk = tmpTok.next()
            TT(on[0:rows].rearrange("p (h e) -> p h e", h=4), o3[0:rows],
               cl[0:rows, 0:4].unsqueeze(2).to_broadcast([rows, 4, 128]), ALU.mult, pko + [ck], [onk])
            yb_, ybk = ya.next()
            TT(yb_[0:rows, 0:512], on[0:rows, :], gv[0:rows, j, :], ALU.mult, [onk, GVK(j)], [ybk], eng="pool")
            to_feature(yb_, rows, 4, big, 4, t0, gnT[:, 8:12], [ybk], BIGK)
        proj_token_major("w_out_cd", 8, big, BIGK, tiles,
                         lambda j, rows, a, ak, b, bk: postnorm_add(Hc, HK, j, rows, a, ak, b, bk, postg[0], "postg0"))

    last_store = None
    prev_n = None
    NGR = min(NG, KNG) if KPHASE == "all" else 0

    def group_geom(g):
        if g == 0:
            return NMETA, [(0, NMETA)], 0, 0
        f0 = (g - 1) * T
        return T, [(j, 128) for j in range(NT)], NMETA + f0, f0

    rope_bufs = {}

    def issue_loads(g):
        Hc, HK = H[g % 2], f"H{g % 2}"
        n, tiles, pos0, f0 = group_geom(g)
        rb, rkey = ropeT.next()
        rope_bufs[g] = (rb, rkey)
        if g == 0:
            DMA("pool", Hc[0:NMETA, 0, :], meta_d, "ld_x", [], [HK])
            DMA("pool", rb[0:NMETA, 0, :], cd["rope"][0:NMETA, :], "ld_rope", [], [rkey])
        else:
            DMA("pool", Hc[:, :, :], x_d[f0:f0 + T, :].rearrange("(j p) d -> p j d", p=128), "ld_x", [], [HK])
            DMA("pool", rb[:, :, :], cd["rope"][pos0:pos0 + T, :].rearrange("(j p) d -> p j d", p=128), "ld_rope",
                [], [rkey])

    if NGR > 0:
        issue_loads(0)
    for g in range(NGR):
        Hc = H[g % 2]
        HK = f"H{g % 2}"
        meta_group = g == 0
        n, tiles, pos0, f0 = group_geom(g)
        rb, rkey = rope_bufs.pop(g)

        def dbg(slot):
            if DEBUG and (g < 3):
                if meta_group:
                    DMA("pool", dbg_d[slot, 0:NMETA, :], Hc[0:NMETA, 0, :], "st_dbg", [HK], [])
                else:
                    DMA("pool", dbg_d[slot, pos0:pos0 + T, :].rearrange("(j p) d -> p j d", p=128), Hc[:, :, :],
                        "st_dbg", [HK], [])

        load_postg(0)
        layer0_mixer(Hc, HK, tiles, n, meta_group, rb, rkey, do_prenorm=(g == 0 or KSTOP < 3))
        has_next = g + 1 < NGR
        if has_next:
            issue_loads(g + 1)
        dbg(0)
        if KSTOP >= 1:
            mlp(0, Hc, HK, tiles, n)
            dbg(1)
        if KSTOP >= 2:
            load_postg(1)
            layer1_mixer(Hc, HK, tiles, n, meta_group, prev_n)
            dbg(2)
        if KSTOP >= 3:
            hook = None
            if has_next:
                nH, nHK = H[(g + 1) % 2], f"H{(g + 1) % 2}"
                ntiles = group_geom(g + 1)[1]
                hook = lambda nH=nH, nHK=nHK, ntiles=ntiles: prenorm(nH, nHK, ntiles, 0)
            mlp(1, Hc, HK, tiles, n, hook=hook)
            dbg(3)
        if not meta_group:
            last_store = DMA("pool", out_d[f0:f0 + T, :].rearrange("(j p) d -> p j d", p=128), Hc[:, :, :], "st_out",
                             [HK], [])
        prev_n = n
    fin = [d for d in [last_store, S.chan_last.get("st_dbg"), S.chan_last.get("ld_small"), S.chan_last.get("st_scr0")] if d is not None]
    S.op("sp", lambda e: e.nop(), extra_deps=fin)
    print("instr counts", {e: len(S.lists[e]) for e in ENGS}, flush=True)
    S.emit()
    S.stats = {e: max([i.val for i in S.lists[e] if i.chan is None and i.needs_inc] or [0]) for e in ENGS}
    S.stats.update({c: lst[-1].val for c, lst in S.chan_ins.items()})
    print("max sem values", S.stats, flush=True)
    return nc, consts


_CACHE = {}


def kernel(**inputs):
    if "prog" not in _CACHE:
        _CACHE["prog"] = build_program()
    nc, consts = _CACHE["prog"]
    lay = host_layouts(inputs)
    common = dict(lay)
    for n, v in consts.items():
        common["c_" + n] = v
    x = np.asarray(inputs["x"], dtype=np.float32)
    in_maps = []
    for c in range(8):
        m = dict(common)
        m["x"] = np.ascontiguousarray(x[c % 4])
        in_maps.append(m)
    res = run_bass_kernel_spmd(nc, in_maps, core_ids=list(range(8)))
    out = np.stack([np.asarray(res.results[b]["out"], dtype=np.float32) for b in range(4)], axis=0)
    if DEBUG:
        kernel.dbg = [np.asarray(res.results[b]["dbg"]) for b in range(4)]
    return out
```
